# Optimizing a Trainium2 kernel written in Bass

```python
import jax, jax.numpy as jnp
from jax import lax
import numpy as np

D_MODEL = 1024
BATCH = 4
SEQ = 8192
DEPTH = 2
DEC_BATCH = 2
DEC_SEQ = 16384
PAST_LEN = 128

GRID_W = 64
SSM_HEADS = 16
SSM_HEAD_DIM = 64
SSM_WIDTH = SSM_HEADS * SSM_HEAD_DIM
SSM_GROUPS = 2
SSM_STATE = 128
SSM_CONV = 5
SSM_CHUNK = 128
SSM_CONV_CH = SSM_WIDTH + 2 * SSM_GROUPS * SSM_STATE
GLA_HEADS = 4
GLA_DK_HEAD = 128
GLA_DV_HEAD = 256
GLA_DK = GLA_HEADS * GLA_DK_HEAD
GLA_DV = GLA_HEADS * GLA_DV_HEAD
GLA_RANK = 16
GLA_NORMALIZER = 16.0
GLA_CHUNK = 64
NA_HEADS = 16
NA_HEAD_DIM = 64
NA_WIDTH = NA_HEADS * NA_HEAD_DIM
NA_KH = 8
NA_KW = 16
D_FF = 2816
N_BRANCH = 3
EPS = 1e-6

IN_SPLITS = (SSM_WIDTH,
             SSM_CONV_CH,
             SSM_HEADS,
             SSM_HEADS,
             GLA_DK,
             GLA_DK,
             GLA_DV,
             GLA_DV,
             GLA_RANK,
             GLA_RANK,
             NA_WIDTH,
             NA_WIDTH,
             NA_WIDTH,
             N_BRANCH * D_MODEL)
IN_WIDTH = sum(IN_SPLITS)

kernel_name = "hybrid_bidir_ssd_gla_natten_encoder"


def split_cols(u):
    outs = []
    o = 0
    for w in IN_SPLITS:
        outs.append(u[..., o:o + w])
        o += w
    return outs


def rmsnorm(x, w):
    x32 = x.astype(jnp.float32)
    y = x32 * lax.rsqrt(jnp.mean(x32 * x32, axis=-1, keepdims=True) + EPS)
    return (y * w.astype(jnp.float32)).astype(x.dtype)


def swiglu(x, w_in, w_out):
    a, b = jnp.split(x @ w_in, 2, axis=-1)
    return (jax.nn.silu(a) * b) @ w_out


def flip(t):
    return jnp.flip(t, axis=1)


def dwconv_centred(u, w, b):
    y = lax.conv_general_dilated(u, w[:, None, :].astype(u.dtype), window_strides=(1,),
                                 padding=[(SSM_CONV // 2, SSM_CONV // 2)],
                                 dimension_numbers=('NWC', 'WIO', 'NWC'),
                                 feature_group_count=u.shape[-1])
    return y + b.astype(u.dtype)


def ssd_chunked(x, dt, a, Bm, Cm):
    bsz, l, h, p = x.shape
    g, n = Bm.shape[2], Bm.shape[3]
    hg = h // g
    Q = SSM_CHUNK
    c = l // Q
    xr = (x * dt[..., None]).reshape(bsz, c, Q, g, hg, p)
    cs = jnp.cumsum((dt * a).reshape(bsz, c, Q, g, hg), axis=2)
    Bc = Bm.reshape(bsz, c, Q, g, n)
    Cc = Cm.reshape(bsz, c, Q, g, n)
    causal = jnp.tril(jnp.ones((Q, Q), dtype=bool))
    seg = cs[:, :, :, None] - cs[:, :, None, :]
    Lm = jnp.exp(jnp.where(causal[:, :, None, None], seg, -jnp.inf))
    cb = jnp.einsum('bcign,bcjgn->bcijg', Cc, Bc)
    y_diag = jnp.einsum('bcijg,bcijgk,bcjgkp->bcigkp', cb, Lm, xr)
    decay_states = jnp.exp(cs[:, :, -1:] - cs)
    states = jnp.einsum('bcjgn,bcjgk,bcjgkp->bcgkpn', Bc, decay_states, xr)
    chunk_decay = jnp.exp(cs[:, :, -1])

    def step(S, inp):
        st, dec = inp
        return S * dec[..., None, None] + st, S

    S0 = jnp.zeros((bsz, g, hg, p, n), jnp.float32)
    _, S_in = lax.scan(step, S0, (jnp.moveaxis(states, 1, 0), jnp.moveaxis(chunk_decay, 1, 0)))
    S_in = jnp.moveaxis(S_in, 0, 1)
    y_off = jnp.einsum('bcign,bcgkpn,bcigk->bcigkp', Cc, S_in, jnp.exp(cs))
    return (y_diag + y_off).reshape(bsz, l, h, p)


def ssd_mixer(z, xbc, dt_f, dt_b, conv_w, conv_b, dt_bias, a_log, d_skip, norm_w):
    bsz, l, _ = z.shape
    gn = SSM_GROUPS * SSM_STATE
    xbc = jax.nn.silu(dwconv_centred(xbc, conv_w, conv_b)).astype(jnp.float32)
    xs = xbc[..., :SSM_WIDTH].reshape(bsz, l, SSM_HEADS, SSM_HEAD_DIM)
    Bm = xbc[..., SSM_WIDTH:SSM_WIDTH + gn].reshape(bsz, l, SSM_GROUPS, SSM_STATE)
    Cm = xbc[..., SSM_WIDTH + gn:].reshape(bsz, l, SSM_GROUPS, SSM_STATE)
    a = -jnp.exp(a_log.astype(jnp.float32))
    dbias = dt_bias.astype(jnp.float32)
    dtf = jax.nn.softplus(dt_f.astype(jnp.float32) + dbias[0])
    dtb = jax.nn.softplus(dt_b.astype(jnp.float32) + dbias[1])
    y = ssd_chunked(xs, dtf, a[0], Bm, Cm) + flip(ssd_chunked(flip(xs), flip(dtb), a[1], flip(Bm), flip(Cm)))
    y = y + d_skip.astype(jnp.float32)[:, None] * xs
    y = y.reshape(bsz, l, SSM_WIDTH) * jax.nn.silu(z.astype(jnp.float32))
    yg = y.reshape(bsz, l, SSM_GROUPS, SSM_WIDTH // SSM_GROUPS)
    yg = yg * lax.rsqrt(jnp.mean(yg * yg, axis=-1, keepdims=True) + EPS)
    return (yg.reshape(bsz, l, SSM_WIDTH) * norm_w.astype(jnp.float32)).astype(z.dtype)


def gla_chunked(q, k, v, gk):
    bsz, l, h, dk = q.shape
    dv = v.shape[-1]
    Q = GLA_CHUNK
    c = l // Q
    q = q.reshape(bsz, c, Q, h, dk)
    k = k.reshape(bsz, c, Q, h, dk)
    v = v.reshape(bsz, c, Q, h, dv)
    bcum = jnp.cumsum(gk.reshape(bsz, c, Q, h, dk), axis=2)
    q_e = q * jnp.exp(bcum)
    k_e = k * jnp.exp(-bcum)
    causal = jnp.tril(jnp.ones((Q, Q), dtype=bool))
    att = jnp.where(causal, jnp.einsum('bcihd,bcjhd->bchij', q_e, k_e), 0.0)
    o_intra = jnp.einsum('bchij,bcjhv->bcihv', att, v)
    k_dec = k * jnp.exp(bcum[:, :, -1:] - bcum)
    dec = jnp.exp(bcum[:, :, -1])

    def step(S, inp):
        qe, kd, vc, dc = inp
        o = jnp.einsum('bihd,bhdv->bihv', qe, S)
        S = S * dc[..., None] + jnp.einsum('bjhd,bjhv->bhdv', kd, vc)
        return S, o

    S0 = jnp.zeros((bsz, h, dk, dv), jnp.float32)
    xs = (jnp.moveaxis(q_e, 1, 0), jnp.moveaxis(k_dec, 1, 0), jnp.moveaxis(v, 1, 0), jnp.moveaxis(dec, 1, 0))
    _, o_inter = lax.scan(step, S0, xs)
    o = o_intra + jnp.moveaxis(o_inter, 0, 1)
    return o.reshape(bsz, l, h, dv)


def gla_mixer(q, k, v, g, down_f, down_b, gate_up, gate_b, norm_w):
    bsz, l, _ = q.shape
    f32 = jnp.float32
    qh = q.astype(f32).reshape(bsz, l, GLA_HEADS, GLA_DK_HEAD) * (GLA_DK_HEAD ** -0.5)
    kh = k.astype(f32).reshape(bsz, l, GLA_HEADS, GLA_DK_HEAD)
    vh = v.astype(f32).reshape(bsz, l, GLA_HEADS, GLA_DV_HEAD)

    def log_gate(down, up, bias):
        logits = down.astype(f32) @ up.astype(f32) + bias.astype(f32)
        return (jax.nn.log_sigmoid(logits) / GLA_NORMALIZER).reshape(bsz, l, GLA_HEADS, GLA_DK_HEAD)

    gk_f = log_gate(down_f, gate_up[0], gate_b[0])
    gk_b = log_gate(down_b, gate_up[1], gate_b[1])
    o = gla_chunked(qh, kh, vh, gk_f) + flip(gla_chunked(flip(qh), flip(kh), flip(vh), flip(gk_b)))
    o = o * lax.rsqrt(jnp.mean(o * o, axis=-1, keepdims=True) + EPS) * norm_w.astype(f32)
    o = o.reshape(bsz, l, GLA_DV) * jax.nn.silu(g.astype(f32))
    return o.astype(q.dtype)


def na_mixer(q, k, v, rpb):
    bsz, l, _ = q.shape
    rows = l // GRID_W
    kh = min(NA_KH, rows)
    qg = q.reshape(bsz, rows, GRID_W, NA_HEADS, NA_HEAD_DIM) * (NA_HEAD_DIM ** -0.5)
    kg = k.reshape(bsz, rows, GRID_W, NA_HEADS, NA_HEAD_DIM)
    vg = v.reshape(bsz, rows, GRID_W, NA_HEADS, NA_HEAD_DIM)
    row_start = jnp.clip(jnp.arange(rows) - kh // 2, 0, rows - kh)
    col_start = jnp.clip(jnp.arange(GRID_W) - NA_KW // 2, 0, GRID_W - NA_KW)
    col_idx = col_start[:, None] + jnp.arange(NA_KW)
    col_off = col_idx - jnp.arange(GRID_W)[:, None] + (NA_KW - 1)

    def one_row(r):
        rs = row_start[r]
        k_rows = lax.dynamic_slice_in_dim(kg, rs, kh, axis=1)
        v_rows = lax.dynamic_slice_in_dim(vg, rs, kh, axis=1)
        k_win = k_rows[:, :, col_idx]
        v_win = v_rows[:, :, col_idx]
        q_row = lax.dynamic_index_in_dim(qg, r, axis=1, keepdims=False)
        s = jnp.einsum('bqhd,bkqwhd->bhqkw', q_row, k_win, preferred_element_type=jnp.float32)
        row_off = rs + jnp.arange(kh) - r + (NA_KH - 1)
        bias = rpb[:, row_off[None, :, None], col_off[:, None, :]]
        s = s + bias.astype(jnp.float32)[None]
        p = jax.nn.softmax(s.reshape(bsz, NA_HEADS, GRID_W, kh * NA_KW), axis=-1).reshape(s.shape)
        return jnp.einsum('bhqkw,bkqwhd->bqhd', p.astype(v_win.dtype), v_win)

    out = lax.map(one_row, jnp.arange(rows))
    return jnp.moveaxis(out, 0, 1).reshape(bsz, l, NA_WIDTH)


def setup_inputs(seed: int = 0) -> dict:
    key = jax.random.key(seed)
    ks = jax.random.split(key, 32)
    f32 = jnp.float32

    def nrm(k, shape, fan_in):
        return jax.random.normal(k, shape, f32) * (fan_in ** -0.5)

    def gain(k, shape):
        return 1.0 + 0.01 * jax.random.normal(k, shape, f32)

    u_dt = jax.random.uniform(ks[9], (DEPTH, 2, SSM_HEADS), f32)
    dt0 = jnp.exp(u_dt * (jnp.log(0.1) - jnp.log(0.001)) + jnp.log(0.001))
    dt_bias = dt0 + jnp.log(-jnp.expm1(-dt0))
    a_log = jnp.log(jax.random.uniform(ks[10], (DEPTH, 2, SSM_HEADS), f32, 1.0, 16.0))
    return {
        "x_prompt": jax.random.normal(ks[0], (BATCH, SEQ, D_MODEL), f32),
        "x_sample": jax.random.normal(ks[1], (DEC_BATCH, DEC_SEQ, D_MODEL), f32),
        "ln_ffn1_w": gain(ks[2], (DEPTH, D_MODEL)),
        "ffn1_w_in": nrm(ks[3], (DEPTH, D_MODEL, 2 * D_FF), D_MODEL),
        "ffn1_w_out": nrm(ks[4], (DEPTH, D_FF, D_MODEL), D_FF),
        "ln_mix_w": gain(ks[5], (DEPTH, D_MODEL)),
        "w_in": nrm(ks[6], (DEPTH, D_MODEL, IN_WIDTH), D_MODEL),
        "ssm_conv_w": nrm(ks[7], (DEPTH, SSM_CONV, SSM_CONV_CH), SSM_CONV),
        "ssm_conv_b": 0.02 * jax.random.normal(ks[8], (DEPTH, SSM_CONV_CH), f32),
        "ssm_dt_bias": dt_bias,
        "ssm_a_log": a_log,
        "ssm_d": gain(ks[11], (DEPTH, SSM_HEADS)),
        "ssm_norm_w": gain(ks[12], (DEPTH, SSM_WIDTH)),
        "gla_gate_up": nrm(ks[13], (DEPTH, 2, GLA_RANK, GLA_DK), GLA_RANK),
        "gla_gate_b": 0.02 * jax.random.normal(ks[14], (DEPTH, 2, GLA_DK), f32),
        "gla_norm_w": gain(ks[15], (DEPTH, GLA_DV_HEAD)),
        "na_rpb": 0.02 * jax.random.normal(ks[16], (DEPTH, NA_HEADS, 2 * NA_KH - 1, 2 * NA_KW - 1), f32),
        "w_branch_a": nrm(ks[17], (DEPTH, SSM_WIDTH, D_MODEL), SSM_WIDTH),
        "w_branch_b": nrm(ks[18], (DEPTH, GLA_DV, D_MODEL), GLA_DV),
        "w_branch_c": nrm(ks[19], (DEPTH, NA_WIDTH, D_MODEL), NA_WIDTH),
        "w_out": nrm(ks[20], (DEPTH, D_MODEL, D_MODEL), D_MODEL),
        "ln_ffn2_w": gain(ks[21], (DEPTH, D_MODEL)),
        "ffn2_w_in": nrm(ks[22], (DEPTH, D_MODEL, 2 * D_FF), D_MODEL),
        "ffn2_w_out": nrm(ks[23], (DEPTH, D_FF, D_MODEL), D_FF),
        "ln_final_w": gain(ks[24], (D_MODEL,)),
    }


def reference(x_prompt, x_sample, ln_ffn1_w, ffn1_w_in, ffn1_w_out, ln_mix_w, w_in,
              ssm_conv_w, ssm_conv_b, ssm_dt_bias, ssm_a_log, ssm_d, ssm_norm_w,
              gla_gate_up, gla_gate_b, gla_norm_w, na_rpb,
              w_branch_a, w_branch_b, w_branch_c, w_out,
              ln_ffn2_w, ffn2_w_in, ffn2_w_out, ln_final_w):

    def layer(x, i):
        x = x + 0.5 * swiglu(rmsnorm(x, ln_ffn1_w[i]), ffn1_w_in[i], ffn1_w_out[i])
        h = rmsnorm(x, ln_mix_w[i])
        (z, xbc, dt_f, dt_b, q_b, k_b, v_b, g_b, dn_f, dn_b,
         q_c, k_c, v_c, gates) = split_cols(h @ w_in[i])
        y_a = ssd_mixer(z, xbc, dt_f, dt_b, ssm_conv_w[i], ssm_conv_b[i], ssm_dt_bias[i],
                        ssm_a_log[i], ssm_d[i], ssm_norm_w[i])
        y_b = gla_mixer(q_b, k_b, v_b, g_b, dn_f, dn_b, gla_gate_up[i], gla_gate_b[i], gla_norm_w[i])
        y_c = na_mixer(q_c, k_c, v_c, na_rpb[i])
        g = jax.nn.sigmoid(gates.astype(jnp.float32)).astype(x.dtype)
        g = g.reshape(g.shape[:-1] + (N_BRANCH, D_MODEL))
        merged = (g[..., 0, :] * (y_a @ w_branch_a[i])
                  + g[..., 1, :] * (y_b @ w_branch_b[i])
                  + g[..., 2, :] * (y_c @ w_branch_c[i]))
        x = x + merged @ w_out[i]
        x = x + 0.5 * swiglu(rmsnorm(x, ln_ffn2_w[i]), ffn2_w_in[i], ffn2_w_out[i])
        return x

    def trunk(x):
        for i in range(DEPTH):
            x = layer(x, i)
        return rmsnorm(x, ln_final_w)

    y_prompt = trunk(x_prompt)
    y_sample = trunk(x_sample)
    return (y_prompt, y_sample)
```

```python
import numpy as np
from contextlib import ExitStack
import concourse.bass as bass
import concourse.mybir as mybir
from concourse.bass_utils import run_bass_kernel_spmd

F32 = mybir.dt.float32
BF16 = mybir.dt.bfloat16
AF = mybir.ActivationFunctionType
ALU = mybir.AluOpType
AX = mybir.AxisListType

D = 1024
DFF = 2816
DEPTH = 2
EPS = 1e-6
ENGS = ("pe", "act", "dve", "pool", "sp")
STRICT = True


class Buf:
    __slots__ = ("name", "w", "r", "sem")

    def __init__(self, name):
        self.name = name
        self.w = []
        self.r = []
        self.sem = None


class Op:
    __slots__ = ("fn", "rw", "war", "signal", "sigval", "dma", "waits")

    def __init__(self, fn, rw, war, dma=None):
        self.fn = fn
        self.rw = rw
        self.war = war
        self.signal = False
        self.sigval = 0
        self.dma = dma
        self.waits = None


class Graph:
    def __init__(self, nc, n_dma_sems=40):
        self.nc = nc
        self.es = ExitStack()
        self.eng_sem = {e: self.es.enter_context(nc.semaphore("s_" + e)) for e in ENGS}
        self.eng_cnt = {e: 0 for e in ENGS}
        self.dma_sem = [self.es.enter_context(nc.semaphore("d%d" % i)) for i in range(n_dma_sems)]
        self.dma_cnt = [0] * n_dma_sems
        self.next_slot = 0
        self.ops = {e: [] for e in ENGS}
        self.waited = {e: {} for e in ENGS}
        self.n_instr = 0

    def close(self):
        self.es.close()

    def slot(self, buf):
        if buf.sem is None:
            buf.sem = self.next_slot % len(self.dma_sem)
            self.next_slot += 1
        return buf.sem

    def _record(self, eng, fn, reads, writes, dma=None):
        rw, war = set(), set()
        for b in reads:
            rw.update(b.w)
        for b in writes:
            for ref in b.w:
                if dma is not None and ref[0] == "d" and ref[1] == dma[0]:
                    continue
                rw.add(ref)
            war.update(b.r)
        idx = len(self.ops[eng])
        op = Op(fn, rw, war, dma)
        self.ops[eng].append(op)
        ref = ("d", dma[0], dma[1]) if dma is not None else ("c", eng, idx)
        for b in reads:
            b.r.append(ref)
        for b in writes:
            b.w = [ref]
            b.r = []
        return ref

    def op(self, eng, fn, reads=(), writes=()):
        return self._record(eng, fn, reads, writes)

    def dma(self, eng, out, in_, sbuf, reads=(), writes=(), slow=False):
        s = self.slot(sbuf)
        self.dma_cnt[s] += 16
        cnt = self.dma_cnt[s]
        sem = self.dma_sem[s]
        kw = {"allow_slow_non_contiguous": True} if slow else {}

        def fn(e):
            return e.dma_start(out=out, in_=in_, **kw).then_inc(sem, 16)
        return self._record(eng, fn, reads, writes, dma=(s, cnt))

    def mm(self, out, lhsT, rhs, start, stop, reads, writes):
        return self._record("pe", lambda e: e.matmul(out, lhsT, rhs, start=start, stop=stop), reads, writes)

    def tr(self, out, in_, ident, reads, writes):
        return self._record("pe", lambda e: e.transpose(out=out, in_=in_, identity=ident), reads, writes)

    def act(self, out, in_, func, reads, writes, bias=None, scale=None):
        kw = {}
        if bias is not None:
            kw["bias"] = bias
        if scale is not None:
            kw["scale"] = scale
        return self._record("act", lambda e: e.activation(out=out, in_=in_, func=func, **kw), reads, writes)

    def cp(self, eng, out, in_, reads, writes):
        if eng == "act":
            return self._record("act", lambda e: e.copy(out=out, in_=in_), reads, writes)
        return self._record(eng, lambda e: e.tensor_copy(out=out, in_=in_), reads, writes)

    def tt(self, eng, out, in0, in1, op, reads, writes):
        return self._record(eng, lambda e: e.tensor_tensor(out=out, in0=in0, in1=in1, op=op), reads, writes)

    def ts(self, eng, out, in0, s1, s2, op0, op1, reads, writes):
        if op1 is None:
            return self._record(eng, lambda e: e.tensor_scalar(out=out, in0=in0, scalar1=s1, scalar2=None, op0=op0),
                                reads, writes)
        return self._record(eng, lambda e: e.tensor_scalar(out=out, in0=in0, scalar1=s1, scalar2=s2, op0=op0, op1=op1),
                            reads, writes)

    def stt(self, eng, out, in0, scalar, in1, op0, op1, reads, writes):
        return self._record(eng, lambda e: e.scalar_tensor_tensor(out=out, in0=in0, scalar=scalar, in1=in1,
                                                                  op0=op0, op1=op1), reads, writes)

    def red(self, eng, out, in_, reads, writes):
        return self._record(eng, lambda e: e.reduce_sum(out=out, in_=in_, axis=AX.X), reads, writes)

    def memset(self, eng, ap, val, writes):
        return self._record(eng, lambda e: e.memset(ap, val), (), writes)

    def flush(self, engines):
        ops = self.ops
        for e in ENGS:
            for i, op in enumerate(ops[e]):
                need = set()
                for ref in op.rw:
                    if ref[0] == "d":
                        need.add(ref)
                    elif ref[1] != e:
                        need.add(ref)
                    elif op.dma is not None or (STRICT and e in ("act", "dve", "pool")):
                        if ref[2] < i:
                            need.add(ref)
                for ref in op.war:
                    if ref[0] == "d":
                        need.add(ref)
                    elif ref[1] != e:
                        need.add(ref)
                    elif op.dma is not None:
                        need.add(ref)
                op.waits = need
                for ref in need:
                    if ref[0] == "c":
                        ops[ref[1]][ref[2]].signal = True
        for e in ENGS:
            for op in reversed(ops[e]):
                if op.dma is None:
                    op.signal = True
                    break
        for e in ENGS:
            c = self.eng_cnt[e]
            for op in ops[e]:
                if op.signal and op.dma is None:
                    c += 1
                    op.sigval = c
            self.eng_cnt[e] = c
        nc = self.nc
        final_eng = dict(self.eng_cnt)
        final_dma = list(self.dma_cnt)
        g = self

        def emit(e, handle):
            waited = g.waited[e]
            for op in ops[e]:
                wl = {}
                for ref in op.waits:
                    if ref[0] == "d":
                        key, val = ("d", ref[1]), ref[2]
                    else:
                        key, val = ("c", ref[1]), ops[ref[1]][ref[2]].sigval
                    if wl.get(key, 0) < val:
                        wl[key] = val
                for key, val in wl.items():
                    if waited.get(key, 0) >= val:
                        continue
                    waited[key] = val
                    sem = g.dma_sem[key[1]] if key[0] == "d" else g.eng_sem[key[1]]
                    handle.wait_ge(sem, val)
                    g.n_instr += 1
                ins = op.fn(handle)
                g.n_instr += 1
                if op.signal and op.dma is None:
                    ins.then_inc(g.eng_sem[e], 1)
            for x in ENGS:
                if x != e and waited.get(("c", x), 0) < final_eng[x]:
                    waited[("c", x)] = final_eng[x]
                    handle.wait_ge(g.eng_sem[x], final_eng[x])
            for s, v in enumerate(final_dma):
                if v > 0 and waited.get(("d", s), 0) < v:
                    waited[("d", s)] = v
                    handle.wait_ge(g.dma_sem[s], v)

        with nc.Block() as block:
            @block.tensor
            def _(h):
                emit("pe", h)

            @block.scalar
            def _(h):
                emit("act", h)

            @block.vector
            def _(h):
                emit("dve", h)

            @block.gpsimd
            def _(h):
                emit("pool", h)

            @block.sync
            def _(h):
                emit("sp", h)
        self.ops = {e: [] for e in ENGS}


C_Z, C_XBC, C_DTF, C_DTB = 0, 1024, 2560, 2576
C_GQ, C_GK, C_GV, C_GG = 2592, 3104, 3616, 4640
C_DNF, C_DNB = 5664, 5680
C_NQ, C_NK, C_NV = 5696, 6720, 7744
C_GATE = 8768
IN_WIDTH = 11840


class Cfg:
    def __init__(self, nt, stop_after=None, debug_outs=(), debug_ins=()):
        self.nt = nt
        self.seg = nt // 2
        self.stop_after = stop_after
        self.debug_outs = debug_outs
        self.debug_ins = debug_ins
        self.only = None


def build_program(cfg):
    nc = bass.Bass("TRN2", target_bir_lowering=False)
    NT = cfg.nt
    TB = 512
    NB = NT // TB
    NTI = TB // 128

    cfg.in_shapes = {}

    def din(name, shape, dt=F32):
        cfg.in_shapes[name] = (list(shape), dt)
        return nc.dram_tensor(name, list(shape), dt, kind="ExternalInput").ap()

    def dout(name, shape, dt=F32):
        return nc.dram_tensor(name, list(shape), dt, kind="ExternalOutput").ap()

    def dscr(name, shape, dt=F32):
        kind = "Internal"
        if name in cfg.debug_outs:
            kind = "ExternalOutput"
        if name in cfg.debug_ins:
            kind = "ExternalInput"
            cfg.in_shapes[name] = (list(shape), dt)
        return nc.dram_tensor(name, list(shape), dt, kind=kind).ap()

    x_in = din("x", [NT, D])
    flag = din("flag", [128, 2])
    ident_in = din("ident", [128, 128])
    ln_ffn1_w = din("ln_ffn1_w", [DEPTH, D])
    ffn1_w_in = din("ffn1_w_in", [DEPTH, D, 2 * DFF])
    ffn1_w_out = din("ffn1_w_out", [DEPTH, DFF, D])
    ln_mix_w = din("ln_mix_w", [DEPTH, D])
    w_in = din("w_in", [DEPTH, D, IN_WIDTH])
    w_branch = [din("w_branch_" + c, [DEPTH, D, D]) for c in "abc"]
    w_out = din("w_out", [DEPTH, D, D])
    ln_ffn2_w = din("ln_ffn2_w", [DEPTH, D])
    ffn2_w_in = din("ffn2_w_in", [DEPTH, D, 2 * DFF])
    ffn2_w_out = din("ffn2_w_out", [DEPTH, DFF, D])
    ln_final_w = din("ln_final_w", [1, D])
    y_out = dout("y", [NT, D])

    xa = dscr("xa", [NT, D])
    xb = dscr("xb", [NT, D])
    xc = dscr("xc", [NT, D])
    hT_f1 = dscr("hT_f1", [8, 128, NT], BF16)
    hT_mix = dscr("hT_mix", [8, 128, NT], BF16)
    hT_f2 = dscr("hT_f2", [8, 128, NT], BF16)
    z_tm = dscr("z_tm", [NT, 1024])
    kg_tm = dscr("kg_tm", [NT, 512])
    vg_tm = dscr("vg_tm", [NT, 1024], BF16)
    gg_tm = dscr("gg_tm", [NT, 1024])
    vn_tm = dscr("vn_tm", [NT, 1024], BF16)
    dt_tm = dscr("dt_tm", [NT, 32])
    xbcT = dscr("xbcT", [12, 128, NT])
    gqT = dscr("gqT", [4, 128, NT])
    gkT = dscr("gkT", [4, 128, NT])
    nqT = dscr("nqT", [8, 128, NT], BF16)
    nkT = dscr("nkT", [8, 128, NT], BF16)
    gatesT = dscr("gatesT", [24, 128, NT], BF16)
    dnT = dscr("dnT", [2, 16, NT])
    yT = [dscr("yT_" + c, [8, 128, NT], BF16) for c in "abc"]

    g = Graph(nc)

    uid = [0]

    def sb(es, name, shape, dt):
        uid[0] += 1
        return es.enter_context(nc.sbuf_tensor("s%d_%s" % (uid[0], name), list(shape), dt))

    def ps(es, name, shape, dt):
        uid[0] += 1
        return es.enter_context(nc.psum_tensor("p%d_%s" % (uid[0], name), list(shape), dt))

    class NormCtx:
        def __init__(self, es, tag, transpose=True):
            self.transpose = transpose
            self.wB = sb(es, "wB_" + tag, [128, D], F32)
            self.sq = sb(es, "sq_" + tag, [128, D], F32)
            self.ss = [sb(es, "ss%d_" % i + tag, [128, 2], F32) for i in range(2)]
            self.b_wB, self.b_sq = Buf("wB"), Buf("sq")
            self.b_ss = [Buf("ss0"), Buf("ss1")]
            if transpose:
                self.ident = sb(es, "ident_" + tag, [128, 128], BF16)
                self.hn = [sb(es, "hn%d_" % i + tag, [128, D], BF16) for i in range(2)]
                self.hTs = sb(es, "hTs_" + tag, [128, 8, TB], BF16)
                self.pT = [ps(es, "pT%d_" % i + tag, [128, D], BF16) for i in range(2)]
                self.b_ident = Buf("ident")
                self.b_hn = [Buf("hn0"), Buf("hn1")]
                self.b_hTs = Buf("hTs")
                self.b_pT = [Buf("pT0"), Buf("pT1")]
            else:
                self.yt = [sb(es, "yt%d_" % i + tag, [128, D], F32) for i in range(2)]
                self.b_yt = [Buf("yt0"), Buf("yt1")]
            self.n = 0

        def setup(self, w_row_ap):
            if self.transpose:
                g.dma("pool", self.ident[:], ident_in, self.b_ident, writes=[self.b_ident])
            g.dma("sp", self.wB[:], w_row_ap.partition_broadcast(128), self.b_wB, writes=[self.b_wB])

        def stats(self, xt_ap, b_x):
            i = self.n % 2
            self.n += 1
            ss = self.ss[i]
            g.act(self.sq[:], xt_ap, AF.Square, [b_x], [self.b_sq])
            g.red("dve", ss[:, 0:1], self.sq[:], [self.b_sq], [self.b_ss[i]])
            g.ts("dve", ss[:, 1:2], ss[:, 0:1], 1.0 / D, EPS, ALU.mult, ALU.add, [self.b_ss[i]], [self.b_ss[i]])
            g.act(ss[:, 1:2], ss[:, 1:2], AF.Sqrt, [self.b_ss[i]], [self.b_ss[i]])
            g._record("dve", lambda e: e.reciprocal(out=ss[:, 1:2], in_=ss[:, 1:2]), [self.b_ss[i]], [self.b_ss[i]])
            return i

        def run(self, xt_ap, b_x, hT_dram, blk, ti):
            i = self.stats(xt_ap, b_x)
            hn, pT, hTs = self.hn[i], self.pT[i], self.hTs
            g.stt("dve", hn[:], xt_ap, self.ss[i][:, 1:2], self.wB[:], ALU.mult, ALU.mult,
                  [b_x, self.b_ss[i], self.b_wB], [self.b_hn[i]])
            for k in range(8):
                g.tr(pT[:, k * 128:(k + 1) * 128], hn[:, k * 128:(k + 1) * 128], self.ident[:],
                     [self.b_hn[i], self.b_ident], [self.b_pT[i]])
            g.cp("act", hTs[:, :, ti * 128:(ti + 1) * 128], pT[:].rearrange("p (k t) -> p k t", k=8),
                 [self.b_pT[i]], [self.b_hTs])
            if ti == NTI - 1:
                g.dma("pool", hT_dram[:, :, blk * TB:(blk + 1) * TB].rearrange("k p t -> p k t"), hTs[:],
                      self.b_hTs, reads=[self.b_hTs])

        def run_final(self, xt_ap, b_x, y_dram, t):
            i = self.stats(xt_ap, b_x)
            g.stt("dve", self.yt[i][:], xt_ap, self.ss[i][:, 1:2], self.wB[:], ALU.mult, ALU.mult,
                  [b_x, self.b_ss[i], self.b_wB], [self.b_yt[i]])
            g.dma("pool", y_dram[t * 128:(t + 1) * 128, :], self.yt[i][:], self.b_yt[i], reads=[self.b_yt[i]])

    def phase_p0():
        with ExitStack() as es:
            nrm = NormCtx(es, "p0")
            xt = [sb(es, "p0x%d" % i, [128, D], F32) for i in range(2)]
            b_xt = [Buf("x0"), Buf("x1")]
            nrm.setup(ln_ffn1_w[0:1, :])
            for blk in range(NB):
                for ti in range(NTI):
                    t = blk * NTI + ti
                    i = t % 2
                    g.dma("sp", xt[i][:], x_in[t * 128:(t + 1) * 128, :], b_xt[i], writes=[b_xt[i]])
                    nrm.run(xt[i][:], b_xt[i], hT_f1, blk, ti)
            g.flush(ENGS)

    def phase_ffn(tag, w_in_ap, w_out_ap, x_src, hT_src, x_dst, next_w_row, hT_dst, final_out=None):
        with ExitStack() as es:
            w1 = sb(es, "w1_" + tag, [128, 8, 2 * DFF], BF16)
            w2 = sb(es, "w2_" + tag, [128, 22, D], BF16)
            b_w1, b_w2 = Buf("w1"), Buf("w2")
            hTb = sb(es, "hTb_" + tag, [128, 8, TB], BF16)
            b_hTb = Buf("hTb")
            gT = sb(es, "gT_" + tag, [128, 22, TB], BF16)
            b_gT = Buf("gT")
            sa = [sb(es, "sa%d_" % i + tag, [128, TB], F32) for i in range(2)]
            b_sa = [Buf("sa0"), Buf("sa1")]
            xt = sb(es, "xt_" + tag, [128, D], F32)
            b_xt = Buf("xt")
            xn = [sb(es, "xn%d_" % i + tag, [128, D], F32) for i in range(2)]
            b_xn = [Buf("xn0"), Buf("xn1")]
            pa = [ps(es, "pa%d_" % i + tag, [128, TB], F32) for i in range(2)]
            pb = [ps(es, "pb%d_" % i + tag, [128, TB], F32) for i in range(2)]
            b_pa = [Buf("pa0"), Buf("pa1")]
            b_pb = [Buf("pb0"), Buf("pb1")]
            po = [ps(es, "po%d_" % i + tag, [128, 512], F32) for i in range(2)]
            b_po = [Buf("po0"), Buf("po1")]
            nrm = NormCtx(es, tag, transpose=(final_out is None))
            for k in range(8):
                g.dma("pool", w1[:, k, :], w_in_ap[k * 128:(k + 1) * 128, :], b_w1, writes=[b_w1])
            for k in range(22):
                g.dma("pool", w2[:, k, :], w_out_ap[k * 128:(k + 1) * 128, :], b_w2, writes=[b_w2])
            nrm.setup(next_w_row)

            def load_h(blk):
                g.dma("sp", hTb[:], hT_src[:, :, blk * TB:(blk + 1) * TB].rearrange("k p t -> p k t"),
                      b_hTb, writes=[b_hTb])
            load_h(0)
            ntl = 0
            for blk in range(NB):
                for c in range(22):
                    q = c % 2
                    for k in range(8):
                        g.mm(pa[q][:], w1[:, k, c * 128:(c + 1) * 128], hTb[:, k, :], k == 0, k == 7,
                             [b_w1, b_hTb], [b_pa[q]])
                    for k in range(8):
                        g.mm(pb[q][:], w1[:, k, DFF + c * 128:DFF + (c + 1) * 128], hTb[:, k, :], k == 0, k == 7,
                             [b_w1, b_hTb], [b_pb[q]])
                    g.act(sa[q][:], pa[q][:], AF.Silu, [b_pa[q]], [b_sa[q]])
                    g.tt("dve", gT[:, c, :], sa[q][:], pb[q][:], ALU.mult, [b_sa[q], b_pb[q]], [b_gT])
                if blk + 1 < NB:
                    load_h(blk + 1)
                for ti in range(NTI):
                    t = blk * NTI + ti
                    i = ntl % 2
                    ntl += 1
                    g.dma("sp", xt[:], x_src[t * 128:(t + 1) * 128, :], b_xt, writes=[b_xt])
                    for ch in range(2):
                        for k in range(22):
                            g.mm(po[ch][:], gT[:, k, ti * 128:(ti + 1) * 128], w2[:, k, ch * 512:(ch + 1) * 512],
                                 k == 0, k == 21, [b_gT, b_w2], [b_po[ch]])
                        g.stt("dve", xn[i][:, ch * 512:(ch + 1) * 512], po[ch][:], 0.5,
                              xt[:, ch * 512:(ch + 1) * 512], ALU.mult, ALU.add, [b_po[ch], b_xt], [b_xn[i]])
                    if final_out is None:
                        g.dma("pool", x_dst[t * 128:(t + 1) * 128, :], xn[i][:], b_xn[i], reads=[b_xn[i]])
                        nrm.run(xn[i][:], b_xn[i], hT_dst, blk, ti)
                    else:
                        nrm.run_final(xn[i][:], b_xn[i], final_out, t)
            g.flush(ENGS)

    def phase_inproj_tm(l):
        with ExitStack() as es:
            W = w_in[l]
            parts = [(C_Z, 1024, z_tm, F32), (C_GK, 512, kg_tm, F32), (C_GV, 1024, vg_tm, BF16),
                     (C_GG, 1024, gg_tm, F32), (C_NV, 1024, vn_tm, BF16), (C_DTF, 32, dt_tm, F32)]
            NC_TM = sum(p[1] for p in parts)
            wt = sb(es, "wtm", [128, 8, NC_TM], BF16)
            b_wt = Buf("wtm")
            hTb = [sb(es, "tm_hTb%d" % i, [128, 8, TB], BF16) for i in range(2)]
            b_hTb = [Buf("hTb0"), Buf("hTb1")]
            st = [[sb(es, "tm_st%d_%d" % (i, pi), [128, p[1]], p[3]) for pi, p in enumerate(parts)] for i in range(2)]
            b_st = [[Buf("st%d_%d" % (i, pi)) for pi in range(len(parts))] for i in range(2)]
            pp = [ps(es, "tm_p%d" % i, [128, 512], F32) for i in range(4)]
            b_pp = [Buf("pp%d" % i) for i in range(4)]
            off = 0
            offs = []
            for (c0, wd, _, _) in parts:
                offs.append(off)
                for k in range(8):
                    g.dma("pool", wt[:, k, off:off + wd], W[k * 128:(k + 1) * 128, c0:c0 + wd], b_wt, writes=[b_wt])
                off += wd
            nps = 0
            nev = 0
            for blk in range(NB):
                j = blk % 2
                g.dma("sp", hTb[j][:], hT_mix[:, :, blk * TB:(blk + 1) * TB].rearrange("k p t -> p k t"),
                      b_hTb[j], writes=[b_hTb[j]])
                for ti in range(NTI):
                    t = blk * NTI + ti
                    i = t % 2
                    for pi, (c0, wd, dst, dt) in enumerate(parts):
                        for cb in range(0, wd, 512):
                            n = min(512, wd - cb)
                            q = nps % 4
                            nps += 1
                            for k in range(8):
                                g.mm(pp[q][:, 0:n], hTb[j][:, k, ti * 128:(ti + 1) * 128],
                                     wt[:, k, offs[pi] + cb:offs[pi] + cb + n], k == 0, k == 7,
                                     [b_hTb[j], b_wt], [b_pp[q]])
                            eng = "act" if nev % 2 == 0 else "dve"
                            nev += 1
                            g.cp(eng, st[i][pi][:, cb:cb + n], pp[q][:, 0:n], [b_pp[q]], [b_st[i][pi]])
                        g.dma("pool", dst[t * 128:(t + 1) * 128, :], st[i][pi][:], b_st[i][pi], reads=[b_st[i][pi]])
            g.flush(ENGS)

    def phase_inproj_fm(l):
        with ExitStack() as es:
            W = w_in[l]
            parts = [(C_XBC, 12, xbcT, F32, None, 1.0), (C_GQ, 4, gqT, F32, None, 128.0 ** -0.5),
                     (C_GK, 4, gkT, F32, None, 1.0), (C_NQ, 8, nqT, BF16, None, 0.125),
                     (C_NK, 8, nkT, BF16, None, 1.0), (C_GATE, 24, gatesT, BF16, AF.Sigmoid, 1.0)]
            NCH = sum(p[1] for p in parts)
            NC_FM = NCH * 128 + 32
            wt = sb(es, "wfm", [128, 8, NC_FM], BF16)
            b_wt = Buf("wfm")
            hTb = [sb(es, "fm_hTb%d" % i, [128, 8, TB], BF16) for i in range(2)]
            b_hTb = [Buf("hTb0"), Buf("hTb1")]
            stf = [sb(es, "fm_stf%d" % i, [128, TB], F32) for i in range(4)]
            stb = [sb(es, "fm_stb%d" % i, [128, TB], BF16) for i in range(4)]
            b_stf = [Buf("stf%d" % i) for i in range(4)]
            b_stb = [Buf("stb%d" % i) for i in range(4)]
            pp = [ps(es, "fm_p%d" % i, [128, 512], F32) for i in range(4)]
            b_pp = [Buf("pp%d" % i) for i in range(4)]
            off = 0
            offs = []
            for (c0, nch, _, _, _, _) in parts:
                offs.append(off)
                for k in range(8):
                    g.dma("pool", wt[:, k, off:off + nch * 128], W[k * 128:(k + 1) * 128, c0:c0 + nch * 128],
                          b_wt, writes=[b_wt])
                off += nch * 128
            off_dn = off
            for k in range(8):
                g.dma("pool", wt[:, k, off:off + 32], W[k * 128:(k + 1) * 128, C_DNF:C_DNF + 32], b_wt, writes=[b_wt])
            nps = 0
            nf = 0
            nbf = 0
            nev = 0
            for blk in range(NB):
                j = blk % 2
                tok = slice(blk * TB, (blk + 1) * TB)
                g.dma("sp", hTb[j][:], hT_mix[:, :, tok].rearrange("k p t -> p k t"), b_hTb[j], writes=[b_hTb[j]])
                for pi, (c0, nch, dst, dt, func, scale) in enumerate(parts):
                    for c in range(nch):
                        q = nps % 4
                        nps += 1
                        wc = offs[pi] + c * 128
                        for k in range(8):
                            g.mm(pp[q][:], wt[:, k, wc:wc + 128], hTb[j][:, k, :], k == 0, k == 7,
                                 [b_wt, b_hTb[j]], [b_pp[q]])
                        if dt == F32:
                            s_, b_s = stf[nf % 4], b_stf[nf % 4]
                            nf += 1
                        else:
                            s_, b_s = stb[nbf % 4], b_stb[nbf % 4]
                            nbf += 1
                        if func is not None:
                            g.act(s_[:], pp[q][:], func, [b_pp[q]], [b_s])
                        elif scale != 1.0:
                            if nev % 2 == 0:
                                g.act(s_[:], pp[q][:], AF.Copy, [b_pp[q]], [b_s], scale=scale)
                            else:
                                g.ts("dve", s_[:], pp[q][:], scale, None, ALU.mult, None, [b_pp[q]], [b_s])
                            nev += 1
                        else:
                            g.cp("act" if nev % 2 == 0 else "dve", s_[:], pp[q][:], [b_pp[q]], [b_s])
                            nev += 1
                        g.dma("pool", dst[c, :, tok], s_[:], b_s, reads=[b_s])
                for dd in range(2):
                    q = nps % 4
                    nps += 1
                    wc = off_dn + dd * 16
                    for k in range(8):
                        g.mm(pp[q][0:16, :], wt[:, k, wc:wc + 16], hTb[j][:, k, :], k == 0, k == 7,
                             [b_wt, b_hTb[j]], [b_pp[q]])
                    s_, b_s = stf[nf % 4], b_stf[nf % 4]
                    nf += 1
                    g.cp("dve", s_[0:16, :], pp[q][0:16, :], [b_pp[q]], [b_s])
                    g.dma("pool", dnT[dd, :, tok], s_[0:16, :], b_s, reads=[b_s])
            g.flush(ENGS)

    def phase_merge(l, x_src, x_dst, next_w_row, hT_dst):
        with ExitStack() as es:
            wb_ = [sb(es, "wbr%d" % i, [128, 8, D], BF16) for i in range(3)]
            wo = sb(es, "wo", [128, 8, D], BF16)
            b_w = Buf("w")
            yb = [sb(es, "yb%d" % i, [128, 8, TB], BF16) for i in range(3)]
            b_yb = [Buf("yb%d" % i) for i in range(3)]
            gt = sb(es, "gt", [128, 24, TB], BF16)
            b_gt = Buf("gt")
            tmp = [sb(es, "mtmp%d" % i, [128, TB], F32) for i in range(3)]
            b_tmp = [Buf("tmp%d" % i) for i in range(3)]
            mT = sb(es, "mT", [128, 8, TB], BF16)
            b_mT = Buf("mT")
            xt = sb(es, "m_xt", [128, D], F32)
            b_xt = Buf("xt")
            xn = [sb(es, "m_xn%d" % i, [128, D], F32) for i in range(2)]
            b_xn = [Buf("xn0"), Buf("xn1")]
            pp = [ps(es, "m_p%d" % i, [128, 512], F32) for i in range(6)]
            b_pp = [Buf("pp%d" % i) for i in range(6)]
            nrm = NormCtx(es, "mrg")
            for i in range(3):
                for k in range(8):
                    g.dma("pool", wb_[i][:, k, :], w_branch[i][l, k * 128:(k + 1) * 128, :], b_w, writes=[b_w])
            for k in range(8):
                g.dma("pool", wo[:, k, :], w_out[l, k * 128:(k + 1) * 128, :], b_w, writes=[b_w])
            nrm.setup(next_w_row)
            ntl = 0
            for blk in range(NB):
                tok = slice(blk * TB, (blk + 1) * TB)
                for i in range(3):
                    g.dma("sp", yb[i][:], yT[i][:, :, tok].rearrange("k p t -> p k t"), b_yb[i], writes=[b_yb[i]])
                g.dma("sp", gt[:], gatesT[:, :, tok].rearrange("k p t -> p k t"), b_gt, writes=[b_gt])
                for oc in range(8):
                    par = (oc % 2) * 3
                    for i in range(3):
                        for k in range(8):
                            g.mm(pp[par + i][:], wb_[i][:, k, oc * 128:(oc + 1) * 128], yb[i][:, k, :], k == 0, k == 7,
                                 [b_w, b_yb[i]], [b_pp[par + i]])
                    for i in range(3):
                        g.tt("dve", tmp[i][:], pp[par + i][:], gt[:, i * 8 + oc, :], ALU.mult,
                             [b_pp[par + i], b_gt], [b_tmp[i]])
                    g.tt("pool", tmp[0][:], tmp[0][:], tmp[1][:], ALU.add, [b_tmp[0], b_tmp[1]], [b_tmp[0]])
                    g.tt("pool", mT[:, oc, :], tmp[0][:], tmp[2][:], ALU.add, [b_tmp[0], b_tmp[2]], [b_mT])
                for ti in range(NTI):
                    t = blk * NTI + ti
                    i = ntl % 2
                    ntl += 1
                    g.dma("sp", xt[:], x_src[t * 128:(t + 1) * 128, :], b_xt, writes=[b_xt])
                    for ch in range(2):
                        q = ch * 3
                        for k in range(8):
                            g.mm(pp[q][:], mT[:, k, ti * 128:(ti + 1) * 128], wo[:, k, ch * 512:(ch + 1) * 512],
                                 k == 0, k == 7, [b_mT, b_w], [b_pp[q]])
                        g.tt("dve", xn[i][:, ch * 512:(ch + 1) * 512], pp[q][:], xt[:, ch * 512:(ch + 1) * 512], ALU.add,
                             [b_pp[q], b_xt], [b_xn[i]])
                    g.dma("pool", x_dst[t * 128:(t + 1) * 128, :], xn[i][:], b_xn[i], reads=[b_xn[i]])
                    nrm.run(xn[i][:], b_xn[i], hT_dst, blk, ti)
            g.flush(ENGS)

    ssm_conv_w = din("ssm_conv_w", [DEPTH, 5, 1536])
    ssm_conv_b = din("ssm_conv_b", [DEPTH, 1536])
    ssm_dt_bias = din("ssm_dt_bias", [DEPTH, 32])
    ssm_a_log = din("ssm_a_log", [DEPTH, 32])
    ssm_d = din("ssm_d", [DEPTH, 16])
    ssm_norm_w = din("ssm_norm_w", [DEPTH, 1024])
    tri_in = din("tri", [2, 128, 128])
    maskadd_in = din("maskadd", [2, 128, 128])
    mask01_in = din("mask01", [2, 128, 128])
    ustrict_in = din("ustrict", [2, 128, 128])
    ones_in = din("ones", [128, 128])
    xs_tm = dscr("xs_tm", [NT, 1024])
    B_tm = dscr("B_tm", [NT, 256], BF16)
    BCT = dscr("BCT", [4, 128, NT], BF16)
    dtsp = dscr("dtsp", [NT, 32])
    dta_d = dscr("dta", [NT, 32])
    yf_d = dscr("yf", [NT, 1024])
    NCH = NT // 128
    MIDC = NCH // 2

    def phase_ssd_prep(l):
        with ExitStack() as es:
            cw = sb(es, "cw", [128, 12, 5], F32)
            cb = sb(es, "cb", [128, 12], F32)
            fl = sb(es, "fl", [128, 2], F32)
            identf = sb(es, "identf", [128, 128], F32)
            dbias = sb(es, "dbias", [128, 32], F32)
            aB = sb(es, "aB", [128, 32], F32)
            b_c = Buf("consts")
            xin = [sb(es, "xin%d" % i, [128, 12, TB + 4], F32) for i in range(2)]
            b_xin = [Buf("xin0"), Buf("xin1")]
            acc = [sb(es, "acc%d" % i, [128, TB], F32) for i in range(3)]
            b_acc = [Buf("acc%d" % i) for i in range(3)]
            sil = sb(es, "sil", [128, 10, TB], F32)
            b_sil = Buf("sil")
            silb = [sb(es, "silb%d" % i, [128, TB], BF16) for i in range(2)]
            b_silb = [Buf("silb0"), Buf("silb1")]
            xst = [sb(es, "xst%d" % i, [128, 1024], F32) for i in range(2)]
            b_xst = [Buf("xst0"), Buf("xst1")]
            bst = [sb(es, "bst%d" % i, [128, 256], BF16) for i in range(2)]
            b_bst = [Buf("bst0"), Buf("bst1")]
            dtt = [sb(es, "dtt%d" % i, [128, 3, NTI, 32], F32) for i in range(2)]
            b_dtt = [Buf("dtt0"), Buf("dtt1")]
            pt = [ps(es, "pp_t%d" % i, [128, 1024], F32) for i in range(2)]
            b_pt = [Buf("pt0"), Buf("pt1")]
            pbt = [ps(es, "pp_b%d" % i, [128, 256], F32) for i in range(2)]
            b_pbt = [Buf("pbt0"), Buf("pbt1")]
            for k in range(5):
                g.dma("sp", cw[:, :, k], ssm_conv_w[l, k].rearrange("(c p) -> p c", p=128), b_c, writes=[b_c], slow=True)
            g.dma("sp", cb[:], ssm_conv_b[l].rearrange("(c p) -> p c", p=128), b_c, writes=[b_c], slow=True)
            g.dma("sp", fl[:], flag, b_c, writes=[b_c])
            g.dma("sp", identf[:], ident_in, b_c, writes=[b_c])
            g.dma("sp", dbias[:], ssm_dt_bias[l:l + 1, :].partition_broadcast(128), b_c, writes=[b_c])
            g.dma("sp", aB[:], ssm_a_log[l:l + 1, :].partition_broadcast(128), b_c, writes=[b_c])
            g.act(aB[:], aB[:], AF.Exp, [b_c], [b_c])
            g.ts("dve", aB[:], aB[:], -1.0, None, ALU.mult, None, [b_c], [b_c])
            na = 0
            nsb = 0
            nt_ = 0
            for blk in range(NB):
                j = blk % 2
                t0 = blk * TB
                lo, hi = t0 - 2, t0 + TB + 2
                xi = xin[j]
                lo_c, hi_c = 0, TB + 4
                if lo < 0:
                    g.memset("pool", xi[:, :, 0:2], 0.0, [b_xin[j]])
                    lo_c, lo = 2, 0
                if hi > NT:
                    g.memset("pool", xi[:, :, TB + 2:TB + 4], 0.0, [b_xin[j]])
                    hi_c, hi = TB + 2, NT
                g.dma("sp", xi[:, :, lo_c:hi_c], xbcT[:, :, lo:hi].rearrange("c p t -> p c t"), b_xin[j],
                      writes=[b_xin[j]])
                if t0 == NT // 2:
                    g.ts("dve", xi[:, :, 0:2], xi[:, :, 0:2], fl[:, 0:1], None, ALU.mult, None, [b_xin[j], b_c], [b_xin[j]])
                if t0 + TB == NT // 2:
                    g.ts("dve", xi[:, :, TB + 2:TB + 4], xi[:, :, TB + 2:TB + 4], fl[:, 0:1], None, ALU.mult, None,
                         [b_xin[j], b_c], [b_xin[j]])
                dq = dtt[j]
                g.dma("sp", dq[:, 0, :, :], dt_tm[t0:t0 + TB, :].rearrange("(i p) c -> p i c", p=128), b_dtt[j],
                      writes=[b_dtt[j]])
                g.tt("dve", dq[:, 0, :, :], dq[:, 0, :, :], dbias[:, None, :].to_broadcast([128, NTI, 32]), ALU.add,
                     [b_dtt[j], b_c], [b_dtt[j]])
                g.act(dq[:, 1, :, :], dq[:, 0, :, :], AF.Exp, [b_dtt[j]], [b_dtt[j]])
                g.ts("dve", dq[:, 1, :, :], dq[:, 1, :, :], 1.0, None, ALU.add, None, [b_dtt[j]], [b_dtt[j]])
                g.act(dq[:, 1, :, :], dq[:, 1, :, :], AF.Ln, [b_dtt[j]], [b_dtt[j]])
                g.tt("dve", dq[:, 2, :, :], dq[:, 1, :, :], aB[:, None, :].to_broadcast([128, NTI, 32]), ALU.mult,
                     [b_dtt[j], b_c], [b_dtt[j]])
                g.dma("pool", dtsp[t0:t0 + TB, :].rearrange("(i p) c -> p i c", p=128), dq[:, 1, :, :], b_dtt[j],
                      reads=[b_dtt[j]])
                g.dma("pool", dta_d[t0:t0 + TB, :].rearrange("(i p) c -> p i c", p=128), dq[:, 2, :, :], b_dtt[j],
                      reads=[b_dtt[j]])
                for c in range(12):
                    a = na % 3
                    na += 1
                    g.act(acc[a][:], xi[:, c, 0:TB], AF.Identity, [b_xin[j], b_c], [b_acc[a]],
                          bias=cb[:, c:c + 1], scale=cw[:, c, 0:1])
                    for k in range(1, 5):
                        g.stt("dve", acc[a][:], xi[:, c, k:k + TB], cw[:, c, k:k + 1], acc[a][:],
                              ALU.mult, ALU.add, [b_xin[j], b_c, b_acc[a]], [b_acc[a]])
                    if c < 10:
                        g.act(sil[:, c, :], acc[a][:], AF.Silu, [b_acc[a]], [b_sil])
                    if c >= 8:
                        q = nsb % 2
                        nsb += 1
                        g.act(silb[q][:], acc[a][:], AF.Silu, [b_acc[a]], [b_silb[q]])
                        g.dma("pool", BCT[c - 8, :, t0:t0 + TB], silb[q][:], b_silb[q], reads=[b_silb[q]])
                for ti in range(NTI):
                    i = nt_ % 2
                    nt_ += 1
                    tk = slice(t0 + ti * 128, t0 + (ti + 1) * 128)
                    for c in range(8):
                        g.tr(pt[i][:, c * 128:(c + 1) * 128], sil[:, c, ti * 128:(ti + 1) * 128], identf[:],
                             [b_sil, b_c], [b_pt[i]])
                    for c in range(2):
                        g.tr(pbt[i][:, c * 128:(c + 1) * 128], sil[:, 8 + c, ti * 128:(ti + 1) * 128], identf[:],
                             [b_sil, b_c], [b_pbt[i]])
                    g.cp("act", xst[i][:], pt[i][:], [b_pt[i]], [b_xst[i]])
                    g.cp("dve", bst[i][:], pbt[i][:], [b_pbt[i]], [b_bst[i]])
                    g.dma("pool", xs_tm[tk, :], xst[i][:], b_xst[i], reads=[b_xst[i]])
                    g.dma("pool", B_tm[tk, :], bst[i][:], b_bst[i], reads=[b_bst[i]])
            g.flush(ENGS)

    def phase_ssd_pass(l, d):
        with ExitStack() as es:
            tri = sb(es, "tri", [128, 128], F32)
            ones = sb(es, "ones", [128, 128], F32)
            identf = sb(es, "identf", [128, 128], F32)
            identb = sb(es, "identb", [128, 128], BF16)
            maskB = sb(es, "maskB", [128, 16, 128], F32)
            fl = sb(es, "fl", [128, 2], F32)
            b_c = Buf("consts")
            xs = [sb(es, "xs%d" % i, [128, 1024], F32) for i in range(2)]
            Bt = [sb(es, "Bt%d" % i, [128, 256], BF16) for i in range(2)]
            bct = [sb(es, "bct%d" % i, [128, 4, 128], BF16) for i in range(2)]
            dts = [sb(es, "dts%d" % i, [128, 2, 16], F32) for i in range(2)]
            b_in = [Buf("in0"), Buf("in1")]
            R = sb(es, "R", [128, 16, 128], F32)
            b_R = Buf("R")
            cs_sb = sb(es, "cs_sb", [128, 16], F32)
            ecs = sb(es, "ecs", [128, 16], F32)
            b_cs = Buf("cs_sb")
            b_ecs = Buf("ecs")
            seg = sb(es, "seg", [128, 16, 128], F32)
            b_seg = Buf("seg")
            cbT = sb(es, "cbT", [128, 2, 128], F32)
            b_cbT = Buf("cbT")
            MT = sb(es, "MT", [128, 16, 128], BF16)
            b_MT = Buf("MT")
            xr = sb(es, "xr", [128, 1024], BF16)
            b_xr = Buf("xr")
            xd = sb(es, "xd", [128, 1024], BF16)
            b_xd = Buf("xd")
            S = sb(es, "S", [128, 1024], F32)
            Sbf = sb(es, "Sbf", [128, 1024], BF16)
            b_S, b_Sbf = Buf("S"), Buf("Sbf")
            decB = sb(es, "decB", [128, 16], F32)
            b_decB = Buf("decB")
            t1 = sb(es, "t1", [128, 1024], F32)
            b_t1 = Buf("t1")
            yd = [sb(es, "yd%d" % i, [128, 1024], F32) for i in range(2)]
            b_yd = [Buf("yd0"), Buf("yd1")]
            pA = ps(es, "pA", [128, 1024], F32)
            pM = ps(es, "pM", [128, 512], F32)
            pY = ps(es, "pY", [128, 1024], F32)
            pO = ps(es, "pO", [128, 1024], F32)
            b_pA, b_pM, b_pY, b_pO = Buf("pA"), Buf("pM"), Buf("pY"), Buf("pO")
            if d == 1:
                dB = sb(es, "dB", [128, 16], F32)
                nwB = sb(es, "nwB", [128, 1024], F32)
                yfl = [sb(es, "yfl%d" % i, [128, 1024], F32) for i in range(2)]
                zt = [sb(es, "zt%d" % i, [128, 1024], F32) for i in range(2)]
                b_in2 = [Buf("in2_0"), Buf("in2_1")]
                sq = sb(es, "sq", [128, 1024], F32)
                b_sq = Buf("sq")
                gs = sb(es, "gs", [128, 4], F32)
                b_gs = Buf("gs")
                ynb = sb(es, "ynb", [128, 1024], BF16)
                b_ynb = Buf("ynb")
                yTs = [sb(es, "yTs%d" % i, [128, 8, 128], BF16) for i in range(2)]
                b_yTs = [Buf("yTs0"), Buf("yTs1")]
                pT = ps(es, "pT", [128, 1024], BF16)
                b_pT = Buf("pT")
                g.dma("sp", dB[:], ssm_d[l:l + 1, :].partition_broadcast(128), b_c, writes=[b_c])
                g.dma("sp", nwB[:], ssm_norm_w[l:l + 1, :].partition_broadcast(128), b_c, writes=[b_c])
                g.dma("pool", identb[:], ident_in, b_c, writes=[b_c])
            g.dma("sp", tri[:], tri_in[d], b_c, writes=[b_c])
            g.dma("sp", ones[:], ones_in, b_c, writes=[b_c])
            g.dma("sp", identf[:], ident_in, b_c, writes=[b_c])
            g.dma("sp", fl[:], flag, b_c, writes=[b_c])
            for h in range(16):
                g.dma("sp", maskB[:, h, :], maskadd_in[d], b_c, writes=[b_c])
            g.memset("dve", S[:], 0.0, [b_S])
            g.memset("pool", Sbf[:], 0.0, [b_Sbf])
            last = 127 if d == 0 else 0
            order = list(range(NCH)) if d == 0 else list(range(NCH - 1, -1, -1))
            for n, c in enumerate(order):
                i = n % 2
                tk = slice(c * 128, (c + 1) * 128)
                g.dma("sp", xs[i][:], xs_tm[tk, :], b_in[i], writes=[b_in[i]])
                g.dma("sp", Bt[i][:], B_tm[tk, :], b_in[i], writes=[b_in[i]])
                g.dma("sp", bct[i][:], BCT[:, :, tk].rearrange("c p t -> p c t"), b_in[i], writes=[b_in[i]])
                g.dma("sp", dts[i][:, 0, :], dtsp[tk, d * 16:(d + 1) * 16], b_in[i], writes=[b_in[i]])
                g.dma("sp", dts[i][:, 1, :], dta_d[tk, d * 16:(d + 1) * 16], b_in[i], writes=[b_in[i]])
                if d == 1:
                    g.dma("sp", yfl[i][:], yf_d[tk, :], b_in2[i], writes=[b_in2[i]])
                    g.dma("sp", zt[i][:], z_tm[tk, :], b_in2[i], writes=[b_in2[i]])
                dta = dts[i][:, 1, :]
                dsp = dts[i][:, 0, :]
                if (d == 0 and c == MIDC) or (d == 1 and c == MIDC - 1):
                    g.ts("dve", S[:], S[:], fl[:, 0:1], None, ALU.mult, None, [b_S, b_c], [b_S])
                    g.ts("pool", Sbf[:], Sbf[:], fl[:, 0:1], None, ALU.mult, None, [b_Sbf, b_c], [b_Sbf])
                g.mm(pM[:, 0:16], tri[:], dta, True, True, [b_c, b_in[i]], [b_pM])
                for gi in range(2):
                    g.mm(pM[:, 128 + gi * 128:256 + gi * 128], bct[i][:, gi, :], bct[i][:, 2 + gi, :], True, True,
                         [b_in[i]], [b_pM])
                g.cp("act", cs_sb[:], pM[:, 0:16], [b_pM], [b_cs])
                g.cp("act", cbT[:], pM[:, 128:384].rearrange("p (g t) -> p g t", g=2), [b_pM], [b_cbT])
                g.act(ecs[:], cs_sb[:], AF.Exp, [b_cs], [b_ecs])
                g.tt("dve", R[:], tri[:, None, :].to_broadcast([128, 16, 128]), dta[:, :, None].to_broadcast([128, 16, 128]),
                     ALU.mult, [b_c, b_in[i]], [b_R])
                g.tt("pool", xr[:].rearrange("p (h q) -> p h q", h=16), xs[i][:].rearrange("p (h q) -> p h q", h=16),
                     dsp[:, :, None].to_broadcast([128, 16, 64]), ALU.mult, [b_in[i]], [b_xr])
                for gi in range(2):
                    g.mm(pO[:, gi * 512:(gi + 1) * 512], bct[i][:, 2 + gi, :], Sbf[:, gi * 512:(gi + 1) * 512], True, True,
                         [b_in[i], b_Sbf], [b_pO])
                for half in range(2):
                    hs = slice(half * 8, half * 8 + 8)
                    for q in range(2):
                        h0 = half * 8 + q * 4
                        g.mm(pA[:, q * 512:(q + 1) * 512], ones[:], R[:, h0:h0 + 4, :].rearrange("p h t -> p (h t)"),
                             True, False, [b_c, b_R], [b_pA])
                        g.mm(pA[:, q * 512:(q + 1) * 512], identf[:], maskB[:, h0:h0 + 4, :].rearrange("p h t -> p (h t)"),
                             False, True, [b_c], [b_pA])
                    g.tt("dve", seg[:, hs, :], pA[:].rearrange("p (h t) -> p h t", h=8),
                         cs_sb[:, hs][:, :, None].to_broadcast([128, 8, 128]), ALU.subtract, [b_pA, b_cs], [b_seg])
                    if half == 1:
                        pass
                    g.act(seg[:, hs, :], seg[:, hs, :], AF.Exp, [b_seg], [b_seg])
                    g.tt("pool", MT[:, hs, :], seg[:, hs, :],
                         cbT[:, half:half + 1, :].to_broadcast([128, 8, 128]), ALU.mult, [b_seg, b_cbT], [b_MT])
                for h in range(16):
                    g.mm(pY[:, h * 64:(h + 1) * 64], MT[:, h, :], xr[:, h * 64:(h + 1) * 64], True, True,
                         [b_MT, b_xr], [b_pY])
                g.tt("dve", t1[:].rearrange("p (h q) -> p h q", h=16), pO[:].rearrange("p (h q) -> p h q", h=16),
                     ecs[:, :, None].to_broadcast([128, 16, 64]), ALU.mult, [b_pO, b_ecs], [b_t1])
                g.tt("dve", yd[i][:], pY[:], t1[:], ALU.add, [b_pY, b_t1], [b_yd[i]])
                g.tt("pool", xd[:].rearrange("p (h q) -> p h q", h=16), xr[:].rearrange("p (h q) -> p h q", h=16),
                     seg[:, :, last:last + 1].to_broadcast([128, 16, 64]), ALU.mult, [b_xr, b_seg], [b_xd])
                g.mm(pM[:, 16:32], ones[:], dta, True, True, [b_c, b_in[i]], [b_pM])
                g.act(decB[:], pM[:, 16:32], AF.Exp, [b_pM], [b_decB])
                for gi in range(2):
                    g.mm(pA[:, gi * 512:(gi + 1) * 512], Bt[i][:, gi * 128:(gi + 1) * 128], xd[:, gi * 512:(gi + 1) * 512],
                         True, True, [b_in[i], b_xd], [b_pA])
                g.tt("pool", S[:].rearrange("p (h q) -> p h q", h=16), S[:].rearrange("p (h q) -> p h q", h=16),
                     decB[:, :, None].to_broadcast([128, 16, 64]), ALU.mult, [b_S, b_decB], [b_S])
                g.tt("dve", S[:], S[:], pA[:], ALU.add, [b_S, b_pA], [b_S])
                g.cp("act", Sbf[:], S[:], [b_S], [b_Sbf])
                if d == 0:
                    g.dma("pool", yf_d[tk, :], yd[i][:], b_yd[i], reads=[b_yd[i]])
                else:
                    y = yd[i]
                    g.tt("pool", y[:], y[:], yfl[i][:], ALU.add, [b_yd[i], b_in2[i]], [b_yd[i]])
                    g.tt("pool", t1[:].rearrange("p (h q) -> p h q", h=16), xs[i][:].rearrange("p (h q) -> p h q", h=16),
                         dB[:, :, None].to_broadcast([128, 16, 64]), ALU.mult, [b_in[i], b_c], [b_t1])
                    g.tt("pool", y[:], y[:], t1[:], ALU.add, [b_yd[i], b_t1], [b_yd[i]])
                    g.act(zt[i][:], zt[i][:], AF.Silu, [b_in2[i]], [b_in2[i]])
                    g.tt("dve", y[:], y[:], zt[i][:], ALU.mult, [b_yd[i], b_in2[i]], [b_yd[i]])
                    g.act(sq[:], y[:], AF.Square, [b_yd[i]], [b_sq])
                    g.red("dve", gs[:, 0:2], sq[:].rearrange("p (g q) -> p g q", g=2), [b_sq], [b_gs])
                    g.ts("dve", gs[:, 2:4], gs[:, 0:2], 1.0 / 512, EPS, ALU.mult, ALU.add, [b_gs], [b_gs])
                    g.act(gs[:, 2:4], gs[:, 2:4], AF.Sqrt, [b_gs], [b_gs])
                    g._record("dve", lambda e, gs=gs: e.reciprocal(out=gs[:, 2:4], in_=gs[:, 2:4]), [b_gs], [b_gs])
                    g.tt("dve", y[:].rearrange("p (g q) -> p g q", g=2), y[:].rearrange("p (g q) -> p g q", g=2),
                         gs[:, 2:4][:, :, None].to_broadcast([128, 2, 512]), ALU.mult, [b_yd[i], b_gs], [b_yd[i]])
                    g.tt("pool", ynb[:], y[:], nwB[:], ALU.mult, [b_yd[i], b_c], [b_ynb])
                    for k in range(8):
                        g.tr(pT[:, k * 128:(k + 1) * 128], ynb[:, k * 128:(k + 1) * 128], identb[:], [b_ynb, b_c], [b_pT])
                    g.cp("act", yTs[i][:], pT[:].rearrange("p (k t) -> p k t", k=8), [b_pT], [b_yTs[i]])
                    g.dma("pool", yT[0][:, :, tk].rearrange("k p t -> p k t"), yTs[i][:], b_yTs[i], reads=[b_yTs[i]])
            g.flush(ENGS)

    gla_gate_up = din("gla_gate_up", [DEPTH, 2, 16, 512])
    gla_gate_b = din("gla_gate_b", [DEPTH, 2, 512])
    gla_norm_w = din("gla_norm_w", [DEPTH, 256])
    of_d = dscr("of", [NT, 1024])

    def phase_gla_pass(l, d):
        with ExitStack() as es:
            triS = sb(es, "triS", [128, 128], F32)
            uS = sb(es, "uS", [128, 128], F32)
            m01 = sb(es, "m01", [128, 128], F32)
            ones = sb(es, "ones", [128, 128], F32)
            up = sb(es, "up", [16, 512], F32)
            gbr = sb(es, "gbr", [1, 512], F32)
            fl = sb(es, "fl", [128, 2], F32)
            b_c = Buf("consts")
            qT = [sb(es, "qT%d" % i, [128, 4, 128], F32) for i in range(2)]
            kT = [sb(es, "kT%d" % i, [128, 4, 128], F32) for i in range(2)]
            ktm = [sb(es, "ktm%d" % i, [128, 512], F32) for i in range(2)]
            v = [sb(es, "v%d" % i, [128, 1024], BF16) for i in range(2)]
            dn = [sb(es, "dn%d" % i, [16, 128], F32) for i in range(2)]
            b_in = [Buf("in0"), Buf("in1")]
            lsp = sb(es, "lsp", [128, 512], F32)
            b_lsp = Buf("lsp")
            eb = sb(es, "eb", [128, 4, 128], F32)
            enb = sb(es, "enb", [128, 4, 128], F32)
            er = sb(es, "er", [128, 512], F32)
            b_eb, b_enb, b_er = Buf("eb"), Buf("enb"), Buf("er")
            qe = sb(es, "qe", [128, 4, 128], BF16)
            ke = sb(es, "ke", [128, 4, 128], BF16)
            kdec = sb(es, "kdec", [128, 512], BF16)
            att = sb(es, "att", [128, 4, 128], BF16)
            b_qe, b_ke, b_kdec, b_att = Buf("qe"), Buf("ke"), Buf("kdec"), Buf("att")
            S = sb(es, "S", [128, 4, 256], F32)
            Sbf = sb(es, "Sbf", [128, 4, 256], BF16)
            b_S, b_Sbf = Buf("S"), Buf("Sbf")
            od = [sb(es, "od%d" % i, [128, 1024], F32) for i in range(2)]
            b_od = [Buf("od0"), Buf("od1")]
            pLR = ps(es, "pLR", [128, 512], F32)
            pB = ps(es, "pB", [128, 512], F32)
            pAt = ps(es, "pAt", [128, 512], F32)
            pOo = ps(es, "pOo", [128, 1024], F32)
            pSp = ps(es, "pSp", [128, 1024], F32)
            b_pLR, b_pB, b_pAt, b_pOo, b_pSp = Buf("pLR"), Buf("pB"), Buf("pAt"), Buf("pOo"), Buf("pSp")
            if d == 1:
                identb = sb(es, "identb", [128, 128], BF16)
                nwB = sb(es, "nwB", [128, 1024], F32)
                ofl = [sb(es, "ofl%d" % i, [128, 1024], F32) for i in range(2)]
                gg = [sb(es, "gg%d" % i, [128, 1024], F32) for i in range(2)]
                b_in2 = [Buf("in2_0"), Buf("in2_1")]
                sq = sb(es, "sq", [128, 1024], F32)
                b_sq = Buf("sq")
                gs = sb(es, "gs", [128, 8], F32)
                b_gs = Buf("gs")
                onb = sb(es, "onb", [128, 1024], BF16)
                b_onb = Buf("onb")
                yTs = [sb(es, "yTs%d" % i, [128, 8, 128], BF16) for i in range(2)]
                b_yTs = [Buf("yTs0"), Buf("yTs1")]
                pT = ps(es, "pT", [128, 1024], BF16)
                b_pT = Buf("pT")
                g.dma("pool", identb[:], ident_in, b_c, writes=[b_c])
                for h in range(4):
                    g.dma("sp", nwB[:, h * 256:(h + 1) * 256], gla_norm_w[l:l + 1, :].partition_broadcast(128), b_c,
                          writes=[b_c])
            g.dma("sp", triS[:], tri_in[d], b_c, writes=[b_c])
            g.dma("sp", uS[:], ustrict_in[d], b_c, writes=[b_c])
            g.dma("sp", m01[:], mask01_in[d], b_c, writes=[b_c])
            g.dma("sp", ones[:], ones_in, b_c, writes=[b_c])
            g.dma("sp", up[:], gla_gate_up[l, d], b_c, writes=[b_c])
            g.dma("sp", gbr[:], gla_gate_b[l, d:d + 1, :], b_c, writes=[b_c])
            g.dma("sp", fl[:], flag, b_c, writes=[b_c])
            g.ts("dve", triS[:], triS[:], -1.0 / 16.0, None, ALU.mult, None, [b_c], [b_c])
            g.ts("dve", uS[:], uS[:], -1.0 / 16.0, None, ALU.mult, None, [b_c], [b_c])
            g.memset("dve", S[:], 0.0, [b_S])
            g.memset("pool", Sbf[:], 0.0, [b_Sbf])
            last = 127 if d == 0 else 0
            order = list(range(NCH)) if d == 0 else list(range(NCH - 1, -1, -1))
            for n, c in enumerate(order):
                i = n % 2
                tk = slice(c * 128, (c + 1) * 128)
                g.dma("sp", qT[i][:], gqT[:, :, tk].rearrange("h p t -> p h t"), b_in[i], writes=[b_in[i]])
                g.dma("sp", kT[i][:], gkT[:, :, tk].rearrange("h p t -> p h t"), b_in[i], writes=[b_in[i]])
                g.dma("sp", ktm[i][:], kg_tm[tk, :], b_in[i], writes=[b_in[i]])
                g.dma("sp", v[i][:], vg_tm[tk, :], b_in[i], writes=[b_in[i]])
                g.dma("sp", dn[i][:], dnT[d, :, tk], b_in[i], writes=[b_in[i]])
                if d == 1:
                    g.dma("sp", ofl[i][:], of_d[tk, :], b_in2[i], writes=[b_in2[i]])
                    g.dma("sp", gg[i][:], gg_tm[tk, :], b_in2[i], writes=[b_in2[i]])
                if (d == 0 and c == MIDC) or (d == 1 and c == MIDC - 1):
                    g.ts("dve", S[:], S[:], fl[:, 0:1], None, ALU.mult, None, [b_S, b_c], [b_S])
                    g.ts("pool", Sbf[:], Sbf[:], fl[:, 0:1], None, ALU.mult, None, [b_Sbf, b_c], [b_Sbf])
                g.mm(pLR[:], dn[i][:], up[:], True, False, [b_in[i], b_c], [b_pLR])
                g.mm(pLR[:], ones[0:1, :], gbr[:], False, True, [b_c], [b_pLR])
                g.act(lsp[:], pLR[:], AF.Exp, [b_pLR], [b_lsp], scale=-1.0)
                g.ts("dve", lsp[:], lsp[:], 1.0, None, ALU.add, None, [b_lsp], [b_lsp])
                g.act(lsp[:], lsp[:], AF.Ln, [b_lsp], [b_lsp])
                for h in range(4):
                    g.mm(pB[:, h * 128:(h + 1) * 128], lsp[:, h * 128:(h + 1) * 128], triS[:], True, True,
                         [b_lsp, b_c], [b_pB])
                g.mm(pLR[:], uS[:], lsp[:], True, True, [b_c, b_lsp], [b_pLR])
                pB3 = pB[:].rearrange("p (h t) -> p h t", h=4)
                g.act(eb[:], pB3, AF.Exp, [b_pB], [b_eb])
                g.act(enb[:], pB3, AF.Exp, [b_pB], [b_enb], scale=-1.0)
                g.act(er[:], pLR[:], AF.Exp, [b_pLR], [b_er])
                g.tt("dve", qe[:], qT[i][:], eb[:], ALU.mult, [b_in[i], b_eb], [b_qe])
                g.tt("pool", ke[:], kT[i][:], enb[:], ALU.mult, [b_in[i], b_enb], [b_ke])
                g.tt("pool", kdec[:], ktm[i][:], er[:], ALU.mult, [b_in[i], b_er], [b_kdec])
                for h in range(4):
                    g.mm(pAt[:, h * 128:(h + 1) * 128], ke[:, h, :], qe[:, h, :], True, True, [b_ke, b_qe], [b_pAt])
                g.tt("dve", att[:], pAt[:].rearrange("p (h t) -> p h t", h=4), m01[:, None, :].to_broadcast([128, 4, 128]),
                     ALU.mult, [b_pAt, b_c], [b_att])
                for h in range(4):
                    g.mm(pOo[:, h * 256:(h + 1) * 256], att[:, h, :], v[i][:, h * 256:(h + 1) * 256], True, False,
                         [b_att, b_in[i]], [b_pOo])
                    g.mm(pOo[:, h * 256:(h + 1) * 256], qe[:, h, :], Sbf[:, h, :], False, True, [b_qe, b_Sbf], [b_pOo])
                for h in range(4):
                    g.mm(pSp[:, h * 256:(h + 1) * 256], kdec[:, h * 128:(h + 1) * 128], v[i][:, h * 256:(h + 1) * 256],
                         True, True, [b_kdec, b_in[i]], [b_pSp])
                for h in range(4):
                    g.stt("dve", S[:, h, :], S[:, h, :], eb[:, h, last:last + 1], pSp[:, h * 256:(h + 1) * 256],
                          ALU.mult, ALU.add, [b_S, b_eb, b_pSp], [b_S])
                g.cp("act", Sbf[:], S[:], [b_S], [b_Sbf])
                if d == 0:
                    g.cp("act", od[i][:], pOo[:], [b_pOo], [b_od[i]])
                    g.dma("pool", of_d[tk, :], od[i][:], b_od[i], reads=[b_od[i]])
                else:
                    o = od[i]
                    g.tt("dve", o[:], pOo[:], ofl[i][:], ALU.add, [b_pOo, b_in2[i]], [b_od[i]])
                    g.act(sq[:], o[:], AF.Square, [b_od[i]], [b_sq])
                    g.red("dve", gs[:, 0:4], sq[:].rearrange("p (h q) -> p h q", h=4), [b_sq], [b_gs])
                    g.ts("dve", gs[:, 4:8], gs[:, 0:4], 1.0 / 256, EPS, ALU.mult, ALU.add, [b_gs], [b_gs])
                    g.act(gs[:, 4:8], gs[:, 4:8], AF.Sqrt, [b_gs], [b_gs])
                    g._record("dve", lambda e, gs=gs: e.reciprocal(out=gs[:, 4:8], in_=gs[:, 4:8]), [b_gs], [b_gs])
                    g.tt("dve", o[:].rearrange("p (h q) -> p h q", h=4), o[:].rearrange("p (h q) -> p h q", h=4),
                         gs[:, 4:8][:, :, None].to_broadcast([128, 4, 256]), ALU.mult, [b_od[i], b_gs], [b_od[i]])
                    g.tt("pool", o[:], o[:], nwB[:], ALU.mult, [b_od[i], b_c], [b_od[i]])
                    g.act(gg[i][:], gg[i][:], AF.Silu, [b_in2[i]], [b_in2[i]])
                    g.tt("pool", onb[:], o[:], gg[i][:], ALU.mult, [b_od[i], b_in2[i]], [b_onb])
                    for k in range(8):
                        g.tr(pT[:, k * 128:(k + 1) * 128], onb[:, k * 128:(k + 1) * 128], identb[:], [b_onb, b_c], [b_pT])
                    g.cp("act", yTs[i][:], pT[:].rearrange("p (k t) -> p k t", k=8), [b_pT], [b_yTs[i]])
                    g.dma("pool", yT[1][:, :, tk].rearrange("k p t -> p k t"), yTs[i][:], b_yTs[i], reads=[b_yTs[i]])
            g.flush(ENGS)

    na_rpb = din("na_rpb", [DEPTH, 16, 15, 31])
    jflip_in = din("jflip", [64, 64])
    mint_in = din("m_int", [128, 16, 64])
    medge_in = din("m_edge", [128, 16, 64])
    rpbpad = dscr("rpbpad", [16, 17, 192])

    def phase_na(l):
        with ExitStack() as es:
            T2 = [sb(es, "T2i", [128, 16, 16, 64], BF16), sb(es, "T2e", [128, 16, 16, 64], BF16)]
            b_T2 = Buf("T2")
            Mx = [sb(es, "Mi", [128, 16, 64], F32), sb(es, "Me", [128, 16, 64], F32)]
            jf = sb(es, "jf", [64, 64], F32)
            identb = sb(es, "identb", [128, 128], BF16)
            fl = sb(es, "fl", [128, 2], F32)
            b_c = Buf("consts")
            padt = sb(es, "padt", [16, 17, 192], F32)
            rp = sb(es, "rp", [16, 15, 31], F32)
            b_pad = Buf("pad")
            Tpp = [sb(es, "Tpp%d" % i, [64, 4, 17, 64], F32) for i in range(2)]
            b_Tpp = [Buf("Tpp0"), Buf("Tpp1")]
            eW = [sb(es, "eW%d" % i, [128, 640], F32) for i in range(2)]
            b_eW = [Buf("eW0"), Buf("eW1")]
            Kc = [sb(es, "Kc%d" % i, [128, 8, 128], BF16) for i in range(8)]
            Vc = [sb(es, "Vc%d" % i, [128, 16, 65], BF16) for i in range(8)]
            b_K = [Buf("K%d" % i) for i in range(8)]
            b_V = [Buf("V%d" % i) for i in range(8)]
            Qc = [sb(es, "Qc%d" % i, [128, 8, 128], BF16) for i in range(2)]
            b_Q = [Buf("Q0"), Buf("Q1")]
            Pt = [sb(es, "Pt%d" % i, [128, 5, 2, 64], BF16) for i in range(2)]
            b_P = [Buf("P0"), Buf("P1")]
            rden = sb(es, "rden", [128, 16], F32)
            b_rden = Buf("rden")
            oA = sb(es, "oA", [128, 1024], F32)
            oB = sb(es, "oB", [128, 1024], F32)
            b_oA, b_oB = Buf("oA"), Buf("oB")
            onb = sb(es, "onb", [128, 1024], BF16)
            b_onb = Buf("onb")
            yTs = [sb(es, "yTs%d" % i, [128, 8, 128], BF16) for i in range(2)]
            b_yTs = [Buf("yTs0"), Buf("yTs1")]
            pS = [ps(es, "pS%d" % i, [128, 1024], F32) for i in range(2)]
            b_pS = [Buf("pS0"), Buf("pS1")]
            pOv = [ps(es, "pOv%d" % i, [128, 512], F32) for i in range(3)]
            b_pOv = Buf("pOv")
            pT = ps(es, "pT", [128, 1024], BF16)
            b_pT = Buf("pT")
            g.dma("sp", Mx[0][:], mint_in, b_c, writes=[b_c])
            g.dma("sp", Mx[1][:], medge_in, b_c, writes=[b_c])
            g.dma("sp", jf[:], jflip_in, b_c, writes=[b_c])
            g.dma("sp", fl[:], flag, b_c, writes=[b_c])
            g.dma("pool", identb[:], ident_in, b_c, writes=[b_c])
            for i in range(8):
                g.memset("pool", Vc[i][:], 1.0, [b_V[i]])
            g.memset("dve", padt[:], 0.0, [b_pad])
            g.dma("sp", rp[:], na_rpb[l], b_pad, writes=[b_pad])
            g.cp("dve", padt[:, 1:16, 64:95], rp[:], [b_pad], [b_pad])
            g.dma("sp", rpbpad, padt[:], b_pad, reads=[b_pad], writes=[b_pad])
            nb = 0
            for hg in range(4):
                tp = Tpp[hg % 2]
                b_tp = b_Tpp[hg % 2]
                for hh in range(4):
                    h = hg * 4 + hh
                    src = bass.AP(tensor=rpbpad.tensor, offset=h * 17 * 192 + 16, ap=[[1, 64], [192, 17], [1, 64]])
                    g.dma("sp", tp[:, hh, :, :], src, b_tp, reads=[b_pad], writes=[b_tp])
                for hh in range(4):
                    h = hg * 4 + hh
                    for sbt in range(2):
                        q = nb % 2
                        nb += 1
                        for si in range(8):
                            s_ = sbt * 8 + si
                            g.mm(pS[q][:, si * 64:(si + 1) * 64], tp[:, hh, s_:s_ + 2, :].rearrange("p r k -> p (r k)"),
                                 jf[:], True, True, [b_tp, b_c], [b_pS[q]])
                        g.act(eW[q][:, 0:512], pS[q][:, 0:512], AF.Exp, [b_pS[q]], [b_eW[q]])
                        g.tt("dve", T2[0][:, h, sbt * 8:sbt * 8 + 8, :], eW[q][:, 0:512].rearrange("p (s q) -> p s q", s=8),
                             Mx[0][:, sbt * 8:sbt * 8 + 8, :], ALU.mult, [b_eW[q], b_c], [b_T2])
                        g.tt("pool", T2[1][:, h, sbt * 8:sbt * 8 + 8, :], eW[q][:, 0:512].rearrange("p (s q) -> p s q", s=8),
                             Mx[1][:, sbt * 8:sbt * 8 + 8, :], ALU.mult, [b_eW[q], b_c], [b_T2])
            loaded = [-1]
            cnt = {"s": 0, "p": 0, "y": 0}

            def ensure(upto):
                while loaded[0] < min(upto, NCH - 1):
                    kt = loaded[0] + 1
                    sl_ = kt % 8
                    tk_ = slice(kt * 128, (kt + 1) * 128)
                    g.dma("sp", Kc[sl_][:], nkT[:, :, tk_].rearrange("c p t -> p c t"), b_K[sl_], writes=[b_K[sl_]])
                    g.dma("sp", Vc[sl_][:, :, 0:64], vn_tm[tk_, :].rearrange("p (h d) -> p h d", h=16), b_V[sl_],
                          writes=[b_V[sl_]])
                    loaded[0] = kt

            def na_pair(qp, kts, var, qi, o_dst, b_o):
                nt = len(kts)
                d0 = kts[0] - qp
                for h in range(16):
                    ch, p0 = h // 2, (h % 2) * 64
                    q = cnt["s"] % 2
                    cnt["s"] += 1
                    for ti, kt in enumerate(kts):
                        g.mm(pS[q][:, ti * 128:(ti + 1) * 128], Kc[kt % 8][p0:p0 + 64, ch, :], Qc[qi][p0:p0 + 64, ch, :],
                             True, True, [b_K[kt % 8], b_Q[qi]], [b_pS[q]])
                    g.act(eW[q][:, 0:nt * 128], pS[q][:, 0:nt * 128], AF.Exp, [b_pS[q]], [b_eW[q]])
                    pq = cnt["p"] % 2
                    cnt["p"] += 1
                    e4 = eW[q][:, 0:nt * 128].rearrange("p (t r c) -> p t r c", t=nt, r=2)
                    for qr2 in range(2):
                        s0 = 2 * d0 + 8 - qr2
                        base = T2[var][:, h, s0, :]
                        tv = bass.AP(tensor=base.tensor, offset=base.offset, ap=[list(base.ap[0]), [128, nt], [1, 64]])
                        g.tt("dve" if qr2 == 0 else "pool", Pt[pq][:, 0:nt, qr2, :], e4[:, :, qr2, :], tv, ALU.mult,
                             [b_eW[q], b_T2], [b_P[pq]])
                    bank, off = h // 7, (h % 7) * 65
                    for ti, kt in enumerate(kts):
                        g.mm(pOv[bank][:, off:off + 65], Pt[pq][:, ti, :, :].rearrange("p r c -> p (r c)"),
                             Vc[kt % 8][:, h, :], ti == 0, ti == nt - 1, [b_P[pq], b_V[kt % 8]], [b_pOv])
                for bank in range(3):
                    nh = 7 if bank < 2 else 2
                    pv = pOv[bank][:, 0:nh * 65].rearrange("p (h e) -> p h e", h=nh)
                    g._record("dve", lambda e, bank=bank, nh=nh, pv=pv: e.reciprocal(out=rden[:, bank * 7:bank * 7 + nh],
                                                                                   in_=pv[:, :, 64]), [b_pOv], [b_rden])
                    g.tt("dve", o_dst[:, bank * 448:bank * 448 + nh * 64].rearrange("p (h d) -> p h d", h=nh), pv[:, :, 0:64],
                         rden[:, bank * 7:bank * 7 + nh][:, :, None].to_broadcast([128, nh, 64]), ALU.mult,
                         [b_pOv, b_rden], [b_o])

            for qp in range(NCH):
                qi = qp % 2
                tk = slice(qp * 128, (qp + 1) * 128)
                ensure(qp + 3)
                g.dma("sp", Qc[qi][:], nqT[:, :, tk].rearrange("c p t -> p c t"), b_Q[qi], writes=[b_Q[qi]])
                interior = [qp - 2, qp - 1, qp, qp + 1, qp + 2]
                if qp < 2:
                    na_pair(qp, [0, 1, 2, 3], 1, qi, onb, b_onb)
                elif qp >= NCH - 2:
                    na_pair(qp, [NCH - 4, NCH - 3, NCH - 2, NCH - 1], 1, qi, onb, b_onb)
                elif MIDC - 2 <= qp < MIDC + 2:
                    na_pair(qp, interior, 0, qi, oA, b_oA)
                    ekts = [MIDC - 4, MIDC - 3, MIDC - 2, MIDC - 1] if qp < MIDC else [MIDC, MIDC + 1, MIDC + 2, MIDC + 3]
                    na_pair(qp, ekts, 1, qi, oB, b_oB)
                    g.ts("dve", oA[:], oA[:], fl[:, 0:1], None, ALU.mult, None, [b_oA, b_c], [b_oA])
                    g.stt("dve", onb[:], oB[:], fl[:, 1:2], oA[:], ALU.mult, ALU.add, [b_oB, b_oA, b_c], [b_onb])
                else:
                    na_pair(qp, interior, 0, qi, onb, b_onb)
                yi = cnt["y"] % 2
                cnt["y"] += 1
                for k in range(8):
                    g.tr(pT[:, k * 128:(k + 1) * 128], onb[:, k * 128:(k + 1) * 128], identb[:], [b_onb, b_c], [b_pT])
                g.cp("act", yTs[yi][:], pT[:].rearrange("p (k t) -> p k t", k=8), [b_pT], [b_yTs[yi]])
                g.dma("pool", yT[2][:, :, tk].rearrange("k p t -> p k t"), yTs[yi][:], b_yTs[yi], reads=[b_yTs[yi]])
            g.flush(ENGS)

    stages = []
    stages.append(("p0", phase_p0))
    for l in range(DEPTH):
        x_src = x_in if l == 0 else xc
        stages.append(("ffn1_%d" % l, lambda l=l, x_src=x_src: phase_ffn(
            "f1l%d" % l, ffn1_w_in[l], ffn1_w_out[l], x_src, hT_f1, xa, ln_mix_w[l:l + 1, :], hT_mix)))
        stages.append(("ptm_%d" % l, lambda l=l: phase_inproj_tm(l)))
        stages.append(("pfm_%d" % l, lambda l=l: phase_inproj_fm(l)))
        stages.append(("ssdprep_%d" % l, lambda l=l: phase_ssd_prep(l)))
        stages.append(("ssdf_%d" % l, lambda l=l: phase_ssd_pass(l, 0)))
        stages.append(("ssdb_%d" % l, lambda l=l: phase_ssd_pass(l, 1)))
        stages.append(("glaf_%d" % l, lambda l=l: phase_gla_pass(l, 0)))
        stages.append(("glab_%d" % l, lambda l=l: phase_gla_pass(l, 1)))
        stages.append(("na_%d" % l, lambda l=l: phase_na(l)))
        stages.append(("merge_%d" % l, lambda l=l: phase_merge(l, xa, xb, ln_ffn2_w[l:l + 1, :], hT_f2)))
        if l + 1 < DEPTH:
            stages.append(("ffn2_%d" % l, lambda l=l: phase_ffn(
                "f2l%d" % l, ffn2_w_in[l], ffn2_w_out[l], xb, hT_f2, xc, ln_ffn1_w[l + 1:l + 2, :], hT_f1)))
        else:
            stages.append(("ffn2_%d" % l, lambda l=l: phase_ffn(
                "f2l%d" % l, ffn2_w_in[l], ffn2_w_out[l], xb, hT_f2, None, ln_final_w[0:1, :], None, final_out=y_out)))
    only = getattr(cfg, "only", None)
    for name, fn in stages:
        if only is None or name in only:
            fn()
        if cfg.stop_after == name:
            break
    g.close()
    return nc, g


def _consts():
    k = np.arange(128)
    tri_f = (k[:, None] <= k[None, :]).astype(np.float32)
    tri_b = (k[:, None] >= k[None, :]).astype(np.float32)
    allow_f = (k[:, None] <= k[None, :])
    allow_b = (k[:, None] >= k[None, :])
    c = {
        "ident": np.eye(128, dtype=np.float32),
        "ones": np.ones((128, 128), np.float32),
        "tri": np.stack([tri_f, tri_b]),
        "maskadd": np.stack([np.where(allow_f, 0.0, -30000.0), np.where(allow_b, 0.0, -30000.0)]).astype(np.float32),
        "mask01": np.stack([allow_f, allow_b]).astype(np.float32),
        "ustrict": np.stack([(k[:, None] > k[None, :]), (k[:, None] < k[None, :])]).astype(np.float32),
    }
    kc = np.arange(64)[:, None]
    qc = np.arange(64)[None, :]
    cs = np.clip(qc - 8, 0, 48)
    colmask = ((kc >= cs) & (kc < cs + 16)).astype(np.float32)
    m_int = np.zeros((128, 16, 64), np.float32)
    m_edge = np.zeros((128, 16, 64), np.float32)
    for kr2 in range(2):
        for s_ in range(16):
            ro = s_ + kr2 - 1
            if 0 <= ro <= 14:
                m_edge[kr2 * 64:(kr2 + 1) * 64, s_, :] = colmask
            if 3 <= ro <= 10:
                m_int[kr2 * 64:(kr2 + 1) * 64, s_, :] = colmask
    jflip = np.zeros((64, 64), np.float32)
    jflip[63 - np.arange(64), np.arange(64)] = 1.0
    c.update({"jflip": jflip, "m_int": m_int, "m_edge": m_edge})
    return c


_PROGRAM_CACHE = {}


def run_streams(streams, flags, w, cfg_extra=None):
    nt = streams[0].shape[0]
    cfg = Cfg(nt)
    nc, g = build_program(cfg)
    base = dict(_consts())
    f32 = np.float32
    for k in ("ln_ffn1_w", "ffn1_w_in", "ffn1_w_out", "ln_mix_w", "w_in", "ssm_conv_w", "ssm_conv_b", "ssm_d",
              "ssm_norm_w", "gla_gate_up", "gla_gate_b", "gla_norm_w", "na_rpb", "w_branch_a", "w_branch_b",
              "w_branch_c", "w_out", "ln_ffn2_w", "ffn2_w_in", "ffn2_w_out"):
        base[k] = np.ascontiguousarray(np.asarray(w[k], dtype=f32))
    base["ssm_dt_bias"] = np.ascontiguousarray(np.asarray(w["ssm_dt_bias"], f32).reshape(DEPTH, 32))
    base["ssm_a_log"] = np.ascontiguousarray(np.asarray(w["ssm_a_log"], f32).reshape(DEPTH, 32))
    base["ln_final_w"] = np.ascontiguousarray(np.asarray(w["ln_final_w"], f32).reshape(1, D))
    in_maps = []
    for x, f in zip(streams, flags):
        m = dict(base)
        m["x"] = np.ascontiguousarray(np.asarray(x, f32))
        m["flag"] = np.tile(np.array([[f, 1.0 - f]], f32), (128, 1))
        in_maps.append(m)
    res = run_bass_kernel_spmd(nc, in_maps, core_ids=list(range(len(in_maps))))
    return [np.asarray(r["y"], dtype=f32) for r in res.results]


def kernel(**inputs):
    xp = np.asarray(inputs["x_prompt"], np.float32)
    xs = np.asarray(inputs["x_sample"], np.float32)
    B, S, _ = xp.shape
    B2, S2, _ = xs.shape
    assert S2 == 2 * S and B % 2 == 0
    streams, flags = [], []
    for i in range(B // 2):
        streams.append(xp[2 * i:2 * i + 2].reshape(2 * S, D))
        flags.append(0.0)
    for i in range(B2):
        streams.append(xs[i])
        flags.append(1.0)
    outs = run_streams(streams, flags, inputs)
    yp = np.stack([o.reshape(2, S, D) for o in outs[:B // 2]]).reshape(B, S, D)
    ys = np.stack(outs[B // 2:]).reshape(B2, S2, D)
    return (yp.astype(np.float32), ys.astype(np.float32))
```

```python
import numpy as np
from contextlib import ExitStack
import concourse.bass as bass
import concourse.mybir as mybir
from concourse.bass_utils import run_bass_kernel_spmd

F32 = mybir.dt.float32
BF16 = mybir.dt.bfloat16
AF = mybir.ActivationFunctionType
ALU = mybir.AluOpType
AX = mybir.AxisListType

D = 1024
DFF = 2816
DEPTH = 2
EPS = 1e-6
ENGS = ("pe", "act", "dve", "pool", "sp")
NA_LAG = 1
STRICT = True


class Buf:
    __slots__ = ("name", "w", "r", "sem")

    def __init__(self, name):
        self.name = name
        self.w = []
        self.r = []
        self.sem = None


class Op:
    __slots__ = ("fn", "rw", "war", "signal", "sigval", "dma", "waits")

    def __init__(self, fn, rw, war, dma=None):
        self.fn = fn
        self.rw = rw
        self.war = war
        self.signal = False
        self.sigval = 0
        self.dma = dma
        self.waits = None


class Graph:
    def __init__(self, nc, n_dma_sems=40):
        self.nc = nc
        self.es = ExitStack()
        self.eng_sem = {e: self.es.enter_context(nc.semaphore("s_" + e)) for e in ENGS}
        self.eng_cnt = {e: 0 for e in ENGS}
        self.dma_sem = [self.es.enter_context(nc.semaphore("d%d" % i)) for i in range(n_dma_sems)]
        self.dma_cnt = [0] * n_dma_sems
        self.next_slot = 0
        self.ops = {e: [] for e in ENGS}
        self.waited = {e: {} for e in ENGS}
        self.n_instr = 0

    def close(self):
        self.es.close()

    def slot(self, buf):
        if buf.sem is None:
            buf.sem = self.next_slot % len(self.dma_sem)
            self.next_slot += 1
        return buf.sem

    def _record(self, eng, fn, reads, writes, dma=None):
        rw, war = set(), set()
        for b in reads:
            rw.update(b.w)
        for b in writes:
            for ref in b.w:
                if dma is not None and ref[0] == "d" and ref[1] == dma[0]:
                    continue
                rw.add(ref)
            war.update(b.r)
        idx = len(self.ops[eng])
        op = Op(fn, rw, war, dma)
        self.ops[eng].append(op)
        ref = ("d", dma[0], dma[1]) if dma is not None else ("c", eng, idx)
        for b in reads:
            b.r.append(ref)
        for b in writes:
            b.w = [ref]
            b.r = []
        return ref

    def op(self, eng, fn, reads=(), writes=()):
        return self._record(eng, fn, reads, writes)

    def dma(self, eng, out, in_, sbuf, reads=(), writes=(), slow=False):
        s = self.slot(sbuf)
        self.dma_cnt[s] += 16
        cnt = self.dma_cnt[s]
        sem = self.dma_sem[s]
        kw = {"allow_slow_non_contiguous": True} if slow else {}

        def fn(e):
            return e.dma_start(out=out, in_=in_, **kw).then_inc(sem, 16)
        return self._record(eng, fn, reads, writes, dma=(s, cnt))

    def mm(self, out, lhsT, rhs, start, stop, reads, writes):
        return self._record("pe", lambda e: e.matmul(out, lhsT, rhs, start=start, stop=stop), reads, writes)

    def tr(self, out, in_, ident, reads, writes):
        return self._record("pe", lambda e: e.transpose(out=out, in_=in_, identity=ident), reads, writes)

    def act(self, out, in_, func, reads, writes, bias=None, scale=None):
        kw = {}
        if bias is not None:
            kw["bias"] = bias
        if scale is not None:
            kw["scale"] = scale
        return self._record("act", lambda e: e.activation(out=out, in_=in_, func=func, **kw), reads, writes)

    def cp(self, eng, out, in_, reads, writes):
        if eng == "act":
            return self._record("act", lambda e: e.copy(out=out, in_=in_), reads, writes)
        return self._record(eng, lambda e: e.tensor_copy(out=out, in_=in_), reads, writes)

    def tt(self, eng, out, in0, in1, op, reads, writes):
        return self._record(eng, lambda e: e.tensor_tensor(out=out, in0=in0, in1=in1, op=op), reads, writes)

    def ts(self, eng, out, in0, s1, s2, op0, op1, reads, writes):
        if op1 is None:
            return self._record(eng, lambda e: e.tensor_scalar(out=out, in0=in0, scalar1=s1, scalar2=None, op0=op0),
                                reads, writes)
        return self._record(eng, lambda e: e.tensor_scalar(out=out, in0=in0, scalar1=s1, scalar2=s2, op0=op0, op1=op1),
                            reads, writes)

    def stt(self, eng, out, in0, scalar, in1, op0, op1, reads, writes):
        return self._record(eng, lambda e: e.scalar_tensor_tensor(out=out, in0=in0, scalar=scalar, in1=in1,
                                                                  op0=op0, op1=op1), reads, writes)

    def red(self, eng, out, in_, reads, writes):
        return self._record(eng, lambda e: e.reduce_sum(out=out, in_=in_, axis=AX.X), reads, writes)

    def memset(self, eng, ap, val, writes):
        return self._record(eng, lambda e: e.memset(ap, val), (), writes)

    def flush(self, engines):
        ops = self.ops
        for e in ENGS:
            for i, op in enumerate(ops[e]):
                need = set()
                for ref in op.rw:
                    if ref[0] == "d":
                        need.add(ref)
                    elif ref[1] != e:
                        need.add(ref)
                    elif op.dma is not None or (STRICT and e in ("act", "dve", "pool")):
                        if ref[2] < i:
                            need.add(ref)
                for ref in op.war:
                    if ref[0] == "d":
                        need.add(ref)
                    elif ref[1] != e:
                        need.add(ref)
                    elif op.dma is not None:
                        need.add(ref)
                op.waits = need
                for ref in need:
                    if ref[0] == "c":
                        ops[ref[1]][ref[2]].signal = True
        for e in ENGS:
            for op in reversed(ops[e]):
                if op.dma is None:
                    op.signal = True
                    break
        for e in ENGS:
            c = self.eng_cnt[e]
            for op in ops[e]:
                if op.signal and op.dma is None:
                    c += 1
                    op.sigval = c
            self.eng_cnt[e] = c
        nc = self.nc
        final_eng = dict(self.eng_cnt)
        final_dma = list(self.dma_cnt)
        g = self

        def emit(e, handle):
            waited = g.waited[e]
            for op in ops[e]:
                wl = {}
                for ref in op.waits:
                    if ref[0] == "d":
                        key, val = ("d", ref[1]), ref[2]
                    else:
                        key, val = ("c", ref[1]), ops[ref[1]][ref[2]].sigval
                    if wl.get(key, 0) < val:
                        wl[key] = val
                for key, val in wl.items():
                    if waited.get(key, 0) >= val:
                        continue
                    waited[key] = val
                    sem = g.dma_sem[key[1]] if key[0] == "d" else g.eng_sem[key[1]]
                    handle.wait_ge(sem, val)
                    g.n_instr += 1
                ins = op.fn(handle)
                g.n_instr += 1
                if op.signal and op.dma is None:
                    ins.then_inc(g.eng_sem[e], 1)
            for x in ENGS:
                if x != e and waited.get(("c", x), 0) < final_eng[x]:
                    waited[("c", x)] = final_eng[x]
                    handle.wait_ge(g.eng_sem[x], final_eng[x])
            for s, v in enumerate(final_dma):
                if v > 0 and waited.get(("d", s), 0) < v:
                    waited[("d", s)] = v
                    handle.wait_ge(g.dma_sem[s], v)

        with nc.Block() as block:
            @block.tensor
            def _(h):
                emit("pe", h)

            @block.scalar
            def _(h):
                emit("act", h)

            @block.vector
            def _(h):
                emit("dve", h)

            @block.gpsimd
            def _(h):
                emit("pool", h)

            @block.sync
            def _(h):
                emit("sp", h)
        self.ops = {e: [] for e in ENGS}


C_Z, C_XBC, C_DTF, C_DTB = 0, 1024, 2560, 2576
C_GQ, C_GK, C_GV, C_GG = 2592, 3104, 3616, 4640
C_DNF, C_DNB = 5664, 5680
C_NQ, C_NK, C_NV = 5696, 6720, 7744
C_GATE = 8768
IN_WIDTH = 11840


class Cfg:
    def __init__(self, nt, stop_after=None, debug_outs=(), debug_ins=()):
        self.nt = nt
        self.seg = nt // 2
        self.stop_after = stop_after
        self.debug_outs = debug_outs
        self.debug_ins = debug_ins
        self.only = None


def build_program(cfg):
    nc = bass.Bass("TRN2", target_bir_lowering=False)
    NT = cfg.nt
    TB = 512
    NB = NT // TB
    NTI = TB // 128

    cfg.in_shapes = {}

    def din(name, shape, dt=F32):
        cfg.in_shapes[name] = (list(shape), dt)
        return nc.dram_tensor(name, list(shape), dt, kind="ExternalInput").ap()

    def dout(name, shape, dt=F32):
        return nc.dram_tensor(name, list(shape), dt, kind="ExternalOutput").ap()

    def dscr(name, shape, dt=F32):
        kind = "Internal"
        if name in cfg.debug_outs:
            kind = "ExternalOutput"
        if name in cfg.debug_ins:
            kind = "ExternalInput"
            cfg.in_shapes[name] = (list(shape), dt)
        return nc.dram_tensor(name, list(shape), dt, kind=kind).ap()

    x_in = din("x", [NT, D])
    flag = din("flag", [128, 2])
    ident_in = din("ident", [128, 128])
    ln_ffn1_w = din("ln_ffn1_w", [DEPTH, D])
    ffn1_w_in = din("ffn1_w_in", [DEPTH, D, 2 * DFF])
    ffn1_w_out = din("ffn1_w_out", [DEPTH, DFF, D])
    ln_mix_w = din("ln_mix_w", [DEPTH, D])
    w_in = din("w_in", [DEPTH, D, IN_WIDTH])
    w_branch = [din("w_branch_" + c, [DEPTH, D, D]) for c in "abc"]
    w_out = din("w_out", [DEPTH, D, D])
    ln_ffn2_w = din("ln_ffn2_w", [DEPTH, D])
    ffn2_w_in = din("ffn2_w_in", [DEPTH, D, 2 * DFF])
    ffn2_w_out = din("ffn2_w_out", [DEPTH, DFF, D])
    ln_final_w = din("ln_final_w", [1, D])
    y_out = dout("y", [NT, D])

    xa = dscr("xa", [NT, D])
    xb = dscr("xb", [NT, D])
    xc = dscr("xc", [NT, D])
    hT_f1 = dscr("hT_f1", [8, 128, NT], BF16)
    hT_mix = dscr("hT_mix", [8, 128, NT], BF16)
    hT_f2 = dscr("hT_f2", [8, 128, NT], BF16)
    z_tm = dscr("z_tm", [NT, 1024])
    kg_tm = dscr("kg_tm", [NT, 512])
    vg_tm = dscr("vg_tm", [NT, 1024], BF16)
    gg_tm = dscr("gg_tm", [NT, 1024])
    vn_tm = dscr("vn_tm", [NT, 1024], BF16)
    dt_tm = dscr("dt_tm", [NT, 32])
    xbcT = dscr("xbcT", [12, 128, NT])
    gqT = dscr("gqT", [4, 128, NT])
    gkT = dscr("gkT", [4, 128, NT])
    nqT = dscr("nqT", [8, 128, NT], BF16)
    nkT = dscr("nkT", [8, 128, NT], BF16)
    gatesT = dscr("gatesT", [24, 128, NT], BF16)
    dnT = dscr("dnT", [2, 16, NT])
    yT = [dscr("yT_" + c, [8, 128, NT], BF16) for c in "abc"]

    g = Graph(nc)

    uid = [0]

    def sb(es, name, shape, dt):
        uid[0] += 1
        return es.enter_context(nc.sbuf_tensor("s%d_%s" % (uid[0], name), list(shape), dt))

    def ps(es, name, shape, dt):
        uid[0] += 1
        return es.enter_context(nc.psum_tensor("p%d_%s" % (uid[0], name), list(shape), dt))

    class NormCtx:
        def __init__(self, es, tag, transpose=True):
            self.transpose = transpose
            self.wB = sb(es, "wB_" + tag, [128, D], F32)
            self.sq = sb(es, "sq_" + tag, [128, D], F32)
            self.ss = [sb(es, "ss%d_" % i + tag, [128, 2], F32) for i in range(2)]
            self.b_wB, self.b_sq = Buf("wB"), Buf("sq")
            self.b_ss = [Buf("ss0"), Buf("ss1")]
            if transpose:
                self.ident = sb(es, "ident_" + tag, [128, 128], BF16)
                self.hn = [sb(es, "hn%d_" % i + tag, [128, D], BF16) for i in range(2)]
                self.hTs = sb(es, "hTs_" + tag, [128, 8, TB], BF16)
                self.pT = [ps(es, "pT%d_" % i + tag, [128, D], BF16) for i in range(2)]
                self.b_ident = Buf("ident")
                self.b_hn = [Buf("hn0"), Buf("hn1")]
                self.b_hTs = Buf("hTs")
                self.b_pT = [Buf("pT0"), Buf("pT1")]
            else:
                self.yt = [sb(es, "yt%d_" % i + tag, [128, D], F32) for i in range(2)]
                self.b_yt = [Buf("yt0"), Buf("yt1")]
            self.n = 0

        def setup(self, w_row_ap):
            if self.transpose:
                g.dma("pool", self.ident[:], ident_in, self.b_ident, writes=[self.b_ident])
            g.dma("sp", self.wB[:], w_row_ap.partition_broadcast(128), self.b_wB, writes=[self.b_wB])

        def stats(self, xt_ap, b_x):
            i = self.n % 2
            self.n += 1
            ss = self.ss[i]
            g.act(self.sq[:], xt_ap, AF.Square, [b_x], [self.b_sq])
            g.red("dve", ss[:, 0:1], self.sq[:], [self.b_sq], [self.b_ss[i]])
            g.ts("dve", ss[:, 1:2], ss[:, 0:1], 1.0 / D, EPS, ALU.mult, ALU.add, [self.b_ss[i]], [self.b_ss[i]])
            g.act(ss[:, 1:2], ss[:, 1:2], AF.Sqrt, [self.b_ss[i]], [self.b_ss[i]])
            g._record("dve", lambda e: e.reciprocal(out=ss[:, 1:2], in_=ss[:, 1:2]), [self.b_ss[i]], [self.b_ss[i]])
            return i

        def run(self, xt_ap, b_x, hT_dram, blk, ti):
            self.flush_pending()
            i = self.stats(xt_ap, b_x)
            hn = self.hn[i]
            g.stt("dve", hn[:], xt_ap, self.ss[i][:, 1:2], self.wB[:], ALU.mult, ALU.mult,
                  [b_x, self.b_ss[i], self.b_wB], [self.b_hn[i]])
            self.pending = (i, hT_dram, blk, ti)

        def flush_pending(self):
            if getattr(self, "pending", None) is None:
                return
            i, hT_dram, blk, ti = self.pending
            self.pending = None
            hn, pT, hTs = self.hn[i], self.pT[i], self.hTs
            for k in range(8):
                g.tr(pT[:, k * 128:(k + 1) * 128], hn[:, k * 128:(k + 1) * 128], self.ident[:],
                     [self.b_hn[i], self.b_ident], [self.b_pT[i]])
            g.cp("act", hTs[:, :, ti * 128:(ti + 1) * 128], pT[:].rearrange("p (k t) -> p k t", k=8),
                 [self.b_pT[i]], [self.b_hTs])
            if ti == NTI - 1:
                g.dma("pool", hT_dram[:, :, blk * TB:(blk + 1) * TB].rearrange("k p t -> p k t"), hTs[:],
                      self.b_hTs, reads=[self.b_hTs])

        def run_final(self, xt_ap, b_x, y_dram, t):
            i = self.stats(xt_ap, b_x)
            g.stt("dve", self.yt[i][:], xt_ap, self.ss[i][:, 1:2], self.wB[:], ALU.mult, ALU.mult,
                  [b_x, self.b_ss[i], self.b_wB], [self.b_yt[i]])
            g.dma("pool", y_dram[t * 128:(t + 1) * 128, :], self.yt[i][:], self.b_yt[i], reads=[self.b_yt[i]])

    def phase_p0():
        with ExitStack() as es:
            nrm = NormCtx(es, "p0")
            xt = [sb(es, "p0x%d" % i, [128, D], F32) for i in range(2)]
            b_xt = [Buf("x0"), Buf("x1")]
            nrm.setup(ln_ffn1_w[0:1, :])
            for blk in range(NB):
                for ti in range(NTI):
                    t = blk * NTI + ti
                    i = t % 2
                    g.dma("sp", xt[i][:], x_in[t * 128:(t + 1) * 128, :], b_xt[i], writes=[b_xt[i]])
                    nrm.run(xt[i][:], b_xt[i], hT_f1, blk, ti)
            nrm.flush_pending()
            g.flush(ENGS)

    def phase_ffn(tag, w_in_ap, w_out_ap, x_src, hT_src, x_dst, next_w_row, hT_dst, final_out=None):
        with ExitStack() as es:
            w1 = sb(es, "w1_" + tag, [128, 8, 2 * DFF], BF16)
            w2 = sb(es, "w2_" + tag, [128, 22, D], BF16)
            b_w1, b_w2 = Buf("w1"), Buf("w2")
            hTb = sb(es, "hTb_" + tag, [128, 8, TB], BF16)
            b_hTb = Buf("hTb")
            gT = sb(es, "gT_" + tag, [128, 22, TB], BF16)
            b_gT = Buf("gT")
            sa = [sb(es, "sa%d_" % i + tag, [128, TB], F32) for i in range(2)]
            b_sa = [Buf("sa0"), Buf("sa1")]
            xt = sb(es, "xt_" + tag, [128, D], F32)
            b_xt = Buf("xt")
            xn = [sb(es, "xn%d_" % i + tag, [128, D], F32) for i in range(2)]
            b_xn = [Buf("xn0"), Buf("xn1")]
            pa = [ps(es, "pa%d_" % i + tag, [128, TB], F32) for i in range(2)]
            pb = [ps(es, "pb%d_" % i + tag, [128, TB], F32) for i in range(2)]
            b_pa = [Buf("pa0"), Buf("pa1")]
            b_pb = [Buf("pb0"), Buf("pb1")]
            po = [ps(es, "po%d_" % i + tag, [128, 512], F32) for i in range(2)]
            b_po = [Buf("po0"), Buf("po1")]
            nrm = NormCtx(es, tag, transpose=(final_out is None))
            for k in range(8):
                g.dma("pool", w1[:, k, :], w_in_ap[k * 128:(k + 1) * 128, :], b_w1, writes=[b_w1])
            for k in range(22):
                g.dma("pool", w2[:, k, :], w_out_ap[k * 128:(k + 1) * 128, :], b_w2, writes=[b_w2])
            nrm.setup(next_w_row)

            def load_h(blk):
                g.dma("sp", hTb[:], hT_src[:, :, blk * TB:(blk + 1) * TB].rearrange("k p t -> p k t"),
                      b_hTb, writes=[b_hTb])
            load_h(0)
            ntl = 0
            for blk in range(NB):
                for c in range(22):
                    q = c % 2
                    for k in range(8):
                        g.mm(pa[q][:], w1[:, k, c * 128:(c + 1) * 128], hTb[:, k, :], k == 0, k == 7,
                             [b_w1, b_hTb], [b_pa[q]])
                    for k in range(8):
                        g.mm(pb[q][:], w1[:, k, DFF + c * 128:DFF + (c + 1) * 128], hTb[:, k, :], k == 0, k == 7,
                             [b_w1, b_hTb], [b_pb[q]])
                    g.act(sa[q][:], pa[q][:], AF.Silu, [b_pa[q]], [b_sa[q]])
                    g.tt("dve", gT[:, c, :], sa[q][:], pb[q][:], ALU.mult, [b_sa[q], b_pb[q]], [b_gT])
                if blk + 1 < NB:
                    load_h(blk + 1)
                for ti in range(NTI):
                    t = blk * NTI + ti
                    i = ntl % 2
                    ntl += 1
                    g.dma("sp", xt[:], x_src[t * 128:(t + 1) * 128, :], b_xt, writes=[b_xt])
                    for ch in range(2):
                        for k in range(22):
                            g.mm(po[ch][:], gT[:, k, ti * 128:(ti + 1) * 128], w2[:, k, ch * 512:(ch + 1) * 512],
                                 k == 0, k == 21, [b_gT, b_w2], [b_po[ch]])
                        g.stt("dve", xn[i][:, ch * 512:(ch + 1) * 512], po[ch][:], 0.5,
                              xt[:, ch * 512:(ch + 1) * 512], ALU.mult, ALU.add, [b_po[ch], b_xt], [b_xn[i]])
                    if final_out is None:
                        nrm.flush_pending()
                    if final_out is None:
                        g.dma("pool", x_dst[t * 128:(t + 1) * 128, :], xn[i][:], b_xn[i], reads=[b_xn[i]])
                        nrm.run(xn[i][:], b_xn[i], hT_dst, blk, ti)
                    else:
                        nrm.run_final(xn[i][:], b_xn[i], final_out, t)
            if final_out is None:
                nrm.flush_pending()
            g.flush(ENGS)

    def phase_inproj_tm(l):
        with ExitStack() as es:
            W = w_in[l]
            parts = [(C_Z, 1024, z_tm, F32), (C_GK, 512, kg_tm, F32), (C_GV, 1024, vg_tm, BF16),
                     (C_GG, 1024, gg_tm, F32), (C_NV, 1024, vn_tm, BF16), (C_DTF, 32, dt_tm, F32)]
            NC_TM = sum(p[1] for p in parts)
            wt = sb(es, "wtm", [128, 8, NC_TM], BF16)
            b_wt = Buf("wtm")
            hTb = [sb(es, "tm_hTb%d" % i, [128, 8, TB], BF16) for i in range(2)]
            b_hTb = [Buf("hTb0"), Buf("hTb1")]
            st = [[sb(es, "tm_st%d_%d" % (i, pi), [128, p[1]], p[3]) for pi, p in enumerate(parts)] for i in range(2)]
            b_st = [[Buf("st%d_%d" % (i, pi)) for pi in range(len(parts))] for i in range(2)]
            pp = [ps(es, "tm_p%d" % i, [128, 512], F32) for i in range(4)]
            b_pp = [Buf("pp%d" % i) for i in range(4)]
            off = 0
            offs = []
            for (c0, wd, _, _) in parts:
                offs.append(off)
                for k in range(8):
                    g.dma("pool", wt[:, k, off:off + wd], W[k * 128:(k + 1) * 128, c0:c0 + wd], b_wt, writes=[b_wt])
                off += wd
            nps = 0
            nev = 0
            for blk in range(NB):
                j = blk % 2
                g.dma("sp", hTb[j][:], hT_mix[:, :, blk * TB:(blk + 1) * TB].rearrange("k p t -> p k t"),
                      b_hTb[j], writes=[b_hTb[j]])
                for ti in range(NTI):
                    t = blk * NTI + ti
                    i = t % 2
                    for pi, (c0, wd, dst, dt) in enumerate(parts):
                        for cb in range(0, wd, 512):
                            n = min(512, wd - cb)
                            q = nps % 4
                            nps += 1
                            for k in range(8):
                                g.mm(pp[q][:, 0:n], hTb[j][:, k, ti * 128:(ti + 1) * 128],
                                     wt[:, k, offs[pi] + cb:offs[pi] + cb + n], k == 0, k == 7,
                                     [b_hTb[j], b_wt], [b_pp[q]])
                            eng = "act" if nev % 2 == 0 else "dve"
                            nev += 1
                            g.cp(eng, st[i][pi][:, cb:cb + n], pp[q][:, 0:n], [b_pp[q]], [b_st[i][pi]])
                        g.dma("pool", dst[t * 128:(t + 1) * 128, :], st[i][pi][:], b_st[i][pi], reads=[b_st[i][pi]])
            g.flush(ENGS)

    def phase_inproj_fm(l):
        with ExitStack() as es:
            W = w_in[l]
            parts = [(C_XBC, 12, xbcT, F32, None, 1.0), (C_GQ, 4, gqT, F32, None, 128.0 ** -0.5),
                     (C_GK, 4, gkT, F32, None, 1.0), (C_NQ, 8, nqT, BF16, None, 0.125),
                     (C_NK, 8, nkT, BF16, None, 1.0), (C_GATE, 24, gatesT, BF16, AF.Sigmoid, 1.0)]
            NCH = sum(p[1] for p in parts)
            NC_FM = NCH * 128 + 32
            wt = sb(es, "wfm", [128, 8, NC_FM], BF16)
            b_wt = Buf("wfm")
            hTb = [sb(es, "fm_hTb%d" % i, [128, 8, TB], BF16) for i in range(2)]
            b_hTb = [Buf("hTb0"), Buf("hTb1")]
            stf = [sb(es, "fm_stf%d" % i, [128, TB], F32) for i in range(4)]
            stb = [sb(es, "fm_stb%d" % i, [128, TB], BF16) for i in range(4)]
            b_stf = [Buf("stf%d" % i) for i in range(4)]
            b_stb = [Buf("stb%d" % i) for i in range(4)]
            pp = [ps(es, "fm_p%d" % i, [128, 512], F32) for i in range(4)]
            b_pp = [Buf("pp%d" % i) for i in range(4)]
            off = 0
            offs = []
            for (c0, nch, _, _, _, _) in parts:
                offs.append(off)
                for k in range(8):
                    g.dma("pool", wt[:, k, off:off + nch * 128], W[k * 128:(k + 1) * 128, c0:c0 + nch * 128],
                          b_wt, writes=[b_wt])
                off += nch * 128
            off_dn = off
            for k in range(8):
                g.dma("pool", wt[:, k, off:off + 32], W[k * 128:(k + 1) * 128, C_DNF:C_DNF + 32], b_wt, writes=[b_wt])
            nps = 0
            nf = 0
            nbf = 0
            nev = 0
            for blk in range(NB):
                j = blk % 2
                tok = slice(blk * TB, (blk + 1) * TB)
                g.dma("sp", hTb[j][:], hT_mix[:, :, tok].rearrange("k p t -> p k t"), b_hTb[j], writes=[b_hTb[j]])
                for pi, (c0, nch, dst, dt, func, scale) in enumerate(parts):
                    for c in range(nch):
                        q = nps % 4
                        nps += 1
                        wc = offs[pi] + c * 128
                        for k in range(8):
                            g.mm(pp[q][:], wt[:, k, wc:wc + 128], hTb[j][:, k, :], k == 0, k == 7,
                                 [b_wt, b_hTb[j]], [b_pp[q]])
                        if dt == F32:
                            s_, b_s = stf[nf % 4], b_stf[nf % 4]
                            nf += 1
                        else:
                            s_, b_s = stb[nbf % 4], b_stb[nbf % 4]
                            nbf += 1
                        if func is not None:
                            g.act(s_[:], pp[q][:], func, [b_pp[q]], [b_s])
                        elif scale != 1.0:
                            if nev % 2 == 0:
                                g.act(s_[:], pp[q][:], AF.Copy, [b_pp[q]], [b_s], scale=scale)
                            else:
                                g.ts("dve", s_[:], pp[q][:], scale, None, ALU.mult, None, [b_pp[q]], [b_s])
                            nev += 1
                        else:
                            g.cp("act" if nev % 2 == 0 else "dve", s_[:], pp[q][:], [b_pp[q]], [b_s])
                            nev += 1
                        g.dma("pool", dst[c, :, tok], s_[:], b_s, reads=[b_s])
                for dd in range(2):
                    q = nps % 4
                    nps += 1
                    wc = off_dn + dd * 16
                    for k in range(8):
                        g.mm(pp[q][0:16, :], wt[:, k, wc:wc + 16], hTb[j][:, k, :], k == 0, k == 7,
                             [b_wt, b_hTb[j]], [b_pp[q]])
                    s_, b_s = stf[nf % 4], b_stf[nf % 4]
                    nf += 1
                    g.cp("dve", s_[0:16, :], pp[q][0:16, :], [b_pp[q]], [b_s])
                    g.dma("pool", dnT[dd, :, tok], s_[0:16, :], b_s, reads=[b_s])
            g.flush(ENGS)

    def phase_merge(l, x_src, x_dst, next_w_row, hT_dst):
        with ExitStack() as es:
            wb_ = [sb(es, "wbr%d" % i, [128, 8, D], BF16) for i in range(3)]
            wo = sb(es, "wo", [128, 8, D], BF16)
            b_w = Buf("w")
            yb = [sb(es, "yb%d" % i, [128, 8, TB], BF16) for i in range(3)]
            b_yb = [Buf("yb%d" % i) for i in range(3)]
            gt = sb(es, "gt", [128, 24, TB], BF16)
            b_gt = Buf("gt")
            tmp = [sb(es, "mtmp%d" % i, [128, TB], F32) for i in range(3)]
            b_tmp = [Buf("tmp%d" % i) for i in range(3)]
            mT = sb(es, "mT", [128, 8, TB], BF16)
            b_mT = Buf("mT")
            xt = sb(es, "m_xt", [128, D], F32)
            b_xt = Buf("xt")
            xn = [sb(es, "m_xn%d" % i, [128, D], F32) for i in range(2)]
            b_xn = [Buf("xn0"), Buf("xn1")]
            pp = [ps(es, "m_p%d" % i, [128, 512], F32) for i in range(6)]
            b_pp = [Buf("pp%d" % i) for i in range(6)]
            nrm = NormCtx(es, "mrg")
            for i in range(3):
                for k in range(8):
                    g.dma("pool", wb_[i][:, k, :], w_branch[i][l, k * 128:(k + 1) * 128, :], b_w, writes=[b_w])
            for k in range(8):
                g.dma("pool", wo[:, k, :], w_out[l, k * 128:(k + 1) * 128, :], b_w, writes=[b_w])
            nrm.setup(next_w_row)
            ntl = 0
            for blk in range(NB):
                tok = slice(blk * TB, (blk + 1) * TB)
                for i in range(3):
                    g.dma("sp", yb[i][:], yT[i][:, :, tok].rearrange("k p t -> p k t"), b_yb[i], writes=[b_yb[i]])
                g.dma("sp", gt[:], gatesT[:, :, tok].rearrange("k p t -> p k t"), b_gt, writes=[b_gt])
                for oc in range(8):
                    par = (oc % 2) * 3
                    for i in range(3):
                        for k in range(8):
                            g.mm(pp[par + i][:], wb_[i][:, k, oc * 128:(oc + 1) * 128], yb[i][:, k, :], k == 0, k == 7,
                                 [b_w, b_yb[i]], [b_pp[par + i]])
                    for i in range(3):
                        g.tt("dve", tmp[i][:], pp[par + i][:], gt[:, i * 8 + oc, :], ALU.mult,
                             [b_pp[par + i], b_gt], [b_tmp[i]])
                    g.tt("pool", tmp[0][:], tmp[0][:], tmp[1][:], ALU.add, [b_tmp[0], b_tmp[1]], [b_tmp[0]])
                    g.tt("pool", mT[:, oc, :], tmp[0][:], tmp[2][:], ALU.add, [b_tmp[0], b_tmp[2]], [b_mT])
                for ti in range(NTI):
                    t = blk * NTI + ti
                    i = ntl % 2
                    ntl += 1
                    g.dma("sp", xt[:], x_src[t * 128:(t + 1) * 128, :], b_xt, writes=[b_xt])
                    for ch in range(2):
                        q = ch * 3
                        for k in range(8):
                            g.mm(pp[q][:], mT[:, k, ti * 128:(ti + 1) * 128], wo[:, k, ch * 512:(ch + 1) * 512],
                                 k == 0, k == 7, [b_mT, b_w], [b_pp[q]])
                        g.tt("dve", xn[i][:, ch * 512:(ch + 1) * 512], pp[q][:], xt[:, ch * 512:(ch + 1) * 512], ALU.add,
                             [b_pp[q], b_xt], [b_xn[i]])
                    nrm.flush_pending()
                    g.dma("pool", x_dst[t * 128:(t + 1) * 128, :], xn[i][:], b_xn[i], reads=[b_xn[i]])
                    nrm.run(xn[i][:], b_xn[i], hT_dst, blk, ti)
            nrm.flush_pending()
            g.flush(ENGS)

    ssm_conv_w = din("ssm_conv_w", [DEPTH, 5, 1536])
    ssm_conv_b = din("ssm_conv_b", [DEPTH, 1536])
    ssm_dt_bias = din("ssm_dt_bias", [DEPTH, 32])
    ssm_a_log = din("ssm_a_log", [DEPTH, 32])
    ssm_d = din("ssm_d", [DEPTH, 16])
    ssm_norm_w = din("ssm_norm_w", [DEPTH, 1024])
    tri_in = din("tri", [2, 128, 128])
    maskadd_in = din("maskadd", [2, 128, 128])
    mask01_in = din("mask01", [2, 128, 128])
    ustrict_in = din("ustrict", [2, 128, 128])
    ones_in = din("ones", [128, 128])
    xs_tm = dscr("xs_tm", [NT, 1024])
    B_tm = dscr("B_tm", [NT, 256], BF16)
    BCT = dscr("BCT", [4, 128, NT], BF16)
    dtsp = dscr("dtsp", [NT, 32])
    dta_d = dscr("dta", [NT, 32])
    dlog_d = dscr("dlog", [NT, 32])
    yf_d = dscr("yf", [NT, 1024])
    NCH = NT // 128
    MIDC = NCH // 2

    def phase_ssd_prep(l):
        with ExitStack() as es:
            cw = sb(es, "cw", [128, 12, 5], F32)
            cb = sb(es, "cb", [128, 12], F32)
            fl = sb(es, "fl", [128, 2], F32)
            identf = sb(es, "identf", [128, 128], F32)
            dbias = sb(es, "dbias", [128, 32], F32)
            aB = sb(es, "aB", [128, 32], F32)
            b_c = Buf("consts")
            xin = [sb(es, "xin%d" % i, [128, 12, TB + 4], F32) for i in range(2)]
            b_xin = [Buf("xin0"), Buf("xin1")]
            acc = [sb(es, "acc%d" % i, [128, TB], F32) for i in range(3)]
            b_acc = [Buf("acc%d" % i) for i in range(3)]
            sil = sb(es, "sil", [128, 10, TB], F32)
            b_sil = Buf("sil")
            silb = [sb(es, "silb%d" % i, [128, TB], BF16) for i in range(2)]
            b_silb = [Buf("silb0"), Buf("silb1")]
            xst = [sb(es, "xst%d" % i, [128, 1024], F32) for i in range(2)]
            b_xst = [Buf("xst0"), Buf("xst1")]
            bst = [sb(es, "bst%d" % i, [128, 256], BF16) for i in range(2)]
            b_bst = [Buf("bst0"), Buf("bst1")]
            dtt = [sb(es, "dtt%d" % i, [128, 4, NTI, 32], F32) for i in range(2)]
            b_dtt = [Buf("dtt0"), Buf("dtt1")]
            pt = [ps(es, "pp_t%d" % i, [128, 1024], F32) for i in range(2)]
            b_pt = [Buf("pt0"), Buf("pt1")]
            pbt = [ps(es, "pp_b%d" % i, [128, 256], F32) for i in range(2)]
            b_pbt = [Buf("pbt0"), Buf("pbt1")]
            for k in range(5):
                g.dma("sp", cw[:, :, k], ssm_conv_w[l, k].rearrange("(c p) -> p c", p=128), b_c, writes=[b_c], slow=True)
            g.dma("sp", cb[:], ssm_conv_b[l].rearrange("(c p) -> p c", p=128), b_c, writes=[b_c], slow=True)
            g.dma("sp", fl[:], flag, b_c, writes=[b_c])
            g.dma("sp", identf[:], ident_in, b_c, writes=[b_c])
            g.dma("sp", dbias[:], ssm_dt_bias[l:l + 1, :].partition_broadcast(128), b_c, writes=[b_c])
            g.dma("sp", aB[:], ssm_a_log[l:l + 1, :].partition_broadcast(128), b_c, writes=[b_c])
            g.act(aB[:], aB[:], AF.Exp, [b_c], [b_c])
            g.ts("dve", aB[:], aB[:], -1.0, None, ALU.mult, None, [b_c], [b_c])
            na = 0
            nsb = 0
            nt_ = 0
            for blk in range(NB):
                j = blk % 2
                t0 = blk * TB
                lo, hi = t0 - 2, t0 + TB + 2
                xi = xin[j]
                lo_c, hi_c = 0, TB + 4
                if lo < 0:
                    g.memset("pool", xi[:, :, 0:2], 0.0, [b_xin[j]])
                    lo_c, lo = 2, 0
                if hi > NT:
                    g.memset("pool", xi[:, :, TB + 2:TB + 4], 0.0, [b_xin[j]])
                    hi_c, hi = TB + 2, NT
                g.dma("sp", xi[:, :, lo_c:hi_c], xbcT[:, :, lo:hi].rearrange("c p t -> p c t"), b_xin[j],
                      writes=[b_xin[j]])
                if t0 == NT // 2:
                    g.ts("dve", xi[:, :, 0:2], xi[:, :, 0:2], fl[:, 0:1], None, ALU.mult, None, [b_xin[j], b_c], [b_xin[j]])
                if t0 + TB == NT // 2:
                    g.ts("dve", xi[:, :, TB + 2:TB + 4], xi[:, :, TB + 2:TB + 4], fl[:, 0:1], None, ALU.mult, None,
                         [b_xin[j], b_c], [b_xin[j]])
                dq = dtt[j]
                g.dma("sp", dq[:, 0, :, :], dt_tm[t0:t0 + TB, :].rearrange("(i p) c -> p i c", p=128), b_dtt[j],
                      writes=[b_dtt[j]])
                g.tt("dve", dq[:, 0, :, :], dq[:, 0, :, :], dbias[:, None, :].to_broadcast([128, NTI, 32]), ALU.add,
                     [b_dtt[j], b_c], [b_dtt[j]])
                g.act(dq[:, 1, :, :], dq[:, 0, :, :], AF.Exp, [b_dtt[j]], [b_dtt[j]])
                g.ts("dve", dq[:, 1, :, :], dq[:, 1, :, :], 1.0, None, ALU.add, None, [b_dtt[j]], [b_dtt[j]])
                g.act(dq[:, 1, :, :], dq[:, 1, :, :], AF.Ln, [b_dtt[j]], [b_dtt[j]])
                g.tt("dve", dq[:, 2, :, :], dq[:, 1, :, :], aB[:, None, :].to_broadcast([128, NTI, 32]), ALU.mult,
                     [b_dtt[j], b_c], [b_dtt[j]])
                g.dma("pool", dtsp[t0:t0 + TB, :].rearrange("(i p) c -> p i c", p=128), dq[:, 1, :, :], b_dtt[j],
                      reads=[b_dtt[j]])
                g.act(dq[:, 3, :, :], dq[:, 1, :, :], AF.Ln, [b_dtt[j]], [b_dtt[j]])
                g.dma("pool", dta_d[t0:t0 + TB, :].rearrange("(i p) c -> p i c", p=128), dq[:, 2, :, :], b_dtt[j],
                      reads=[b_dtt[j]])
                g.dma("pool", dlog_d[t0:t0 + TB, :].rearrange("(i p) c -> p i c", p=128), dq[:, 3, :, :], b_dtt[j],
                      reads=[b_dtt[j]])
                for c in range(12):
                    a = na % 3
                    na += 1
                    g.act(acc[a][:], xi[:, c, 0:TB], AF.Identity, [b_xin[j], b_c], [b_acc[a]],
                          bias=cb[:, c:c + 1], scale=cw[:, c, 0:1])
                    for k in range(1, 5):
                        g.stt("dve", acc[a][:], xi[:, c, k:k + TB], cw[:, c, k:k + 1], acc[a][:],
                              ALU.mult, ALU.add, [b_xin[j], b_c, b_acc[a]], [b_acc[a]])
                    if c < 10:
                        g.act(sil[:, c, :], acc[a][:], AF.Silu, [b_acc[a]], [b_sil])
                    if c >= 8:
                        q = nsb % 2
                        nsb += 1
                        g.act(silb[q][:], acc[a][:], AF.Silu, [b_acc[a]], [b_silb[q]])
                        g.dma("pool", BCT[c - 8, :, t0:t0 + TB], silb[q][:], b_silb[q], reads=[b_silb[q]])
                for ti in range(NTI):
                    i = nt_ % 2
                    nt_ += 1
                    tk = slice(t0 + ti * 128, t0 + (ti + 1) * 128)
                    for c in range(8):
                        g.tr(pt[i][:, c * 128:(c + 1) * 128], sil[:, c, ti * 128:(ti + 1) * 128], identf[:],
                             [b_sil, b_c], [b_pt[i]])
                    for c in range(2):
                        g.tr(pbt[i][:, c * 128:(c + 1) * 128], sil[:, 8 + c, ti * 128:(ti + 1) * 128], identf[:],
                             [b_sil, b_c], [b_pbt[i]])
                    g.cp("act", xst[i][:], pt[i][:], [b_pt[i]], [b_xst[i]])
                    g.cp("dve", bst[i][:], pbt[i][:], [b_pbt[i]], [b_bst[i]])
                    g.dma("pool", xs_tm[tk, :], xst[i][:], b_xst[i], reads=[b_xst[i]])
                    g.dma("pool", B_tm[tk, :], bst[i][:], b_bst[i], reads=[b_bst[i]])
            g.flush(ENGS)

    def phase_ssd_pass(l, d):
        with ExitStack() as es:
            tri = sb(es, "tri", [128, 128], F32)
            ones = sb(es, "ones", [128, 128], F32)
            identf = sb(es, "identf", [128, 128], F32)
            identb = sb(es, "identb", [128, 128], BF16)
            maskB = sb(es, "maskB", [128, 16, 128], F32)
            fl = sb(es, "fl", [128, 2], F32)
            b_c = Buf("consts")
            xs = [sb(es, "xs%d" % i, [128, 1024], F32) for i in range(3)]
            xsb = [sb(es, "xsb%d" % i, [128, 1024], BF16) for i in range(2)]
            b_xsb = [Buf("xsb0"), Buf("xsb1")]
            Bt = [sb(es, "Bt%d" % i, [128, 256], BF16) for i in range(3)]
            bct = [sb(es, "bct%d" % i, [128, 4, 128], BF16) for i in range(3)]
            dts = [sb(es, "dts%d" % i, [128, 3, 16], F32) for i in range(3)]
            b_in = [Buf("in0"), Buf("in1"), Buf("in2")]
            R = sb(es, "R", [128, 16, 128], F32)
            b_R = Buf("R")
            cs_sb = sb(es, "cs_sb", [128, 16], F32)
            ecs = sb(es, "ecs", [128, 16], F32)
            b_cs = Buf("cs_sb")
            b_ecs = Buf("ecs")
            seg = sb(es, "seg", [128, 16, 128], F32)
            b_seg = Buf("seg")
            cbT = sb(es, "cbT", [128, 2, 128], F32)
            b_cbT = Buf("cbT")
            MT = sb(es, "MT", [128, 16, 128], BF16)
            b_MT = Buf("MT")
            xr = sb(es, "xr", [128, 1024], BF16)
            b_xr = Buf("xr")
            xd = sb(es, "xd", [128, 1024], BF16)
            b_xd = Buf("xd")
            S = sb(es, "S", [128, 1024], F32)
            Sbf = sb(es, "Sbf", [128, 1024], BF16)
            b_S, b_Sbf = Buf("S"), Buf("Sbf")
            decB = sb(es, "decB", [128, 16], F32)
            b_decB = Buf("decB")
            t1 = sb(es, "t1", [128, 1024], F32)
            b_t1 = Buf("t1")
            yd = [sb(es, "yd%d" % i, [128, 1024], F32) for i in range(2)]
            b_yd = [Buf("yd0"), Buf("yd1")]
            pA = ps(es, "pA", [128, 1024], F32)
            pM = ps(es, "pM", [128, 512], F32)
            pY = ps(es, "pY", [128, 1024], F32)
            pO = ps(es, "pO", [128, 1024], F32)
            b_pA, b_pM, b_pY, b_pO = Buf("pA"), Buf("pM"), Buf("pY"), Buf("pO")
            if d == 1:
                dB = sb(es, "dB", [128, 16], F32)
                nwB = sb(es, "nwB", [128, 1024], F32)
                yfl = [sb(es, "yfl%d" % i, [128, 1024], F32) for i in range(3)]
                zt = [sb(es, "zt%d" % i, [128, 1024], F32) for i in range(3)]
                b_in2 = [Buf("in2_0"), Buf("in2_1"), Buf("in2_2")]
                sq = sb(es, "sq", [128, 1024], F32)
                b_sq = Buf("sq")
                t2 = sb(es, "t2", [128, 1024], F32)
                b_t2 = Buf("t2")
                gs = sb(es, "gs", [128, 4], F32)
                b_gs = Buf("gs")
                ynb = sb(es, "ynb", [128, 1024], BF16)
                b_ynb = Buf("ynb")
                yTs = [sb(es, "yTs%d" % i, [128, 8, 128], BF16) for i in range(2)]
                b_yTs = [Buf("yTs0"), Buf("yTs1")]
                pT = ps(es, "pT", [128, 1024], BF16)
                b_pT = Buf("pT")
                g.dma("sp", dB[:], ssm_d[l:l + 1, :].partition_broadcast(128), b_c, writes=[b_c])
                g.dma("sp", nwB[:], ssm_norm_w[l:l + 1, :].partition_broadcast(128), b_c, writes=[b_c])
                g.dma("pool", identb[:], ident_in, b_c, writes=[b_c])
            g.dma("sp", tri[:], tri_in[d], b_c, writes=[b_c])
            g.dma("sp", ones[:], ones_in, b_c, writes=[b_c])
            g.dma("sp", identf[:], ident_in, b_c, writes=[b_c])
            g.dma("sp", fl[:], flag, b_c, writes=[b_c])
            for h in range(16):
                g.dma("sp", maskB[:, h, :], maskadd_in[d], b_c, writes=[b_c])
            g.memset("dve", S[:], 0.0, [b_S])
            g.memset("pool", Sbf[:], 0.0, [b_Sbf])
            last = 127 if d == 0 else 0
            order = list(range(NCH)) if d == 0 else list(range(NCH - 1, -1, -1))
            pend = []

            def drain(k):
                for _ in range(k):
                    if pend:
                        pend.pop(0)()

            for n, c in enumerate(order):
                i = n % 3
                j2 = n % 2
                tk = slice(c * 128, (c + 1) * 128)
                g.dma("sp", xs[i][:], xs_tm[tk, :], b_in[i], writes=[b_in[i]])
                g.dma("sp", Bt[i][:], B_tm[tk, :], b_in[i], writes=[b_in[i]])
                g.dma("sp", bct[i][:], BCT[:, :, tk].rearrange("c p t -> p c t"), b_in[i], writes=[b_in[i]])
                g.dma("sp", dts[i][:, 0, :], dtsp[tk, d * 16:(d + 1) * 16], b_in[i], writes=[b_in[i]])
                g.dma("sp", dts[i][:, 1, :], dta_d[tk, d * 16:(d + 1) * 16], b_in[i], writes=[b_in[i]])
                g.dma("sp", dts[i][:, 2, :], dlog_d[tk, d * 16:(d + 1) * 16], b_in[i], writes=[b_in[i]])
                if d == 1:
                    g.dma("sp", yfl[i][:], yf_d[tk, :], b_in2[i], writes=[b_in2[i]])
                    g.dma("sp", zt[i][:], z_tm[tk, :], b_in2[i], writes=[b_in2[i]])
                dta = dts[i][:, 1, :]
                dsp = dts[i][:, 0, :]
                if (d == 0 and c == MIDC) or (d == 1 and c == MIDC - 1):
                    g.ts("dve", S[:], S[:], fl[:, 0:1], None, ALU.mult, None, [b_S, b_c], [b_S])
                    g.ts("pool", Sbf[:], Sbf[:], fl[:, 0:1], None, ALU.mult, None, [b_Sbf, b_c], [b_Sbf])
                g.mm(pM[:, 0:16], tri[:], dta, True, True, [b_c, b_in[i]], [b_pM])
                for gi in range(2):
                    g.mm(pM[:, 128 + gi * 128:256 + gi * 128], bct[i][:, gi, :], bct[i][:, 2 + gi, :], True, True,
                         [b_in[i]], [b_pM])
                g.tt("dve", cs_sb[:], pM[:, 0:16], dts[i][:, 2, :], ALU.subtract, [b_pM, b_in[i]], [b_cs])
                g.cp("act", cbT[:], pM[:, 128:384].rearrange("p (g t) -> p g t", g=2), [b_pM], [b_cbT])
                drain(2)
                g.act(ecs[:], pM[:, 0:16], AF.Exp, [b_pM], [b_ecs])
                g.cp("act", xsb[j2][:], xs[i][:], [b_in[i]], [b_xsb[j2]])
                g.tt("dve", R[:], tri[:, None, :].to_broadcast([128, 16, 128]), dta[:, :, None].to_broadcast([128, 16, 128]),
                     ALU.mult, [b_c, b_in[i]], [b_R])
                drain(2)
                for gi in range(2):
                    g.mm(pO[:, gi * 512:(gi + 1) * 512], bct[i][:, 2 + gi, :], Sbf[:, gi * 512:(gi + 1) * 512], True, True,
                         [b_in[i], b_Sbf], [b_pO])
                for half in range(2):
                    hs = slice(half * 8, half * 8 + 8)
                    for q in range(2):
                        h0 = half * 8 + q * 4
                        g.mm(pA[:, q * 512:(q + 1) * 512], identf[:], maskB[:, h0:h0 + 4, :].rearrange("p h t -> p (h t)"),
                             True, False, [b_c], [b_pA])
                        g.mm(pA[:, q * 512:(q + 1) * 512], ones[:], R[:, h0:h0 + 4, :].rearrange("p h t -> p (h t)"),
                             False, True, [b_c, b_R], [b_pA])
                    g.tt("dve", seg[:, hs, :], pA[:].rearrange("p (h t) -> p h t", h=8),
                         cs_sb[:, hs][:, :, None].to_broadcast([128, 8, 128]), ALU.subtract, [b_pA, b_cs], [b_seg])
                    if half == 1:
                        pass
                    drain(2)
                    g.act(seg[:, hs, :], seg[:, hs, :], AF.Exp, [b_seg], [b_seg])
                    g.tt("pool" if half == 0 else "dve", MT[:, hs, :], seg[:, hs, :],
                         cbT[:, half:half + 1, :].to_broadcast([128, 8, 128]), ALU.mult, [b_seg, b_cbT], [b_MT])
                drain(2)
                for h in range(16):
                    g.mm(pY[:, h * 64:(h + 1) * 64], MT[:, h, :], xsb[j2][:, h * 64:(h + 1) * 64], True, True,
                         [b_MT, b_xsb[j2]], [b_pY])
                drain(2)
                g.tt("dve", t1[:].rearrange("p (h q) -> p h q", h=16), pO[:].rearrange("p (h q) -> p h q", h=16),
                     ecs[:, :, None].to_broadcast([128, 16, 64]), ALU.mult, [b_pO, b_ecs], [b_t1])
                g.tt("dve", yd[j2][:], pY[:], t1[:], ALU.add, [b_pY, b_t1], [b_yd[j2]])
                g.tt("pool", xd[:].rearrange("p (h q) -> p h q", h=16), xsb[j2][:].rearrange("p (h q) -> p h q", h=16),
                     seg[:, :, last:last + 1].to_broadcast([128, 16, 64]), ALU.mult, [b_xsb[j2], b_seg], [b_xd])
                drain(2)
                g.mm(pM[:, 16:32], ones[:], dta, True, True, [b_c, b_in[i]], [b_pM])
                g.act(decB[:], pM[:, 16:32], AF.Exp, [b_pM], [b_decB])
                for gi in range(2):
                    g.mm(pA[:, gi * 512:(gi + 1) * 512], Bt[i][:, gi * 128:(gi + 1) * 128], xd[:, gi * 512:(gi + 1) * 512],
                         True, True, [b_in[i], b_xd], [b_pA])
                g.tt("pool", S[:].rearrange("p (h q) -> p h q", h=16), S[:].rearrange("p (h q) -> p h q", h=16),
                     decB[:, :, None].to_broadcast([128, 16, 64]), ALU.mult, [b_S, b_decB], [b_S])
                g.tt("dve", S[:], S[:], pA[:], ALU.add, [b_S, b_pA], [b_S])
                g.cp("act", Sbf[:], S[:], [b_S], [b_Sbf])
                if d == 0:
                    g.dma("pool", yf_d[tk, :], yd[j2][:], b_yd[j2], reads=[b_yd[j2]])
                else:
                    drain(len(pend))
                    y = yd[j2]
                    b_y = b_yd[j2]
                    yfl_i, zt_i, xs_i, b2_i, bi_i, yTs_j, b_yTs_j = yfl[i], zt[i], xs[i], b_in2[i], b_in[i], yTs[j2], b_yTs[j2]

                    def mk(y=y, b_y=b_y, yfl_i=yfl_i, zt_i=zt_i, xs_i=xs_i, b2_i=b2_i, bi_i=bi_i, yTs_j=yTs_j,
                           b_yTs_j=b_yTs_j, tk=tk):
                        ops = []
                        ops.append(lambda: g.tt("pool", y[:], y[:], yfl_i[:], ALU.add, [b_y, b2_i], [b_y]))
                        ops.append(lambda: g.tt("pool", t2[:].rearrange("p (h q) -> p h q", h=16),
                                                xs_i[:].rearrange("p (h q) -> p h q", h=16),
                                                dB[:, :, None].to_broadcast([128, 16, 64]), ALU.mult, [bi_i, b_c], [b_t2]))
                        ops.append(lambda: g.act(zt_i[:], zt_i[:], AF.Silu, [b2_i], [b2_i]))
                        ops.append(lambda: g.tt("dve", y[:], y[:], t2[:], ALU.add, [b_y, b_t2], [b_y]))
                        ops.append(lambda: g.tt("dve", y[:], y[:], zt_i[:], ALU.mult, [b_y, b2_i], [b_y]))
                        ops.append(lambda: g.act(sq[:], y[:], AF.Square, [b_y], [b_sq]))
                        ops.append(lambda: g.red("dve", gs[:, 0:2], sq[:].rearrange("p (g q) -> p g q", g=2), [b_sq], [b_gs]))
                        ops.append(lambda: g.ts("dve", gs[:, 2:4], gs[:, 0:2], 1.0 / 512, EPS, ALU.mult, ALU.add, [b_gs], [b_gs]))
                        ops.append(lambda: g.act(gs[:, 2:4], gs[:, 2:4], AF.Sqrt, [b_gs], [b_gs]))
                        ops.append(lambda: g._record("dve", lambda e: e.reciprocal(out=gs[:, 2:4], in_=gs[:, 2:4]),
                                                     [b_gs], [b_gs]))
                        ops.append(lambda: g.tt("dve", y[:].rearrange("p (g q) -> p g q", g=2),
                                                y[:].rearrange("p (g q) -> p g q", g=2),
                                                gs[:, 2:4][:, :, None].to_broadcast([128, 2, 512]), ALU.mult, [b_y, b_gs], [b_y]))
                        ops.append(lambda: g.tt("pool", ynb[:], y[:], nwB[:], ALU.mult, [b_y, b_c], [b_ynb]))

                        def trs():
                            for k in range(8):
                                g.tr(pT[:, k * 128:(k + 1) * 128], ynb[:, k * 128:(k + 1) * 128], identb[:], [b_ynb, b_c], [b_pT])
                        ops.append(trs)
                        ops.append(lambda: g.cp("act", yTs_j[:], pT[:].rearrange("p (k t) -> p k t", k=8), [b_pT], [b_yTs_j]))
                        ops.append(lambda: g.dma("pool", yT[0][:, :, tk].rearrange("k p t -> p k t"), yTs_j[:], b_yTs_j,
                                                 reads=[b_yTs_j]))
                        return ops
                    pend.extend(mk())
            drain(len(pend))
            g.flush(ENGS)

    gla_gate_up = din("gla_gate_up", [DEPTH, 2, 16, 512])
    gla_gate_b = din("gla_gate_b", [DEPTH, 2, 512])
    gla_norm_w = din("gla_norm_w", [DEPTH, 256])
    of_d = dscr("of", [NT, 1024])

    def phase_gla_pass(l, d):
        with ExitStack() as es:
            triS = sb(es, "triS", [128, 128], F32)
            uS = sb(es, "uS", [128, 128], F32)
            m01 = sb(es, "m01", [128, 128], F32)
            ones = sb(es, "ones", [128, 128], F32)
            up = sb(es, "up", [16, 512], F32)
            gbr = sb(es, "gbr", [1, 512], F32)
            fl = sb(es, "fl", [128, 2], F32)
            b_c = Buf("consts")
            qT = [sb(es, "qT%d" % i, [128, 4, 128], F32) for i in range(2)]
            kT = [sb(es, "kT%d" % i, [128, 4, 128], F32) for i in range(2)]
            ktm = [sb(es, "ktm%d" % i, [128, 512], F32) for i in range(2)]
            v = [sb(es, "v%d" % i, [128, 1024], BF16) for i in range(2)]
            dn = [sb(es, "dn%d" % i, [16, 128], F32) for i in range(2)]
            b_in = [Buf("in0"), Buf("in1")]
            lsp = sb(es, "lsp", [128, 512], F32)
            b_lsp = Buf("lsp")
            eb = sb(es, "eb", [128, 4, 128], F32)
            enb = sb(es, "enb", [128, 4, 128], F32)
            er = sb(es, "er", [128, 512], F32)
            b_eb, b_enb, b_er = Buf("eb"), Buf("enb"), Buf("er")
            qe = sb(es, "qe", [128, 4, 128], BF16)
            ke = sb(es, "ke", [128, 4, 128], BF16)
            kdec = sb(es, "kdec", [128, 512], BF16)
            att = sb(es, "att", [128, 4, 128], BF16)
            b_qe, b_ke, b_kdec, b_att = Buf("qe"), Buf("ke"), Buf("kdec"), Buf("att")
            S = sb(es, "S", [128, 4, 256], F32)
            Sbf = sb(es, "Sbf", [128, 4, 256], BF16)
            b_S, b_Sbf = Buf("S"), Buf("Sbf")
            od = [sb(es, "od%d" % i, [128, 1024], F32) for i in range(2)]
            b_od = [Buf("od0"), Buf("od1")]
            pLR = ps(es, "pLR", [128, 512], F32)
            pB = ps(es, "pB", [128, 512], F32)
            pAt = ps(es, "pAt", [128, 512], F32)
            pOo = ps(es, "pOo", [128, 1024], F32)
            pSp = ps(es, "pSp", [128, 1024], F32)
            b_pLR, b_pB, b_pAt, b_pOo, b_pSp = Buf("pLR"), Buf("pB"), Buf("pAt"), Buf("pOo"), Buf("pSp")
            if d == 1:
                identb = sb(es, "identb", [128, 128], BF16)
                nwB = sb(es, "nwB", [128, 1024], F32)
                ofl = [sb(es, "ofl%d" % i, [128, 1024], F32) for i in range(2)]
                gg = [sb(es, "gg%d" % i, [128, 1024], F32) for i in range(2)]
                b_in2 = [Buf("in2_0"), Buf("in2_1")]
                sq = sb(es, "sq", [128, 1024], F32)
                b_sq = Buf("sq")
                gs = sb(es, "gs", [128, 8], F32)
                b_gs = Buf("gs")
                onb = sb(es, "onb", [128, 1024], BF16)
                b_onb = Buf("onb")
                yTs = [sb(es, "yTs%d" % i, [128, 8, 128], BF16) for i in range(2)]
                b_yTs = [Buf("yTs0"), Buf("yTs1")]
                pT = ps(es, "pT", [128, 1024], BF16)
                b_pT = Buf("pT")
                g.dma("pool", identb[:], ident_in, b_c, writes=[b_c])
                for h in range(4):
                    g.dma("sp", nwB[:, h * 256:(h + 1) * 256], gla_norm_w[l:l + 1, :].partition_broadcast(128), b_c,
                          writes=[b_c])
            g.dma("sp", triS[:], tri_in[d], b_c, writes=[b_c])
            g.dma("sp", uS[:], ustrict_in[d], b_c, writes=[b_c])
            g.dma("sp", m01[:], mask01_in[d], b_c, writes=[b_c])
            g.dma("sp", ones[:], ones_in, b_c, writes=[b_c])
            g.dma("sp", up[:], gla_gate_up[l, d], b_c, writes=[b_c])
            g.dma("sp", gbr[:], gla_gate_b[l, d:d + 1, :], b_c, writes=[b_c])
            g.dma("sp", fl[:], flag, b_c, writes=[b_c])
            g.ts("dve", triS[:], triS[:], -1.0 / 16.0, None, ALU.mult, None, [b_c], [b_c])
            g.ts("dve", uS[:], uS[:], -1.0 / 16.0, None, ALU.mult, None, [b_c], [b_c])
            g.memset("dve", S[:], 0.0, [b_S])
            g.memset("pool", Sbf[:], 0.0, [b_Sbf])
            last = 127 if d == 0 else 0
            order = list(range(NCH)) if d == 0 else list(range(NCH - 1, -1, -1))
            for n, c in enumerate(order):
                i = n % 2
                tk = slice(c * 128, (c + 1) * 128)
                g.dma("sp", qT[i][:], gqT[:, :, tk].rearrange("h p t -> p h t"), b_in[i], writes=[b_in[i]])
                g.dma("sp", kT[i][:], gkT[:, :, tk].rearrange("h p t -> p h t"), b_in[i], writes=[b_in[i]])
                g.dma("sp", ktm[i][:], kg_tm[tk, :], b_in[i], writes=[b_in[i]])
                g.dma("sp", v[i][:], vg_tm[tk, :], b_in[i], writes=[b_in[i]])
                g.dma("sp", dn[i][:], dnT[d, :, tk], b_in[i], writes=[b_in[i]])
                if d == 1:
                    g.dma("sp", ofl[i][:], of_d[tk, :], b_in2[i], writes=[b_in2[i]])
                    g.dma("sp", gg[i][:], gg_tm[tk, :], b_in2[i], writes=[b_in2[i]])
                if (d == 0 and c == MIDC) or (d == 1 and c == MIDC - 1):
                    g.ts("dve", S[:], S[:], fl[:, 0:1], None, ALU.mult, None, [b_S, b_c], [b_S])
                    g.ts("pool", Sbf[:], Sbf[:], fl[:, 0:1], None, ALU.mult, None, [b_Sbf, b_c], [b_Sbf])
                g.mm(pLR[:], dn[i][:], up[:], True, False, [b_in[i], b_c], [b_pLR])
                g.mm(pLR[:], ones[0:1, :], gbr[:], False, True, [b_c], [b_pLR])
                g.act(lsp[:], pLR[:], AF.Exp, [b_pLR], [b_lsp], scale=-1.0)
                g.ts("dve", lsp[:], lsp[:], 1.0, None, ALU.add, None, [b_lsp], [b_lsp])
                g.act(lsp[:], lsp[:], AF.Ln, [b_lsp], [b_lsp])
                for h in range(4):
                    g.mm(pB[:, h * 128:(h + 1) * 128], lsp[:, h * 128:(h + 1) * 128], triS[:], True, True,
                         [b_lsp, b_c], [b_pB])
                g.mm(pLR[:], uS[:], lsp[:], True, True, [b_c, b_lsp], [b_pLR])
                pB3 = pB[:].rearrange("p (h t) -> p h t", h=4)
                g.act(eb[:], pB3, AF.Exp, [b_pB], [b_eb])
                g.act(enb[:], pB3, AF.Exp, [b_pB], [b_enb], scale=-1.0)
                g.act(er[:], pLR[:], AF.Exp, [b_pLR], [b_er])
                g.tt("dve", qe[:], qT[i][:], eb[:], ALU.mult, [b_in[i], b_eb], [b_qe])
                g.tt("pool", ke[:], kT[i][:], enb[:], ALU.mult, [b_in[i], b_enb], [b_ke])
                g.tt("pool", kdec[:], ktm[i][:], er[:], ALU.mult, [b_in[i], b_er], [b_kdec])
                for h in range(4):
                    g.mm(pAt[:, h * 128:(h + 1) * 128], ke[:, h, :], qe[:, h, :], True, True, [b_ke, b_qe], [b_pAt])
                g.tt("dve", att[:], pAt[:].rearrange("p (h t) -> p h t", h=4), m01[:, None, :].to_broadcast([128, 4, 128]),
                     ALU.mult, [b_pAt, b_c], [b_att])
                for h in range(4):
                    g.mm(pOo[:, h * 256:(h + 1) * 256], att[:, h, :], v[i][:, h * 256:(h + 1) * 256], True, False,
                         [b_att, b_in[i]], [b_pOo])
                    g.mm(pOo[:, h * 256:(h + 1) * 256], qe[:, h, :], Sbf[:, h, :], False, True, [b_qe, b_Sbf], [b_pOo])
                for h in range(4):
                    g.mm(pSp[:, h * 256:(h + 1) * 256], kdec[:, h * 128:(h + 1) * 128], v[i][:, h * 256:(h + 1) * 256],
                         True, True, [b_kdec, b_in[i]], [b_pSp])
                for h in range(4):
                    g.stt("dve", S[:, h, :], S[:, h, :], eb[:, h, last:last + 1], pSp[:, h * 256:(h + 1) * 256],
                          ALU.mult, ALU.add, [b_S, b_eb, b_pSp], [b_S])
                g.cp("act", Sbf[:], S[:], [b_S], [b_Sbf])
                if d == 0:
                    g.cp("act", od[i][:], pOo[:], [b_pOo], [b_od[i]])
                    g.dma("pool", of_d[tk, :], od[i][:], b_od[i], reads=[b_od[i]])
                else:
                    o = od[i]
                    g.tt("dve", o[:], pOo[:], ofl[i][:], ALU.add, [b_pOo, b_in2[i]], [b_od[i]])
                    g.act(sq[:], o[:], AF.Square, [b_od[i]], [b_sq])
                    g.red("dve", gs[:, 0:4], sq[:].rearrange("p (h q) -> p h q", h=4), [b_sq], [b_gs])
                    g.ts("dve", gs[:, 4:8], gs[:, 0:4], 1.0 / 256, EPS, ALU.mult, ALU.add, [b_gs], [b_gs])
                    g.act(gs[:, 4:8], gs[:, 4:8], AF.Sqrt, [b_gs], [b_gs])
                    g._record("dve", lambda e, gs=gs: e.reciprocal(out=gs[:, 4:8], in_=gs[:, 4:8]), [b_gs], [b_gs])
                    g.tt("dve", o[:].rearrange("p (h q) -> p h q", h=4), o[:].rearrange("p (h q) -> p h q", h=4),
                         gs[:, 4:8][:, :, None].to_broadcast([128, 4, 256]), ALU.mult, [b_od[i], b_gs], [b_od[i]])
                    g.tt("pool", o[:], o[:], nwB[:], ALU.mult, [b_od[i], b_c], [b_od[i]])
                    g.act(gg[i][:], gg[i][:], AF.Silu, [b_in2[i]], [b_in2[i]])
                    g.tt("pool", onb[:], o[:], gg[i][:], ALU.mult, [b_od[i], b_in2[i]], [b_onb])
                    for k in range(8):
                        g.tr(pT[:, k * 128:(k + 1) * 128], onb[:, k * 128:(k + 1) * 128], identb[:], [b_onb, b_c], [b_pT])
                    g.cp("act", yTs[i][:], pT[:].rearrange("p (k t) -> p k t", k=8), [b_pT], [b_yTs[i]])
                    g.dma("pool", yT[1][:, :, tk].rearrange("k p t -> p k t"), yTs[i][:], b_yTs[i], reads=[b_yTs[i]])
            g.flush(ENGS)

    na_rpb = din("na_rpb", [DEPTH, 16, 15, 31])
    jflip_in = din("jflip", [64, 64])
    mint_in = din("m_int", [128, 16, 64])
    medge_in = din("m_edge", [128, 16, 64])
    rpbpad = dscr("rpbpad", [16, 17, 192])

    def phase_na(l):
        with ExitStack() as es:
            T2 = [sb(es, "T2i", [128, 16, 16, 64], BF16), sb(es, "T2e", [128, 16, 16, 64], BF16)]
            b_T2 = Buf("T2")
            Mx = [sb(es, "Mi", [128, 16, 64], F32), sb(es, "Me", [128, 16, 64], F32)]
            jf = sb(es, "jf", [64, 64], F32)
            identb = sb(es, "identb", [128, 128], BF16)
            fl = sb(es, "fl", [128, 2], F32)
            b_c = Buf("consts")
            padt = sb(es, "padt", [16, 17, 192], F32)
            rp = sb(es, "rp", [16, 15, 31], F32)
            b_pad = Buf("pad")
            Tpp = [sb(es, "Tpp%d" % i, [64, 4, 17, 64], F32) for i in range(2)]
            b_Tpp = [Buf("Tpp0"), Buf("Tpp1")]
            eW = [sb(es, "eW%d" % i, [128, 640], F32) for i in range(3)]
            b_eW = [Buf("eW%d" % i) for i in range(3)]
            Kc = [sb(es, "Kc%d" % i, [128, 8, 128], BF16) for i in range(8)]
            Vc = [sb(es, "Vc%d" % i, [128, 16, 65], BF16) for i in range(8)]
            b_K = [Buf("K%d" % i) for i in range(8)]
            b_V = [Buf("V%d" % i) for i in range(8)]
            Qc = [sb(es, "Qc%d" % i, [128, 8, 128], BF16) for i in range(2)]
            b_Q = [Buf("Q0"), Buf("Q1")]
            Pt = [sb(es, "Pt%d" % i, [128, 5, 2, 64], BF16) for i in range(3)]
            b_P = [Buf("P%d" % i) for i in range(3)]
            rden = sb(es, "rden", [128, 16], F32)
            b_rden = Buf("rden")
            oA = sb(es, "oA", [128, 1024], F32)
            oB = sb(es, "oB", [128, 1024], F32)
            b_oA, b_oB = Buf("oA"), Buf("oB")
            onb = sb(es, "onb", [128, 1024], BF16)
            b_onb = Buf("onb")
            yTs = [sb(es, "yTs%d" % i, [128, 8, 128], BF16) for i in range(2)]
            b_yTs = [Buf("yTs0"), Buf("yTs1")]
            pS_t = [ps(es, "pS%d" % i, [128, 1024], F32) for i in range(2)]
            pS = [t[:, 0:640] for t in pS_t]
            b_pS = [Buf("pS%d" % i) for i in range(2)]
            pOv = [ps(es, "pOv%d" % i, [128, 512], F32) for i in range(3)]
            b_pOv = Buf("pOv")
            pT = ps(es, "pT", [128, 1024], BF16)
            b_pT = Buf("pT")
            g.dma("sp", Mx[0][:], mint_in, b_c, writes=[b_c])
            g.dma("sp", Mx[1][:], medge_in, b_c, writes=[b_c])
            g.dma("sp", jf[:], jflip_in, b_c, writes=[b_c])
            g.dma("sp", fl[:], flag, b_c, writes=[b_c])
            g.dma("pool", identb[:], ident_in, b_c, writes=[b_c])
            for i in range(8):
                g.memset("pool", Vc[i][:], 1.0, [b_V[i]])
            g.memset("dve", padt[:], 0.0, [b_pad])
            g.dma("sp", rp[:], na_rpb[l], b_pad, writes=[b_pad])
            g.cp("dve", padt[:, 1:16, 64:95], rp[:], [b_pad], [b_pad])
            g.dma("sp", rpbpad, padt[:], b_pad, reads=[b_pad], writes=[b_pad])
            nb = 0
            for hg in range(4):
                tp = Tpp[hg % 2]
                b_tp = b_Tpp[hg % 2]
                for hh in range(4):
                    h = hg * 4 + hh
                    src = bass.AP(tensor=rpbpad.tensor, offset=h * 17 * 192 + 16, ap=[[1, 64], [192, 17], [1, 64]])
                    g.dma("sp", tp[:, hh, :, :], src, b_tp, reads=[b_pad], writes=[b_tp])
                for hh in range(4):
                    h = hg * 4 + hh
                    for sbt in range(2):
                        q = nb % 2
                        nb += 1
                        for si in range(8):
                            s_ = sbt * 8 + si
                            g.mm(pS[q][:, si * 64:(si + 1) * 64], tp[:, hh, s_:s_ + 2, :].rearrange("p r k -> p (r k)"),
                                 jf[:], True, True, [b_tp, b_c], [b_pS[q]])
                        g.act(eW[q][:, 0:512], pS[q][:, 0:512], AF.Exp, [b_pS[q]], [b_eW[q]])
                        g.tt("dve", T2[0][:, h, sbt * 8:sbt * 8 + 8, :], eW[q][:, 0:512].rearrange("p (s q) -> p s q", s=8),
                             Mx[0][:, sbt * 8:sbt * 8 + 8, :], ALU.mult, [b_eW[q], b_c], [b_T2])
                        g.tt("pool", T2[1][:, h, sbt * 8:sbt * 8 + 8, :], eW[q][:, 0:512].rearrange("p (s q) -> p s q", s=8),
                             Mx[1][:, sbt * 8:sbt * 8 + 8, :], ALU.mult, [b_eW[q], b_c], [b_T2])
            loaded = [-1]
            cnt = {"s": 0, "p": 0, "y": 0}

            def ensure(upto):
                while loaded[0] < min(upto, NCH - 1):
                    kt = loaded[0] + 1
                    sl_ = kt % 8
                    tk_ = slice(kt * 128, (kt + 1) * 128)
                    g.dma("sp", Kc[sl_][:], nkT[:, :, tk_].rearrange("c p t -> p c t"), b_K[sl_], writes=[b_K[sl_]])
                    g.dma("sp", Vc[sl_][:, :, 0:64], vn_tm[tk_, :].rearrange("p (h d) -> p h d", h=16), b_V[sl_],
                          writes=[b_V[sl_]])
                    loaded[0] = kt

            def na_pair(qp, kts, var, qi, o_dst, b_o):
                nt = len(kts)
                d0 = kts[0] - qp

                def st_s(h):
                    ch, p0 = h // 2, (h % 2) * 64
                    q = h % 2
                    for ti, kt in enumerate(kts):
                        g.mm(pS[q][:, ti * 128:(ti + 1) * 128], Kc[kt % 8][p0:p0 + 64, ch, :], Qc[qi][p0:p0 + 64, ch, :],
                             True, True, [b_K[kt % 8], b_Q[qi]], [b_pS[q]])

                def st_e(h):
                    q = h % 3
                    g.act(eW[q][:, 0:nt * 128], pS[h % 2][:, 0:nt * 128], AF.Exp, [b_pS[h % 2]], [b_eW[q]])
                    e4 = eW[q][:, 0:nt * 128].rearrange("p (t r c) -> p t r c", t=nt, r=2)
                    for qr2 in range(2):
                        s0 = 2 * d0 + 8 - qr2
                        base = T2[var][:, h, s0, :]
                        tv = bass.AP(tensor=base.tensor, offset=base.offset, ap=[list(base.ap[0]), [128, nt], [1, 64]])
                        g.tt("dve" if qr2 == 0 else "pool", Pt[q][:, 0:nt, qr2, :], e4[:, :, qr2, :], tv, ALU.mult,
                             [b_eW[q], b_T2], [b_P[q]])

                def st_v(h):
                    q = h % 3
                    bank, off = h // 7, (h % 7) * 65
                    for ti, kt in enumerate(kts):
                        g.mm(pOv[bank][:, off:off + 65], Pt[q][:, ti, :, :].rearrange("p r c -> p (r c)"),
                             Vc[kt % 8][:, h, :], ti == 0, ti == nt - 1, [b_P[q], b_V[kt % 8]], [b_pOv])

                LAG = NA_LAG
                for step in range(16 + 2 * LAG):
                    if step < 16:
                        st_s(step)
                    if LAG <= step < 16 + LAG:
                        st_e(step - LAG)
                    if step >= 2 * LAG:
                        st_v(step - 2 * LAG)
                for bank in range(3):
                    nh = 7 if bank < 2 else 2
                    pv = pOv[bank][:, 0:nh * 65].rearrange("p (h e) -> p h e", h=nh)
                    g._record("dve", lambda e, bank=bank, nh=nh, pv=pv: e.reciprocal(out=rden[:, bank * 7:bank * 7 + nh],
                                                                                   in_=pv[:, :, 64]), [b_pOv], [b_rden])
                    g.tt("dve", o_dst[:, bank * 448:bank * 448 + nh * 64].rearrange("p (h d) -> p h d", h=nh), pv[:, :, 0:64],
                         rden[:, bank * 7:bank * 7 + nh][:, :, None].to_broadcast([128, nh, 64]), ALU.mult,
                         [b_pOv, b_rden], [b_o])

            for qp in range(NCH):
                qi = qp % 2
                tk = slice(qp * 128, (qp + 1) * 128)
                ensure(qp + 3)
                g.dma("sp", Qc[qi][:], nqT[:, :, tk].rearrange("c p t -> p c t"), b_Q[qi], writes=[b_Q[qi]])
                interior = [qp - 2, qp - 1, qp, qp + 1, qp + 2]
                if qp < 2:
                    na_pair(qp, [0, 1, 2, 3], 1, qi, onb, b_onb)
                elif qp >= NCH - 2:
                    na_pair(qp, [NCH - 4, NCH - 3, NCH - 2, NCH - 1], 1, qi, onb, b_onb)
                elif MIDC - 2 <= qp < MIDC + 2:
                    na_pair(qp, interior, 0, qi, oA, b_oA)
                    ekts = [MIDC - 4, MIDC - 3, MIDC - 2, MIDC - 1] if qp < MIDC else [MIDC, MIDC + 1, MIDC + 2, MIDC + 3]
                    na_pair(qp, ekts, 1, qi, oB, b_oB)
                    g.ts("dve", oA[:], oA[:], fl[:, 0:1], None, ALU.mult, None, [b_oA, b_c], [b_oA])
                    g.stt("dve", onb[:], oB[:], fl[:, 1:2], oA[:], ALU.mult, ALU.add, [b_oB, b_oA, b_c], [b_onb])
                else:
                    na_pair(qp, interior, 0, qi, onb, b_onb)
                yi = cnt["y"] % 2
                cnt["y"] += 1
                for k in range(8):
                    g.tr(pT[:, k * 128:(k + 1) * 128], onb[:, k * 128:(k + 1) * 128], identb[:], [b_onb, b_c], [b_pT])
                g.cp("act", yTs[yi][:], pT[:].rearrange("p (k t) -> p k t", k=8), [b_pT], [b_yTs[yi]])
                g.dma("pool", yT[2][:, :, tk].rearrange("k p t -> p k t"), yTs[yi][:], b_yTs[yi], reads=[b_yTs[yi]])
            g.flush(ENGS)

    stages = []
    stages.append(("p0", phase_p0))
    for l in range(DEPTH):
        x_src = x_in if l == 0 else xc
        stages.append(("ffn1_%d" % l, lambda l=l, x_src=x_src: phase_ffn(
            "f1l%d" % l, ffn1_w_in[l], ffn1_w_out[l], x_src, hT_f1, xa, ln_mix_w[l:l + 1, :], hT_mix)))
        stages.append(("ptm_%d" % l, lambda l=l: phase_inproj_tm(l)))
        stages.append(("pfm_%d" % l, lambda l=l: phase_inproj_fm(l)))
        stages.append(("ssdprep_%d" % l, lambda l=l: phase_ssd_prep(l)))
        stages.append(("ssdf_%d" % l, lambda l=l: phase_ssd_pass(l, 0)))
        stages.append(("ssdb_%d" % l, lambda l=l: phase_ssd_pass(l, 1)))
        stages.append(("glaf_%d" % l, lambda l=l: phase_gla_pass(l, 0)))
        stages.append(("glab_%d" % l, lambda l=l: phase_gla_pass(l, 1)))
        stages.append(("na_%d" % l, lambda l=l: phase_na(l)))
        stages.append(("merge_%d" % l, lambda l=l: phase_merge(l, xa, xb, ln_ffn2_w[l:l + 1, :], hT_f2)))
        if l + 1 < DEPTH:
            stages.append(("ffn2_%d" % l, lambda l=l: phase_ffn(
                "f2l%d" % l, ffn2_w_in[l], ffn2_w_out[l], xb, hT_f2, xc, ln_ffn1_w[l + 1:l + 2, :], hT_f1)))
        else:
            stages.append(("ffn2_%d" % l, lambda l=l: phase_ffn(
                "f2l%d" % l, ffn2_w_in[l], ffn2_w_out[l], xb, hT_f2, None, ln_final_w[0:1, :], None, final_out=y_out)))
    only = getattr(cfg, "only", None)
    for name, fn in stages:
        if only is None or name in only:
            fn()
        if cfg.stop_after == name:
            break
    g.close()
    return nc, g


def _consts():
    k = np.arange(128)
    tri_f = (k[:, None] <= k[None, :]).astype(np.float32)
    tri_b = (k[:, None] >= k[None, :]).astype(np.float32)
    allow_f = (k[:, None] <= k[None, :])
    allow_b = (k[:, None] >= k[None, :])
    c = {
        "ident": np.eye(128, dtype=np.float32),
        "ones": np.ones((128, 128), np.float32),
        "tri": np.stack([tri_f, tri_b]),
        "maskadd": np.stack([np.where(allow_f, 0.0, -30000.0), np.where(allow_b, 0.0, -30000.0)]).astype(np.float32),
        "mask01": np.stack([allow_f, allow_b]).astype(np.float32),
        "ustrict": np.stack([(k[:, None] > k[None, :]), (k[:, None] < k[None, :])]).astype(np.float32),
    }
    kc = np.arange(64)[:, None]
    qc = np.arange(64)[None, :]
    cs = np.clip(qc - 8, 0, 48)
    colmask = ((kc >= cs) & (kc < cs + 16)).astype(np.float32)
    m_int = np.zeros((128, 16, 64), np.float32)
    m_edge = np.zeros((128, 16, 64), np.float32)
    for kr2 in range(2):
        for s_ in range(16):
            ro = s_ + kr2 - 1
            if 0 <= ro <= 14:
                m_edge[kr2 * 64:(kr2 + 1) * 64, s_, :] = colmask
            if 3 <= ro <= 10:
                m_int[kr2 * 64:(kr2 + 1) * 64, s_, :] = colmask
    jflip = np.zeros((64, 64), np.float32)
    jflip[63 - np.arange(64), np.arange(64)] = 1.0
    c.update({"jflip": jflip, "m_int": m_int, "m_edge": m_edge})
    return c


_PROGRAM_CACHE = {}


def run_streams(streams, flags, w, cfg_extra=None):
    nt = streams[0].shape[0]
    cfg = Cfg(nt)
    nc, g = build_program(cfg)
    base = dict(_consts())
    f32 = np.float32
    for k in ("ln_ffn1_w", "ffn1_w_in", "ffn1_w_out", "ln_mix_w", "w_in", "ssm_conv_w", "ssm_conv_b", "ssm_d",
              "ssm_norm_w", "gla_gate_up", "gla_gate_b", "gla_norm_w", "na_rpb", "w_branch_a", "w_branch_b",
              "w_branch_c", "w_out", "ln_ffn2_w", "ffn2_w_in", "ffn2_w_out"):
        base[k] = np.ascontiguousarray(np.asarray(w[k], dtype=f32))
    base["ssm_dt_bias"] = np.ascontiguousarray(np.asarray(w["ssm_dt_bias"], f32).reshape(DEPTH, 32))
    base["ssm_a_log"] = np.ascontiguousarray(np.asarray(w["ssm_a_log"], f32).reshape(DEPTH, 32))
    base["ln_final_w"] = np.ascontiguousarray(np.asarray(w["ln_final_w"], f32).reshape(1, D))
    in_maps = []
    for x, f in zip(streams, flags):
        m = dict(base)
        m["x"] = np.ascontiguousarray(np.asarray(x, f32))
        m["flag"] = np.tile(np.array([[f, 1.0 - f]], f32), (128, 1))
        in_maps.append(m)
    res = run_bass_kernel_spmd(nc, in_maps, core_ids=list(range(len(in_maps))))
    return [np.asarray(r["y"], dtype=f32) for r in res.results]


def kernel(**inputs):
    xp = np.asarray(inputs["x_prompt"], np.float32)
    xs = np.asarray(inputs["x_sample"], np.float32)
    B, S, _ = xp.shape
    B2, S2, _ = xs.shape
    assert S2 == 2 * S and B % 2 == 0
    streams, flags = [], []
    for i in range(B // 2):
        streams.append(xp[2 * i:2 * i + 2].reshape(2 * S, D))
        flags.append(0.0)
    for i in range(B2):
        streams.append(xs[i])
        flags.append(1.0)
    outs = run_streams(streams, flags, inputs)
    yp = np.stack([o.reshape(2, S, D) for o in outs[:B // 2]]).reshape(B, S, D)
    ys = np.stack(outs[B // 2:]).reshape(B2, S2, D)
    return (yp.astype(np.float32), ys.astype(np.float32))
```

```python
import numpy as np
from contextlib import ExitStack
import concourse.bass as bass
import concourse.mybir as mybir
from concourse.bass_utils import run_bass_kernel_spmd

F32 = mybir.dt.float32
BF16 = mybir.dt.bfloat16
AF = mybir.ActivationFunctionType
ALU = mybir.AluOpType
AX = mybir.AxisListType

D = 1024
DFF = 2816
DEPTH = 2
EPS = 1e-6
ENGS = ("pe", "act", "dve", "pool", "sp")
NA_LAG = 1
STRICT = True


class Buf:
    __slots__ = ("name", "w", "r", "sem", "excl")

    def __init__(self, name):
        self.name = name
        self.w = []
        self.r = []
        self.sem = None
        self.excl = name.startswith("p") and name != "pad"


class Op:
    __slots__ = ("eng", "seq", "fn", "rw", "war", "signal", "sigval", "dma", "waits", "cost",
                 "deps", "nun", "users", "ready", "done", "pos")

    def __init__(self, eng, seq, fn, rw, war, dma, cost):
        self.eng = eng
        self.seq = seq
        self.fn = fn
        self.rw = rw
        self.war = war
        self.signal = False
        self.sigval = 0
        self.dma = dma
        self.waits = None
        self.cost = cost


def _fsize(ap):
    n = 1
    for d_ in tuple(ap.shape)[1:]:
        n *= int(d_)
    return n


DEBUG_LINES = False
NA_SPLIT = False
SCHED = True
SCHED_K = 6
SYNC_NS = 180.0
DMA_LAT_NS = 2600.0


class Graph:
    def __init__(self, nc, n_dma_sems=56):
        self.nc = nc
        self.es = ExitStack()
        self.eng_sem = {e: self.es.enter_context(nc.semaphore("s_" + e)) for e in ENGS}
        self.eng_cnt = {e: 0 for e in ENGS}
        self.dma_sem = [self.es.enter_context(nc.semaphore("d%d" % i)) for i in range(n_dma_sems)]
        self.dma_cnt = [0] * n_dma_sems
        self.next_slot = 0
        self.ops = {e: [] for e in ENGS}
        self.waited = {e: {} for e in ENGS}
        self.n_instr = 0
        self.seq = 0

    def close(self):
        self.es.close()

    def slot(self, buf):
        if buf.sem is None:
            assert self.next_slot < len(self.dma_sem), "too many dma buffers in one phase"
            buf.sem = self.next_slot
            self.next_slot += 1
        return buf.sem

    def _record(self, eng, fn, reads, writes, dma=None, cost=400.0):
        rw, war = set(), set()
        ex = [b for b in reads if b.excl and b not in writes]
        if ex:
            reads = [b for b in reads if not b.excl or b in writes]
            writes = list(writes) + ex
        for b in reads:
            rw.update(b.w)
        for b in writes:
            for o in b.w:
                if dma is not None and o.dma is not None and o.dma[0] == dma[0]:
                    rw.update(o.rw)
                    war.update(o.war)
                    continue
                rw.add(o)
            war.update(b.r)
        self.seq += 1
        op = Op(eng, self.seq, fn, rw, war, dma, cost)
        if DEBUG_LINES:
            import sys as _s
            f_ = _s._getframe(2)
            op.waits = (f_.f_lineno, f_.f_back.f_lineno if f_.f_back else 0)
        self.ops[eng].append(op)
        for b in reads:
            b.r.append(op)
        for b in writes:
            b.w = [op]
            b.r = []
        return op

    def op(self, eng, fn, reads=(), writes=()):
        return self._record(eng, fn, reads, writes)

    def dma(self, eng, out, in_, sbuf, reads=(), writes=(), slow=False):
        s = self.slot(sbuf)
        self.dma_cnt[s] += 16
        cnt = self.dma_cnt[s]
        sem = self.dma_sem[s]
        kw = {"allow_slow_non_contiguous": True} if slow else {}

        def fn(e):
            return e.dma_start(out=out, in_=in_, **kw).then_inc(sem, 16)
        return self._record(eng, fn, reads, writes, dma=(s, cnt), cost=DMA_LAT_NS)

    def mm(self, out, lhsT, rhs, start, stop, reads, writes):
        n = _fsize(rhs)
        cost = 64.0 + n * (0.42 if rhs.dtype == BF16 else 0.85)
        return self._record("pe", lambda e: e.matmul(out, lhsT, rhs, start=start, stop=stop), reads, writes, cost=cost)

    def tr(self, out, in_, ident, reads, writes):
        return self._record("pe", lambda e: e.transpose(out=out, in_=in_, identity=ident), reads, writes, cost=300.0)

    def _ec(self, eng, out):
        n = _fsize(out)
        if eng == "pool":
            return 300.0 + 2.0 * n
        return 220.0 + 1.0 * n

    def act(self, out, in_, func, reads, writes, bias=None, scale=None):
        kw = {}
        if bias is not None:
            kw["bias"] = bias
        if scale is not None:
            kw["scale"] = scale
        return self._record("act", lambda e: e.activation(out=out, in_=in_, func=func, **kw), reads, writes,
                            cost=self._ec("act", out))

    def cp(self, eng, out, in_, reads, writes):
        if eng == "act":
            return self._record("act", lambda e: e.copy(out=out, in_=in_), reads, writes, cost=self._ec(eng, out))
        return self._record(eng, lambda e: e.tensor_copy(out=out, in_=in_), reads, writes, cost=self._ec(eng, out))

    def tt(self, eng, out, in0, in1, op, reads, writes):
        return self._record(eng, lambda e: e.tensor_tensor(out=out, in0=in0, in1=in1, op=op), reads, writes,
                            cost=self._ec(eng, out))

    def ts(self, eng, out, in0, s1, s2, op0, op1, reads, writes):
        if op1 is None:
            return self._record(eng, lambda e: e.tensor_scalar(out=out, in0=in0, scalar1=s1, scalar2=None, op0=op0),
                                reads, writes, cost=self._ec(eng, out))
        return self._record(eng, lambda e: e.tensor_scalar(out=out, in0=in0, scalar1=s1, scalar2=s2, op0=op0, op1=op1),
                            reads, writes, cost=self._ec(eng, out))

    def stt(self, eng, out, in0, scalar, in1, op0, op1, reads, writes):
        return self._record(eng, lambda e: e.scalar_tensor_tensor(out=out, in0=in0, scalar=scalar, in1=in1,
                                                                  op0=op0, op1=op1), reads, writes,
                            cost=self._ec(eng, out))

    def red(self, eng, out, in_, reads, writes):
        return self._record(eng, lambda e: e.reduce_sum(out=out, in_=in_, axis=AX.X), reads, writes,
                            cost=self._ec(eng, in_))

    def memset(self, eng, ap, val, writes):
        return self._record(eng, lambda e: e.memset(ap, val), (), writes, cost=self._ec(eng, ap))

    def _schedule(self):
        import bisect
        ops = self.ops
        allops = [o for e in ENGS for o in ops[e]]
        for o in allops:
            o.rw = set(d_ for d_ in o.rw if d_.fn is not None)
            o.war = set(d_ for d_ in o.war if d_.fn is not None)
            o.deps = list(o.rw | o.war)
            o.nun = len(o.deps)
            o.users = []
            o.ready = 0.0
            o.done = False
        for o in allops:
            for d_ in o.deps:
                d_.users.append(o)
        avail = {e: [] for e in ENGS}
        for o in allops:
            if o.nun == 0:
                avail[o.eng].append((o.seq, o))
        for e in ENGS:
            avail[e].sort(key=lambda t: t[0])
        free = {e: 0.0 for e in ENGS}
        new = {e: [] for e in ENGS}
        remaining = len(allops)
        while remaining:
            best = None
            for e in ENGS:
                av = avail[e]
                fe = free[e]
                for k in range(min(SCHED_K, len(av))):
                    o = av[k][1]
                    st = o.ready if o.ready > fe else fe
                    if best is None or st < best[0] or (st == best[0] and o.seq < best[2].seq):
                        best = (st, k, o)
                    if o.ready <= fe:
                        break
            st, k, o = best
            e = o.eng
            del avail[e][k]
            fin = st + o.cost
            free[e] = st + 64.0 if o.dma is not None else fin
            o.done = True
            new[e].append(o)
            remaining -= 1
            for u in o.users:
                r = fin + SYNC_NS
                if r > u.ready:
                    u.ready = r
                u.nun -= 1
                if u.nun == 0:
                    bisect.insort(avail[u.eng], (u.seq, u))
        self.ops = new
        if DEBUG_LINES:
            for e in ENGS:
                print("ENGINE", e)
                for o in new[e][:DEBUG_LINES]:
                    print("   seq", o.seq, "line", o.waits, "dma" if o.dma else "", "deps", sorted(d_.seq for d_ in o.deps)[:8])

    def flush(self, engines, sched=True):
        if SCHED and sched:
            self._schedule()
        ops = self.ops
        for e in ENGS:
            for i, op in enumerate(ops[e]):
                op.pos = i
        for e in ENGS:
            for op in ops[e]:
                need = set()
                for d_ in op.rw:
                    if d_.dma is not None or d_.eng != e:
                        need.add(d_)
                    elif op.dma is not None or (STRICT and e in ("act", "dve", "pool")):
                        need.add(d_)
                for d_ in op.war:
                    if d_.dma is not None or d_.eng != e:
                        need.add(d_)
                    elif op.dma is not None:
                        need.add(d_)
                for d_ in need:
                    if d_.dma is None and d_.eng == e:
                        assert d_.pos < op.pos
                op.waits = need
                for d_ in need:
                    if d_.dma is None:
                        d_.signal = True
        for e in ENGS:
            for op in reversed(ops[e]):
                if op.dma is None:
                    op.signal = True
                    break
        for e in ENGS:
            c = self.eng_cnt[e]
            for op in ops[e]:
                if op.signal and op.dma is None:
                    c += 1
                    op.sigval = c
            self.eng_cnt[e] = c
        nc = self.nc
        final_eng = dict(self.eng_cnt)
        final_dma = list(self.dma_cnt)
        g = self

        def emit(e, handle):
            waited = g.waited[e]
            for op in ops[e]:
                wl = {}
                for d_ in op.waits:
                    if d_.dma is not None:
                        key, val = ("d", d_.dma[0]), d_.dma[1]
                    else:
                        key, val = ("c", d_.eng), d_.sigval
                    if wl.get(key, 0) < val:
                        wl[key] = val
                for key, val in wl.items():
                    if waited.get(key, 0) >= val:
                        continue
                    waited[key] = val
                    sem = g.dma_sem[key[1]] if key[0] == "d" else g.eng_sem[key[1]]
                    handle.wait_ge(sem, val)
                    g.n_instr += 1
                ins = op.fn(handle)
                g.n_instr += 1
                if op.signal and op.dma is None:
                    ins.then_inc(g.eng_sem[e], 1)
            for x in ENGS:
                if x != e and waited.get(("c", x), 0) < final_eng[x]:
                    waited[("c", x)] = final_eng[x]
                    handle.wait_ge(g.eng_sem[x], final_eng[x])
            for s_, v in enumerate(final_dma):
                if v > 0 and waited.get(("d", s_), 0) < v:
                    waited[("d", s_)] = v
                    handle.wait_ge(g.dma_sem[s_], v)

        with nc.Block() as block:
            @block.tensor
            def _(h):
                emit("pe", h)

            @block.scalar
            def _(h):
                emit("act", h)

            @block.vector
            def _(h):
                emit("dve", h)

            @block.gpsimd
            def _(h):
                emit("pool", h)

            @block.sync
            def _(h):
                emit("sp", h)
        for e in ENGS:
            for op in ops[e]:
                op.fn = None
        self.ops = {e: [] for e in ENGS}
        self.next_slot = 0


C_Z, C_XBC, C_DTF, C_DTB = 0, 1024, 2560, 2576
C_GQ, C_GK, C_GV, C_GG = 2592, 3104, 3616, 4640
C_DNF, C_DNB = 5664, 5680
C_NQ, C_NK, C_NV = 5696, 6720, 7744
C_GATE = 8768
IN_WIDTH = 11840


class Cfg:
    def __init__(self, nt, stop_after=None, debug_outs=(), debug_ins=()):
        self.nt = nt
        self.seg = nt // 2
        self.stop_after = stop_after
        self.debug_outs = debug_outs
        self.debug_ins = debug_ins
        self.only = None


def build_program(cfg):
    nc = bass.Bass("TRN2", target_bir_lowering=False)
    NT = cfg.nt
    TB = 512
    NB = NT // TB
    NTI = TB // 128

    cfg.in_shapes = {}

    def din(name, shape, dt=F32):
        cfg.in_shapes[name] = (list(shape), dt)
        return nc.dram_tensor(name, list(shape), dt, kind="ExternalInput").ap()

    def dout(name, shape, dt=F32):
        return nc.dram_tensor(name, list(shape), dt, kind="ExternalOutput").ap()

    def dscr(name, shape, dt=F32):
        kind = "Internal"
        if name in cfg.debug_outs:
            kind = "ExternalOutput"
        if name in cfg.debug_ins:
            kind = "ExternalInput"
            cfg.in_shapes[name] = (list(shape), dt)
        return nc.dram_tensor(name, list(shape), dt, kind=kind).ap()

    x_in = din("x", [NT, D])
    flag = din("flag", [128, 2])
    ident_in = din("ident", [128, 128])
    ln_ffn1_w = din("ln_ffn1_w", [DEPTH, D])
    ffn1_w_in = din("ffn1_w_in", [DEPTH, D, 2 * DFF])
    ffn1_w_out = din("ffn1_w_out", [DEPTH, DFF, D])
    ln_mix_w = din("ln_mix_w", [DEPTH, D])
    w_in = din("w_in", [DEPTH, D, IN_WIDTH])
    w_branch = [din("w_branch_" + c, [DEPTH, D, D]) for c in "abc"]
    w_out = din("w_out", [DEPTH, D, D])
    ln_ffn2_w = din("ln_ffn2_w", [DEPTH, D])
    ffn2_w_in = din("ffn2_w_in", [DEPTH, D, 2 * DFF])
    ffn2_w_out = din("ffn2_w_out", [DEPTH, DFF, D])
    ln_final_w = din("ln_final_w", [1, D])
    y_out = dout("y", [NT, D])

    xa = dscr("xa", [NT, D])
    xb = dscr("xb", [NT, D])
    xc = dscr("xc", [NT, D])
    hT_f1 = dscr("hT_f1", [8, 128, NT], BF16)
    hT_mix = dscr("hT_mix", [8, 128, NT], BF16)
    hT_f2 = dscr("hT_f2", [8, 128, NT], BF16)
    z_tm = dscr("z_tm", [NT, 1024])
    kg_tm = dscr("kg_tm", [NT, 512])
    vg_tm = dscr("vg_tm", [NT, 1024], BF16)
    gg_tm = dscr("gg_tm", [NT, 1024])
    vn_tm = dscr("vn_tm", [NT, 1024], BF16)
    dt_tm = dscr("dt_tm", [NT, 32])
    xbcT = dscr("xbcT", [12, 128, NT])
    gqT = dscr("gqT", [4, 128, NT])
    gkT = dscr("gkT", [4, 128, NT])
    nqT = dscr("nqT", [8, 128, NT], BF16)
    nkT = dscr("nkT", [8, 128, NT], BF16)
    gatesT = dscr("gatesT", [24, 128, NT], BF16)
    dnT = dscr("dnT", [2, 16, NT])
    yT = [dscr("yT_" + c, [8, 128, NT], BF16) for c in "abc"]

    g = Graph(nc)

    uid = [0]

    def sb(es, name, shape, dt):
        uid[0] += 1
        return es.enter_context(nc.sbuf_tensor("s%d_%s" % (uid[0], name), list(shape), dt))

    def ps(es, name, shape, dt):
        uid[0] += 1
        return es.enter_context(nc.psum_tensor("p%d_%s" % (uid[0], name), list(shape), dt))

    class NormCtx:
        def __init__(self, es, tag, transpose=True):
            self.transpose = transpose
            self.wB = sb(es, "wB_" + tag, [128, D], F32)
            self.sq = sb(es, "sq_" + tag, [128, D], F32)
            self.ss = [sb(es, "ss%d_" % i + tag, [128, 2], F32) for i in range(2)]
            self.b_wB, self.b_sq = Buf("wB"), Buf("sq")
            self.b_ss = [Buf("ss0"), Buf("ss1")]
            if transpose:
                self.ident = sb(es, "ident_" + tag, [128, 128], BF16)
                self.hn = [sb(es, "hn%d_" % i + tag, [128, D], BF16) for i in range(2)]
                self.hTs = sb(es, "hTs_" + tag, [128, 8, TB], BF16)
                self.pT = [ps(es, "pT%d_" % i + tag, [128, D], BF16) for i in range(2)]
                self.b_ident = Buf("ident")
                self.b_hn = [Buf("hn0"), Buf("hn1")]
                self.b_hTs = Buf("hTs")
                self.b_pT = [Buf("pT0"), Buf("pT1")]
            else:
                self.yt = [sb(es, "yt%d_" % i + tag, [128, D], F32) for i in range(2)]
                self.b_yt = [Buf("yt0"), Buf("yt1")]
            self.n = 0

        def setup(self, w_row_ap):
            if self.transpose:
                g.dma("pool", self.ident[:], ident_in, self.b_ident, writes=[self.b_ident])
            g.dma("sp", self.wB[:], w_row_ap.partition_broadcast(128), self.b_wB, writes=[self.b_wB])

        def stats(self, xt_ap, b_x):
            i = self.n % 2
            self.n += 1
            ss = self.ss[i]
            g.act(self.sq[:], xt_ap, AF.Square, [b_x], [self.b_sq])
            g.red("dve", ss[:, 0:1], self.sq[:], [self.b_sq], [self.b_ss[i]])
            g.ts("dve", ss[:, 1:2], ss[:, 0:1], 1.0 / D, EPS, ALU.mult, ALU.add, [self.b_ss[i]], [self.b_ss[i]])
            g.act(ss[:, 1:2], ss[:, 1:2], AF.Sqrt, [self.b_ss[i]], [self.b_ss[i]])
            g._record("dve", lambda e: e.reciprocal(out=ss[:, 1:2], in_=ss[:, 1:2]), [self.b_ss[i]], [self.b_ss[i]])
            return i

        def run(self, xt_ap, b_x, hT_dram, blk, ti):
            self.flush_pending()
            i = self.stats(xt_ap, b_x)
            hn = self.hn[i]
            g.stt("dve", hn[:], xt_ap, self.ss[i][:, 1:2], self.wB[:], ALU.mult, ALU.mult,
                  [b_x, self.b_ss[i], self.b_wB], [self.b_hn[i]])
            self.pending = (i, hT_dram, blk, ti)

        def flush_pending(self):
            if getattr(self, "pending", None) is None:
                return
            i, hT_dram, blk, ti = self.pending
            self.pending = None
            hn, pT, hTs = self.hn[i], self.pT[i], self.hTs
            for k in range(8):
                g.tr(pT[:, k * 128:(k + 1) * 128], hn[:, k * 128:(k + 1) * 128], self.ident[:],
                     [self.b_hn[i], self.b_ident], [self.b_pT[i]])
            g.cp("act", hTs[:, :, ti * 128:(ti + 1) * 128], pT[:].rearrange("p (k t) -> p k t", k=8),
                 [self.b_pT[i]], [self.b_hTs])
            if ti == NTI - 1:
                g.dma("pool", hT_dram[:, :, blk * TB:(blk + 1) * TB].rearrange("k p t -> p k t"), hTs[:],
                      self.b_hTs, reads=[self.b_hTs])

        def run_final(self, xt_ap, b_x, y_dram, t):
            i = self.stats(xt_ap, b_x)
            g.stt("dve", self.yt[i][:], xt_ap, self.ss[i][:, 1:2], self.wB[:], ALU.mult, ALU.mult,
                  [b_x, self.b_ss[i], self.b_wB], [self.b_yt[i]])
            g.dma("pool", y_dram[t * 128:(t + 1) * 128, :], self.yt[i][:], self.b_yt[i], reads=[self.b_yt[i]])

    def phase_p0():
        with ExitStack() as es:
            nrm = NormCtx(es, "p0")
            xt = [sb(es, "p0x%d" % i, [128, D], F32) for i in range(2)]
            b_xt = [Buf("x0"), Buf("x1")]
            nrm.setup(ln_ffn1_w[0:1, :])
            for blk in range(NB):
                for ti in range(NTI):
                    t = blk * NTI + ti
                    i = t % 2
                    g.dma("sp", xt[i][:], x_in[t * 128:(t + 1) * 128, :], b_xt[i], writes=[b_xt[i]])
                    nrm.run(xt[i][:], b_xt[i], hT_f1, blk, ti)
            nrm.flush_pending()
            g.flush(ENGS)

    def phase_ffn(tag, w_in_ap, w_out_ap, x_src, hT_src, x_dst, next_w_row, hT_dst, final_out=None):
        with ExitStack() as es:
            w1 = sb(es, "w1_" + tag, [128, 8, 2 * DFF], BF16)
            w2 = sb(es, "w2_" + tag, [128, 22, D], BF16)
            b_w1, b_w2 = Buf("w1"), Buf("w2")
            hTb = sb(es, "hTb_" + tag, [128, 8, TB], BF16)
            b_hTb = Buf("hTb")
            gT = sb(es, "gT_" + tag, [128, 22, TB], BF16)
            b_gT = Buf("gT")
            sa = [sb(es, "sa%d_" % i + tag, [128, TB], F32) for i in range(2)]
            b_sa = [Buf("sa0"), Buf("sa1")]
            xt = sb(es, "xt_" + tag, [128, D], F32)
            b_xt = Buf("xt")
            xn = [sb(es, "xn%d_" % i + tag, [128, D], F32) for i in range(2)]
            b_xn = [Buf("xn0"), Buf("xn1")]
            pa = [ps(es, "pa%d_" % i + tag, [128, TB], F32) for i in range(2)]
            pb = [ps(es, "pb%d_" % i + tag, [128, TB], F32) for i in range(2)]
            b_pa = [Buf("pa0"), Buf("pa1")]
            b_pb = [Buf("pb0"), Buf("pb1")]
            po = [ps(es, "po%d_" % i + tag, [128, 512], F32) for i in range(2)]
            b_po = [Buf("po0"), Buf("po1")]
            nrm = NormCtx(es, tag, transpose=(final_out is None))
            for k in range(8):
                g.dma("pool", w1[:, k, :], w_in_ap[k * 128:(k + 1) * 128, :], b_w1, writes=[b_w1])
            for k in range(22):
                g.dma("pool", w2[:, k, :], w_out_ap[k * 128:(k + 1) * 128, :], b_w2, writes=[b_w2])
            nrm.setup(next_w_row)

            def load_h(blk):
                g.dma("sp", hTb[:], hT_src[:, :, blk * TB:(blk + 1) * TB].rearrange("k p t -> p k t"),
                      b_hTb, writes=[b_hTb])
            load_h(0)
            ntl = 0
            for blk in range(NB):
                for c in range(22):
                    q = c % 2
                    for k in range(8):
                        g.mm(pa[q][:], w1[:, k, c * 128:(c + 1) * 128], hTb[:, k, :], k == 0, k == 7,
                             [b_w1, b_hTb], [b_pa[q]])
                    for k in range(8):
                        g.mm(pb[q][:], w1[:, k, DFF + c * 128:DFF + (c + 1) * 128], hTb[:, k, :], k == 0, k == 7,
                             [b_w1, b_hTb], [b_pb[q]])
                    g.act(sa[q][:], pa[q][:], AF.Silu, [b_pa[q]], [b_sa[q]])
                    g.tt("dve", gT[:, c, :], sa[q][:], pb[q][:], ALU.mult, [b_sa[q], b_pb[q]], [b_gT])
                if blk + 1 < NB:
                    load_h(blk + 1)
                for ti in range(NTI):
                    t = blk * NTI + ti
                    i = ntl % 2
                    ntl += 1
                    g.dma("sp", xt[:], x_src[t * 128:(t + 1) * 128, :], b_xt, writes=[b_xt])
                    for ch in range(2):
                        for k in range(22):
                            g.mm(po[ch][:], gT[:, k, ti * 128:(ti + 1) * 128], w2[:, k, ch * 512:(ch + 1) * 512],
                                 k == 0, k == 21, [b_gT, b_w2], [b_po[ch]])
                        g.stt("dve", xn[i][:, ch * 512:(ch + 1) * 512], po[ch][:], 0.5,
                              xt[:, ch * 512:(ch + 1) * 512], ALU.mult, ALU.add, [b_po[ch], b_xt], [b_xn[i]])
                    if final_out is None:
                        nrm.flush_pending()
                    if final_out is None:
                        g.dma("pool", x_dst[t * 128:(t + 1) * 128, :], xn[i][:], b_xn[i], reads=[b_xn[i]])
                        nrm.run(xn[i][:], b_xn[i], hT_dst, blk, ti)
                    else:
                        nrm.run_final(xn[i][:], b_xn[i], final_out, t)
            if final_out is None:
                nrm.flush_pending()
            g.flush(ENGS)

    def phase_inproj_tm(l):
        with ExitStack() as es:
            W = w_in[l]
            parts = [(C_Z, 1024, z_tm, F32), (C_GK, 512, kg_tm, F32), (C_GV, 1024, vg_tm, BF16),
                     (C_GG, 1024, gg_tm, F32), (C_NV, 1024, vn_tm, BF16), (C_DTF, 32, dt_tm, F32)]
            NC_TM = sum(p[1] for p in parts)
            wt = sb(es, "wtm", [128, 8, NC_TM], BF16)
            b_wt = Buf("wtm")
            hTb = [sb(es, "tm_hTb%d" % i, [128, 8, TB], BF16) for i in range(2)]
            b_hTb = [Buf("hTb0"), Buf("hTb1")]
            st = [[sb(es, "tm_st%d_%d" % (i, pi), [128, p[1]], p[3]) for pi, p in enumerate(parts)] for i in range(2)]
            b_st = [[Buf("st%d_%d" % (i, pi)) for pi in range(len(parts))] for i in range(2)]
            pp = [ps(es, "tm_p%d" % i, [128, 512], F32) for i in range(4)]
            b_pp = [Buf("pp%d" % i) for i in range(4)]
            off = 0
            offs = []
            for (c0, wd, _, _) in parts:
                offs.append(off)
                for k in range(8):
                    g.dma("pool", wt[:, k, off:off + wd], W[k * 128:(k + 1) * 128, c0:c0 + wd], b_wt, writes=[b_wt])
                off += wd
            nps = 0
            nev = 0
            for blk in range(NB):
                j = blk % 2
                g.dma("sp", hTb[j][:], hT_mix[:, :, blk * TB:(blk + 1) * TB].rearrange("k p t -> p k t"),
                      b_hTb[j], writes=[b_hTb[j]])
                for ti in range(NTI):
                    t = blk * NTI + ti
                    i = t % 2
                    for pi, (c0, wd, dst, dt) in enumerate(parts):
                        for cb in range(0, wd, 512):
                            n = min(512, wd - cb)
                            q = nps % 4
                            nps += 1
                            for k in range(8):
                                g.mm(pp[q][:, 0:n], hTb[j][:, k, ti * 128:(ti + 1) * 128],
                                     wt[:, k, offs[pi] + cb:offs[pi] + cb + n], k == 0, k == 7,
                                     [b_hTb[j], b_wt], [b_pp[q]])
                            eng = "act" if nev % 2 == 0 else "dve"
                            nev += 1
                            g.cp(eng, st[i][pi][:, cb:cb + n], pp[q][:, 0:n], [b_pp[q]], [b_st[i][pi]])
                        g.dma("pool", dst[t * 128:(t + 1) * 128, :], st[i][pi][:], b_st[i][pi], reads=[b_st[i][pi]])
            g.flush(ENGS)

    def phase_inproj_fm(l):
        with ExitStack() as es:
            W = w_in[l]
            parts = [(C_XBC, 12, xbcT, F32, None, 1.0), (C_GQ, 4, gqT, F32, None, 128.0 ** -0.5),
                     (C_GK, 4, gkT, F32, None, 1.0), (C_NQ, 8, nqT, BF16, None, 0.125),
                     (C_NK, 8, nkT, BF16, None, 1.0), (C_GATE, 24, gatesT, BF16, AF.Sigmoid, 1.0)]
            NCH = sum(p[1] for p in parts)
            NC_FM = NCH * 128 + 32
            wt = sb(es, "wfm", [128, 8, NC_FM], BF16)
            b_wt = Buf("wfm")
            hTb = [sb(es, "fm_hTb%d" % i, [128, 8, TB], BF16) for i in range(2)]
            b_hTb = [Buf("hTb0"), Buf("hTb1")]
            stf = [sb(es, "fm_stf%d" % i, [128, TB], F32) for i in range(4)]
            stb = [sb(es, "fm_stb%d" % i, [128, TB], BF16) for i in range(4)]
            b_stf = [Buf("stf%d" % i) for i in range(4)]
            b_stb = [Buf("stb%d" % i) for i in range(4)]
            pp = [ps(es, "fm_p%d" % i, [128, 512], F32) for i in range(4)]
            b_pp = [Buf("pp%d" % i) for i in range(4)]
            off = 0
            offs = []
            for (c0, nch, _, _, _, _) in parts:
                offs.append(off)
                for k in range(8):
                    g.dma("pool", wt[:, k, off:off + nch * 128], W[k * 128:(k + 1) * 128, c0:c0 + nch * 128],
                          b_wt, writes=[b_wt])
                off += nch * 128
            off_dn = off
            for k in range(8):
                g.dma("pool", wt[:, k, off:off + 32], W[k * 128:(k + 1) * 128, C_DNF:C_DNF + 32], b_wt, writes=[b_wt])
            nps = 0
            nf = 0
            nbf = 0
            nev = 0
            for blk in range(NB):
                j = blk % 2
                tok = slice(blk * TB, (blk + 1) * TB)
                g.dma("sp", hTb[j][:], hT_mix[:, :, tok].rearrange("k p t -> p k t"), b_hTb[j], writes=[b_hTb[j]])
                for pi, (c0, nch, dst, dt, func, scale) in enumerate(parts):
                    for c in range(nch):
                        q = nps % 4
                        nps += 1
                        wc = offs[pi] + c * 128
                        for k in range(8):
                            g.mm(pp[q][:], wt[:, k, wc:wc + 128], hTb[j][:, k, :], k == 0, k == 7,
                                 [b_wt, b_hTb[j]], [b_pp[q]])
                        if dt == F32:
                            s_, b_s = stf[nf % 4], b_stf[nf % 4]
                            nf += 1
                        else:
                            s_, b_s = stb[nbf % 4], b_stb[nbf % 4]
                            nbf += 1
                        if func is not None:
                            g.act(s_[:], pp[q][:], func, [b_pp[q]], [b_s])
                        elif scale != 1.0:
                            if nev % 2 == 0:
                                g.act(s_[:], pp[q][:], AF.Copy, [b_pp[q]], [b_s], scale=scale)
                            else:
                                g.ts("dve", s_[:], pp[q][:], scale, None, ALU.mult, None, [b_pp[q]], [b_s])
                            nev += 1
                        else:
                            g.cp("act" if nev % 2 == 0 else "dve", s_[:], pp[q][:], [b_pp[q]], [b_s])
                            nev += 1
                        g.dma("pool", dst[c, :, tok], s_[:], b_s, reads=[b_s])
                for dd in range(2):
                    q = nps % 4
                    nps += 1
                    wc = off_dn + dd * 16
                    for k in range(8):
                        g.mm(pp[q][0:16, :], wt[:, k, wc:wc + 16], hTb[j][:, k, :], k == 0, k == 7,
                             [b_wt, b_hTb[j]], [b_pp[q]])
                    s_, b_s = stf[nf % 4], b_stf[nf % 4]
                    nf += 1
                    g.cp("dve", s_[0:16, :], pp[q][0:16, :], [b_pp[q]], [b_s])
                    g.dma("pool", dnT[dd, :, tok], s_[0:16, :], b_s, reads=[b_s])
            g.flush(ENGS)

    def phase_merge(l, x_src, x_dst, next_w_row, hT_dst):
        with ExitStack() as es:
            wb_ = [sb(es, "wbr%d" % i, [128, 8, D], BF16) for i in range(3)]
            wo = sb(es, "wo", [128, 8, D], BF16)
            b_w = Buf("w")
            yb = [sb(es, "yb%d" % i, [128, 8, TB], BF16) for i in range(3)]
            b_yb = [Buf("yb%d" % i) for i in range(3)]
            gt = sb(es, "gt", [128, 24, TB], BF16)
            b_gt = Buf("gt")
            tmp = [sb(es, "mtmp%d" % i, [128, TB], F32) for i in range(3)]
            b_tmp = [Buf("tmp%d" % i) for i in range(3)]
            mT = sb(es, "mT", [128, 8, TB], BF16)
            b_mT = Buf("mT")
            xt = sb(es, "m_xt", [128, D], F32)
            b_xt = Buf("xt")
            xn = [sb(es, "m_xn%d" % i, [128, D], F32) for i in range(2)]
            b_xn = [Buf("xn0"), Buf("xn1")]
            pp = [ps(es, "m_p%d" % i, [128, 512], F32) for i in range(6)]
            b_pp = [Buf("pp%d" % i) for i in range(6)]
            nrm = NormCtx(es, "mrg")
            for i in range(3):
                for k in range(8):
                    g.dma("pool", wb_[i][:, k, :], w_branch[i][l, k * 128:(k + 1) * 128, :], b_w, writes=[b_w])
            for k in range(8):
                g.dma("pool", wo[:, k, :], w_out[l, k * 128:(k + 1) * 128, :], b_w, writes=[b_w])
            nrm.setup(next_w_row)
            ntl = 0
            for blk in range(NB):
                tok = slice(blk * TB, (blk + 1) * TB)
                for i in range(3):
                    g.dma("sp", yb[i][:], yT[i][:, :, tok].rearrange("k p t -> p k t"), b_yb[i], writes=[b_yb[i]])
                g.dma("sp", gt[:], gatesT[:, :, tok].rearrange("k p t -> p k t"), b_gt, writes=[b_gt])
                for oc in range(8):
                    par = (oc % 2) * 3
                    for i in range(3):
                        for k in range(8):
                            g.mm(pp[par + i][:], wb_[i][:, k, oc * 128:(oc + 1) * 128], yb[i][:, k, :], k == 0, k == 7,
                                 [b_w, b_yb[i]], [b_pp[par + i]])
                    for i in range(3):
                        g.tt("dve", tmp[i][:], pp[par + i][:], gt[:, i * 8 + oc, :], ALU.mult,
                             [b_pp[par + i], b_gt], [b_tmp[i]])
                    g.tt("pool", tmp[0][:], tmp[0][:], tmp[1][:], ALU.add, [b_tmp[0], b_tmp[1]], [b_tmp[0]])
                    g.tt("pool", mT[:, oc, :], tmp[0][:], tmp[2][:], ALU.add, [b_tmp[0], b_tmp[2]], [b_mT])
                for ti in range(NTI):
                    t = blk * NTI + ti
                    i = ntl % 2
                    ntl += 1
                    g.dma("sp", xt[:], x_src[t * 128:(t + 1) * 128, :], b_xt, writes=[b_xt])
                    for ch in range(2):
                        q = ch * 3
                        for k in range(8):
                            g.mm(pp[q][:], mT[:, k, ti * 128:(ti + 1) * 128], wo[:, k, ch * 512:(ch + 1) * 512],
                                 k == 0, k == 7, [b_mT, b_w], [b_pp[q]])
                        g.tt("dve", xn[i][:, ch * 512:(ch + 1) * 512], pp[q][:], xt[:, ch * 512:(ch + 1) * 512], ALU.add,
                             [b_pp[q], b_xt], [b_xn[i]])
                    nrm.flush_pending()
                    g.dma("pool", x_dst[t * 128:(t + 1) * 128, :], xn[i][:], b_xn[i], reads=[b_xn[i]])
                    nrm.run(xn[i][:], b_xn[i], hT_dst, blk, ti)
            nrm.flush_pending()
            g.flush(ENGS)

    ssm_conv_w = din("ssm_conv_w", [DEPTH, 5, 1536])
    ssm_conv_b = din("ssm_conv_b", [DEPTH, 1536])
    ssm_dt_bias = din("ssm_dt_bias", [DEPTH, 32])
    ssm_a_log = din("ssm_a_log", [DEPTH, 32])
    ssm_d = din("ssm_d", [DEPTH, 16])
    ssm_norm_w = din("ssm_norm_w", [DEPTH, 1024])
    tri_in = din("tri", [2, 128, 128])
    maskadd_in = din("maskadd", [2, 128, 128])
    mask01_in = din("mask01", [2, 128, 128])
    ustrict_in = din("ustrict", [2, 128, 128])
    ones_in = din("ones", [128, 128])
    xs_tm = dscr("xs_tm", [NT, 1024])
    B_tm = dscr("B_tm", [NT, 256], BF16)
    BCT = dscr("BCT", [4, 128, NT], BF16)
    dtsp = dscr("dtsp", [NT, 32])
    dta_d = dscr("dta", [NT, 32])
    dlog_d = dscr("dlog", [NT, 32])
    yf_d = dscr("yf", [NT, 1024])
    NCH = NT // 128
    MIDC = NCH // 2

    def phase_ssd_prep(l):
        with ExitStack() as es:
            cw = sb(es, "cw", [128, 12, 5], F32)
            cb = sb(es, "cb", [128, 12], F32)
            fl = sb(es, "fl", [128, 2], F32)
            identf = sb(es, "identf", [128, 128], F32)
            dbias = sb(es, "dbias", [128, 32], F32)
            aB = sb(es, "aB", [128, 32], F32)
            b_c = Buf("consts")
            xin = [sb(es, "xin%d" % i, [128, 12, TB + 4], F32) for i in range(2)]
            b_xin = [Buf("xin0"), Buf("xin1")]
            acc = [sb(es, "acc%d" % i, [128, TB], F32) for i in range(3)]
            b_acc = [Buf("acc%d" % i) for i in range(3)]
            sil = sb(es, "sil", [128, 10, TB], F32)
            b_sil = Buf("sil")
            silb = [sb(es, "silb%d" % i, [128, TB], BF16) for i in range(2)]
            b_silb = [Buf("silb0"), Buf("silb1")]
            xst = [sb(es, "xst%d" % i, [128, 1024], F32) for i in range(2)]
            b_xst = [Buf("xst0"), Buf("xst1")]
            bst = [sb(es, "bst%d" % i, [128, 256], BF16) for i in range(2)]
            b_bst = [Buf("bst0"), Buf("bst1")]
            dtt = [sb(es, "dtt%d" % i, [128, 4, NTI, 32], F32) for i in range(2)]
            b_dtt = [Buf("dtt0"), Buf("dtt1")]
            pt = [ps(es, "pp_t%d" % i, [128, 1024], F32) for i in range(2)]
            b_pt = [Buf("pt0"), Buf("pt1")]
            pbt = [ps(es, "pp_b%d" % i, [128, 512], F32) for i in range(2)]
            b_pbt = [Buf("pbt0"), Buf("pbt1")]
            for k in range(5):
                g.dma("sp", cw[:, :, k], ssm_conv_w[l, k].rearrange("(c p) -> p c", p=128), b_c, writes=[b_c], slow=True)
            g.dma("sp", cb[:], ssm_conv_b[l].rearrange("(c p) -> p c", p=128), b_c, writes=[b_c], slow=True)
            g.dma("sp", fl[:], flag, b_c, writes=[b_c])
            g.dma("sp", identf[:], ident_in, b_c, writes=[b_c])
            g.dma("sp", dbias[:], ssm_dt_bias[l:l + 1, :].partition_broadcast(128), b_c, writes=[b_c])
            g.dma("sp", aB[:], ssm_a_log[l:l + 1, :].partition_broadcast(128), b_c, writes=[b_c])
            g.act(aB[:], aB[:], AF.Exp, [b_c], [b_c])
            g.ts("dve", aB[:], aB[:], -1.0, None, ALU.mult, None, [b_c], [b_c])
            na = 0
            nsb = 0
            nt_ = 0
            for blk in range(NB):
                j = blk % 2
                t0 = blk * TB
                lo, hi = t0 - 2, t0 + TB + 2
                xi = xin[j]
                lo_c, hi_c = 0, TB + 4
                if lo < 0:
                    g.memset("pool", xi[:, :, 0:2], 0.0, [b_xin[j]])
                    lo_c, lo = 2, 0
                if hi > NT:
                    g.memset("pool", xi[:, :, TB + 2:TB + 4], 0.0, [b_xin[j]])
                    hi_c, hi = TB + 2, NT
                g.dma("sp", xi[:, :, lo_c:hi_c], xbcT[:, :, lo:hi].rearrange("c p t -> p c t"), b_xin[j],
                      writes=[b_xin[j]])
                if t0 == NT // 2:
                    g.ts("dve", xi[:, :, 0:2], xi[:, :, 0:2], fl[:, 0:1], None, ALU.mult, None, [b_xin[j], b_c], [b_xin[j]])
                if t0 + TB == NT // 2:
                    g.ts("dve", xi[:, :, TB + 2:TB + 4], xi[:, :, TB + 2:TB + 4], fl[:, 0:1], None, ALU.mult, None,
                         [b_xin[j], b_c], [b_xin[j]])
                dq = dtt[j]
                g.dma("sp", dq[:, 0, :, :], dt_tm[t0:t0 + TB, :].rearrange("(i p) c -> p i c", p=128), b_dtt[j],
                      writes=[b_dtt[j]])
                g.tt("dve", dq[:, 0, :, :], dq[:, 0, :, :], dbias[:, None, :].to_broadcast([128, NTI, 32]), ALU.add,
                     [b_dtt[j], b_c], [b_dtt[j]])
                g.act(dq[:, 1, :, :], dq[:, 0, :, :], AF.Exp, [b_dtt[j]], [b_dtt[j]])
                g.ts("dve", dq[:, 1, :, :], dq[:, 1, :, :], 1.0, None, ALU.add, None, [b_dtt[j]], [b_dtt[j]])
                g.act(dq[:, 1, :, :], dq[:, 1, :, :], AF.Ln, [b_dtt[j]], [b_dtt[j]])
                g.tt("dve", dq[:, 2, :, :], dq[:, 1, :, :], aB[:, None, :].to_broadcast([128, NTI, 32]), ALU.mult,
                     [b_dtt[j], b_c], [b_dtt[j]])
                g.dma("pool", dtsp[t0:t0 + TB, :].rearrange("(i p) c -> p i c", p=128), dq[:, 1, :, :], b_dtt[j],
                      reads=[b_dtt[j]])
                g.act(dq[:, 3, :, :], dq[:, 1, :, :], AF.Ln, [b_dtt[j]], [b_dtt[j]])
                g.dma("pool", dta_d[t0:t0 + TB, :].rearrange("(i p) c -> p i c", p=128), dq[:, 2, :, :], b_dtt[j],
                      reads=[b_dtt[j]])
                g.dma("pool", dlog_d[t0:t0 + TB, :].rearrange("(i p) c -> p i c", p=128), dq[:, 3, :, :], b_dtt[j],
                      reads=[b_dtt[j]])
                for c in range(12):
                    a = na % 3
                    na += 1
                    g.act(acc[a][:], xi[:, c, 0:TB], AF.Identity, [b_xin[j], b_c], [b_acc[a]],
                          bias=cb[:, c:c + 1], scale=cw[:, c, 0:1])
                    for k in range(1, 5):
                        g.stt("dve", acc[a][:], xi[:, c, k:k + TB], cw[:, c, k:k + 1], acc[a][:],
                              ALU.mult, ALU.add, [b_xin[j], b_c, b_acc[a]], [b_acc[a]])
                    if c < 10:
                        g.act(sil[:, c, :], acc[a][:], AF.Silu, [b_acc[a]], [b_sil])
                    if c >= 8:
                        q = nsb % 2
                        nsb += 1
                        g.act(silb[q][:], acc[a][:], AF.Silu, [b_acc[a]], [b_silb[q]])
                        g.dma("pool", BCT[c - 8, :, t0:t0 + TB], silb[q][:], b_silb[q], reads=[b_silb[q]])
                for ti in range(NTI):
                    i = nt_ % 2
                    nt_ += 1
                    tk = slice(t0 + ti * 128, t0 + (ti + 1) * 128)
                    for c in range(8):
                        g.tr(pt[i][:, c * 128:(c + 1) * 128], sil[:, c, ti * 128:(ti + 1) * 128], identf[:],
                             [b_sil, b_c], [b_pt[i]])
                    for c in range(2):
                        g.tr(pbt[i][:, c * 128:(c + 1) * 128], sil[:, 8 + c, ti * 128:(ti + 1) * 128], identf[:],
                             [b_sil, b_c], [b_pbt[i]])
                    g.cp("act", xst[i][:], pt[i][:], [b_pt[i]], [b_xst[i]])
                    g.cp("dve", bst[i][:], pbt[i][:, 0:256], [b_pbt[i]], [b_bst[i]])
                    g.dma("pool", xs_tm[tk, :], xst[i][:], b_xst[i], reads=[b_xst[i]])
                    g.dma("pool", B_tm[tk, :], bst[i][:], b_bst[i], reads=[b_bst[i]])
            g.flush(ENGS)

    def phase_ssd_pass(l, d):
        with ExitStack() as es:
            tri = sb(es, "tri", [128, 128], F32)
            ones = sb(es, "ones", [128, 128], F32)
            identf = sb(es, "identf", [128, 128], F32)
            identb = sb(es, "identb", [128, 128], BF16)
            maskB = sb(es, "maskB", [128, 16, 128], F32)
            fl = sb(es, "fl", [128, 2], F32)
            b_c = Buf("consts")
            xs = [sb(es, "xs%d" % i, [128, 1024], F32) for i in range(3)]
            xsb = [sb(es, "xsb%d" % i, [128, 1024], BF16) for i in range(2)]
            b_xsb = [Buf("xsb0"), Buf("xsb1")]
            Bt = [sb(es, "Bt%d" % i, [128, 256], BF16) for i in range(3)]
            bct = [sb(es, "bct%d" % i, [128, 4, 128], BF16) for i in range(3)]
            dts = [sb(es, "dts%d" % i, [128, 3, 16], F32) for i in range(3)]
            b_in = [Buf("in0"), Buf("in1"), Buf("in2")]
            R_ = [sb(es, "R%d" % i, [128, 16, 128], F32) for i in range(2)]
            b_R_ = [Buf("R0"), Buf("R1")]
            cs_sb_ = [sb(es, "cs_sb%d" % i, [128, 16], F32) for i in range(2)]
            ecs_ = [sb(es, "ecs%d" % i, [128, 16], F32) for i in range(2)]
            b_cs_ = [Buf("cs_sb0"), Buf("cs_sb1")]
            b_ecs_ = [Buf("ecs0"), Buf("ecs1")]
            seg_ = [sb(es, "seg%d" % i, [128, 16, 128], F32) for i in range(2)]
            b_seg_ = [Buf("seg0"), Buf("seg1")]
            cbT_ = [sb(es, "cbT%d" % i, [128, 2, 128], F32) for i in range(2)]
            b_cbT_ = [Buf("cbT0"), Buf("cbT1")]
            MT_ = [sb(es, "MT%d" % i, [128, 16, 128], BF16) for i in range(2)]
            b_MT_ = [Buf("MT0"), Buf("MT1")]
            xr = sb(es, "xr", [128, 1024], BF16)
            b_xr = Buf("xr")
            xd_ = [sb(es, "xd%d" % i, [128, 1024], BF16) for i in range(2)]
            b_xd_ = [Buf("xd0"), Buf("xd1")]
            S = sb(es, "S", [128, 1024], F32)
            Sbf = sb(es, "Sbf", [128, 1024], BF16)
            b_S, b_Sbf = Buf("S"), Buf("Sbf")
            decB_ = [sb(es, "decB%d" % i, [128, 16], F32) for i in range(2)]
            b_decB_ = [Buf("decB0"), Buf("decB1")]
            t1_ = [sb(es, "t1_%d" % i, [128, 1024], F32) for i in range(2)]
            b_t1_ = [Buf("t1_0"), Buf("t1_1")]
            yd = [sb(es, "yd%d" % i, [128, 1024], F32) for i in range(2)]
            b_yd = [Buf("yd0"), Buf("yd1")]
            pA = ps(es, "pA", [128, 1024], F32)
            pM = ps(es, "pM", [128, 512], F32)
            pY = ps(es, "pY", [128, 1024], F32)
            pO = ps(es, "pO", [128, 1024], F32)
            b_pA, b_pM, b_pY, b_pO = Buf("pA"), Buf("pM"), Buf("pY"), Buf("pO")
            if d == 1:
                dB = sb(es, "dB", [128, 16], F32)
                nwB = sb(es, "nwB", [128, 1024], F32)
                yfl = [sb(es, "yfl%d" % i, [128, 1024], F32) for i in range(3)]
                zt = [sb(es, "zt%d" % i, [128, 1024], F32) for i in range(3)]
                b_in2 = [Buf("in2_0"), Buf("in2_1"), Buf("in2_2")]
                sq = sb(es, "sq", [128, 1024], F32)
                b_sq = Buf("sq")
                t2 = sb(es, "t2", [128, 1024], F32)
                b_t2 = Buf("t2")
                gs = sb(es, "gs", [128, 4], F32)
                b_gs = Buf("gs")
                ynb = sb(es, "ynb", [128, 1024], BF16)
                b_ynb = Buf("ynb")
                yTs = [sb(es, "yTs%d" % i, [128, 8, 128], BF16) for i in range(2)]
                b_yTs = [Buf("yTs0"), Buf("yTs1")]
                pT = ps(es, "pT", [128, 1024], BF16)
                b_pT = Buf("pT")
                g.dma("sp", dB[:], ssm_d[l:l + 1, :].partition_broadcast(128), b_c, writes=[b_c])
                g.dma("sp", nwB[:], ssm_norm_w[l:l + 1, :].partition_broadcast(128), b_c, writes=[b_c])
                g.dma("pool", identb[:], ident_in, b_c, writes=[b_c])
            g.dma("sp", tri[:], tri_in[d], b_c, writes=[b_c])
            g.dma("sp", ones[:], ones_in, b_c, writes=[b_c])
            g.dma("sp", identf[:], ident_in, b_c, writes=[b_c])
            g.dma("sp", fl[:], flag, b_c, writes=[b_c])
            for h in range(16):
                g.dma("sp", maskB[:, h, :], maskadd_in[d], b_c, writes=[b_c])
            g.memset("dve", S[:], 0.0, [b_S])
            g.memset("pool", Sbf[:], 0.0, [b_Sbf])
            last = 127 if d == 0 else 0
            order = list(range(NCH)) if d == 0 else list(range(NCH - 1, -1, -1))
            pend = []

            def drain(k):
                for _ in range(k):
                    if pend:
                        pend.pop(0)()

            for n, c in enumerate(order):
                i = n % 3
                j2 = n % 2
                R, b_R, cs_sb, b_cs, ecs, b_ecs = R_[j2], b_R_[j2], cs_sb_[j2], b_cs_[j2], ecs_[j2], b_ecs_[j2]
                seg, b_seg, cbT, b_cbT, MT, b_MT = seg_[j2], b_seg_[j2], cbT_[j2], b_cbT_[j2], MT_[j2], b_MT_[j2]
                xd, b_xd, decB, b_decB, t1, b_t1 = xd_[j2], b_xd_[j2], decB_[j2], b_decB_[j2], t1_[j2], b_t1_[j2]
                tk = slice(c * 128, (c + 1) * 128)
                g.dma("sp", xs[i][:], xs_tm[tk, :], b_in[i], writes=[b_in[i]])
                g.dma("sp", Bt[i][:], B_tm[tk, :], b_in[i], writes=[b_in[i]])
                g.dma("sp", bct[i][:], BCT[:, :, tk].rearrange("c p t -> p c t"), b_in[i], writes=[b_in[i]])
                g.dma("sp", dts[i][:, 0, :], dtsp[tk, d * 16:(d + 1) * 16], b_in[i], writes=[b_in[i]])
                g.dma("sp", dts[i][:, 1, :], dta_d[tk, d * 16:(d + 1) * 16], b_in[i], writes=[b_in[i]])
                g.dma("sp", dts[i][:, 2, :], dlog_d[tk, d * 16:(d + 1) * 16], b_in[i], writes=[b_in[i]])
                if d == 1:
                    g.dma("sp", yfl[i][:], yf_d[tk, :], b_in2[i], writes=[b_in2[i]])
                    g.dma("sp", zt[i][:], z_tm[tk, :], b_in2[i], writes=[b_in2[i]])
                dta = dts[i][:, 1, :]
                dsp = dts[i][:, 0, :]
                if (d == 0 and c == MIDC) or (d == 1 and c == MIDC - 1):
                    g.ts("dve", S[:], S[:], fl[:, 0:1], None, ALU.mult, None, [b_S, b_c], [b_S])
                    g.ts("pool", Sbf[:], Sbf[:], fl[:, 0:1], None, ALU.mult, None, [b_Sbf, b_c], [b_Sbf])
                g.mm(pM[:, 0:16], tri[:], dta, True, True, [b_c, b_in[i]], [b_pM])
                for gi in range(2):
                    g.mm(pM[:, 128 + gi * 128:256 + gi * 128], bct[i][:, gi, :], bct[i][:, 2 + gi, :], True, True,
                         [b_in[i]], [b_pM])
                g.tt("dve", cs_sb[:], pM[:, 0:16], dts[i][:, 2, :], ALU.subtract, [b_pM, b_in[i]], [b_cs])
                g.cp("act", cbT[:], pM[:, 128:384].rearrange("p (g t) -> p g t", g=2), [b_pM], [b_cbT])
                drain(2)
                g.act(ecs[:], pM[:, 0:16], AF.Exp, [b_pM], [b_ecs])
                g.cp("act", xsb[j2][:], xs[i][:], [b_in[i]], [b_xsb[j2]])
                g.tt("dve", R[:], tri[:, None, :].to_broadcast([128, 16, 128]), dta[:, :, None].to_broadcast([128, 16, 128]),
                     ALU.mult, [b_c, b_in[i]], [b_R])
                drain(2)
                for gi in range(2):
                    g.mm(pO[:, gi * 512:(gi + 1) * 512], bct[i][:, 2 + gi, :], Sbf[:, gi * 512:(gi + 1) * 512], True, True,
                         [b_in[i], b_Sbf], [b_pO])
                for half in range(2):
                    hs = slice(half * 8, half * 8 + 8)
                    for q in range(2):
                        h0 = half * 8 + q * 4
                        g.mm(pA[:, q * 512:(q + 1) * 512], identf[:], maskB[:, h0:h0 + 4, :].rearrange("p h t -> p (h t)"),
                             True, False, [b_c], [b_pA])
                        g.mm(pA[:, q * 512:(q + 1) * 512], ones[:], R[:, h0:h0 + 4, :].rearrange("p h t -> p (h t)"),
                             False, True, [b_c, b_R], [b_pA])
                    g.tt("dve", seg[:, hs, :], pA[:].rearrange("p (h t) -> p h t", h=8),
                         cs_sb[:, hs][:, :, None].to_broadcast([128, 8, 128]), ALU.subtract, [b_pA, b_cs], [b_seg])
                    if half == 1:
                        pass
                    drain(2)
                    g.act(seg[:, hs, :], seg[:, hs, :], AF.Exp, [b_seg], [b_seg])
                    g.tt("pool" if half == 0 else "dve", MT[:, hs, :], seg[:, hs, :],
                         cbT[:, half:half + 1, :].to_broadcast([128, 8, 128]), ALU.mult, [b_seg, b_cbT], [b_MT])
                drain(2)
                for h in range(16):
                    g.mm(pY[:, h * 64:(h + 1) * 64], MT[:, h, :], xsb[j2][:, h * 64:(h + 1) * 64], True, True,
                         [b_MT, b_xsb[j2]], [b_pY])
                drain(2)
                g.tt("dve", t1[:].rearrange("p (h q) -> p h q", h=16), pO[:].rearrange("p (h q) -> p h q", h=16),
                     ecs[:, :, None].to_broadcast([128, 16, 64]), ALU.mult, [b_pO, b_ecs], [b_t1])
                g.tt("dve", yd[j2][:], pY[:], t1[:], ALU.add, [b_pY, b_t1], [b_yd[j2]])
                g.tt("pool", xd[:].rearrange("p (h q) -> p h q", h=16), xsb[j2][:].rearrange("p (h q) -> p h q", h=16),
                     seg[:, :, last:last + 1].to_broadcast([128, 16, 64]), ALU.mult, [b_xsb[j2], b_seg], [b_xd])
                drain(2)
                g.mm(pM[:, 16:32], ones[:], dta, True, True, [b_c, b_in[i]], [b_pM])
                g.act(decB[:], pM[:, 16:32], AF.Exp, [b_pM], [b_decB])
                for gi in range(2):
                    g.mm(pA[:, gi * 512:(gi + 1) * 512], Bt[i][:, gi * 128:(gi + 1) * 128], xd[:, gi * 512:(gi + 1) * 512],
                         True, True, [b_in[i], b_xd], [b_pA])
                g.tt("pool", S[:].rearrange("p (h q) -> p h q", h=16), S[:].rearrange("p (h q) -> p h q", h=16),
                     decB[:, :, None].to_broadcast([128, 16, 64]), ALU.mult, [b_S, b_decB], [b_S])
                g.tt("dve", S[:], S[:], pA[:], ALU.add, [b_S, b_pA], [b_S])
                g.cp("act", Sbf[:], S[:], [b_S], [b_Sbf])
                if d == 0:
                    g.dma("pool", yf_d[tk, :], yd[j2][:], b_yd[j2], reads=[b_yd[j2]])
                else:
                    drain(len(pend))
                    y = yd[j2]
                    b_y = b_yd[j2]
                    yfl_i, zt_i, xs_i, b2_i, bi_i, yTs_j, b_yTs_j = yfl[i], zt[i], xs[i], b_in2[i], b_in[i], yTs[j2], b_yTs[j2]

                    def mk(y=y, b_y=b_y, yfl_i=yfl_i, zt_i=zt_i, xs_i=xs_i, b2_i=b2_i, bi_i=bi_i, yTs_j=yTs_j,
                           b_yTs_j=b_yTs_j, tk=tk):
                        ops = []
                        ops.append(lambda: g.tt("pool", y[:], y[:], yfl_i[:], ALU.add, [b_y, b2_i], [b_y]))
                        ops.append(lambda: g.tt("pool", t2[:].rearrange("p (h q) -> p h q", h=16),
                                                xs_i[:].rearrange("p (h q) -> p h q", h=16),
                                                dB[:, :, None].to_broadcast([128, 16, 64]), ALU.mult, [bi_i, b_c], [b_t2]))
                        ops.append(lambda: g.act(zt_i[:], zt_i[:], AF.Silu, [b2_i], [b2_i]))
                        ops.append(lambda: g.tt("dve", y[:], y[:], t2[:], ALU.add, [b_y, b_t2], [b_y]))
                        ops.append(lambda: g.tt("dve", y[:], y[:], zt_i[:], ALU.mult, [b_y, b2_i], [b_y]))
                        ops.append(lambda: g.act(sq[:], y[:], AF.Square, [b_y], [b_sq]))
                        ops.append(lambda: g.red("dve", gs[:, 0:2], sq[:].rearrange("p (g q) -> p g q", g=2), [b_sq], [b_gs]))
                        ops.append(lambda: g.ts("dve", gs[:, 2:4], gs[:, 0:2], 1.0 / 512, EPS, ALU.mult, ALU.add, [b_gs], [b_gs]))
                        ops.append(lambda: g.act(gs[:, 2:4], gs[:, 2:4], AF.Sqrt, [b_gs], [b_gs]))
                        ops.append(lambda: g._record("dve", lambda e: e.reciprocal(out=gs[:, 2:4], in_=gs[:, 2:4]),
                                                     [b_gs], [b_gs]))
                        ops.append(lambda: g.tt("dve", y[:].rearrange("p (g q) -> p g q", g=2),
                                                y[:].rearrange("p (g q) -> p g q", g=2),
                                                gs[:, 2:4][:, :, None].to_broadcast([128, 2, 512]), ALU.mult, [b_y, b_gs], [b_y]))
                        ops.append(lambda: g.tt("pool", ynb[:], y[:], nwB[:], ALU.mult, [b_y, b_c], [b_ynb]))

                        def trs():
                            for k in range(8):
                                g.tr(pT[:, k * 128:(k + 1) * 128], ynb[:, k * 128:(k + 1) * 128], identb[:], [b_ynb, b_c], [b_pT])
                        ops.append(trs)
                        ops.append(lambda: g.cp("act", yTs_j[:], pT[:].rearrange("p (k t) -> p k t", k=8), [b_pT], [b_yTs_j]))
                        ops.append(lambda: g.dma("pool", yT[0][:, :, tk].rearrange("k p t -> p k t"), yTs_j[:], b_yTs_j,
                                                 reads=[b_yTs_j]))
                        return ops
                    pend.extend(mk())
            drain(len(pend))
            g.flush(ENGS)

    gla_gate_up = din("gla_gate_up", [DEPTH, 2, 16, 512])
    gla_gate_b = din("gla_gate_b", [DEPTH, 2, 512])
    gla_norm_w = din("gla_norm_w", [DEPTH, 256])
    of_d = dscr("of", [NT, 1024])

    def phase_gla_pass(l, d):
        with ExitStack() as es:
            triS = sb(es, "triS", [128, 128], F32)
            uS = sb(es, "uS", [128, 128], F32)
            m01 = sb(es, "m01", [128, 128], F32)
            ones = sb(es, "ones", [128, 128], F32)
            up = sb(es, "up", [16, 512], F32)
            gbr = sb(es, "gbr", [1, 512], F32)
            fl = sb(es, "fl", [128, 2], F32)
            b_c = Buf("consts")
            qT = [sb(es, "qT%d" % i, [128, 4, 128], F32) for i in range(2)]
            kT = [sb(es, "kT%d" % i, [128, 4, 128], F32) for i in range(2)]
            ktm = [sb(es, "ktm%d" % i, [128, 512], F32) for i in range(2)]
            v = [sb(es, "v%d" % i, [128, 1024], BF16) for i in range(2)]
            dn = [sb(es, "dn%d" % i, [16, 128], F32) for i in range(2)]
            b_in = [Buf("in0"), Buf("in1")]
            lsp_ = [sb(es, "lsp%d" % i, [128, 512], F32) for i in range(2)]
            b_lsp_ = [Buf("lsp0"), Buf("lsp1")]
            eb_ = [sb(es, "eb%d" % i, [128, 4, 128], F32) for i in range(2)]
            enb_ = [sb(es, "enb%d" % i, [128, 4, 128], F32) for i in range(2)]
            er_ = [sb(es, "er%d" % i, [128, 512], F32) for i in range(2)]
            b_eb_, b_enb_, b_er_ = [Buf("eb0"), Buf("eb1")], [Buf("enb0"), Buf("enb1")], [Buf("er0"), Buf("er1")]
            qe_ = [sb(es, "qe%d" % i, [128, 4, 128], BF16) for i in range(2)]
            ke_ = [sb(es, "ke%d" % i, [128, 4, 128], BF16) for i in range(2)]
            kdec_ = [sb(es, "kdec%d" % i, [128, 512], BF16) for i in range(2)]
            att_ = [sb(es, "att%d" % i, [128, 4, 128], BF16) for i in range(2)]
            b_qe_, b_ke_ = [Buf("qe0"), Buf("qe1")], [Buf("ke0"), Buf("ke1")]
            b_kdec_, b_att_ = [Buf("kdec0"), Buf("kdec1")], [Buf("att0"), Buf("att1")]
            S = sb(es, "S", [128, 4, 256], F32)
            Sbf = sb(es, "Sbf", [128, 4, 256], BF16)
            b_S, b_Sbf = Buf("S"), Buf("Sbf")
            od = [sb(es, "od%d" % i, [128, 1024], F32) for i in range(2)]
            b_od = [Buf("od0"), Buf("od1")]
            pLR = ps(es, "pLR", [128, 512], F32)
            pB = ps(es, "pB", [128, 512], F32)
            pAt = ps(es, "pAt", [128, 512], F32)
            pOo = ps(es, "pOo", [128, 1024], F32)
            pSp = ps(es, "pSp", [128, 1024], F32)
            b_pLR, b_pB, b_pAt, b_pOo, b_pSp = Buf("pLR"), Buf("pB"), Buf("pAt"), Buf("pOo"), Buf("pSp")
            if d == 1:
                identb = sb(es, "identb", [128, 128], BF16)
                nwB = sb(es, "nwB", [128, 1024], F32)
                ofl = [sb(es, "ofl%d" % i, [128, 1024], F32) for i in range(2)]
                gg = [sb(es, "gg%d" % i, [128, 1024], F32) for i in range(2)]
                b_in2 = [Buf("in2_0"), Buf("in2_1")]
                sq = sb(es, "sq", [128, 1024], F32)
                b_sq = Buf("sq")
                gs = sb(es, "gs", [128, 8], F32)
                b_gs = Buf("gs")
                onb = sb(es, "onb", [128, 1024], BF16)
                b_onb = Buf("onb")
                yTs = [sb(es, "yTs%d" % i, [128, 8, 128], BF16) for i in range(2)]
                b_yTs = [Buf("yTs0"), Buf("yTs1")]
                pT = ps(es, "pT", [128, 1024], BF16)
                b_pT = Buf("pT")
                g.dma("pool", identb[:], ident_in, b_c, writes=[b_c])
                for h in range(4):
                    g.dma("sp", nwB[:, h * 256:(h + 1) * 256], gla_norm_w[l:l + 1, :].partition_broadcast(128), b_c,
                          writes=[b_c])
            g.dma("sp", triS[:], tri_in[d], b_c, writes=[b_c])
            g.dma("sp", uS[:], ustrict_in[d], b_c, writes=[b_c])
            g.dma("sp", m01[:], mask01_in[d], b_c, writes=[b_c])
            g.dma("sp", ones[:], ones_in, b_c, writes=[b_c])
            g.dma("sp", up[:], gla_gate_up[l, d], b_c, writes=[b_c])
            g.dma("sp", gbr[:], gla_gate_b[l, d:d + 1, :], b_c, writes=[b_c])
            g.dma("sp", fl[:], flag, b_c, writes=[b_c])
            g.ts("dve", triS[:], triS[:], -1.0 / 16.0, None, ALU.mult, None, [b_c], [b_c])
            g.ts("dve", uS[:], uS[:], -1.0 / 16.0, None, ALU.mult, None, [b_c], [b_c])
            g.memset("dve", S[:], 0.0, [b_S])
            g.memset("pool", Sbf[:], 0.0, [b_Sbf])
            last = 127 if d == 0 else 0
            order = list(range(NCH)) if d == 0 else list(range(NCH - 1, -1, -1))
            for n, c in enumerate(order):
                i = n % 2
                lsp, b_lsp, eb, b_eb, enb, b_enb, er, b_er = lsp_[i], b_lsp_[i], eb_[i], b_eb_[i], enb_[i], b_enb_[i], er_[i], b_er_[i]
                qe, b_qe, ke, b_ke, kdec, b_kdec, att, b_att = qe_[i], b_qe_[i], ke_[i], b_ke_[i], kdec_[i], b_kdec_[i], att_[i], b_att_[i]
                tk = slice(c * 128, (c + 1) * 128)
                g.dma("sp", qT[i][:], gqT[:, :, tk].rearrange("h p t -> p h t"), b_in[i], writes=[b_in[i]])
                g.dma("sp", kT[i][:], gkT[:, :, tk].rearrange("h p t -> p h t"), b_in[i], writes=[b_in[i]])
                g.dma("sp", ktm[i][:], kg_tm[tk, :], b_in[i], writes=[b_in[i]])
                g.dma("sp", v[i][:], vg_tm[tk, :], b_in[i], writes=[b_in[i]])
                g.dma("sp", dn[i][:], dnT[d, :, tk], b_in[i], writes=[b_in[i]])
                if d == 1:
                    g.dma("sp", ofl[i][:], of_d[tk, :], b_in2[i], writes=[b_in2[i]])
                    g.dma("sp", gg[i][:], gg_tm[tk, :], b_in2[i], writes=[b_in2[i]])
                if (d == 0 and c == MIDC) or (d == 1 and c == MIDC - 1):
                    g.ts("dve", S[:], S[:], fl[:, 0:1], None, ALU.mult, None, [b_S, b_c], [b_S])
                    g.ts("pool", Sbf[:], Sbf[:], fl[:, 0:1], None, ALU.mult, None, [b_Sbf, b_c], [b_Sbf])
                g.mm(pLR[:], dn[i][:], up[:], True, False, [b_in[i], b_c], [b_pLR])
                g.mm(pLR[:], ones[0:1, :], gbr[:], False, True, [b_c], [b_pLR])
                g.act(lsp[:], pLR[:], AF.Exp, [b_pLR], [b_lsp], scale=-1.0)
                g.ts("dve", lsp[:], lsp[:], 1.0, None, ALU.add, None, [b_lsp], [b_lsp])
                g.act(lsp[:], lsp[:], AF.Ln, [b_lsp], [b_lsp])
                for h in range(4):
                    g.mm(pB[:, h * 128:(h + 1) * 128], lsp[:, h * 128:(h + 1) * 128], triS[:], True, True,
                         [b_lsp, b_c], [b_pB])
                g.mm(pLR[:], uS[:], lsp[:], True, True, [b_c, b_lsp], [b_pLR])
                pB3 = pB[:].rearrange("p (h t) -> p h t", h=4)
                g.act(eb[:], pB3, AF.Exp, [b_pB], [b_eb])
                g.act(enb[:], pB3, AF.Exp, [b_pB], [b_enb], scale=-1.0)
                g.act(er[:], pLR[:], AF.Exp, [b_pLR], [b_er])
                g.tt("dve", qe[:], qT[i][:], eb[:], ALU.mult, [b_in[i], b_eb], [b_qe])
                g.tt("pool", ke[:], kT[i][:], enb[:], ALU.mult, [b_in[i], b_enb], [b_ke])
                g.tt("pool", kdec[:], ktm[i][:], er[:], ALU.mult, [b_in[i], b_er], [b_kdec])
                for h in range(4):
                    g.mm(pAt[:, h * 128:(h + 1) * 128], ke[:, h, :], qe[:, h, :], True, True, [b_ke, b_qe], [b_pAt])
                g.tt("dve", att[:], pAt[:].rearrange("p (h t) -> p h t", h=4), m01[:, None, :].to_broadcast([128, 4, 128]),
                     ALU.mult, [b_pAt, b_c], [b_att])
                for h in range(4):
                    g.mm(pOo[:, h * 256:(h + 1) * 256], att[:, h, :], v[i][:, h * 256:(h + 1) * 256], True, False,
                         [b_att, b_in[i]], [b_pOo])
                    g.mm(pOo[:, h * 256:(h + 1) * 256], qe[:, h, :], Sbf[:, h, :], False, True, [b_qe, b_Sbf], [b_pOo])
                for h in range(4):
                    g.mm(pSp[:, h * 256:(h + 1) * 256], kdec[:, h * 128:(h + 1) * 128], v[i][:, h * 256:(h + 1) * 256],
                         True, True, [b_kdec, b_in[i]], [b_pSp])
                for h in range(4):
                    g.stt("dve", S[:, h, :], S[:, h, :], eb[:, h, last:last + 1], pSp[:, h * 256:(h + 1) * 256],
                          ALU.mult, ALU.add, [b_S, b_eb, b_pSp], [b_S])
                g.cp("act", Sbf[:], S[:], [b_S], [b_Sbf])
                if d == 0:
                    g.cp("act", od[i][:], pOo[:], [b_pOo], [b_od[i]])
                    g.dma("pool", of_d[tk, :], od[i][:], b_od[i], reads=[b_od[i]])
                else:
                    o = od[i]
                    g.tt("dve", o[:], pOo[:], ofl[i][:], ALU.add, [b_pOo, b_in2[i]], [b_od[i]])
                    g.act(sq[:], o[:], AF.Square, [b_od[i]], [b_sq])
                    g.red("dve", gs[:, 0:4], sq[:].rearrange("p (h q) -> p h q", h=4), [b_sq], [b_gs])
                    g.ts("dve", gs[:, 4:8], gs[:, 0:4], 1.0 / 256, EPS, ALU.mult, ALU.add, [b_gs], [b_gs])
                    g.act(gs[:, 4:8], gs[:, 4:8], AF.Sqrt, [b_gs], [b_gs])
                    g._record("dve", lambda e, gs=gs: e.reciprocal(out=gs[:, 4:8], in_=gs[:, 4:8]), [b_gs], [b_gs])
                    g.tt("dve", o[:].rearrange("p (h q) -> p h q", h=4), o[:].rearrange("p (h q) -> p h q", h=4),
                         gs[:, 4:8][:, :, None].to_broadcast([128, 4, 256]), ALU.mult, [b_od[i], b_gs], [b_od[i]])
                    g.tt("pool", o[:], o[:], nwB[:], ALU.mult, [b_od[i], b_c], [b_od[i]])
                    g.act(gg[i][:], gg[i][:], AF.Silu, [b_in2[i]], [b_in2[i]])
                    g.tt("pool", onb[:], o[:], gg[i][:], ALU.mult, [b_od[i], b_in2[i]], [b_onb])
                    for k in range(8):
                        g.tr(pT[:, k * 128:(k + 1) * 128], onb[:, k * 128:(k + 1) * 128], identb[:], [b_onb, b_c], [b_pT])
                    g.cp("act", yTs[i][:], pT[:].rearrange("p (k t) -> p k t", k=8), [b_pT], [b_yTs[i]])
                    g.dma("pool", yT[1][:, :, tk].rearrange("k p t -> p k t"), yTs[i][:], b_yTs[i], reads=[b_yTs[i]])
            g.flush(ENGS)

    na_rpb = din("na_rpb", [DEPTH, 16, 15, 31])
    jflip_in = din("jflip", [64, 64])
    mint_in = din("m_int", [128, 16, 64])
    medge_in = din("m_edge", [128, 16, 64])
    rpbpad = dscr("rpbpad", [16, 17, 192])

    def phase_na(l):
        with ExitStack() as es:
            T2 = [sb(es, "T2i", [128, 16, 16, 64], BF16), sb(es, "T2e", [128, 16, 16, 64], BF16)]
            b_T2 = Buf("T2")
            Mx = [sb(es, "Mi", [128, 16, 64], F32), sb(es, "Me", [128, 16, 64], F32)]
            jf = sb(es, "jf", [64, 64], F32)
            identb = sb(es, "identb", [128, 128], BF16)
            fl = sb(es, "fl", [128, 2], F32)
            b_c = Buf("consts")
            padt = sb(es, "padt", [16, 17, 192], F32)
            rp = sb(es, "rp", [16, 15, 31], F32)
            b_pad = Buf("pad")
            Tpp = [sb(es, "Tpp%d" % i, [64, 4, 17, 64], F32) for i in range(2)]
            b_Tpp = [Buf("Tpp0"), Buf("Tpp1")]
            eW = [sb(es, "eW%d" % i, [128, 640], F32) for i in range(3)]
            b_eW = [Buf("eW%d" % i) for i in range(3)]
            Kc = [sb(es, "Kc%d" % i, [128, 8, 128], BF16) for i in range(8)]
            Vc = [sb(es, "Vc%d" % i, [128, 16, 65], BF16) for i in range(8)]
            b_K = [Buf("K%d" % i) for i in range(8)]
            b_V = [Buf("V%d" % i) for i in range(8)]
            Qc = [sb(es, "Qc%d" % i, [128, 8, 128], BF16) for i in range(2)]
            b_Q = [Buf("Q0"), Buf("Q1")]
            Pt = [sb(es, "Pt%d" % i, [128, 5, 2, 64], BF16) for i in range(3)]
            b_P = [Buf("P%d" % i) for i in range(3)]
            rden = sb(es, "rden", [128, 16], F32)
            b_rden = Buf("rden")
            oA = sb(es, "oA", [128, 1024], F32)
            oB = sb(es, "oB", [128, 1024], F32)
            b_oA, b_oB = Buf("oA"), Buf("oB")
            onb = sb(es, "onb", [128, 1024], BF16)
            b_onb = Buf("onb")
            yTs = [sb(es, "yTs%d" % i, [128, 8, 128], BF16) for i in range(2)]
            b_yTs = [Buf("yTs0"), Buf("yTs1")]
            pS_t = [ps(es, "pS%d" % i, [128, 1024], F32) for i in range(2)]
            pS = [t[:, 0:640] for t in pS_t]
            b_pS = [Buf("pS%d" % i) for i in range(2)]
            pOv = [ps(es, "pOv%d" % i, [128, 512], F32) for i in range(3)]
            b_pOv = Buf("pOv")
            pT = ps(es, "pT", [128, 1024], BF16)
            b_pT = Buf("pT")
            g.dma("sp", Mx[0][:], mint_in, b_c, writes=[b_c])
            g.dma("sp", Mx[1][:], medge_in, b_c, writes=[b_c])
            g.dma("sp", jf[:], jflip_in, b_c, writes=[b_c])
            g.dma("sp", fl[:], flag, b_c, writes=[b_c])
            g.dma("pool", identb[:], ident_in, b_c, writes=[b_c])
            for i in range(8):
                g.memset("pool", Vc[i][:], 1.0, [b_V[i]])
            g.memset("dve", padt[:], 0.0, [b_pad])
            g.dma("sp", rp[:], na_rpb[l], b_pad, writes=[b_pad])
            g.cp("dve", padt[:, 1:16, 64:95], rp[:], [b_pad], [b_pad])
            g.dma("sp", rpbpad, padt[:], b_pad, reads=[b_pad], writes=[b_pad])
            nb = 0
            for hg in range(4):
                tp = Tpp[hg % 2]
                b_tp = b_Tpp[hg % 2]
                for hh in range(4):
                    h = hg * 4 + hh
                    src = bass.AP(tensor=rpbpad.tensor, offset=h * 17 * 192 + 16, ap=[[1, 64], [192, 17], [1, 64]])
                    g.dma("sp", tp[:, hh, :, :], src, b_tp, reads=[b_pad], writes=[b_tp])
                for hh in range(4):
                    h = hg * 4 + hh
                    for sbt in range(2):
                        q = nb % 2
                        nb += 1
                        for si in range(8):
                            s_ = sbt * 8 + si
                            g.mm(pS[q][:, si * 64:(si + 1) * 64], tp[:, hh, s_:s_ + 2, :].rearrange("p r k -> p (r k)"),
                                 jf[:], True, True, [b_tp, b_c], [b_pS[q]])
                        g.act(eW[q][:, 0:512], pS[q][:, 0:512], AF.Exp, [b_pS[q]], [b_eW[q]])
                        g.tt("dve", T2[0][:, h, sbt * 8:sbt * 8 + 8, :], eW[q][:, 0:512].rearrange("p (s q) -> p s q", s=8),
                             Mx[0][:, sbt * 8:sbt * 8 + 8, :], ALU.mult, [b_eW[q], b_c], [b_T2])
                        g.tt("pool", T2[1][:, h, sbt * 8:sbt * 8 + 8, :], eW[q][:, 0:512].rearrange("p (s q) -> p s q", s=8),
                             Mx[1][:, sbt * 8:sbt * 8 + 8, :], ALU.mult, [b_eW[q], b_c], [b_T2])
            if NA_SPLIT:
                g.flush(ENGS)
            loaded = [-1]
            cnt = {"s": 0, "p": 0, "y": 0}

            def ensure(upto):
                while loaded[0] < min(upto, NCH - 1):
                    kt = loaded[0] + 1
                    sl_ = kt % 8
                    tk_ = slice(kt * 128, (kt + 1) * 128)
                    g.dma("sp", Kc[sl_][:], nkT[:, :, tk_].rearrange("c p t -> p c t"), b_K[sl_], writes=[b_K[sl_]])
                    g.dma("sp", Vc[sl_][:, :, 0:64], vn_tm[tk_, :].rearrange("p (h d) -> p h d", h=16), b_V[sl_],
                          writes=[b_V[sl_]])
                    loaded[0] = kt

            def na_pair(qp, kts, var, qi, o_dst, b_o):
                nt = len(kts)
                d0 = kts[0] - qp

                def st_s(h):
                    ch, p0 = h // 2, (h % 2) * 64
                    q = h % 2
                    for ti, kt in enumerate(kts):
                        g.mm(pS[q][:, ti * 128:(ti + 1) * 128], Kc[kt % 8][p0:p0 + 64, ch, :], Qc[qi][p0:p0 + 64, ch, :],
                             True, True, [b_K[kt % 8], b_Q[qi]], [b_pS[q]])

                def st_e(h):
                    q = h % 3
                    g.act(eW[q][:, 0:nt * 128], pS[h % 2][:, 0:nt * 128], AF.Exp, [b_pS[h % 2]], [b_eW[q]])
                    e4 = eW[q][:, 0:nt * 128].rearrange("p (t r c) -> p t r c", t=nt, r=2)
                    for qr2 in range(2):
                        s0 = 2 * d0 + 8 - qr2
                        base = T2[var][:, h, s0, :]
                        tv = bass.AP(tensor=base.tensor, offset=base.offset, ap=[list(base.ap[0]), [128, nt], [1, 64]])
                        g.tt("dve" if qr2 == 0 else "pool", Pt[q][:, 0:nt, qr2, :], e4[:, :, qr2, :], tv, ALU.mult,
                             [b_eW[q], b_T2], [b_P[q]])

                def st_v(h):
                    q = h % 3
                    bank, off = h // 7, (h % 7) * 65
                    for ti, kt in enumerate(kts):
                        g.mm(pOv[bank][:, off:off + 65], Pt[q][:, ti, :, :].rearrange("p r c -> p (r c)"),
                             Vc[kt % 8][:, h, :], ti == 0, ti == nt - 1, [b_P[q], b_V[kt % 8]], [b_pOv])

                LAG = NA_LAG
                for step in range(16 + 2 * LAG):
                    if step < 16:
                        st_s(step)
                    if LAG <= step < 16 + LAG:
                        st_e(step - LAG)
                    if step >= 2 * LAG:
                        st_v(step - 2 * LAG)
                for bank in range(3):
                    nh = 7 if bank < 2 else 2
                    pv = pOv[bank][:, 0:nh * 65].rearrange("p (h e) -> p h e", h=nh)
                    g._record("dve", lambda e, bank=bank, nh=nh, pv=pv: e.reciprocal(out=rden[:, bank * 7:bank * 7 + nh],
                                                                                   in_=pv[:, :, 64]), [b_pOv], [b_rden])
                    g.tt("dve", o_dst[:, bank * 448:bank * 448 + nh * 64].rearrange("p (h d) -> p h d", h=nh), pv[:, :, 0:64],
                         rden[:, bank * 7:bank * 7 + nh][:, :, None].to_broadcast([128, nh, 64]), ALU.mult,
                         [b_pOv, b_rden], [b_o])

            for qp in range(NCH):
                qi = qp % 2
                tk = slice(qp * 128, (qp + 1) * 128)
                ensure(qp + 3)
                g.dma("sp", Qc[qi][:], nqT[:, :, tk].rearrange("c p t -> p c t"), b_Q[qi], writes=[b_Q[qi]])
                interior = [qp - 2, qp - 1, qp, qp + 1, qp + 2]
                if qp < 2:
                    na_pair(qp, [0, 1, 2, 3], 1, qi, onb, b_onb)
                elif qp >= NCH - 2:
                    na_pair(qp, [NCH - 4, NCH - 3, NCH - 2, NCH - 1], 1, qi, onb, b_onb)
                elif MIDC - 2 <= qp < MIDC + 2:
                    na_pair(qp, interior, 0, qi, oA, b_oA)
                    ekts = [MIDC - 4, MIDC - 3, MIDC - 2, MIDC - 1] if qp < MIDC else [MIDC, MIDC + 1, MIDC + 2, MIDC + 3]
                    na_pair(qp, ekts, 1, qi, oB, b_oB)
                    g.ts("dve", oA[:], oA[:], fl[:, 0:1], None, ALU.mult, None, [b_oA, b_c], [b_oA])
                    g.stt("dve", onb[:], oB[:], fl[:, 1:2], oA[:], ALU.mult, ALU.add, [b_oB, b_oA, b_c], [b_onb])
                else:
                    na_pair(qp, interior, 0, qi, onb, b_onb)
                yi = cnt["y"] % 2
                cnt["y"] += 1
                for k in range(8):
                    g.tr(pT[:, k * 128:(k + 1) * 128], onb[:, k * 128:(k + 1) * 128], identb[:], [b_onb, b_c], [b_pT])
                g.cp("act", yTs[yi][:], pT[:].rearrange("p (k t) -> p k t", k=8), [b_pT], [b_yTs[yi]])
                g.dma("pool", yT[2][:, :, tk].rearrange("k p t -> p k t"), yTs[yi][:], b_yTs[yi], reads=[b_yTs[yi]])
            g.flush(ENGS, sched=False)

    stages = []
    stages.append(("p0", phase_p0))
    for l in range(DEPTH):
        x_src = x_in if l == 0 else xc
        stages.append(("ffn1_%d" % l, lambda l=l, x_src=x_src: phase_ffn(
            "f1l%d" % l, ffn1_w_in[l], ffn1_w_out[l], x_src, hT_f1, xa, ln_mix_w[l:l + 1, :], hT_mix)))
        stages.append(("ptm_%d" % l, lambda l=l: phase_inproj_tm(l)))
        stages.append(("pfm_%d" % l, lambda l=l: phase_inproj_fm(l)))
        stages.append(("ssdprep_%d" % l, lambda l=l: phase_ssd_prep(l)))
        stages.append(("ssdf_%d" % l, lambda l=l: phase_ssd_pass(l, 0)))
        stages.append(("ssdb_%d" % l, lambda l=l: phase_ssd_pass(l, 1)))
        stages.append(("glaf_%d" % l, lambda l=l: phase_gla_pass(l, 0)))
        stages.append(("glab_%d" % l, lambda l=l: phase_gla_pass(l, 1)))
        stages.append(("na_%d" % l, lambda l=l: phase_na(l)))
        stages.append(("merge_%d" % l, lambda l=l: phase_merge(l, xa, xb, ln_ffn2_w[l:l + 1, :], hT_f2)))
        if l + 1 < DEPTH:
            stages.append(("ffn2_%d" % l, lambda l=l: phase_ffn(
                "f2l%d" % l, ffn2_w_in[l], ffn2_w_out[l], xb, hT_f2, xc, ln_ffn1_w[l + 1:l + 2, :], hT_f1)))
        else:
            stages.append(("ffn2_%d" % l, lambda l=l: phase_ffn(
                "f2l%d" % l, ffn2_w_in[l], ffn2_w_out[l], xb, hT_f2, None, ln_final_w[0:1, :], None, final_out=y_out)))
    only = getattr(cfg, "only", None)
    for name, fn in stages:
        if only is None or name in only:
            fn()
        if cfg.stop_after == name:
            break
    g.close()
    return nc, g


def _consts():
    k = np.arange(128)
    tri_f = (k[:, None] <= k[None, :]).astype(np.float32)
    tri_b = (k[:, None] >= k[None, :]).astype(np.float32)
    allow_f = (k[:, None] <= k[None, :])
    allow_b = (k[:, None] >= k[None, :])
    c = {
        "ident": np.eye(128, dtype=np.float32),
        "ones": np.ones((128, 128), np.float32),
        "tri": np.stack([tri_f, tri_b]),
        "maskadd": np.stack([np.where(allow_f, 0.0, -30000.0), np.where(allow_b, 0.0, -30000.0)]).astype(np.float32),
        "mask01": np.stack([allow_f, allow_b]).astype(np.float32),
        "ustrict": np.stack([(k[:, None] > k[None, :]), (k[:, None] < k[None, :])]).astype(np.float32),
    }
    kc = np.arange(64)[:, None]
    qc = np.arange(64)[None, :]
    cs = np.clip(qc - 8, 0, 48)
    colmask = ((kc >= cs) & (kc < cs + 16)).astype(np.float32)
    m_int = np.zeros((128, 16, 64), np.float32)
    m_edge = np.zeros((128, 16, 64), np.float32)
    for kr2 in range(2):
        for s_ in range(16):
            ro = s_ + kr2 - 1
            if 0 <= ro <= 14:
                m_edge[kr2 * 64:(kr2 + 1) * 64, s_, :] = colmask
            if 3 <= ro <= 10:
                m_int[kr2 * 64:(kr2 + 1) * 64, s_, :] = colmask
    jflip = np.zeros((64, 64), np.float32)
    jflip[63 - np.arange(64), np.arange(64)] = 1.0
    c.update({"jflip": jflip, "m_int": m_int, "m_edge": m_edge})
    return c


_PROGRAM_CACHE = {}


def run_streams(streams, flags, w, cfg_extra=None):
    nt = streams[0].shape[0]
    cfg = Cfg(nt)
    nc, g = build_program(cfg)
    base = dict(_consts())
    f32 = np.float32
    for k in ("ln_ffn1_w", "ffn1_w_in", "ffn1_w_out", "ln_mix_w", "w_in", "ssm_conv_w", "ssm_conv_b", "ssm_d",
              "ssm_norm_w", "gla_gate_up", "gla_gate_b", "gla_norm_w", "na_rpb", "w_branch_a", "w_branch_b",
              "w_branch_c", "w_out", "ln_ffn2_w", "ffn2_w_in", "ffn2_w_out"):
        base[k] = np.ascontiguousarray(np.asarray(w[k], dtype=f32))
    base["ssm_dt_bias"] = np.ascontiguousarray(np.asarray(w["ssm_dt_bias"], f32).reshape(DEPTH, 32))
    base["ssm_a_log"] = np.ascontiguousarray(np.asarray(w["ssm_a_log"], f32).reshape(DEPTH, 32))
    base["ln_final_w"] = np.ascontiguousarray(np.asarray(w["ln_final_w"], f32).reshape(1, D))
    in_maps = []
    for x, f in zip(streams, flags):
        m = dict(base)
        m["x"] = np.ascontiguousarray(np.asarray(x, f32))
        m["flag"] = np.tile(np.array([[f, 1.0 - f]], f32), (128, 1))
        in_maps.append(m)
    res = run_bass_kernel_spmd(nc, in_maps, core_ids=list(range(len(in_maps))))
    return [np.asarray(r["y"], dtype=f32) for r in res.results]


def kernel(**inputs):
    xp = np.asarray(inputs["x_prompt"], np.float32)
    xs = np.asarray(inputs["x_sample"], np.float32)
    B, S, _ = xp.shape
    B2, S2, _ = xs.shape
    assert S2 == 2 * S and B % 2 == 0
    streams, flags = [], []
    for i in range(B // 2):
        streams.append(xp[2 * i:2 * i + 2].reshape(2 * S, D))
        flags.append(0.0)
    for i in range(B2):
        streams.append(xs[i])
        flags.append(1.0)
    outs = run_streams(streams, flags, inputs)
    yp = np.stack([o.reshape(2, S, D) for o in outs[:B // 2]]).reshape(B, S, D)
    ys = np.stack(outs[B // 2:]).reshape(B2, S2, D)
    return (yp.astype(np.float32), ys.astype(np.float32))
```

```python
import numpy as np
from contextlib import ExitStack
import concourse.bass as bass
import concourse.mybir as mybir
from concourse.bass_utils import run_bass_kernel_spmd

F32 = mybir.dt.float32
BF16 = mybir.dt.bfloat16
AF = mybir.ActivationFunctionType
ALU = mybir.AluOpType
AX = mybir.AxisListType

D = 1024
DFF = 2816
DEPTH = 2
EPS = 1e-6
ENGS = ("pe", "act", "dve", "pool", "sp")
NA_LAG = 1
STRICT = True


class Buf:
    __slots__ = ("name", "w", "r", "sem", "excl")

    def __init__(self, name):
        self.name = name
        self.w = []
        self.r = []
        self.sem = None
        self.excl = name.startswith("p") and name != "pad"


class Op:
    __slots__ = ("eng", "seq", "fn", "rw", "war", "signal", "sigval", "dma", "waits", "cost",
                 "deps", "nun", "users", "ready", "done", "pos")

    def __init__(self, eng, seq, fn, rw, war, dma, cost):
        self.eng = eng
        self.seq = seq
        self.fn = fn
        self.rw = rw
        self.war = war
        self.signal = False
        self.sigval = 0
        self.dma = dma
        self.waits = None
        self.cost = cost


def _fsize(ap):
    n = 1
    for d_ in tuple(ap.shape)[1:]:
        n *= int(d_)
    return n


DEBUG_LINES = False
NA_SPLIT = False
SCHED = True
SCHED_K = 6
SYNC_NS = 180.0
DMA_LAT_NS = 2600.0


class Graph:
    def __init__(self, nc, n_dma_sems=56):
        self.nc = nc
        self.es = ExitStack()
        self.eng_sem = {e: self.es.enter_context(nc.semaphore("s_" + e)) for e in ENGS}
        self.eng_cnt = {e: 0 for e in ENGS}
        self.dma_sem = [self.es.enter_context(nc.semaphore("d%d" % i)) for i in range(n_dma_sems)]
        self.dma_cnt = [0] * n_dma_sems
        self.next_slot = 0
        self.ops = {e: [] for e in ENGS}
        self.waited = {e: {} for e in ENGS}
        self.n_instr = 0
        self.seq = 0

    def close(self):
        self.es.close()

    def slot(self, buf):
        if buf.sem is None:
            assert self.next_slot < len(self.dma_sem), "too many dma buffers in one phase"
            buf.sem = self.next_slot
            self.next_slot += 1
        return buf.sem

    def _record(self, eng, fn, reads, writes, dma=None, cost=400.0):
        rw, war = set(), set()
        ex = [b for b in reads if b.excl and b not in writes]
        if ex:
            reads = [b for b in reads if not b.excl or b in writes]
            writes = list(writes) + ex
        for b in reads:
            rw.update(b.w)
        for b in writes:
            for o in b.w:
                if dma is not None and o.dma is not None and o.dma[0] == dma[0]:
                    rw.update(o.rw)
                    war.update(o.war)
                    continue
                rw.add(o)
            war.update(b.r)
        self.seq += 1
        op = Op(eng, self.seq, fn, rw, war, dma, cost)
        if DEBUG_LINES:
            import sys as _s
            f_ = _s._getframe(2)
            op.waits = (f_.f_lineno, f_.f_back.f_lineno if f_.f_back else 0)
        self.ops[eng].append(op)
        for b in reads:
            b.r.append(op)
        for b in writes:
            b.w = [op]
            b.r = []
        return op

    def op(self, eng, fn, reads=(), writes=()):
        return self._record(eng, fn, reads, writes)

    def dma(self, eng, out, in_, sbuf, reads=(), writes=(), slow=False):
        s = self.slot(sbuf)
        self.dma_cnt[s] += 16
        cnt = self.dma_cnt[s]
        sem = self.dma_sem[s]
        kw = {"allow_slow_non_contiguous": True} if slow else {}

        def fn(e):
            return e.dma_start(out=out, in_=in_, **kw).then_inc(sem, 16)
        return self._record(eng, fn, reads, writes, dma=(s, cnt), cost=DMA_LAT_NS)

    def mm(self, out, lhsT, rhs, start, stop, reads, writes):
        n = _fsize(rhs)
        cost = 64.0 + n * (0.42 if rhs.dtype == BF16 else 0.85)
        return self._record("pe", lambda e: e.matmul(out, lhsT, rhs, start=start, stop=stop), reads, writes, cost=cost)

    def tr(self, out, in_, ident, reads, writes):
        return self._record("pe", lambda e: e.transpose(out=out, in_=in_, identity=ident), reads, writes, cost=300.0)

    def _ec(self, eng, out):
        n = _fsize(out)
        if eng == "pool":
            return 300.0 + 2.0 * n
        return 220.0 + 1.0 * n

    def act(self, out, in_, func, reads, writes, bias=None, scale=None):
        kw = {}
        if bias is not None:
            kw["bias"] = bias
        if scale is not None:
            kw["scale"] = scale
        return self._record("act", lambda e: e.activation(out=out, in_=in_, func=func, **kw), reads, writes,
                            cost=self._ec("act", out))

    def cp(self, eng, out, in_, reads, writes):
        if eng == "act":
            return self._record("act", lambda e: e.copy(out=out, in_=in_), reads, writes, cost=self._ec(eng, out))
        return self._record(eng, lambda e: e.tensor_copy(out=out, in_=in_), reads, writes, cost=self._ec(eng, out))

    def tt(self, eng, out, in0, in1, op, reads, writes):
        return self._record(eng, lambda e: e.tensor_tensor(out=out, in0=in0, in1=in1, op=op), reads, writes,
                            cost=self._ec(eng, out))

    def ts(self, eng, out, in0, s1, s2, op0, op1, reads, writes):
        if op1 is None:
            return self._record(eng, lambda e: e.tensor_scalar(out=out, in0=in0, scalar1=s1, scalar2=None, op0=op0),
                                reads, writes, cost=self._ec(eng, out))
        return self._record(eng, lambda e: e.tensor_scalar(out=out, in0=in0, scalar1=s1, scalar2=s2, op0=op0, op1=op1),
                            reads, writes, cost=self._ec(eng, out))

    def stt(self, eng, out, in0, scalar, in1, op0, op1, reads, writes):
        return self._record(eng, lambda e: e.scalar_tensor_tensor(out=out, in0=in0, scalar=scalar, in1=in1,
                                                                  op0=op0, op1=op1), reads, writes,
                            cost=self._ec(eng, out))

    def red(self, eng, out, in_, reads, writes):
        return self._record(eng, lambda e: e.reduce_sum(out=out, in_=in_, axis=AX.X), reads, writes,
                            cost=self._ec(eng, in_))

    def memset(self, eng, ap, val, writes):
        return self._record(eng, lambda e: e.memset(ap, val), (), writes, cost=self._ec(eng, ap))

    def _schedule(self):
        import bisect
        ops = self.ops
        allops = [o for e in ENGS for o in ops[e]]
        for o in allops:
            o.rw = set(d_ for d_ in o.rw if d_.fn is not None)
            o.war = set(d_ for d_ in o.war if d_.fn is not None)
            o.deps = list(o.rw | o.war)
            o.nun = len(o.deps)
            o.users = []
            o.ready = 0.0
            o.done = False
        for o in allops:
            for d_ in o.deps:
                d_.users.append(o)
        avail = {e: [] for e in ENGS}
        for o in allops:
            if o.nun == 0:
                avail[o.eng].append((o.seq, o))
        for e in ENGS:
            avail[e].sort(key=lambda t: t[0])
        free = {e: 0.0 for e in ENGS}
        new = {e: [] for e in ENGS}
        remaining = len(allops)
        while remaining:
            best = None
            for e in ENGS:
                av = avail[e]
                fe = free[e]
                for k in range(min(SCHED_K, len(av))):
                    o = av[k][1]
                    st = o.ready if o.ready > fe else fe
                    if best is None or st < best[0] or (st == best[0] and o.seq < best[2].seq):
                        best = (st, k, o)
                    if o.ready <= fe:
                        break
            st, k, o = best
            e = o.eng
            del avail[e][k]
            fin = st + o.cost
            free[e] = st + 64.0 if o.dma is not None else fin
            o.done = True
            new[e].append(o)
            remaining -= 1
            for u in o.users:
                r = fin + SYNC_NS
                if r > u.ready:
                    u.ready = r
                u.nun -= 1
                if u.nun == 0:
                    bisect.insort(avail[u.eng], (u.seq, u))
        self.ops = new
        if DEBUG_LINES:
            for e in ENGS:
                print("ENGINE", e)
                for o in new[e][:DEBUG_LINES]:
                    print("   seq", o.seq, "line", o.waits, "dma" if o.dma else "", "deps", sorted(d_.seq for d_ in o.deps)[:8])

    def flush(self, engines, sched=True):
        if SCHED and sched:
            self._schedule()
        ops = self.ops
        for e in ENGS:
            for i, op in enumerate(ops[e]):
                op.pos = i
        for e in ENGS:
            for op in ops[e]:
                need = set()
                for d_ in op.rw:
                    if d_.dma is not None or d_.eng != e:
                        need.add(d_)
                    elif op.dma is not None or (STRICT and e in ("act", "dve", "pool")):
                        need.add(d_)
                for d_ in op.war:
                    if d_.dma is not None or d_.eng != e:
                        need.add(d_)
                    elif op.dma is not None:
                        need.add(d_)
                for d_ in need:
                    if d_.dma is None and d_.eng == e:
                        assert d_.pos < op.pos
                op.waits = need
                for d_ in need:
                    if d_.dma is None:
                        d_.signal = True
        for e in ENGS:
            for op in reversed(ops[e]):
                if op.dma is None:
                    op.signal = True
                    break
        for e in ENGS:
            c = self.eng_cnt[e]
            for op in ops[e]:
                if op.signal and op.dma is None:
                    c += 1
                    op.sigval = c
            self.eng_cnt[e] = c
        nc = self.nc
        final_eng = dict(self.eng_cnt)
        final_dma = list(self.dma_cnt)
        g = self

        def emit(e, handle):
            waited = g.waited[e]
            for op in ops[e]:
                wl = {}
                for d_ in op.waits:
                    if d_.dma is not None:
                        key, val = ("d", d_.dma[0]), d_.dma[1]
                    else:
                        key, val = ("c", d_.eng), d_.sigval
                    if wl.get(key, 0) < val:
                        wl[key] = val
                for key, val in wl.items():
                    if waited.get(key, 0) >= val:
                        continue
                    waited[key] = val
                    sem = g.dma_sem[key[1]] if key[0] == "d" else g.eng_sem[key[1]]
                    handle.wait_ge(sem, val)
                    g.n_instr += 1
                ins = op.fn(handle)
                g.n_instr += 1
                if op.signal and op.dma is None:
                    ins.then_inc(g.eng_sem[e], 1)
            for x in ENGS:
                if x != e and waited.get(("c", x), 0) < final_eng[x]:
                    waited[("c", x)] = final_eng[x]
                    handle.wait_ge(g.eng_sem[x], final_eng[x])
            for s_, v in enumerate(final_dma):
                if v > 0 and waited.get(("d", s_), 0) < v:
                    waited[("d", s_)] = v
                    handle.wait_ge(g.dma_sem[s_], v)

        with nc.Block() as block:
            @block.tensor
            def _(h):
                emit("pe", h)

            @block.scalar
            def _(h):
                emit("act", h)

            @block.vector
            def _(h):
                emit("dve", h)

            @block.gpsimd
            def _(h):
                emit("pool", h)

            @block.sync
            def _(h):
                emit("sp", h)
        for e in ENGS:
            for op in ops[e]:
                op.fn = None
        self.ops = {e: [] for e in ENGS}
        self.next_slot = 0


C_Z, C_XBC, C_DTF, C_DTB = 0, 1024, 2560, 2576
C_GQ, C_GK, C_GV, C_GG = 2592, 3104, 3616, 4640
C_DNF, C_DNB = 5664, 5680
C_NQ, C_NK, C_NV = 5696, 6720, 7744
C_GATE = 8768
IN_WIDTH = 11840


class Cfg:
    def __init__(self, nt, stop_after=None, debug_outs=(), debug_ins=()):
        self.nt = nt
        self.seg = nt // 2
        self.stop_after = stop_after
        self.debug_outs = debug_outs
        self.debug_ins = debug_ins
        self.only = None


def build_program(cfg):
    nc = bass.Bass("TRN2", target_bir_lowering=False)
    NT = cfg.nt
    TB = 512
    NB = NT // TB
    NTI = TB // 128

    cfg.in_shapes = {}

    def din(name, shape, dt=F32):
        cfg.in_shapes[name] = (list(shape), dt)
        return nc.dram_tensor(name, list(shape), dt, kind="ExternalInput").ap()

    def dout(name, shape, dt=F32):
        return nc.dram_tensor(name, list(shape), dt, kind="ExternalOutput").ap()

    def dscr(name, shape, dt=F32):
        kind = "Internal"
        if name in cfg.debug_outs:
            kind = "ExternalOutput"
        if name in cfg.debug_ins:
            kind = "ExternalInput"
            cfg.in_shapes[name] = (list(shape), dt)
        return nc.dram_tensor(name, list(shape), dt, kind=kind).ap()

    x_in = din("x", [NT, D])
    flag = din("flag", [128, 2])
    ident_in = din("ident", [128, 128])
    ln_ffn1_w = din("ln_ffn1_w", [DEPTH, D])
    ffn1_w_in = din("ffn1_w_in", [DEPTH, D, 2 * DFF])
    ffn1_w_out = din("ffn1_w_out", [DEPTH, DFF, D])
    ln_mix_w = din("ln_mix_w", [DEPTH, D])
    w_in = din("w_in", [DEPTH, D, IN_WIDTH])
    w_branch = [din("w_branch_" + c, [DEPTH, D, D]) for c in "abc"]
    w_out = din("w_out", [DEPTH, D, D])
    ln_ffn2_w = din("ln_ffn2_w", [DEPTH, D])
    ffn2_w_in = din("ffn2_w_in", [DEPTH, D, 2 * DFF])
    ffn2_w_out = din("ffn2_w_out", [DEPTH, DFF, D])
    ln_final_w = din("ln_final_w", [1, D])
    y_out = dout("y", [NT, D])

    xa = dscr("xa", [NT, D])
    xb = dscr("xb", [NT, D])
    xc = dscr("xc", [NT, D])
    hT_f1 = dscr("hT_f1", [8, 128, NT], BF16)
    hT_mix = dscr("hT_mix", [8, 128, NT], BF16)
    hT_f2 = dscr("hT_f2", [8, 128, NT], BF16)
    z_tm = dscr("z_tm", [NT, 1024])
    kg_tm = dscr("kg_tm", [NT, 512])
    vg_tm = dscr("vg_tm", [NT, 1024], BF16)
    gg_tm = dscr("gg_tm", [NT, 1024])
    vn_tm = dscr("vn_tm", [NT, 1024], BF16)
    dt_tm = dscr("dt_tm", [NT, 32])
    xbcT = dscr("xbcT", [12, 128, NT])
    gqT = dscr("gqT", [4, 128, NT])
    gkT = dscr("gkT", [4, 128, NT])
    nqT = dscr("nqT", [8, 128, NT], BF16)
    nkT = dscr("nkT", [8, 128, NT], BF16)
    gatesT = dscr("gatesT", [24, 128, NT], BF16)
    dnT = dscr("dnT", [2, 16, NT])
    yT = [dscr("yT_" + c, [8, 128, NT], BF16) for c in "abc"]

    g = Graph(nc)

    uid = [0]

    def sb(es, name, shape, dt):
        uid[0] += 1
        return es.enter_context(nc.sbuf_tensor("s%d_%s" % (uid[0], name), list(shape), dt))

    def ps(es, name, shape, dt):
        uid[0] += 1
        return es.enter_context(nc.psum_tensor("p%d_%s" % (uid[0], name), list(shape), dt))

    class NormCtx:
        def __init__(self, es, tag, transpose=True):
            self.transpose = transpose
            self.wB = sb(es, "wB_" + tag, [128, D], F32)
            self.sq = sb(es, "sq_" + tag, [128, D], F32)
            self.ss = [sb(es, "ss%d_" % i + tag, [128, 2], F32) for i in range(2)]
            self.b_wB, self.b_sq = Buf("wB"), Buf("sq")
            self.b_ss = [Buf("ss0"), Buf("ss1")]
            if transpose:
                self.ident = sb(es, "ident_" + tag, [128, 128], BF16)
                self.hn = [sb(es, "hn%d_" % i + tag, [128, D], BF16) for i in range(2)]
                self.hTs = sb(es, "hTs_" + tag, [128, 8, TB], BF16)
                self.pT = [ps(es, "pT%d_" % i + tag, [128, D], BF16) for i in range(2)]
                self.b_ident = Buf("ident")
                self.b_hn = [Buf("hn0"), Buf("hn1")]
                self.b_hTs = Buf("hTs")
                self.b_pT = [Buf("pT0"), Buf("pT1")]
            else:
                self.yt = [sb(es, "yt%d_" % i + tag, [128, D], F32) for i in range(2)]
                self.b_yt = [Buf("yt0"), Buf("yt1")]
            self.n = 0

        def setup(self, w_row_ap):
            if self.transpose:
                g.dma("pool", self.ident[:], ident_in, self.b_ident, writes=[self.b_ident])
            g.dma("sp", self.wB[:], w_row_ap.partition_broadcast(128), self.b_wB, writes=[self.b_wB])

        def stats(self, xt_ap, b_x):
            i = self.n % 2
            self.n += 1
            ss = self.ss[i]
            g.act(self.sq[:], xt_ap, AF.Square, [b_x], [self.b_sq])
            g.red("dve", ss[:, 0:1], self.sq[:], [self.b_sq], [self.b_ss[i]])
            g.ts("dve", ss[:, 1:2], ss[:, 0:1], 1.0 / D, EPS, ALU.mult, ALU.add, [self.b_ss[i]], [self.b_ss[i]])
            g.act(ss[:, 1:2], ss[:, 1:2], AF.Sqrt, [self.b_ss[i]], [self.b_ss[i]])
            g._record("dve", lambda e: e.reciprocal(out=ss[:, 1:2], in_=ss[:, 1:2]), [self.b_ss[i]], [self.b_ss[i]])
            return i

        def run(self, xt_ap, b_x, hT_dram, blk, ti):
            self.flush_pending()
            i = self.stats(xt_ap, b_x)
            hn = self.hn[i]
            g.stt("dve", hn[:], xt_ap, self.ss[i][:, 1:2], self.wB[:], ALU.mult, ALU.mult,
                  [b_x, self.b_ss[i], self.b_wB], [self.b_hn[i]])
            self.pending = (i, hT_dram, blk, ti)

        def flush_pending(self):
            if getattr(self, "pending", None) is None:
                return
            i, hT_dram, blk, ti = self.pending
            self.pending = None
            hn, pT, hTs = self.hn[i], self.pT[i], self.hTs
            for k in range(8):
                g.tr(pT[:, k * 128:(k + 1) * 128], hn[:, k * 128:(k + 1) * 128], self.ident[:],
                     [self.b_hn[i], self.b_ident], [self.b_pT[i]])
            g.cp("act", hTs[:, :, ti * 128:(ti + 1) * 128], pT[:].rearrange("p (k t) -> p k t", k=8),
                 [self.b_pT[i]], [self.b_hTs])
            if ti == NTI - 1:
                g.dma("pool", hT_dram[:, :, blk * TB:(blk + 1) * TB].rearrange("k p t -> p k t"), hTs[:],
                      self.b_hTs, reads=[self.b_hTs])

        def run_final(self, xt_ap, b_x, y_dram, t):
            i = self.stats(xt_ap, b_x)
            g.stt("dve", self.yt[i][:], xt_ap, self.ss[i][:, 1:2], self.wB[:], ALU.mult, ALU.mult,
                  [b_x, self.b_ss[i], self.b_wB], [self.b_yt[i]])
            g.dma("pool", y_dram[t * 128:(t + 1) * 128, :], self.yt[i][:], self.b_yt[i], reads=[self.b_yt[i]])

    def phase_p0():
        with ExitStack() as es:
            nrm = NormCtx(es, "p0")
            xt = [sb(es, "p0x%d" % i, [128, D], F32) for i in range(2)]
            b_xt = [Buf("x0"), Buf("x1")]
            nrm.setup(ln_ffn1_w[0:1, :])
            for blk in range(NB):
                for ti in range(NTI):
                    t = blk * NTI + ti
                    i = t % 2
                    g.dma("sp", xt[i][:], x_in[t * 128:(t + 1) * 128, :], b_xt[i], writes=[b_xt[i]])
                    nrm.run(xt[i][:], b_xt[i], hT_f1, blk, ti)
            nrm.flush_pending()
            g.flush(ENGS)

    def phase_ffn(tag, w_in_ap, w_out_ap, x_src, hT_src, x_dst, next_w_row, hT_dst, final_out=None):
        with ExitStack() as es:
            w1 = sb(es, "w1_" + tag, [128, 8, 2 * DFF], BF16)
            w2 = sb(es, "w2_" + tag, [128, 22, D], BF16)
            b_w1, b_w2 = Buf("w1"), Buf("w2")
            hTb = sb(es, "hTb_" + tag, [128, 8, TB], BF16)
            b_hTb = Buf("hTb")
            gT = sb(es, "gT_" + tag, [128, 22, TB], BF16)
            b_gT = Buf("gT")
            sa = [sb(es, "sa%d_" % i + tag, [128, TB], F32) for i in range(2)]
            b_sa = [Buf("sa0"), Buf("sa1")]
            xt = sb(es, "xt_" + tag, [128, D], F32)
            b_xt = Buf("xt")
            xn = [sb(es, "xn%d_" % i + tag, [128, D], F32) for i in range(2)]
            b_xn = [Buf("xn0"), Buf("xn1")]
            pa = [ps(es, "pa%d_" % i + tag, [128, TB], F32) for i in range(2)]
            pb = [ps(es, "pb%d_" % i + tag, [128, TB], F32) for i in range(2)]
            b_pa = [Buf("pa0"), Buf("pa1")]
            b_pb = [Buf("pb0"), Buf("pb1")]
            po = [ps(es, "po%d_" % i + tag, [128, 512], F32) for i in range(2)]
            b_po = [Buf("po0"), Buf("po1")]
            nrm = NormCtx(es, tag, transpose=(final_out is None))
            for k in range(8):
                g.dma("pool", w1[:, k, :], w_in_ap[k * 128:(k + 1) * 128, :], b_w1, writes=[b_w1])
            for k in range(22):
                g.dma("pool", w2[:, k, :], w_out_ap[k * 128:(k + 1) * 128, :], b_w2, writes=[b_w2])
            nrm.setup(next_w_row)

            def load_h(blk):
                g.dma("sp", hTb[:], hT_src[:, :, blk * TB:(blk + 1) * TB].rearrange("k p t -> p k t"),
                      b_hTb, writes=[b_hTb])
            load_h(0)
            ntl = 0
            for blk in range(NB):
                for c in range(22):
                    q = c % 2
                    for k in range(8):
                        g.mm(pa[q][:], w1[:, k, c * 128:(c + 1) * 128], hTb[:, k, :], k == 0, k == 7,
                             [b_w1, b_hTb], [b_pa[q]])
                    for k in range(8):
                        g.mm(pb[q][:], w1[:, k, DFF + c * 128:DFF + (c + 1) * 128], hTb[:, k, :], k == 0, k == 7,
                             [b_w1, b_hTb], [b_pb[q]])
                    g.act(sa[q][:], pa[q][:], AF.Silu, [b_pa[q]], [b_sa[q]])
                    g.tt("dve", gT[:, c, :], sa[q][:], pb[q][:], ALU.mult, [b_sa[q], b_pb[q]], [b_gT])
                if blk + 1 < NB:
                    load_h(blk + 1)
                for ti in range(NTI):
                    t = blk * NTI + ti
                    i = ntl % 2
                    ntl += 1
                    g.dma("sp", xt[:], x_src[t * 128:(t + 1) * 128, :], b_xt, writes=[b_xt])
                    for ch in range(2):
                        for k in range(22):
                            g.mm(po[ch][:], gT[:, k, ti * 128:(ti + 1) * 128], w2[:, k, ch * 512:(ch + 1) * 512],
                                 k == 0, k == 21, [b_gT, b_w2], [b_po[ch]])
                        g.stt("dve", xn[i][:, ch * 512:(ch + 1) * 512], po[ch][:], 0.5,
                              xt[:, ch * 512:(ch + 1) * 512], ALU.mult, ALU.add, [b_po[ch], b_xt], [b_xn[i]])
                    if final_out is None:
                        nrm.flush_pending()
                    if final_out is None:
                        g.dma("pool", x_dst[t * 128:(t + 1) * 128, :], xn[i][:], b_xn[i], reads=[b_xn[i]])
                        nrm.run(xn[i][:], b_xn[i], hT_dst, blk, ti)
                    else:
                        nrm.run_final(xn[i][:], b_xn[i], final_out, t)
            if final_out is None:
                nrm.flush_pending()
            g.flush(ENGS)

    def phase_inproj_tm(l):
        with ExitStack() as es:
            W = w_in[l]
            parts = [(C_Z, 1024, z_tm, F32), (C_GK, 512, kg_tm, F32), (C_GV, 1024, vg_tm, BF16),
                     (C_GG, 1024, gg_tm, F32), (C_NV, 1024, vn_tm, BF16), (C_DTF, 32, dt_tm, F32)]
            NC_TM = sum(p[1] for p in parts)
            wt = sb(es, "wtm", [128, 8, NC_TM], BF16)
            b_wt = Buf("wtm")
            hTb = [sb(es, "tm_hTb%d" % i, [128, 8, TB], BF16) for i in range(2)]
            b_hTb = [Buf("hTb0"), Buf("hTb1")]
            st = [[sb(es, "tm_st%d_%d" % (i, pi), [128, p[1]], p[3]) for pi, p in enumerate(parts)] for i in range(2)]
            b_st = [[Buf("st%d_%d" % (i, pi)) for pi in range(len(parts))] for i in range(2)]
            pp = [ps(es, "tm_p%d" % i, [128, 512], F32) for i in range(4)]
            b_pp = [Buf("pp%d" % i) for i in range(4)]
            off = 0
            offs = []
            for (c0, wd, _, _) in parts:
                offs.append(off)
                for k in range(8):
                    g.dma("pool", wt[:, k, off:off + wd], W[k * 128:(k + 1) * 128, c0:c0 + wd], b_wt, writes=[b_wt])
                off += wd
            nps = 0
            nev = 0
            for blk in range(NB):
                j = blk % 2
                g.dma("sp", hTb[j][:], hT_mix[:, :, blk * TB:(blk + 1) * TB].rearrange("k p t -> p k t"),
                      b_hTb[j], writes=[b_hTb[j]])
                for ti in range(NTI):
                    t = blk * NTI + ti
                    i = t % 2
                    for pi, (c0, wd, dst, dt) in enumerate(parts):
                        for cb in range(0, wd, 512):
                            n = min(512, wd - cb)
                            q = nps % 4
                            nps += 1
                            for k in range(8):
                                g.mm(pp[q][:, 0:n], hTb[j][:, k, ti * 128:(ti + 1) * 128],
                                     wt[:, k, offs[pi] + cb:offs[pi] + cb + n], k == 0, k == 7,
                                     [b_hTb[j], b_wt], [b_pp[q]])
                            eng = "act" if nev % 2 == 0 else "dve"
                            nev += 1
                            g.cp(eng, st[i][pi][:, cb:cb + n], pp[q][:, 0:n], [b_pp[q]], [b_st[i][pi]])
                        g.dma("pool", dst[t * 128:(t + 1) * 128, :], st[i][pi][:], b_st[i][pi], reads=[b_st[i][pi]])
            g.flush(ENGS)

    def phase_inproj_fm(l):
        with ExitStack() as es:
            W = w_in[l]
            parts = [(C_XBC, 12, xbcT, F32, None, 1.0), (C_GQ, 4, gqT, F32, None, 128.0 ** -0.5),
                     (C_GK, 4, gkT, F32, None, 1.0), (C_NQ, 8, nqT, BF16, None, 0.125),
                     (C_NK, 8, nkT, BF16, None, 1.0), (C_GATE, 24, gatesT, BF16, AF.Sigmoid, 1.0)]
            NCH = sum(p[1] for p in parts)
            NC_FM = NCH * 128 + 32
            wt = sb(es, "wfm", [128, 8, NC_FM], BF16)
            b_wt = Buf("wfm")
            hTb = [sb(es, "fm_hTb%d" % i, [128, 8, TB], BF16) for i in range(2)]
            b_hTb = [Buf("hTb0"), Buf("hTb1")]
            stf = [sb(es, "fm_stf%d" % i, [128, TB], F32) for i in range(4)]
            stb = [sb(es, "fm_stb%d" % i, [128, TB], BF16) for i in range(4)]
            b_stf = [Buf("stf%d" % i) for i in range(4)]
            b_stb = [Buf("stb%d" % i) for i in range(4)]
            pp = [ps(es, "fm_p%d" % i, [128, 512], F32) for i in range(4)]
            b_pp = [Buf("pp%d" % i) for i in range(4)]
            off = 0
            offs = []
            for (c0, nch, _, _, _, _) in parts:
                offs.append(off)
                for k in range(8):
                    g.dma("pool", wt[:, k, off:off + nch * 128], W[k * 128:(k + 1) * 128, c0:c0 + nch * 128],
                          b_wt, writes=[b_wt])
                off += nch * 128
            off_dn = off
            for k in range(8):
                g.dma("pool", wt[:, k, off:off + 32], W[k * 128:(k + 1) * 128, C_DNF:C_DNF + 32], b_wt, writes=[b_wt])
            nps = 0
            nf = 0
            nbf = 0
            nev = 0
            for blk in range(NB):
                j = blk % 2
                tok = slice(blk * TB, (blk + 1) * TB)
                g.dma("sp", hTb[j][:], hT_mix[:, :, tok].rearrange("k p t -> p k t"), b_hTb[j], writes=[b_hTb[j]])
                for pi, (c0, nch, dst, dt, func, scale) in enumerate(parts):
                    for c in range(nch):
                        q = nps % 4
                        nps += 1
                        wc = offs[pi] + c * 128
                        for k in range(8):
                            g.mm(pp[q][:], wt[:, k, wc:wc + 128], hTb[j][:, k, :], k == 0, k == 7,
                                 [b_wt, b_hTb[j]], [b_pp[q]])
                        if dt == F32:
                            s_, b_s = stf[nf % 4], b_stf[nf % 4]
                            nf += 1
                        else:
                            s_, b_s = stb[nbf % 4], b_stb[nbf % 4]
                            nbf += 1
                        if func is not None:
                            g.act(s_[:], pp[q][:], func, [b_pp[q]], [b_s])
                        elif scale != 1.0:
                            if nev % 2 == 0:
                                g.act(s_[:], pp[q][:], AF.Copy, [b_pp[q]], [b_s], scale=scale)
                            else:
                                g.ts("dve", s_[:], pp[q][:], scale, None, ALU.mult, None, [b_pp[q]], [b_s])
                            nev += 1
                        else:
                            g.cp("act" if nev % 2 == 0 else "dve", s_[:], pp[q][:], [b_pp[q]], [b_s])
                            nev += 1
                        g.dma("pool", dst[c, :, tok], s_[:], b_s, reads=[b_s])
                for dd in range(2):
                    q = nps % 4
                    nps += 1
                    wc = off_dn + dd * 16
                    for k in range(8):
                        g.mm(pp[q][0:16, :], wt[:, k, wc:wc + 16], hTb[j][:, k, :], k == 0, k == 7,
                             [b_wt, b_hTb[j]], [b_pp[q]])
                    s_, b_s = stf[nf % 4], b_stf[nf % 4]
                    nf += 1
                    g.cp("dve", s_[0:16, :], pp[q][0:16, :], [b_pp[q]], [b_s])
                    g.dma("pool", dnT[dd, :, tok], s_[0:16, :], b_s, reads=[b_s])
            g.flush(ENGS)

    def phase_merge(l, x_src, x_dst, next_w_row, hT_dst):
        with ExitStack() as es:
            wb_ = [sb(es, "wbr%d" % i, [128, 8, D], BF16) for i in range(3)]
            wo = sb(es, "wo", [128, 8, D], BF16)
            b_w = Buf("w")
            yb = [sb(es, "yb%d" % i, [128, 8, TB], BF16) for i in range(3)]
            b_yb = [Buf("yb%d" % i) for i in range(3)]
            gt = sb(es, "gt", [128, 24, TB], BF16)
            b_gt = Buf("gt")
            tmp = [sb(es, "mtmp%d" % i, [128, TB], F32) for i in range(3)]
            b_tmp = [Buf("tmp%d" % i) for i in range(3)]
            mT = sb(es, "mT", [128, 8, TB], BF16)
            b_mT = Buf("mT")
            xt = sb(es, "m_xt", [128, D], F32)
            b_xt = Buf("xt")
            xn = [sb(es, "m_xn%d" % i, [128, D], F32) for i in range(2)]
            b_xn = [Buf("xn0"), Buf("xn1")]
            pp = [ps(es, "m_p%d" % i, [128, 512], F32) for i in range(6)]
            b_pp = [Buf("pp%d" % i) for i in range(6)]
            nrm = NormCtx(es, "mrg")
            for i in range(3):
                for k in range(8):
                    g.dma("pool", wb_[i][:, k, :], w_branch[i][l, k * 128:(k + 1) * 128, :], b_w, writes=[b_w])
            for k in range(8):
                g.dma("pool", wo[:, k, :], w_out[l, k * 128:(k + 1) * 128, :], b_w, writes=[b_w])
            nrm.setup(next_w_row)
            ntl = 0
            for blk in range(NB):
                tok = slice(blk * TB, (blk + 1) * TB)
                for i in range(3):
                    g.dma("sp", yb[i][:], yT[i][:, :, tok].rearrange("k p t -> p k t"), b_yb[i], writes=[b_yb[i]])
                g.dma("sp", gt[:], gatesT[:, :, tok].rearrange("k p t -> p k t"), b_gt, writes=[b_gt])
                for oc in range(8):
                    par = (oc % 2) * 3
                    for i in range(3):
                        for k in range(8):
                            g.mm(pp[par + i][:], wb_[i][:, k, oc * 128:(oc + 1) * 128], yb[i][:, k, :], k == 0, k == 7,
                                 [b_w, b_yb[i]], [b_pp[par + i]])
                    for i in range(3):
                        g.tt("dve", tmp[i][:], pp[par + i][:], gt[:, i * 8 + oc, :], ALU.mult,
                             [b_pp[par + i], b_gt], [b_tmp[i]])
                    g.tt("pool", tmp[0][:], tmp[0][:], tmp[1][:], ALU.add, [b_tmp[0], b_tmp[1]], [b_tmp[0]])
                    g.tt("pool", mT[:, oc, :], tmp[0][:], tmp[2][:], ALU.add, [b_tmp[0], b_tmp[2]], [b_mT])
                for ti in range(NTI):
                    t = blk * NTI + ti
                    i = ntl % 2
                    ntl += 1
                    g.dma("sp", xt[:], x_src[t * 128:(t + 1) * 128, :], b_xt, writes=[b_xt])
                    for ch in range(2):
                        q = ch * 3
                        for k in range(8):
                            g.mm(pp[q][:], mT[:, k, ti * 128:(ti + 1) * 128], wo[:, k, ch * 512:(ch + 1) * 512],
                                 k == 0, k == 7, [b_mT, b_w], [b_pp[q]])
                        g.tt("dve", xn[i][:, ch * 512:(ch + 1) * 512], pp[q][:], xt[:, ch * 512:(ch + 1) * 512], ALU.add,
                             [b_pp[q], b_xt], [b_xn[i]])
                    nrm.flush_pending()
                    g.dma("pool", x_dst[t * 128:(t + 1) * 128, :], xn[i][:], b_xn[i], reads=[b_xn[i]])
                    nrm.run(xn[i][:], b_xn[i], hT_dst, blk, ti)
            nrm.flush_pending()
            g.flush(ENGS)

    ssm_conv_w = din("ssm_conv_w", [DEPTH, 5, 1536])
    ssm_conv_b = din("ssm_conv_b", [DEPTH, 1536])
    ssm_dt_bias = din("ssm_dt_bias", [DEPTH, 32])
    ssm_a_log = din("ssm_a_log", [DEPTH, 32])
    ssm_d = din("ssm_d", [DEPTH, 16])
    ssm_norm_w = din("ssm_norm_w", [DEPTH, 1024])
    tri_in = din("tri", [2, 128, 128])
    maskadd_in = din("maskadd", [2, 128, 128])
    mask01_in = din("mask01", [2, 128, 128])
    ustrict_in = din("ustrict", [2, 128, 128])
    ones_in = din("ones", [128, 128])
    xs_tm = dscr("xs_tm", [NT, 1024])
    B_tm = dscr("B_tm", [NT, 256], BF16)
    BCT = dscr("BCT", [4, 128, NT], BF16)
    dtsp = dscr("dtsp", [NT, 32])
    dta_d = dscr("dta", [NT, 32])
    dlog_d = dscr("dlog", [NT, 32])
    yf_d = dscr("yf", [NT, 1024])
    NCH = NT // 128
    MIDC = NCH // 2

    def phase_ssd_prep(l):
        with ExitStack() as es:
            cw = sb(es, "cw", [128, 12, 5], F32)
            cb = sb(es, "cb", [128, 12], F32)
            fl = sb(es, "fl", [128, 2], F32)
            identf = sb(es, "identf", [128, 128], F32)
            dbias = sb(es, "dbias", [128, 32], F32)
            aB = sb(es, "aB", [128, 32], F32)
            b_c = Buf("consts")
            xin = [sb(es, "xin%d" % i, [128, 12, TB + 4], F32) for i in range(2)]
            b_xin = [Buf("xin0"), Buf("xin1")]
            acc = [sb(es, "acc%d" % i, [128, TB], F32) for i in range(3)]
            b_acc = [Buf("acc%d" % i) for i in range(3)]
            sil = sb(es, "sil", [128, 10, TB], F32)
            b_sil = Buf("sil")
            silb = [sb(es, "silb%d" % i, [128, TB], BF16) for i in range(2)]
            b_silb = [Buf("silb0"), Buf("silb1")]
            xst = [sb(es, "xst%d" % i, [128, 1024], F32) for i in range(2)]
            b_xst = [Buf("xst0"), Buf("xst1")]
            bst = [sb(es, "bst%d" % i, [128, 256], BF16) for i in range(2)]
            b_bst = [Buf("bst0"), Buf("bst1")]
            dtt = [sb(es, "dtt%d" % i, [128, 4, NTI, 32], F32) for i in range(2)]
            b_dtt = [Buf("dtt0"), Buf("dtt1")]
            pt = [ps(es, "pp_t%d" % i, [128, 1024], F32) for i in range(2)]
            b_pt = [Buf("pt0"), Buf("pt1")]
            pbt = [ps(es, "pp_b%d" % i, [128, 512], F32) for i in range(2)]
            b_pbt = [Buf("pbt0"), Buf("pbt1")]
            for k in range(5):
                g.dma("sp", cw[:, :, k], ssm_conv_w[l, k].rearrange("(c p) -> p c", p=128), b_c, writes=[b_c], slow=True)
            g.dma("sp", cb[:], ssm_conv_b[l].rearrange("(c p) -> p c", p=128), b_c, writes=[b_c], slow=True)
            g.dma("sp", fl[:], flag, b_c, writes=[b_c])
            g.dma("sp", identf[:], ident_in, b_c, writes=[b_c])
            g.dma("sp", dbias[:], ssm_dt_bias[l:l + 1, :].partition_broadcast(128), b_c, writes=[b_c])
            g.dma("sp", aB[:], ssm_a_log[l:l + 1, :].partition_broadcast(128), b_c, writes=[b_c])
            g.act(aB[:], aB[:], AF.Exp, [b_c], [b_c])
            g.ts("dve", aB[:], aB[:], -1.0, None, ALU.mult, None, [b_c], [b_c])
            na = 0
            nsb = 0
            nt_ = 0
            for blk in range(NB):
                j = blk % 2
                t0 = blk * TB
                lo, hi = t0 - 2, t0 + TB + 2
                xi = xin[j]
                lo_c, hi_c = 0, TB + 4
                if lo < 0:
                    g.memset("pool", xi[:, :, 0:2], 0.0, [b_xin[j]])
                    lo_c, lo = 2, 0
                if hi > NT:
                    g.memset("pool", xi[:, :, TB + 2:TB + 4], 0.0, [b_xin[j]])
                    hi_c, hi = TB + 2, NT
                g.dma("sp", xi[:, :, lo_c:hi_c], xbcT[:, :, lo:hi].rearrange("c p t -> p c t"), b_xin[j],
                      writes=[b_xin[j]])
                if t0 == NT // 2:
                    g.ts("dve", xi[:, :, 0:2], xi[:, :, 0:2], fl[:, 0:1], None, ALU.mult, None, [b_xin[j], b_c], [b_xin[j]])
                if t0 + TB == NT // 2:
                    g.ts("dve", xi[:, :, TB + 2:TB + 4], xi[:, :, TB + 2:TB + 4], fl[:, 0:1], None, ALU.mult, None,
                         [b_xin[j], b_c], [b_xin[j]])
                dq = dtt[j]
                g.dma("sp", dq[:, 0, :, :], dt_tm[t0:t0 + TB, :].rearrange("(i p) c -> p i c", p=128), b_dtt[j],
                      writes=[b_dtt[j]])
                g.tt("dve", dq[:, 0, :, :], dq[:, 0, :, :], dbias[:, None, :].to_broadcast([128, NTI, 32]), ALU.add,
                     [b_dtt[j], b_c], [b_dtt[j]])
                g.act(dq[:, 1, :, :], dq[:, 0, :, :], AF.Exp, [b_dtt[j]], [b_dtt[j]])
                g.ts("dve", dq[:, 1, :, :], dq[:, 1, :, :], 1.0, None, ALU.add, None, [b_dtt[j]], [b_dtt[j]])
                g.act(dq[:, 1, :, :], dq[:, 1, :, :], AF.Ln, [b_dtt[j]], [b_dtt[j]])
                g.tt("dve", dq[:, 2, :, :], dq[:, 1, :, :], aB[:, None, :].to_broadcast([128, NTI, 32]), ALU.mult,
                     [b_dtt[j], b_c], [b_dtt[j]])
                g.dma("pool", dtsp[t0:t0 + TB, :].rearrange("(i p) c -> p i c", p=128), dq[:, 1, :, :], b_dtt[j],
                      reads=[b_dtt[j]])
                g.act(dq[:, 3, :, :], dq[:, 1, :, :], AF.Ln, [b_dtt[j]], [b_dtt[j]])
                g.dma("pool", dta_d[t0:t0 + TB, :].rearrange("(i p) c -> p i c", p=128), dq[:, 2, :, :], b_dtt[j],
                      reads=[b_dtt[j]])
                g.dma("pool", dlog_d[t0:t0 + TB, :].rearrange("(i p) c -> p i c", p=128), dq[:, 3, :, :], b_dtt[j],
                      reads=[b_dtt[j]])
                for c in range(12):
                    a = na % 3
                    na += 1
                    g.act(acc[a][:], xi[:, c, 0:TB], AF.Identity, [b_xin[j], b_c], [b_acc[a]],
                          bias=cb[:, c:c + 1], scale=cw[:, c, 0:1])
                    for k in range(1, 5):
                        g.stt("dve", acc[a][:], xi[:, c, k:k + TB], cw[:, c, k:k + 1], acc[a][:],
                              ALU.mult, ALU.add, [b_xin[j], b_c, b_acc[a]], [b_acc[a]])
                    if c < 10:
                        g.act(sil[:, c, :], acc[a][:], AF.Silu, [b_acc[a]], [b_sil])
                    if c >= 8:
                        q = nsb % 2
                        nsb += 1
                        g.act(silb[q][:], acc[a][:], AF.Silu, [b_acc[a]], [b_silb[q]])
                        g.dma("pool", BCT[c - 8, :, t0:t0 + TB], silb[q][:], b_silb[q], reads=[b_silb[q]])
                for ti in range(NTI):
                    i = nt_ % 2
                    nt_ += 1
                    tk = slice(t0 + ti * 128, t0 + (ti + 1) * 128)
                    for c in range(8):
                        g.tr(pt[i][:, c * 128:(c + 1) * 128], sil[:, c, ti * 128:(ti + 1) * 128], identf[:],
                             [b_sil, b_c], [b_pt[i]])
                    for c in range(2):
                        g.tr(pbt[i][:, c * 128:(c + 1) * 128], sil[:, 8 + c, ti * 128:(ti + 1) * 128], identf[:],
                             [b_sil, b_c], [b_pbt[i]])
                    g.cp("act", xst[i][:], pt[i][:], [b_pt[i]], [b_xst[i]])
                    g.cp("dve", bst[i][:], pbt[i][:, 0:256], [b_pbt[i]], [b_bst[i]])
                    g.dma("pool", xs_tm[tk, :], xst[i][:], b_xst[i], reads=[b_xst[i]])
                    g.dma("pool", B_tm[tk, :], bst[i][:], b_bst[i], reads=[b_bst[i]])
            g.flush(ENGS)

    def phase_ssd_pass(l, d):
        with ExitStack() as es:
            tri = sb(es, "tri", [128, 128], F32)
            ones = sb(es, "ones", [128, 128], F32)
            identf = sb(es, "identf", [128, 128], F32)
            identb = sb(es, "identb", [128, 128], BF16)
            maskB = sb(es, "maskB", [128, 16, 128], F32)
            fl = sb(es, "fl", [128, 2], F32)
            b_c = Buf("consts")
            xs = [sb(es, "xs%d" % i, [128, 1024], F32) for i in range(3)]
            xsb = [sb(es, "xsb%d" % i, [128, 1024], BF16) for i in range(2)]
            b_xsb = [Buf("xsb0"), Buf("xsb1")]
            Bt = [sb(es, "Bt%d" % i, [128, 256], BF16) for i in range(3)]
            bct = [sb(es, "bct%d" % i, [128, 4, 128], BF16) for i in range(3)]
            dts = [sb(es, "dts%d" % i, [128, 3, 16], F32) for i in range(3)]
            b_in = [Buf("in0"), Buf("in1"), Buf("in2")]
            R_ = [sb(es, "R%d" % i, [128, 16, 128], F32) for i in range(2)]
            b_R_ = [Buf("R0"), Buf("R1")]
            cs_sb_ = [sb(es, "cs_sb%d" % i, [128, 16], F32) for i in range(2)]
            ecs_ = [sb(es, "ecs%d" % i, [128, 16], F32) for i in range(2)]
            b_cs_ = [Buf("cs_sb0"), Buf("cs_sb1")]
            b_ecs_ = [Buf("ecs0"), Buf("ecs1")]
            seg_ = [sb(es, "seg%d" % i, [128, 16, 128], F32) for i in range(2)]
            b_seg_ = [Buf("seg0"), Buf("seg1")]
            cbT_ = [sb(es, "cbT%d" % i, [128, 2, 128], F32) for i in range(2)]
            b_cbT_ = [Buf("cbT0"), Buf("cbT1")]
            MT_ = [sb(es, "MT%d" % i, [128, 16, 128], BF16) for i in range(2)]
            b_MT_ = [Buf("MT0"), Buf("MT1")]
            xr = sb(es, "xr", [128, 1024], BF16)
            b_xr = Buf("xr")
            xd_ = [sb(es, "xd%d" % i, [128, 1024], BF16) for i in range(2)]
            b_xd_ = [Buf("xd0"), Buf("xd1")]
            S = sb(es, "S", [128, 1024], F32)
            Sbf = sb(es, "Sbf", [128, 1024], BF16)
            b_S, b_Sbf = Buf("S"), Buf("Sbf")
            decB_ = [sb(es, "decB%d" % i, [128, 16], F32) for i in range(2)]
            b_decB_ = [Buf("decB0"), Buf("decB1")]
            t1_ = [sb(es, "t1_%d" % i, [128, 1024], F32) for i in range(2)]
            b_t1_ = [Buf("t1_0"), Buf("t1_1")]
            yd = [sb(es, "yd%d" % i, [128, 1024], F32) for i in range(2)]
            b_yd = [Buf("yd0"), Buf("yd1")]
            pA = ps(es, "pA", [128, 1024], F32)
            pM = ps(es, "pM", [128, 512], F32)
            pY = ps(es, "pY", [128, 1024], F32)
            pO = ps(es, "pO", [128, 1024], F32)
            b_pA, b_pM, b_pY, b_pO = Buf("pA"), Buf("pM"), Buf("pY"), Buf("pO")
            if d == 1:
                dB = sb(es, "dB", [128, 16], F32)
                nwB = sb(es, "nwB", [128, 1024], F32)
                yfl = [sb(es, "yfl%d" % i, [128, 1024], F32) for i in range(3)]
                zt = [sb(es, "zt%d" % i, [128, 1024], F32) for i in range(3)]
                b_in2 = [Buf("in2_0"), Buf("in2_1"), Buf("in2_2")]
                sq = sb(es, "sq", [128, 1024], F32)
                b_sq = Buf("sq")
                t2 = sb(es, "t2", [128, 1024], F32)
                b_t2 = Buf("t2")
                gs = sb(es, "gs", [128, 4], F32)
                b_gs = Buf("gs")
                ynb = sb(es, "ynb", [128, 1024], BF16)
                b_ynb = Buf("ynb")
                yTs = [sb(es, "yTs%d" % i, [128, 8, 128], BF16) for i in range(2)]
                b_yTs = [Buf("yTs0"), Buf("yTs1")]
                pT = ps(es, "pT", [128, 1024], BF16)
                b_pT = Buf("pT")
                g.dma("sp", dB[:], ssm_d[l:l + 1, :].partition_broadcast(128), b_c, writes=[b_c])
                g.dma("sp", nwB[:], ssm_norm_w[l:l + 1, :].partition_broadcast(128), b_c, writes=[b_c])
                g.dma("pool", identb[:], ident_in, b_c, writes=[b_c])
            g.dma("sp", tri[:], tri_in[d], b_c, writes=[b_c])
            g.dma("sp", ones[:], ones_in, b_c, writes=[b_c])
            g.dma("sp", identf[:], ident_in, b_c, writes=[b_c])
            g.dma("sp", fl[:], flag, b_c, writes=[b_c])
            for h in range(16):
                g.dma("sp", maskB[:, h, :], maskadd_in[d], b_c, writes=[b_c])
            g.memset("dve", S[:], 0.0, [b_S])
            g.memset("pool", Sbf[:], 0.0, [b_Sbf])
            last = 127 if d == 0 else 0
            order = list(range(NCH)) if d == 0 else list(range(NCH - 1, -1, -1))
            pend = []

            def drain(k):
                for _ in range(k):
                    if pend:
                        pend.pop(0)()

            for n, c in enumerate(order):
                i = n % 3
                j2 = n % 2
                R, b_R, cs_sb, b_cs, ecs, b_ecs = R_[j2], b_R_[j2], cs_sb_[j2], b_cs_[j2], ecs_[j2], b_ecs_[j2]
                seg, b_seg, cbT, b_cbT, MT, b_MT = seg_[j2], b_seg_[j2], cbT_[j2], b_cbT_[j2], MT_[j2], b_MT_[j2]
                xd, b_xd, decB, b_decB, t1, b_t1 = xd_[j2], b_xd_[j2], decB_[j2], b_decB_[j2], t1_[j2], b_t1_[j2]
                tk = slice(c * 128, (c + 1) * 128)
                g.dma("sp", xs[i][:], xs_tm[tk, :], b_in[i], writes=[b_in[i]])
                g.dma("sp", Bt[i][:], B_tm[tk, :], b_in[i], writes=[b_in[i]])
                g.dma("sp", bct[i][:], BCT[:, :, tk].rearrange("c p t -> p c t"), b_in[i], writes=[b_in[i]])
                g.dma("sp", dts[i][:, 0, :], dtsp[tk, d * 16:(d + 1) * 16], b_in[i], writes=[b_in[i]])
                g.dma("sp", dts[i][:, 1, :], dta_d[tk, d * 16:(d + 1) * 16], b_in[i], writes=[b_in[i]])
                g.dma("sp", dts[i][:, 2, :], dlog_d[tk, d * 16:(d + 1) * 16], b_in[i], writes=[b_in[i]])
                if d == 1:
                    g.dma("sp", yfl[i][:], yf_d[tk, :], b_in2[i], writes=[b_in2[i]])
                    g.dma("sp", zt[i][:], z_tm[tk, :], b_in2[i], writes=[b_in2[i]])
                dta = dts[i][:, 1, :]
                dsp = dts[i][:, 0, :]
                if (d == 0 and c == MIDC) or (d == 1 and c == MIDC - 1):
                    g.ts("dve", S[:], S[:], fl[:, 0:1], None, ALU.mult, None, [b_S, b_c], [b_S])
                    g.ts("pool", Sbf[:], Sbf[:], fl[:, 0:1], None, ALU.mult, None, [b_Sbf, b_c], [b_Sbf])
                g.mm(pM[:, 0:16], tri[:], dta, True, True, [b_c, b_in[i]], [b_pM])
                for gi in range(2):
                    g.mm(pM[:, 128 + gi * 128:256 + gi * 128], bct[i][:, gi, :], bct[i][:, 2 + gi, :], True, True,
                         [b_in[i]], [b_pM])
                g.tt("dve", cs_sb[:], pM[:, 0:16], dts[i][:, 2, :], ALU.subtract, [b_pM, b_in[i]], [b_cs])
                g.cp("act", cbT[:], pM[:, 128:384].rearrange("p (g t) -> p g t", g=2), [b_pM], [b_cbT])
                drain(2)
                g.act(ecs[:], pM[:, 0:16], AF.Exp, [b_pM], [b_ecs])
                g.cp("act", xsb[j2][:], xs[i][:], [b_in[i]], [b_xsb[j2]])
                g.tt("dve", R[:], tri[:, None, :].to_broadcast([128, 16, 128]), dta[:, :, None].to_broadcast([128, 16, 128]),
                     ALU.mult, [b_c, b_in[i]], [b_R])
                drain(2)
                for gi in range(2):
                    g.mm(pO[:, gi * 512:(gi + 1) * 512], bct[i][:, 2 + gi, :], Sbf[:, gi * 512:(gi + 1) * 512], True, True,
                         [b_in[i], b_Sbf], [b_pO])
                for half in range(2):
                    hs = slice(half * 8, half * 8 + 8)
                    for q in range(2):
                        h0 = half * 8 + q * 4
                        g.mm(pA[:, q * 512:(q + 1) * 512], identf[:], maskB[:, h0:h0 + 4, :].rearrange("p h t -> p (h t)"),
                             True, False, [b_c], [b_pA])
                        g.mm(pA[:, q * 512:(q + 1) * 512], ones[:], R[:, h0:h0 + 4, :].rearrange("p h t -> p (h t)"),
                             False, True, [b_c, b_R], [b_pA])
                    g.tt("dve", seg[:, hs, :], pA[:].rearrange("p (h t) -> p h t", h=8),
                         cs_sb[:, hs][:, :, None].to_broadcast([128, 8, 128]), ALU.subtract, [b_pA, b_cs], [b_seg])
                    g.act(decB[:, hs], pA[:].rearrange("p (h t) -> p h t", h=8)[:, :, last], AF.Exp, [b_pA], [b_decB])
                    if half == 1:
                        pass
                    drain(2)
                    g.act(seg[:, hs, :], seg[:, hs, :], AF.Exp, [b_seg], [b_seg])
                    g.tt("pool" if half == 0 else "dve", MT[:, hs, :], seg[:, hs, :],
                         cbT[:, half:half + 1, :].to_broadcast([128, 8, 128]), ALU.mult, [b_seg, b_cbT], [b_MT])
                drain(2)
                for h in range(16):
                    g.mm(pY[:, h * 64:(h + 1) * 64], MT[:, h, :], xsb[j2][:, h * 64:(h + 1) * 64], True, True,
                         [b_MT, b_xsb[j2]], [b_pY])
                drain(2)
                g.tt("dve", t1[:].rearrange("p (h q) -> p h q", h=16), pO[:].rearrange("p (h q) -> p h q", h=16),
                     ecs[:, :, None].to_broadcast([128, 16, 64]), ALU.mult, [b_pO, b_ecs], [b_t1])
                g.tt("dve", yd[j2][:], pY[:], t1[:], ALU.add, [b_pY, b_t1], [b_yd[j2]])
                g.tt("pool", xd[:].rearrange("p (h q) -> p h q", h=16), xsb[j2][:].rearrange("p (h q) -> p h q", h=16),
                     seg[:, :, last:last + 1].to_broadcast([128, 16, 64]), ALU.mult, [b_xsb[j2], b_seg], [b_xd])
                drain(2)
                for gi in range(2):
                    g.mm(pO[:, gi * 512:(gi + 1) * 512], Bt[i][:, gi * 128:(gi + 1) * 128], xd[:, gi * 512:(gi + 1) * 512],
                         True, True, [b_in[i], b_xd], [b_pO])
                g.tt("pool", S[:].rearrange("p (h q) -> p h q", h=16), S[:].rearrange("p (h q) -> p h q", h=16),
                     decB[:, :, None].to_broadcast([128, 16, 64]), ALU.mult, [b_S, b_decB], [b_S])
                g.tt("dve", S[:], S[:], pO[:], ALU.add, [b_S, b_pO], [b_S])
                g.cp("act", Sbf[:], S[:], [b_S], [b_Sbf])
                if d == 0:
                    g.dma("pool", yf_d[tk, :], yd[j2][:], b_yd[j2], reads=[b_yd[j2]])
                else:
                    drain(len(pend))
                    y = yd[j2]
                    b_y = b_yd[j2]
                    yfl_i, zt_i, xs_i, b2_i, bi_i, yTs_j, b_yTs_j = yfl[i], zt[i], xs[i], b_in2[i], b_in[i], yTs[j2], b_yTs[j2]

                    def mk(y=y, b_y=b_y, yfl_i=yfl_i, zt_i=zt_i, xs_i=xs_i, b2_i=b2_i, bi_i=bi_i, yTs_j=yTs_j,
                           b_yTs_j=b_yTs_j, tk=tk):
                        ops = []
                        ops.append(lambda: g.tt("pool", y[:], y[:], yfl_i[:], ALU.add, [b_y, b2_i], [b_y]))
                        ops.append(lambda: g.tt("pool", t2[:].rearrange("p (h q) -> p h q", h=16),
                                                xs_i[:].rearrange("p (h q) -> p h q", h=16),
                                                dB[:, :, None].to_broadcast([128, 16, 64]), ALU.mult, [bi_i, b_c], [b_t2]))
                        ops.append(lambda: g.act(zt_i[:], zt_i[:], AF.Silu, [b2_i], [b2_i]))
                        ops.append(lambda: g.tt("dve", y[:], y[:], t2[:], ALU.add, [b_y, b_t2], [b_y]))
                        ops.append(lambda: g.tt("dve", y[:], y[:], zt_i[:], ALU.mult, [b_y, b2_i], [b_y]))
                        ops.append(lambda: g.act(sq[:], y[:], AF.Square, [b_y], [b_sq]))
                        ops.append(lambda: g.red("dve", gs[:, 0:2], sq[:].rearrange("p (g q) -> p g q", g=2), [b_sq], [b_gs]))
                        ops.append(lambda: g.ts("dve", gs[:, 2:4], gs[:, 0:2], 1.0 / 512, EPS, ALU.mult, ALU.add, [b_gs], [b_gs]))
                        ops.append(lambda: g.act(gs[:, 2:4], gs[:, 2:4], AF.Sqrt, [b_gs], [b_gs]))
                        ops.append(lambda: g._record("dve", lambda e: e.reciprocal(out=gs[:, 2:4], in_=gs[:, 2:4]),
                                                     [b_gs], [b_gs]))
                        ops.append(lambda: g.tt("dve", y[:].rearrange("p (g q) -> p g q", g=2),
                                                y[:].rearrange("p (g q) -> p g q", g=2),
                                                gs[:, 2:4][:, :, None].to_broadcast([128, 2, 512]), ALU.mult, [b_y, b_gs], [b_y]))
                        ops.append(lambda: g.tt("pool", ynb[:], y[:], nwB[:], ALU.mult, [b_y, b_c], [b_ynb]))

                        def trs():
                            for k in range(8):
                                g.tr(pT[:, k * 128:(k + 1) * 128], ynb[:, k * 128:(k + 1) * 128], identb[:], [b_ynb, b_c], [b_pT])
                        ops.append(trs)
                        ops.append(lambda: g.cp("act", yTs_j[:], pT[:].rearrange("p (k t) -> p k t", k=8), [b_pT], [b_yTs_j]))
                        ops.append(lambda: g.dma("pool", yT[0][:, :, tk].rearrange("k p t -> p k t"), yTs_j[:], b_yTs_j,
                                                 reads=[b_yTs_j]))
                        return ops
                    pend.extend(mk())
            drain(len(pend))
            g.flush(ENGS)

    gla_gate_up = din("gla_gate_up", [DEPTH, 2, 16, 512])
    gla_gate_b = din("gla_gate_b", [DEPTH, 2, 512])
    gla_norm_w = din("gla_norm_w", [DEPTH, 256])
    of_d = dscr("of", [NT, 1024])

    def phase_gla_pass(l, d):
        with ExitStack() as es:
            triS = sb(es, "triS", [128, 128], F32)
            uS = sb(es, "uS", [128, 128], F32)
            m01 = sb(es, "m01", [128, 128], F32)
            ones = sb(es, "ones", [128, 128], F32)
            up = sb(es, "up", [16, 512], F32)
            gbr = sb(es, "gbr", [1, 512], F32)
            fl = sb(es, "fl", [128, 2], F32)
            b_c = Buf("consts")
            qT = [sb(es, "qT%d" % i, [128, 4, 128], F32) for i in range(2)]
            kT = [sb(es, "kT%d" % i, [128, 4, 128], F32) for i in range(2)]
            ktm = [sb(es, "ktm%d" % i, [128, 512], F32) for i in range(2)]
            v = [sb(es, "v%d" % i, [128, 1024], BF16) for i in range(2)]
            dn = [sb(es, "dn%d" % i, [16, 128], F32) for i in range(2)]
            b_in = [Buf("in0"), Buf("in1")]
            lsp_ = [sb(es, "lsp%d" % i, [128, 512], F32) for i in range(2)]
            b_lsp_ = [Buf("lsp0"), Buf("lsp1")]
            eb_ = [sb(es, "eb%d" % i, [128, 4, 128], F32) for i in range(2)]
            enb_ = [sb(es, "enb%d" % i, [128, 4, 128], F32) for i in range(2)]
            er_ = [sb(es, "er%d" % i, [128, 512], F32) for i in range(2)]
            b_eb_, b_enb_, b_er_ = [Buf("eb0"), Buf("eb1")], [Buf("enb0"), Buf("enb1")], [Buf("er0"), Buf("er1")]
            qe_ = [sb(es, "qe%d" % i, [128, 4, 128], BF16) for i in range(2)]
            ke_ = [sb(es, "ke%d" % i, [128, 4, 128], BF16) for i in range(2)]
            kdec_ = [sb(es, "kdec%d" % i, [128, 512], BF16) for i in range(2)]
            att_ = [sb(es, "att%d" % i, [128, 4, 128], BF16) for i in range(2)]
            b_qe_, b_ke_ = [Buf("qe0"), Buf("qe1")], [Buf("ke0"), Buf("ke1")]
            b_kdec_, b_att_ = [Buf("kdec0"), Buf("kdec1")], [Buf("att0"), Buf("att1")]
            S = sb(es, "S", [128, 4, 256], F32)
            Sbf = sb(es, "Sbf", [128, 4, 256], BF16)
            b_S, b_Sbf = Buf("S"), Buf("Sbf")
            od = [sb(es, "od%d" % i, [128, 1024], F32) for i in range(2)]
            b_od = [Buf("od0"), Buf("od1")]
            pLR = ps(es, "pLR", [128, 512], F32)
            pB = ps(es, "pB", [128, 512], F32)
            pAt = ps(es, "pAt", [128, 512], F32)
            pOo = ps(es, "pOo", [128, 1024], F32)
            pSp = ps(es, "pSp", [128, 1024], F32)
            b_pLR, b_pB, b_pAt, b_pOo, b_pSp = Buf("pLR"), Buf("pB"), Buf("pAt"), Buf("pOo"), Buf("pSp")
            if d == 1:
                identb = sb(es, "identb", [128, 128], BF16)
                nwB = sb(es, "nwB", [128, 1024], F32)
                ofl = [sb(es, "ofl%d" % i, [128, 1024], F32) for i in range(2)]
                gg = [sb(es, "gg%d" % i, [128, 1024], F32) for i in range(2)]
                b_in2 = [Buf("in2_0"), Buf("in2_1")]
                sq = sb(es, "sq", [128, 1024], F32)
                b_sq = Buf("sq")
                gs = sb(es, "gs", [128, 8], F32)
                b_gs = Buf("gs")
                onb = sb(es, "onb", [128, 1024], BF16)
                b_onb = Buf("onb")
                yTs = [sb(es, "yTs%d" % i, [128, 8, 128], BF16) for i in range(2)]
                b_yTs = [Buf("yTs0"), Buf("yTs1")]
                pT = ps(es, "pT", [128, 1024], BF16)
                b_pT = Buf("pT")
                g.dma("pool", identb[:], ident_in, b_c, writes=[b_c])
                for h in range(4):
                    g.dma("sp", nwB[:, h * 256:(h + 1) * 256], gla_norm_w[l:l + 1, :].partition_broadcast(128), b_c,
                          writes=[b_c])
            g.dma("sp", triS[:], tri_in[d], b_c, writes=[b_c])
            g.dma("sp", uS[:], ustrict_in[d], b_c, writes=[b_c])
            g.dma("sp", m01[:], mask01_in[d], b_c, writes=[b_c])
            g.dma("sp", ones[:], ones_in, b_c, writes=[b_c])
            g.dma("sp", up[:], gla_gate_up[l, d], b_c, writes=[b_c])
            g.dma("sp", gbr[:], gla_gate_b[l, d:d + 1, :], b_c, writes=[b_c])
            g.dma("sp", fl[:], flag, b_c, writes=[b_c])
            g.ts("dve", triS[:], triS[:], -1.0 / 16.0, None, ALU.mult, None, [b_c], [b_c])
            g.ts("dve", uS[:], uS[:], -1.0 / 16.0, None, ALU.mult, None, [b_c], [b_c])
            g.memset("dve", S[:], 0.0, [b_S])
            g.memset("pool", Sbf[:], 0.0, [b_Sbf])
            last = 127 if d == 0 else 0
            order = list(range(NCH)) if d == 0 else list(range(NCH - 1, -1, -1))
            for n, c in enumerate(order):
                i = n % 2
                lsp, b_lsp, eb, b_eb, enb, b_enb, er, b_er = lsp_[i], b_lsp_[i], eb_[i], b_eb_[i], enb_[i], b_enb_[i], er_[i], b_er_[i]
                qe, b_qe, ke, b_ke, kdec, b_kdec, att, b_att = qe_[i], b_qe_[i], ke_[i], b_ke_[i], kdec_[i], b_kdec_[i], att_[i], b_att_[i]
                tk = slice(c * 128, (c + 1) * 128)
                g.dma("sp", qT[i][:], gqT[:, :, tk].rearrange("h p t -> p h t"), b_in[i], writes=[b_in[i]])
                g.dma("sp", kT[i][:], gkT[:, :, tk].rearrange("h p t -> p h t"), b_in[i], writes=[b_in[i]])
                g.dma("sp", ktm[i][:], kg_tm[tk, :], b_in[i], writes=[b_in[i]])
                g.dma("sp", v[i][:], vg_tm[tk, :], b_in[i], writes=[b_in[i]])
                g.dma("sp", dn[i][:], dnT[d, :, tk], b_in[i], writes=[b_in[i]])
                if d == 1:
                    g.dma("sp", ofl[i][:], of_d[tk, :], b_in2[i], writes=[b_in2[i]])
                    g.dma("sp", gg[i][:], gg_tm[tk, :], b_in2[i], writes=[b_in2[i]])
                if (d == 0 and c == MIDC) or (d == 1 and c == MIDC - 1):
                    g.ts("dve", S[:], S[:], fl[:, 0:1], None, ALU.mult, None, [b_S, b_c], [b_S])
                    g.ts("pool", Sbf[:], Sbf[:], fl[:, 0:1], None, ALU.mult, None, [b_Sbf, b_c], [b_Sbf])
                g.mm(pLR[:], dn[i][:], up[:], True, False, [b_in[i], b_c], [b_pLR])
                g.mm(pLR[:], ones[0:1, :], gbr[:], False, True, [b_c], [b_pLR])
                g.act(lsp[:], pLR[:], AF.Exp, [b_pLR], [b_lsp], scale=-1.0)
                g.ts("dve", lsp[:], lsp[:], 1.0, None, ALU.add, None, [b_lsp], [b_lsp])
                g.act(lsp[:], lsp[:], AF.Ln, [b_lsp], [b_lsp])
                for h in range(4):
                    g.mm(pB[:, h * 128:(h + 1) * 128], lsp[:, h * 128:(h + 1) * 128], triS[:], True, True,
                         [b_lsp, b_c], [b_pB])
                g.mm(pLR[:], uS[:], lsp[:], True, True, [b_c, b_lsp], [b_pLR])
                pB3 = pB[:].rearrange("p (h t) -> p h t", h=4)
                g.act(eb[:], pB3, AF.Exp, [b_pB], [b_eb])
                g.act(enb[:], pB3, AF.Exp, [b_pB], [b_enb], scale=-1.0)
                g.act(er[:], pLR[:], AF.Exp, [b_pLR], [b_er])
                g.tt("dve", qe[:], qT[i][:], eb[:], ALU.mult, [b_in[i], b_eb], [b_qe])
                g.tt("pool", ke[:], kT[i][:], enb[:], ALU.mult, [b_in[i], b_enb], [b_ke])
                g.tt("pool", kdec[:], ktm[i][:], er[:], ALU.mult, [b_in[i], b_er], [b_kdec])
                for h in range(4):
                    g.mm(pAt[:, h * 128:(h + 1) * 128], ke[:, h, :], qe[:, h, :], True, True, [b_ke, b_qe], [b_pAt])
                g.tt("dve", att[:], pAt[:].rearrange("p (h t) -> p h t", h=4), m01[:, None, :].to_broadcast([128, 4, 128]),
                     ALU.mult, [b_pAt, b_c], [b_att])
                for h in range(4):
                    g.mm(pOo[:, h * 256:(h + 1) * 256], att[:, h, :], v[i][:, h * 256:(h + 1) * 256], True, False,
                         [b_att, b_in[i]], [b_pOo])
                    g.mm(pOo[:, h * 256:(h + 1) * 256], qe[:, h, :], Sbf[:, h, :], False, True, [b_qe, b_Sbf], [b_pOo])
                for h in range(4):
                    g.mm(pSp[:, h * 256:(h + 1) * 256], kdec[:, h * 128:(h + 1) * 128], v[i][:, h * 256:(h + 1) * 256],
                         True, True, [b_kdec, b_in[i]], [b_pSp])
                for h in range(4):
                    g.stt("dve", S[:, h, :], S[:, h, :], eb[:, h, last:last + 1], pSp[:, h * 256:(h + 1) * 256],
                          ALU.mult, ALU.add, [b_S, b_eb, b_pSp], [b_S])
                g.cp("act", Sbf[:], S[:], [b_S], [b_Sbf])
                if d == 0:
                    g.cp("act", od[i][:], pOo[:], [b_pOo], [b_od[i]])
                    g.dma("pool", of_d[tk, :], od[i][:], b_od[i], reads=[b_od[i]])
                else:
                    o = od[i]
                    g.tt("dve", o[:], pOo[:], ofl[i][:], ALU.add, [b_pOo, b_in2[i]], [b_od[i]])
                    g.act(sq[:], o[:], AF.Square, [b_od[i]], [b_sq])
                    g.red("dve", gs[:, 0:4], sq[:].rearrange("p (h q) -> p h q", h=4), [b_sq], [b_gs])
                    g.ts("dve", gs[:, 4:8], gs[:, 0:4], 1.0 / 256, EPS, ALU.mult, ALU.add, [b_gs], [b_gs])
                    g.act(gs[:, 4:8], gs[:, 4:8], AF.Sqrt, [b_gs], [b_gs])
                    g._record("dve", lambda e, gs=gs: e.reciprocal(out=gs[:, 4:8], in_=gs[:, 4:8]), [b_gs], [b_gs])
                    g.tt("dve", o[:].rearrange("p (h q) -> p h q", h=4), o[:].rearrange("p (h q) -> p h q", h=4),
                         gs[:, 4:8][:, :, None].to_broadcast([128, 4, 256]), ALU.mult, [b_od[i], b_gs], [b_od[i]])
                    g.tt("pool", o[:], o[:], nwB[:], ALU.mult, [b_od[i], b_c], [b_od[i]])
                    g.act(gg[i][:], gg[i][:], AF.Silu, [b_in2[i]], [b_in2[i]])
                    g.tt("pool", onb[:], o[:], gg[i][:], ALU.mult, [b_od[i], b_in2[i]], [b_onb])
                    for k in range(8):
                        g.tr(pT[:, k * 128:(k + 1) * 128], onb[:, k * 128:(k + 1) * 128], identb[:], [b_onb, b_c], [b_pT])
                    g.cp("act", yTs[i][:], pT[:].rearrange("p (k t) -> p k t", k=8), [b_pT], [b_yTs[i]])
                    g.dma("pool", yT[1][:, :, tk].rearrange("k p t -> p k t"), yTs[i][:], b_yTs[i], reads=[b_yTs[i]])
            g.flush(ENGS)

    na_rpb = din("na_rpb", [DEPTH, 16, 15, 31])
    jflip_in = din("jflip", [64, 64])
    mint_in = din("m_int", [128, 16, 64])
    medge_in = din("m_edge", [128, 16, 64])
    rpbpad = dscr("rpbpad", [16, 17, 192])

    def phase_na(l):
        with ExitStack() as es:
            T2 = [sb(es, "T2i", [128, 16, 16, 64], BF16), sb(es, "T2e", [128, 16, 16, 64], BF16)]
            b_T2 = Buf("T2")
            Mx = [sb(es, "Mi", [128, 16, 64], F32), sb(es, "Me", [128, 16, 64], F32)]
            jf = sb(es, "jf", [64, 64], F32)
            identb = sb(es, "identb", [128, 128], BF16)
            fl = sb(es, "fl", [128, 2], F32)
            b_c = Buf("consts")
            padt = sb(es, "padt", [16, 17, 192], F32)
            rp = sb(es, "rp", [16, 15, 31], F32)
            b_pad = Buf("pad")
            Tpp = [sb(es, "Tpp%d" % i, [64, 4, 17, 64], F32) for i in range(2)]
            b_Tpp = [Buf("Tpp0"), Buf("Tpp1")]
            eW = [sb(es, "eW%d" % i, [128, 640], F32) for i in range(3)]
            b_eW = [Buf("eW%d" % i) for i in range(3)]
            Kc = [sb(es, "Kc%d" % i, [128, 8, 128], BF16) for i in range(8)]
            Vc = [sb(es, "Vc%d" % i, [128, 16, 65], BF16) for i in range(8)]
            b_K = [Buf("K%d" % i) for i in range(8)]
            b_V = [Buf("V%d" % i) for i in range(8)]
            Qc = [sb(es, "Qc%d" % i, [128, 8, 128], BF16) for i in range(2)]
            b_Q = [Buf("Q0"), Buf("Q1")]
            Pt = [sb(es, "Pt%d" % i, [128, 5, 2, 64], BF16) for i in range(3)]
            b_P = [Buf("P%d" % i) for i in range(3)]
            rden = sb(es, "rden", [128, 16], F32)
            b_rden = Buf("rden")
            oA = sb(es, "oA", [128, 1024], F32)
            oB = sb(es, "oB", [128, 1024], F32)
            b_oA, b_oB = Buf("oA"), Buf("oB")
            onb = sb(es, "onb", [128, 1024], BF16)
            b_onb = Buf("onb")
            yTs = [sb(es, "yTs%d" % i, [128, 8, 128], BF16) for i in range(2)]
            b_yTs = [Buf("yTs0"), Buf("yTs1")]
            pS_t = [ps(es, "pS%d" % i, [128, 1024], F32) for i in range(2)]
            pS = [t[:, 0:640] for t in pS_t]
            b_pS = [Buf("pS%d" % i) for i in range(2)]
            pOv = [ps(es, "pOv%d" % i, [128, 512], F32) for i in range(3)]
            b_pOv = Buf("pOv")
            pT = ps(es, "pT", [128, 1024], BF16)
            b_pT = Buf("pT")
            g.dma("sp", Mx[0][:], mint_in, b_c, writes=[b_c])
            g.dma("sp", Mx[1][:], medge_in, b_c, writes=[b_c])
            g.dma("sp", jf[:], jflip_in, b_c, writes=[b_c])
            g.dma("sp", fl[:], flag, b_c, writes=[b_c])
            g.dma("pool", identb[:], ident_in, b_c, writes=[b_c])
            for i in range(8):
                g.memset("pool", Vc[i][:], 1.0, [b_V[i]])
            g.memset("dve", padt[:], 0.0, [b_pad])
            g.dma("sp", rp[:], na_rpb[l], b_pad, writes=[b_pad])
            g.cp("dve", padt[:, 1:16, 64:95], rp[:], [b_pad], [b_pad])
            g.dma("sp", rpbpad, padt[:], b_pad, reads=[b_pad], writes=[b_pad])
            nb = 0
            for hg in range(4):
                tp = Tpp[hg % 2]
                b_tp = b_Tpp[hg % 2]
                for hh in range(4):
                    h = hg * 4 + hh
                    src = bass.AP(tensor=rpbpad.tensor, offset=h * 17 * 192 + 16, ap=[[1, 64], [192, 17], [1, 64]])
                    g.dma("sp", tp[:, hh, :, :], src, b_tp, reads=[b_pad], writes=[b_tp])
                for hh in range(4):
                    h = hg * 4 + hh
                    for sbt in range(2):
                        q = nb % 2
                        nb += 1
                        for si in range(8):
                            s_ = sbt * 8 + si
                            g.mm(pS[q][:, si * 64:(si + 1) * 64], tp[:, hh, s_:s_ + 2, :].rearrange("p r k -> p (r k)"),
                                 jf[:], True, True, [b_tp, b_c], [b_pS[q]])
                        g.act(eW[q][:, 0:512], pS[q][:, 0:512], AF.Exp, [b_pS[q]], [b_eW[q]])
                        g.tt("dve", T2[0][:, h, sbt * 8:sbt * 8 + 8, :], eW[q][:, 0:512].rearrange("p (s q) -> p s q", s=8),
                             Mx[0][:, sbt * 8:sbt * 8 + 8, :], ALU.mult, [b_eW[q], b_c], [b_T2])
                        g.tt("pool", T2[1][:, h, sbt * 8:sbt * 8 + 8, :], eW[q][:, 0:512].rearrange("p (s q) -> p s q", s=8),
                             Mx[1][:, sbt * 8:sbt * 8 + 8, :], ALU.mult, [b_eW[q], b_c], [b_T2])
            if NA_SPLIT:
                g.flush(ENGS)
            loaded = [-1]
            cnt = {"s": 0, "p": 0, "y": 0}

            def ensure(upto):
                while loaded[0] < min(upto, NCH - 1):
                    kt = loaded[0] + 1
                    sl_ = kt % 8
                    tk_ = slice(kt * 128, (kt + 1) * 128)
                    g.dma("sp", Kc[sl_][:], nkT[:, :, tk_].rearrange("c p t -> p c t"), b_K[sl_], writes=[b_K[sl_]])
                    g.dma("sp", Vc[sl_][:, :, 0:64], vn_tm[tk_, :].rearrange("p (h d) -> p h d", h=16), b_V[sl_],
                          writes=[b_V[sl_]])
                    loaded[0] = kt

            def na_pair(qp, kts, var, qi, o_dst, b_o):
                nt = len(kts)
                d0 = kts[0] - qp

                def st_s(h):
                    ch, p0 = h // 2, (h % 2) * 64
                    q = h % 2
                    for ti, kt in enumerate(kts):
                        g.mm(pS[q][:, ti * 128:(ti + 1) * 128], Kc[kt % 8][p0:p0 + 64, ch, :], Qc[qi][p0:p0 + 64, ch, :],
                             True, True, [b_K[kt % 8], b_Q[qi]], [b_pS[q]])

                def st_e(h):
                    q = h % 3
                    g.act(eW[q][:, 0:nt * 128], pS[h % 2][:, 0:nt * 128], AF.Exp, [b_pS[h % 2]], [b_eW[q]])
                    e4 = eW[q][:, 0:nt * 128].rearrange("p (t r c) -> p t r c", t=nt, r=2)
                    for qr2 in range(2):
                        s0 = 2 * d0 + 8 - qr2
                        base = T2[var][:, h, s0, :]
                        tv = bass.AP(tensor=base.tensor, offset=base.offset, ap=[list(base.ap[0]), [128, nt], [1, 64]])
                        g.tt("dve" if qr2 == 0 else "pool", Pt[q][:, 0:nt, qr2, :], e4[:, :, qr2, :], tv, ALU.mult,
                             [b_eW[q], b_T2], [b_P[q]])

                def st_v(h):
                    q = h % 3
                    bank, off = h // 7, (h % 7) * 65
                    for ti, kt in enumerate(kts):
                        g.mm(pOv[bank][:, off:off + 65], Pt[q][:, ti, :, :].rearrange("p r c -> p (r c)"),
                             Vc[kt % 8][:, h, :], ti == 0, ti == nt - 1, [b_P[q], b_V[kt % 8]], [b_pOv])

                LAG = NA_LAG
                for step in range(16 + 2 * LAG):
                    if step < 16:
                        st_s(step)
                    if LAG <= step < 16 + LAG:
                        st_e(step - LAG)
                    if step >= 2 * LAG:
                        st_v(step - 2 * LAG)
                for bank in range(3):
                    nh = 7 if bank < 2 else 2
                    pv = pOv[bank][:, 0:nh * 65].rearrange("p (h e) -> p h e", h=nh)
                    g._record("dve", lambda e, bank=bank, nh=nh, pv=pv: e.reciprocal(out=rden[:, bank * 7:bank * 7 + nh],
                                                                                   in_=pv[:, :, 64]), [b_pOv], [b_rden])
                    g.tt("dve", o_dst[:, bank * 448:bank * 448 + nh * 64].rearrange("p (h d) -> p h d", h=nh), pv[:, :, 0:64],
                         rden[:, bank * 7:bank * 7 + nh][:, :, None].to_broadcast([128, nh, 64]), ALU.mult,
                         [b_pOv, b_rden], [b_o])

            for qp in range(NCH):
                qi = qp % 2
                tk = slice(qp * 128, (qp + 1) * 128)
                ensure(qp + 3)
                g.dma("sp", Qc[qi][:], nqT[:, :, tk].rearrange("c p t -> p c t"), b_Q[qi], writes=[b_Q[qi]])
                interior = [qp - 2, qp - 1, qp, qp + 1, qp + 2]
                if qp < 2:
                    na_pair(qp, [0, 1, 2, 3], 1, qi, onb, b_onb)
                elif qp >= NCH - 2:
                    na_pair(qp, [NCH - 4, NCH - 3, NCH - 2, NCH - 1], 1, qi, onb, b_onb)
                elif MIDC - 2 <= qp < MIDC + 2:
                    na_pair(qp, interior, 0, qi, oA, b_oA)
                    ekts = [MIDC - 4, MIDC - 3, MIDC - 2, MIDC - 1] if qp < MIDC else [MIDC, MIDC + 1, MIDC + 2, MIDC + 3]
                    na_pair(qp, ekts, 1, qi, oB, b_oB)
                    g.ts("dve", oA[:], oA[:], fl[:, 0:1], None, ALU.mult, None, [b_oA, b_c], [b_oA])
                    g.stt("dve", onb[:], oB[:], fl[:, 1:2], oA[:], ALU.mult, ALU.add, [b_oB, b_oA, b_c], [b_onb])
                else:
                    na_pair(qp, interior, 0, qi, onb, b_onb)
                yi = cnt["y"] % 2
                cnt["y"] += 1
                for k in range(8):
                    g.tr(pT[:, k * 128:(k + 1) * 128], onb[:, k * 128:(k + 1) * 128], identb[:], [b_onb, b_c], [b_pT])
                g.cp("act", yTs[yi][:], pT[:].rearrange("p (k t) -> p k t", k=8), [b_pT], [b_yTs[yi]])
                g.dma("pool", yT[2][:, :, tk].rearrange("k p t -> p k t"), yTs[yi][:], b_yTs[yi], reads=[b_yTs[yi]])
            g.flush(ENGS, sched=False)

    stages = []
    stages.append(("p0", phase_p0))
    for l in range(DEPTH):
        x_src = x_in if l == 0 else xc
        stages.append(("ffn1_%d" % l, lambda l=l, x_src=x_src: phase_ffn(
            "f1l%d" % l, ffn1_w_in[l], ffn1_w_out[l], x_src, hT_f1, xa, ln_mix_w[l:l + 1, :], hT_mix)))
        stages.append(("ptm_%d" % l, lambda l=l: phase_inproj_tm(l)))
        stages.append(("pfm_%d" % l, lambda l=l: phase_inproj_fm(l)))
        stages.append(("ssdprep_%d" % l, lambda l=l: phase_ssd_prep(l)))
        stages.append(("ssdf_%d" % l, lambda l=l: phase_ssd_pass(l, 0)))
        stages.append(("ssdb_%d" % l, lambda l=l: phase_ssd_pass(l, 1)))
        stages.append(("glaf_%d" % l, lambda l=l: phase_gla_pass(l, 0)))
        stages.append(("glab_%d" % l, lambda l=l: phase_gla_pass(l, 1)))
        stages.append(("na_%d" % l, lambda l=l: phase_na(l)))
        stages.append(("merge_%d" % l, lambda l=l: phase_merge(l, xa, xb, ln_ffn2_w[l:l + 1, :], hT_f2)))
        if l + 1 < DEPTH:
            stages.append(("ffn2_%d" % l, lambda l=l: phase_ffn(
                "f2l%d" % l, ffn2_w_in[l], ffn2_w_out[l], xb, hT_f2, xc, ln_ffn1_w[l + 1:l + 2, :], hT_f1)))
        else:
            stages.append(("ffn2_%d" % l, lambda l=l: phase_ffn(
                "f2l%d" % l, ffn2_w_in[l], ffn2_w_out[l], xb, hT_f2, None, ln_final_w[0:1, :], None, final_out=y_out)))
    only = getattr(cfg, "only", None)
    for name, fn in stages:
        if only is None or name in only:
            fn()
        if cfg.stop_after == name:
            break
    g.close()
    return nc, g


def _consts():
    k = np.arange(128)
    tri_f = (k[:, None] <= k[None, :]).astype(np.float32)
    tri_b = (k[:, None] >= k[None, :]).astype(np.float32)
    allow_f = (k[:, None] <= k[None, :])
    allow_b = (k[:, None] >= k[None, :])
    c = {
        "ident": np.eye(128, dtype=np.float32),
        "ones": np.ones((128, 128), np.float32),
        "tri": np.stack([tri_f, tri_b]),
        "maskadd": np.stack([np.where(allow_f, 0.0, -30000.0), np.where(allow_b, 0.0, -30000.0)]).astype(np.float32),
        "mask01": np.stack([allow_f, allow_b]).astype(np.float32),
        "ustrict": np.stack([(k[:, None] > k[None, :]), (k[:, None] < k[None, :])]).astype(np.float32),
    }
    kc = np.arange(64)[:, None]
    qc = np.arange(64)[None, :]
    cs = np.clip(qc - 8, 0, 48)
    colmask = ((kc >= cs) & (kc < cs + 16)).astype(np.float32)
    m_int = np.zeros((128, 16, 64), np.float32)
    m_edge = np.zeros((128, 16, 64), np.float32)
    for kr2 in range(2):
        for s_ in range(16):
            ro = s_ + kr2 - 1
            if 0 <= ro <= 14:
                m_edge[kr2 * 64:(kr2 + 1) * 64, s_, :] = colmask
            if 3 <= ro <= 10:
                m_int[kr2 * 64:(kr2 + 1) * 64, s_, :] = colmask
    jflip = np.zeros((64, 64), np.float32)
    jflip[63 - np.arange(64), np.arange(64)] = 1.0
    c.update({"jflip": jflip, "m_int": m_int, "m_edge": m_edge})
    return c


_PROGRAM_CACHE = {}


def run_streams(streams, flags, w, cfg_extra=None):
    nt = streams[0].shape[0]
    cfg = Cfg(nt)
    nc, g = build_program(cfg)
    base = dict(_consts())
    f32 = np.float32
    for k in ("ln_ffn1_w", "ffn1_w_in", "ffn1_w_out", "ln_mix_w", "w_in", "ssm_conv_w", "ssm_conv_b", "ssm_d",
              "ssm_norm_w", "gla_gate_up", "gla_gate_b", "gla_norm_w", "na_rpb", "w_branch_a", "w_branch_b",
              "w_branch_c", "w_out", "ln_ffn2_w", "ffn2_w_in", "ffn2_w_out"):
        base[k] = np.ascontiguousarray(np.asarray(w[k], dtype=f32))
    base["ssm_dt_bias"] = np.ascontiguousarray(np.asarray(w["ssm_dt_bias"], f32).reshape(DEPTH, 32))
    base["ssm_a_log"] = np.ascontiguousarray(np.asarray(w["ssm_a_log"], f32).reshape(DEPTH, 32))
    base["ln_final_w"] = np.ascontiguousarray(np.asarray(w["ln_final_w"], f32).reshape(1, D))
    in_maps = []
    for x, f in zip(streams, flags):
        m = dict(base)
        m["x"] = np.ascontiguousarray(np.asarray(x, f32))
        m["flag"] = np.tile(np.array([[f, 1.0 - f]], f32), (128, 1))
        in_maps.append(m)
    res = run_bass_kernel_spmd(nc, in_maps, core_ids=list(range(len(in_maps))))
    return [np.asarray(r["y"], dtype=f32) for r in res.results]


def kernel(**inputs):
    xp = np.asarray(inputs["x_prompt"], np.float32)
    xs = np.asarray(inputs["x_sample"], np.float32)
    B, S, _ = xp.shape
    B2, S2, _ = xs.shape
    assert S2 == 2 * S and B % 2 == 0
    streams, flags = [], []
    for i in range(B // 2):
        streams.append(xp[2 * i:2 * i + 2].reshape(2 * S, D))
        flags.append(0.0)
    for i in range(B2):
        streams.append(xs[i])
        flags.append(1.0)
    outs = run_streams(streams, flags, inputs)
    yp = np.stack([o.reshape(2, S, D) for o in outs[:B // 2]]).reshape(B, S, D)
    ys = np.stack(outs[B // 2:]).reshape(B2, S2, D)
    return (yp.astype(np.float32), ys.astype(np.float32))
```

```python
import numpy as np
from contextlib import ExitStack
import concourse.bass as bass
import concourse.mybir as mybir
from concourse.bass_utils import run_bass_kernel_spmd

F32 = mybir.dt.float32
BF16 = mybir.dt.bfloat16
AF = mybir.ActivationFunctionType
ALU = mybir.AluOpType
AX = mybir.AxisListType

D = 1024
DFF = 2816
DEPTH = 2
EPS = 1e-6
ENGS = ("pe", "act", "dve", "pool", "sp")
NA_LAG = 1
STRICT = True


class Buf:
    __slots__ = ("name", "w", "r", "sem", "excl")

    def __init__(self, name):
        self.name = name
        self.w = []
        self.r = []
        self.sem = None
        self.excl = name.startswith("p") and name != "pad"


class Op:
    __slots__ = ("eng", "seq", "fn", "rw", "war", "signal", "sigval", "dma", "waits", "cost",
                 "deps", "nun", "users", "ready", "done", "pos")

    def __init__(self, eng, seq, fn, rw, war, dma, cost):
        self.eng = eng
        self.seq = seq
        self.fn = fn
        self.rw = rw
        self.war = war
        self.signal = False
        self.sigval = 0
        self.dma = dma
        self.waits = None
        self.cost = cost


def _fsize(ap):
    n = 1
    for d_ in tuple(ap.shape)[1:]:
        n *= int(d_)
    return n


DEBUG_LINES = False
SPREAD_DIES = True
NA_SPLIT = False
SCHED = True
SCHED_K = 6
SYNC_NS = 180.0
DMA_LAT_NS = 2600.0


class Graph:
    def __init__(self, nc, n_dma_sems=56):
        self.nc = nc
        self.es = ExitStack()
        self.eng_sem = {e: self.es.enter_context(nc.semaphore("s_" + e)) for e in ENGS}
        self.eng_cnt = {e: 0 for e in ENGS}
        self.dma_sem = [self.es.enter_context(nc.semaphore("d%d" % i)) for i in range(n_dma_sems)]
        self.dma_cnt = [0] * n_dma_sems
        self.next_slot = 0
        self.ops = {e: [] for e in ENGS}
        self.waited = {e: {} for e in ENGS}
        self.n_instr = 0
        self.seq = 0

    def close(self):
        self.es.close()

    def slot(self, buf):
        if buf.sem is None:
            assert self.next_slot < len(self.dma_sem), "too many dma buffers in one phase"
            buf.sem = self.next_slot
            self.next_slot += 1
        return buf.sem

    def _record(self, eng, fn, reads, writes, dma=None, cost=400.0):
        rw, war = set(), set()
        ex = [b for b in reads if b.excl and b not in writes]
        if ex:
            reads = [b for b in reads if not b.excl or b in writes]
            writes = list(writes) + ex
        for b in reads:
            rw.update(b.w)
        for b in writes:
            for o in b.w:
                if dma is not None and o.dma is not None and o.dma[0] == dma[0]:
                    rw.update(o.rw)
                    war.update(o.war)
                    continue
                rw.add(o)
            war.update(b.r)
        self.seq += 1
        op = Op(eng, self.seq, fn, rw, war, dma, cost)
        if DEBUG_LINES:
            import sys as _s
            f_ = _s._getframe(2)
            op.waits = (f_.f_lineno, f_.f_back.f_lineno if f_.f_back else 0)
        self.ops[eng].append(op)
        for b in reads:
            b.r.append(op)
        for b in writes:
            b.w = [op]
            b.r = []
        return op

    def op(self, eng, fn, reads=(), writes=()):
        return self._record(eng, fn, reads, writes)

    def dma(self, eng, out, in_, sbuf, reads=(), writes=(), slow=False):
        s = self.slot(sbuf)
        self.dma_cnt[s] += 16
        cnt = self.dma_cnt[s]
        sem = self.dma_sem[s]
        kw = {"allow_slow_non_contiguous": True} if slow else {}

        def fn(e):
            return e.dma_start(out=out, in_=in_, **kw).then_inc(sem, 16)
        return self._record(eng, fn, reads, writes, dma=(s, cnt), cost=DMA_LAT_NS)

    def mm(self, out, lhsT, rhs, start, stop, reads, writes):
        n = _fsize(rhs)
        cost = 64.0 + n * (0.42 if rhs.dtype == BF16 else 0.85)
        return self._record("pe", lambda e: e.matmul(out, lhsT, rhs, start=start, stop=stop), reads, writes, cost=cost)

    def tr(self, out, in_, ident, reads, writes):
        return self._record("pe", lambda e: e.transpose(out=out, in_=in_, identity=ident), reads, writes, cost=300.0)

    def _ec(self, eng, out):
        n = _fsize(out)
        if eng == "pool":
            return 300.0 + 2.0 * n
        return 220.0 + 1.0 * n

    def act(self, out, in_, func, reads, writes, bias=None, scale=None):
        kw = {}
        if bias is not None:
            kw["bias"] = bias
        if scale is not None:
            kw["scale"] = scale
        return self._record("act", lambda e: e.activation(out=out, in_=in_, func=func, **kw), reads, writes,
                            cost=self._ec("act", out))

    def cp(self, eng, out, in_, reads, writes):
        if eng == "act":
            return self._record("act", lambda e: e.copy(out=out, in_=in_), reads, writes, cost=self._ec(eng, out))
        return self._record(eng, lambda e: e.tensor_copy(out=out, in_=in_), reads, writes, cost=self._ec(eng, out))

    def tt(self, eng, out, in0, in1, op, reads, writes):
        return self._record(eng, lambda e: e.tensor_tensor(out=out, in0=in0, in1=in1, op=op), reads, writes,
                            cost=self._ec(eng, out))

    def ts(self, eng, out, in0, s1, s2, op0, op1, reads, writes):
        if op1 is None:
            return self._record(eng, lambda e: e.tensor_scalar(out=out, in0=in0, scalar1=s1, scalar2=None, op0=op0),
                                reads, writes, cost=self._ec(eng, out))
        return self._record(eng, lambda e: e.tensor_scalar(out=out, in0=in0, scalar1=s1, scalar2=s2, op0=op0, op1=op1),
                            reads, writes, cost=self._ec(eng, out))

    def stt(self, eng, out, in0, scalar, in1, op0, op1, reads, writes):
        return self._record(eng, lambda e: e.scalar_tensor_tensor(out=out, in0=in0, scalar=scalar, in1=in1,
                                                                  op0=op0, op1=op1), reads, writes,
                            cost=self._ec(eng, out))

    def red(self, eng, out, in_, reads, writes):
        return self._record(eng, lambda e: e.reduce_sum(out=out, in_=in_, axis=AX.X), reads, writes,
                            cost=self._ec(eng, in_))

    def memset(self, eng, ap, val, writes):
        return self._record(eng, lambda e: e.memset(ap, val), (), writes, cost=self._ec(eng, ap))

    def _schedule(self):
        import bisect
        ops = self.ops
        allops = [o for e in ENGS for o in ops[e]]
        for o in allops:
            o.rw = set(d_ for d_ in o.rw if d_.fn is not None)
            o.war = set(d_ for d_ in o.war if d_.fn is not None)
            o.deps = list(o.rw | o.war)
            o.nun = len(o.deps)
            o.users = []
            o.ready = 0.0
            o.done = False
        for o in allops:
            for d_ in o.deps:
                d_.users.append(o)
        avail = {e: [] for e in ENGS}
        for o in allops:
            if o.nun == 0:
                avail[o.eng].append((o.seq, o))
        for e in ENGS:
            avail[e].sort(key=lambda t: t[0])
        free = {e: 0.0 for e in ENGS}
        new = {e: [] for e in ENGS}
        remaining = len(allops)
        while remaining:
            best = None
            for e in ENGS:
                av = avail[e]
                fe = free[e]
                for k in range(min(SCHED_K, len(av))):
                    o = av[k][1]
                    st = o.ready if o.ready > fe else fe
                    if best is None or st < best[0] or (st == best[0] and o.seq < best[2].seq):
                        best = (st, k, o)
                    if o.ready <= fe:
                        break
            st, k, o = best
            e = o.eng
            del avail[e][k]
            fin = st + o.cost
            free[e] = st + 64.0 if o.dma is not None else fin
            o.done = True
            new[e].append(o)
            remaining -= 1
            for u in o.users:
                r = fin + SYNC_NS
                if r > u.ready:
                    u.ready = r
                u.nun -= 1
                if u.nun == 0:
                    bisect.insort(avail[u.eng], (u.seq, u))
        self.ops = new
        if DEBUG_LINES:
            for e in ENGS:
                print("ENGINE", e)
                for o in new[e][:DEBUG_LINES]:
                    print("   seq", o.seq, "line", o.waits, "dma" if o.dma else "", "deps", sorted(d_.seq for d_ in o.deps)[:8])

    def flush(self, engines, sched=True):
        if SCHED and sched:
            self._schedule()
        ops = self.ops
        for e in ENGS:
            for i, op in enumerate(ops[e]):
                op.pos = i
        for e in ENGS:
            for op in ops[e]:
                need = set()
                for d_ in op.rw:
                    if d_.dma is not None or d_.eng != e:
                        need.add(d_)
                    elif op.dma is not None or (STRICT and e in ("act", "dve", "pool")):
                        need.add(d_)
                for d_ in op.war:
                    if d_.dma is not None or d_.eng != e:
                        need.add(d_)
                    elif op.dma is not None:
                        need.add(d_)
                for d_ in need:
                    if d_.dma is None and d_.eng == e:
                        assert d_.pos < op.pos
                op.waits = need
                for d_ in need:
                    if d_.dma is None:
                        d_.signal = True
        for e in ENGS:
            for op in reversed(ops[e]):
                if op.dma is None:
                    op.signal = True
                    break
        for e in ENGS:
            c = self.eng_cnt[e]
            for op in ops[e]:
                if op.signal and op.dma is None:
                    c += 1
                    op.sigval = c
            self.eng_cnt[e] = c
        nc = self.nc
        final_eng = dict(self.eng_cnt)
        final_dma = list(self.dma_cnt)
        g = self

        def emit(e, handle):
            waited = g.waited[e]
            for op in ops[e]:
                wl = {}
                for d_ in op.waits:
                    if d_.dma is not None:
                        key, val = ("d", d_.dma[0]), d_.dma[1]
                    else:
                        key, val = ("c", d_.eng), d_.sigval
                    if wl.get(key, 0) < val:
                        wl[key] = val
                for key, val in wl.items():
                    if waited.get(key, 0) >= val:
                        continue
                    waited[key] = val
                    sem = g.dma_sem[key[1]] if key[0] == "d" else g.eng_sem[key[1]]
                    handle.wait_ge(sem, val)
                    g.n_instr += 1
                ins = op.fn(handle)
                g.n_instr += 1
                if op.signal and op.dma is None:
                    ins.then_inc(g.eng_sem[e], 1)
            for x in ENGS:
                if x != e and waited.get(("c", x), 0) < final_eng[x]:
                    waited[("c", x)] = final_eng[x]
                    handle.wait_ge(g.eng_sem[x], final_eng[x])
            for s_, v in enumerate(final_dma):
                if v > 0 and waited.get(("d", s_), 0) < v:
                    waited[("d", s_)] = v
                    handle.wait_ge(g.dma_sem[s_], v)

        with nc.Block() as block:
            @block.tensor
            def _(h):
                emit("pe", h)

            @block.scalar
            def _(h):
                emit("act", h)

            @block.vector
            def _(h):
                emit("dve", h)

            @block.gpsimd
            def _(h):
                emit("pool", h)

            @block.sync
            def _(h):
                emit("sp", h)
        for e in ENGS:
            for op in ops[e]:
                op.fn = None
        self.ops = {e: [] for e in ENGS}
        self.next_slot = 0


C_Z, C_XBC, C_DTF, C_DTB = 0, 1024, 2560, 2576
C_GQ, C_GK, C_GV, C_GG = 2592, 3104, 3616, 4640
C_DNF, C_DNB = 5664, 5680
C_NQ, C_NK, C_NV = 5696, 6720, 7744
C_GATE = 8768
IN_WIDTH = 11840


class Cfg:
    def __init__(self, nt, stop_after=None, debug_outs=(), debug_ins=()):
        self.nt = nt
        self.seg = nt // 2
        self.stop_after = stop_after
        self.debug_outs = debug_outs
        self.debug_ins = debug_ins
        self.only = None


def build_program(cfg):
    nc = bass.Bass("TRN2", target_bir_lowering=False)
    NT = cfg.nt
    TB = 512
    NB = NT // TB
    NTI = TB // 128

    cfg.in_shapes = {}

    def din(name, shape, dt=F32):
        cfg.in_shapes[name] = (list(shape), dt)
        return nc.dram_tensor(name, list(shape), dt, kind="ExternalInput").ap()

    def dout(name, shape, dt=F32):
        return nc.dram_tensor(name, list(shape), dt, kind="ExternalOutput").ap()

    def dscr(name, shape, dt=F32):
        kind = "Internal"
        if name in cfg.debug_outs:
            kind = "ExternalOutput"
        if name in cfg.debug_ins:
            kind = "ExternalInput"
            cfg.in_shapes[name] = (list(shape), dt)
        return nc.dram_tensor(name, list(shape), dt, kind=kind).ap()

    x_in = din("x", [NT, D])
    flag = din("flag", [128, 2])
    ident_in = din("ident", [128, 128])
    ln_ffn1_w = din("ln_ffn1_w", [DEPTH, D])
    ffn1_w_in = din("ffn1_w_in", [DEPTH, D, 2 * DFF])
    ffn1_w_out = din("ffn1_w_out", [DEPTH, DFF, D])
    ln_mix_w = din("ln_mix_w", [DEPTH, D])
    w_in = din("w_in", [DEPTH, D, IN_WIDTH])
    w_branch = [din("w_branch_" + c, [DEPTH, D, D]) for c in "abc"]
    w_out = din("w_out", [DEPTH, D, D])
    ln_ffn2_w = din("ln_ffn2_w", [DEPTH, D])
    ffn2_w_in = din("ffn2_w_in", [DEPTH, D, 2 * DFF])
    ffn2_w_out = din("ffn2_w_out", [DEPTH, DFF, D])
    ln_final_w = din("ln_final_w", [1, D])
    y_out = dout("y", [NT, D])

    xa = dscr("xa", [NT, D])
    xb = dscr("xb", [NT, D])
    xc = dscr("xc", [NT, D])
    hT_f1 = dscr("hT_f1", [8, 128, NT], BF16)
    hT_mix = dscr("hT_mix", [8, 128, NT], BF16)
    hT_f2 = dscr("hT_f2", [8, 128, NT], BF16)
    z_tm = dscr("z_tm", [NT, 1024])
    kg_tm = dscr("kg_tm", [NT, 512])
    vg_tm = dscr("vg_tm", [NT, 1024], BF16)
    gg_tm = dscr("gg_tm", [NT, 1024])
    vn_tm = dscr("vn_tm", [NT, 1024], BF16)
    dt_tm = dscr("dt_tm", [NT, 32])
    xbcT = dscr("xbcT", [12, 128, NT])
    gqT = dscr("gqT", [4, 128, NT])
    gkT = dscr("gkT", [4, 128, NT])
    nqT = dscr("nqT", [8, 128, NT], BF16)
    nkT = dscr("nkT", [8, 128, NT], BF16)
    gatesT = dscr("gatesT", [24, 128, NT], BF16)
    dnT = dscr("dnT", [2, 16, NT])
    yT = [dscr("yT_" + c, [8, 128, NT], BF16) for c in "abc"]

    g = Graph(nc)

    uid = [0]

    def sb(es, name, shape, dt):
        uid[0] += 1
        return es.enter_context(nc.sbuf_tensor("s%d_%s" % (uid[0], name), list(shape), dt))

    def ps(es, name, shape, dt):
        uid[0] += 1
        return es.enter_context(nc.psum_tensor("p%d_%s" % (uid[0], name), list(shape), dt))

    class NormCtx:
        def __init__(self, es, tag, transpose=True):
            self.transpose = transpose
            self.wB = sb(es, "wB_" + tag, [128, D], F32)
            self.sq = sb(es, "sq_" + tag, [128, D], F32)
            self.ss = [sb(es, "ss%d_" % i + tag, [128, 2], F32) for i in range(2)]
            self.b_wB, self.b_sq = Buf("wB"), Buf("sq")
            self.b_ss = [Buf("ss0"), Buf("ss1")]
            if transpose:
                self.ident = sb(es, "ident_" + tag, [128, 128], BF16)
                self.hn = [sb(es, "hn%d_" % i + tag, [128, D], BF16) for i in range(2)]
                self.hTs = sb(es, "hTs_" + tag, [128, 8, TB], BF16)
                self.pT = [ps(es, "pT%d_" % i + tag, [128, D], BF16) for i in range(2)]
                self.b_ident = Buf("ident")
                self.b_hn = [Buf("hn0"), Buf("hn1")]
                self.b_hTs = Buf("hTs")
                self.b_pT = [Buf("pT0"), Buf("pT1")]
            else:
                self.yt = [sb(es, "yt%d_" % i + tag, [128, D], F32) for i in range(2)]
                self.b_yt = [Buf("yt0"), Buf("yt1")]
            self.n = 0

        def setup(self, w_row_ap):
            if self.transpose:
                g.dma("pool", self.ident[:], ident_in, self.b_ident, writes=[self.b_ident])
            g.dma("sp", self.wB[:], w_row_ap.partition_broadcast(128), self.b_wB, writes=[self.b_wB])

        def stats(self, xt_ap, b_x):
            i = self.n % 2
            self.n += 1
            ss = self.ss[i]
            g.act(self.sq[:], xt_ap, AF.Square, [b_x], [self.b_sq])
            g.red("dve", ss[:, 0:1], self.sq[:], [self.b_sq], [self.b_ss[i]])
            g.ts("dve", ss[:, 1:2], ss[:, 0:1], 1.0 / D, EPS, ALU.mult, ALU.add, [self.b_ss[i]], [self.b_ss[i]])
            g.act(ss[:, 1:2], ss[:, 1:2], AF.Sqrt, [self.b_ss[i]], [self.b_ss[i]])
            g._record("dve", lambda e: e.reciprocal(out=ss[:, 1:2], in_=ss[:, 1:2]), [self.b_ss[i]], [self.b_ss[i]])
            return i

        def run(self, xt_ap, b_x, hT_dram, blk, ti):
            self.flush_pending()
            i = self.stats(xt_ap, b_x)
            hn = self.hn[i]
            g.stt("dve", hn[:], xt_ap, self.ss[i][:, 1:2], self.wB[:], ALU.mult, ALU.mult,
                  [b_x, self.b_ss[i], self.b_wB], [self.b_hn[i]])
            self.pending = (i, hT_dram, blk, ti)

        def flush_pending(self):
            if getattr(self, "pending", None) is None:
                return
            i, hT_dram, blk, ti = self.pending
            self.pending = None
            hn, pT, hTs = self.hn[i], self.pT[i], self.hTs
            for k in range(8):
                g.tr(pT[:, k * 128:(k + 1) * 128], hn[:, k * 128:(k + 1) * 128], self.ident[:],
                     [self.b_hn[i], self.b_ident], [self.b_pT[i]])
            g.cp("act", hTs[:, :, ti * 128:(ti + 1) * 128], pT[:].rearrange("p (k t) -> p k t", k=8),
                 [self.b_pT[i]], [self.b_hTs])
            if ti == NTI - 1:
                g.dma("pool", hT_dram[:, :, blk * TB:(blk + 1) * TB].rearrange("k p t -> p k t"), hTs[:],
                      self.b_hTs, reads=[self.b_hTs])

        def run_final(self, xt_ap, b_x, y_dram, t):
            i = self.stats(xt_ap, b_x)
            g.stt("dve", self.yt[i][:], xt_ap, self.ss[i][:, 1:2], self.wB[:], ALU.mult, ALU.mult,
                  [b_x, self.b_ss[i], self.b_wB], [self.b_yt[i]])
            g.dma("pool", y_dram[t * 128:(t + 1) * 128, :], self.yt[i][:], self.b_yt[i], reads=[self.b_yt[i]])

    def phase_p0():
        with ExitStack() as es:
            nrm = NormCtx(es, "p0")
            xt = [sb(es, "p0x%d" % i, [128, D], F32) for i in range(2)]
            b_xt = [Buf("x0"), Buf("x1")]
            nrm.setup(ln_ffn1_w[0:1, :])
            for blk in range(NB):
                for ti in range(NTI):
                    t = blk * NTI + ti
                    i = t % 2
                    g.dma("sp", xt[i][:], x_in[t * 128:(t + 1) * 128, :], b_xt[i], writes=[b_xt[i]])
                    nrm.run(xt[i][:], b_xt[i], hT_f1, blk, ti)
            nrm.flush_pending()
            g.flush(ENGS)

    def phase_ffn(tag, w_in_ap, w_out_ap, x_src, hT_src, x_dst, next_w_row, hT_dst, final_out=None):
        with ExitStack() as es:
            w1 = sb(es, "w1_" + tag, [128, 8, 2 * DFF], BF16)
            w2 = sb(es, "w2_" + tag, [128, 22, D], BF16)
            b_w1, b_w2 = Buf("w1"), Buf("w2")
            hTb = sb(es, "hTb_" + tag, [128, 8, TB], BF16)
            b_hTb = Buf("hTb")
            gT = sb(es, "gT_" + tag, [128, 22, TB], BF16)
            b_gT = Buf("gT")
            sa = [sb(es, "sa%d_" % i + tag, [128, TB], F32) for i in range(2)]
            b_sa = [Buf("sa0"), Buf("sa1")]
            xt = sb(es, "xt_" + tag, [128, D], F32)
            b_xt = Buf("xt")
            xn = [sb(es, "xn%d_" % i + tag, [128, D], F32) for i in range(2)]
            b_xn = [Buf("xn0"), Buf("xn1")]
            pa = [ps(es, "pa%d_" % i + tag, [128, TB], F32) for i in range(2)]
            pb = [ps(es, "pb%d_" % i + tag, [128, TB], F32) for i in range(2)]
            b_pa = [Buf("pa0"), Buf("pa1")]
            b_pb = [Buf("pb0"), Buf("pb1")]
            po = [ps(es, "po%d_" % i + tag, [128, 512], F32) for i in range(2)]
            b_po = [Buf("po0"), Buf("po1")]
            nrm = NormCtx(es, tag, transpose=(final_out is None))
            for k in range(8):
                g.dma("pool", w1[:, k, :], w_in_ap[k * 128:(k + 1) * 128, :], b_w1, writes=[b_w1])
            for k in range(22):
                g.dma("pool", w2[:, k, :], w_out_ap[k * 128:(k + 1) * 128, :], b_w2, writes=[b_w2])
            nrm.setup(next_w_row)

            def load_h(blk):
                g.dma("sp", hTb[:], hT_src[:, :, blk * TB:(blk + 1) * TB].rearrange("k p t -> p k t"),
                      b_hTb, writes=[b_hTb])
            load_h(0)
            ntl = 0
            for blk in range(NB):
                for c in range(22):
                    q = c % 2
                    for k in range(8):
                        g.mm(pa[q][:], w1[:, k, c * 128:(c + 1) * 128], hTb[:, k, :], k == 0, k == 7,
                             [b_w1, b_hTb], [b_pa[q]])
                    for k in range(8):
                        g.mm(pb[q][:], w1[:, k, DFF + c * 128:DFF + (c + 1) * 128], hTb[:, k, :], k == 0, k == 7,
                             [b_w1, b_hTb], [b_pb[q]])
                    g.act(sa[q][:], pa[q][:], AF.Silu, [b_pa[q]], [b_sa[q]])
                    g.tt("dve", gT[:, c, :], sa[q][:], pb[q][:], ALU.mult, [b_sa[q], b_pb[q]], [b_gT])
                if blk + 1 < NB:
                    load_h(blk + 1)
                for ti in range(NTI):
                    t = blk * NTI + ti
                    i = ntl % 2
                    ntl += 1
                    g.dma("sp", xt[:], x_src[t * 128:(t + 1) * 128, :], b_xt, writes=[b_xt])
                    for ch in range(2):
                        for k in range(22):
                            g.mm(po[ch][:], gT[:, k, ti * 128:(ti + 1) * 128], w2[:, k, ch * 512:(ch + 1) * 512],
                                 k == 0, k == 21, [b_gT, b_w2], [b_po[ch]])
                        g.stt("dve", xn[i][:, ch * 512:(ch + 1) * 512], po[ch][:], 0.5,
                              xt[:, ch * 512:(ch + 1) * 512], ALU.mult, ALU.add, [b_po[ch], b_xt], [b_xn[i]])
                    if final_out is None:
                        nrm.flush_pending()
                    if final_out is None:
                        g.dma("pool", x_dst[t * 128:(t + 1) * 128, :], xn[i][:], b_xn[i], reads=[b_xn[i]])
                        nrm.run(xn[i][:], b_xn[i], hT_dst, blk, ti)
                    else:
                        nrm.run_final(xn[i][:], b_xn[i], final_out, t)
            if final_out is None:
                nrm.flush_pending()
            g.flush(ENGS)

    def phase_inproj_tm(l):
        with ExitStack() as es:
            W = w_in[l]
            parts = [(C_Z, 1024, z_tm, F32), (C_GK, 512, kg_tm, F32), (C_GV, 1024, vg_tm, BF16),
                     (C_GG, 1024, gg_tm, F32), (C_NV, 1024, vn_tm, BF16), (C_DTF, 32, dt_tm, F32)]
            silu_parts = (0, 3)
            NC_TM = sum(p[1] for p in parts)
            wt = sb(es, "wtm", [128, 8, NC_TM], BF16)
            b_wt = Buf("wtm")
            hTb = [sb(es, "tm_hTb%d" % i, [128, 8, TB], BF16) for i in range(2)]
            b_hTb = [Buf("hTb0"), Buf("hTb1")]
            st = [[sb(es, "tm_st%d_%d" % (i, pi), [128, p[1]], p[3]) for pi, p in enumerate(parts)] for i in range(2)]
            b_st = [[Buf("st%d_%d" % (i, pi)) for pi in range(len(parts))] for i in range(2)]
            pp = [ps(es, "tm_p%d" % i, [128, 512], F32) for i in range(4)]
            b_pp = [Buf("pp%d" % i) for i in range(4)]
            off = 0
            offs = []
            for (c0, wd, _, _) in parts:
                offs.append(off)
                for k in range(8):
                    g.dma("pool", wt[:, k, off:off + wd], W[k * 128:(k + 1) * 128, c0:c0 + wd], b_wt, writes=[b_wt])
                off += wd
            nps = 0
            nev = 0
            for blk in range(NB):
                j = blk % 2
                g.dma("sp", hTb[j][:], hT_mix[:, :, blk * TB:(blk + 1) * TB].rearrange("k p t -> p k t"),
                      b_hTb[j], writes=[b_hTb[j]])
                for ti in range(NTI):
                    t = blk * NTI + ti
                    i = t % 2
                    for pi, (c0, wd, dst, dt) in enumerate(parts):
                        for cb in range(0, wd, 512):
                            n = min(512, wd - cb)
                            q = nps % 4
                            nps += 1
                            for k in range(8):
                                g.mm(pp[q][:, 0:n], hTb[j][:, k, ti * 128:(ti + 1) * 128],
                                     wt[:, k, offs[pi] + cb:offs[pi] + cb + n], k == 0, k == 7,
                                     [b_hTb[j], b_wt], [b_pp[q]])
                            if pi in silu_parts:
                                g.act(st[i][pi][:, cb:cb + n], pp[q][:, 0:n], AF.Silu, [b_pp[q]], [b_st[i][pi]])
                            else:
                                g.cp("dve", st[i][pi][:, cb:cb + n], pp[q][:, 0:n], [b_pp[q]], [b_st[i][pi]])
                        g.dma("pool", dst[t * 128:(t + 1) * 128, :], st[i][pi][:], b_st[i][pi], reads=[b_st[i][pi]])
            g.flush(ENGS)

    def phase_inproj_fm(l):
        with ExitStack() as es:
            W = w_in[l]
            parts = [(C_XBC, 12, xbcT, F32, None, 1.0), (C_GQ, 4, gqT, F32, None, 128.0 ** -0.5),
                     (C_GK, 4, gkT, F32, None, 1.0), (C_NQ, 8, nqT, BF16, None, 0.125),
                     (C_NK, 8, nkT, BF16, None, 1.0), (C_GATE, 24, gatesT, BF16, AF.Sigmoid, 1.0)]
            NCH = sum(p[1] for p in parts)
            NC_FM = NCH * 128 + 32
            wt = sb(es, "wfm", [128, 8, NC_FM], BF16)
            b_wt = Buf("wfm")
            hTb = [sb(es, "fm_hTb%d" % i, [128, 8, TB], BF16) for i in range(2)]
            b_hTb = [Buf("hTb0"), Buf("hTb1")]
            stf = [sb(es, "fm_stf%d" % i, [128, TB], F32) for i in range(4)]
            stb = [sb(es, "fm_stb%d" % i, [128, TB], BF16) for i in range(4)]
            b_stf = [Buf("stf%d" % i) for i in range(4)]
            b_stb = [Buf("stb%d" % i) for i in range(4)]
            pp = [ps(es, "fm_p%d" % i, [128, 512], F32) for i in range(4)]
            b_pp = [Buf("pp%d" % i) for i in range(4)]
            off = 0
            offs = []
            for (c0, nch, _, _, _, _) in parts:
                offs.append(off)
                for k in range(8):
                    g.dma("pool", wt[:, k, off:off + nch * 128], W[k * 128:(k + 1) * 128, c0:c0 + nch * 128],
                          b_wt, writes=[b_wt])
                off += nch * 128
            off_dn = off
            for k in range(8):
                g.dma("pool", wt[:, k, off:off + 32], W[k * 128:(k + 1) * 128, C_DNF:C_DNF + 32], b_wt, writes=[b_wt])
            nps = 0
            nf = 0
            nbf = 0
            nev = 0
            for blk in range(NB):
                j = blk % 2
                tok = slice(blk * TB, (blk + 1) * TB)
                g.dma("sp", hTb[j][:], hT_mix[:, :, tok].rearrange("k p t -> p k t"), b_hTb[j], writes=[b_hTb[j]])
                for pi, (c0, nch, dst, dt, func, scale) in enumerate(parts):
                    for c in range(nch):
                        q = nps % 4
                        nps += 1
                        wc = offs[pi] + c * 128
                        for k in range(8):
                            g.mm(pp[q][:], wt[:, k, wc:wc + 128], hTb[j][:, k, :], k == 0, k == 7,
                                 [b_wt, b_hTb[j]], [b_pp[q]])
                        if dt == F32:
                            s_, b_s = stf[nf % 4], b_stf[nf % 4]
                            nf += 1
                        else:
                            s_, b_s = stb[nbf % 4], b_stb[nbf % 4]
                            nbf += 1
                        if func is not None:
                            g.act(s_[:], pp[q][:], func, [b_pp[q]], [b_s])
                        elif scale != 1.0:
                            if nev % 2 == 0:
                                g.act(s_[:], pp[q][:], AF.Copy, [b_pp[q]], [b_s], scale=scale)
                            else:
                                g.ts("dve", s_[:], pp[q][:], scale, None, ALU.mult, None, [b_pp[q]], [b_s])
                            nev += 1
                        else:
                            g.cp("act" if nev % 2 == 0 else "dve", s_[:], pp[q][:], [b_pp[q]], [b_s])
                            nev += 1
                        g.dma("pool", dst[c, :, tok], s_[:], b_s, reads=[b_s])
                for dd in range(2):
                    q = nps % 4
                    nps += 1
                    wc = off_dn + dd * 16
                    for k in range(8):
                        g.mm(pp[q][0:16, :], wt[:, k, wc:wc + 16], hTb[j][:, k, :], k == 0, k == 7,
                             [b_wt, b_hTb[j]], [b_pp[q]])
                    s_, b_s = stf[nf % 4], b_stf[nf % 4]
                    nf += 1
                    g.cp("dve", s_[0:16, :], pp[q][0:16, :], [b_pp[q]], [b_s])
                    g.dma("pool", dnT[dd, :, tok], s_[0:16, :], b_s, reads=[b_s])
            g.flush(ENGS)

    def phase_merge(l, x_src, x_dst, next_w_row, hT_dst):
        with ExitStack() as es:
            wb_ = [sb(es, "wbr%d" % i, [128, 8, D], BF16) for i in range(3)]
            wo = sb(es, "wo", [128, 8, D], BF16)
            b_w = Buf("w")
            yb = [sb(es, "yb%d" % i, [128, 8, TB], BF16) for i in range(3)]
            b_yb = [Buf("yb%d" % i) for i in range(3)]
            gt = sb(es, "gt", [128, 24, TB], BF16)
            b_gt = Buf("gt")
            tmp = [sb(es, "mtmp%d" % i, [128, TB], F32) for i in range(3)]
            b_tmp = [Buf("tmp%d" % i) for i in range(3)]
            mT = sb(es, "mT", [128, 8, TB], BF16)
            b_mT = Buf("mT")
            xt = sb(es, "m_xt", [128, D], F32)
            b_xt = Buf("xt")
            xn = [sb(es, "m_xn%d" % i, [128, D], F32) for i in range(2)]
            b_xn = [Buf("xn0"), Buf("xn1")]
            pp = [ps(es, "m_p%d" % i, [128, 512], F32) for i in range(6)]
            b_pp = [Buf("pp%d" % i) for i in range(6)]
            nrm = NormCtx(es, "mrg")
            for i in range(3):
                for k in range(8):
                    g.dma("pool", wb_[i][:, k, :], w_branch[i][l, k * 128:(k + 1) * 128, :], b_w, writes=[b_w])
            for k in range(8):
                g.dma("pool", wo[:, k, :], w_out[l, k * 128:(k + 1) * 128, :], b_w, writes=[b_w])
            nrm.setup(next_w_row)
            ntl = 0
            for blk in range(NB):
                tok = slice(blk * TB, (blk + 1) * TB)
                for i in range(3):
                    g.dma("sp", yb[i][:], yT[i][:, :, tok].rearrange("k p t -> p k t"), b_yb[i], writes=[b_yb[i]])
                g.dma("sp", gt[:], gatesT[:, :, tok].rearrange("k p t -> p k t"), b_gt, writes=[b_gt])
                for oc in range(8):
                    par = (oc % 2) * 3
                    for i in range(3):
                        for k in range(8):
                            g.mm(pp[par + i][:], wb_[i][:, k, oc * 128:(oc + 1) * 128], yb[i][:, k, :], k == 0, k == 7,
                                 [b_w, b_yb[i]], [b_pp[par + i]])
                    for i in range(3):
                        g.tt("dve", tmp[i][:], pp[par + i][:], gt[:, i * 8 + oc, :], ALU.mult,
                             [b_pp[par + i], b_gt], [b_tmp[i]])
                    g.tt("pool", tmp[0][:], tmp[0][:], tmp[1][:], ALU.add, [b_tmp[0], b_tmp[1]], [b_tmp[0]])
                    g.tt("pool", mT[:, oc, :], tmp[0][:], tmp[2][:], ALU.add, [b_tmp[0], b_tmp[2]], [b_mT])
                for ti in range(NTI):
                    t = blk * NTI + ti
                    i = ntl % 2
                    ntl += 1
                    g.dma("sp", xt[:], x_src[t * 128:(t + 1) * 128, :], b_xt, writes=[b_xt])
                    for ch in range(2):
                        q = ch * 3
                        for k in range(8):
                            g.mm(pp[q][:], mT[:, k, ti * 128:(ti + 1) * 128], wo[:, k, ch * 512:(ch + 1) * 512],
                                 k == 0, k == 7, [b_mT, b_w], [b_pp[q]])
                        g.tt("dve", xn[i][:, ch * 512:(ch + 1) * 512], pp[q][:], xt[:, ch * 512:(ch + 1) * 512], ALU.add,
                             [b_pp[q], b_xt], [b_xn[i]])
                    nrm.flush_pending()
                    g.dma("pool", x_dst[t * 128:(t + 1) * 128, :], xn[i][:], b_xn[i], reads=[b_xn[i]])
                    nrm.run(xn[i][:], b_xn[i], hT_dst, blk, ti)
            nrm.flush_pending()
            g.flush(ENGS)

    ssm_conv_w = din("ssm_conv_w", [DEPTH, 5, 1536])
    ssm_conv_b = din("ssm_conv_b", [DEPTH, 1536])
    ssm_dt_bias = din("ssm_dt_bias", [DEPTH, 32])
    ssm_a_log = din("ssm_a_log", [DEPTH, 32])
    ssm_d = din("ssm_d", [DEPTH, 16])
    ssm_norm_w = din("ssm_norm_w", [DEPTH, 1024])
    tri_in = din("tri", [2, 128, 128])
    maskadd_in = din("maskadd", [2, 128, 128])
    mask01_in = din("mask01", [2, 128, 128])
    ustrict_in = din("ustrict", [2, 128, 128])
    ones_in = din("ones", [128, 128])
    xs_tm = dscr("xs_tm", [NT, 1024])
    B_tm = dscr("B_tm", [NT, 256], BF16)
    BCT = dscr("BCT", [4, 128, NT], BF16)
    dtsp = dscr("dtsp", [NT, 32])
    dta_d = dscr("dta", [NT, 32])
    dlog_d = dscr("dlog", [NT, 32])
    yf_d = dscr("yf", [NT, 1024])
    NCH = NT // 128
    MIDC = NCH // 2

    def phase_ssd_prep(l):
        with ExitStack() as es:
            cw = sb(es, "cw", [128, 12, 5], F32)
            cb = sb(es, "cb", [128, 12], F32)
            fl = sb(es, "fl", [128, 2], F32)
            identf = sb(es, "identf", [128, 128], F32)
            dbias = sb(es, "dbias", [128, 32], F32)
            aB = sb(es, "aB", [128, 32], F32)
            b_c = Buf("consts")
            xin = [sb(es, "xin%d" % i, [128, 12, TB + 4], F32) for i in range(2)]
            b_xin = [Buf("xin0"), Buf("xin1")]
            acc = [sb(es, "acc%d" % i, [128, TB], F32) for i in range(3)]
            b_acc = [Buf("acc%d" % i) for i in range(3)]
            sil = sb(es, "sil", [128, 10, TB], F32)
            b_sil = Buf("sil")
            silb = [sb(es, "silb%d" % i, [128, TB], BF16) for i in range(2)]
            b_silb = [Buf("silb0"), Buf("silb1")]
            xst = [sb(es, "xst%d" % i, [128, 1024], F32) for i in range(2)]
            b_xst = [Buf("xst0"), Buf("xst1")]
            bst = [sb(es, "bst%d" % i, [128, 256], BF16) for i in range(2)]
            b_bst = [Buf("bst0"), Buf("bst1")]
            dtt = [sb(es, "dtt%d" % i, [128, 4, NTI, 32], F32) for i in range(2)]
            b_dtt = [Buf("dtt0"), Buf("dtt1")]
            pt = [ps(es, "pp_t%d" % i, [128, 1024], F32) for i in range(2)]
            b_pt = [Buf("pt0"), Buf("pt1")]
            pbt = [ps(es, "pp_b%d" % i, [128, 512], F32) for i in range(2)]
            b_pbt = [Buf("pbt0"), Buf("pbt1")]
            for k in range(5):
                g.dma("sp", cw[:, :, k], ssm_conv_w[l, k].rearrange("(c p) -> p c", p=128), b_c, writes=[b_c], slow=True)
            g.dma("sp", cb[:], ssm_conv_b[l].rearrange("(c p) -> p c", p=128), b_c, writes=[b_c], slow=True)
            g.dma("sp", fl[:], flag, b_c, writes=[b_c])
            g.dma("sp", identf[:], ident_in, b_c, writes=[b_c])
            g.dma("sp", dbias[:], ssm_dt_bias[l:l + 1, :].partition_broadcast(128), b_c, writes=[b_c])
            g.dma("sp", aB[:], ssm_a_log[l:l + 1, :].partition_broadcast(128), b_c, writes=[b_c])
            g.act(aB[:], aB[:], AF.Exp, [b_c], [b_c])
            g.ts("dve", aB[:], aB[:], -1.0, None, ALU.mult, None, [b_c], [b_c])
            na = 0
            nsb = 0
            nt_ = 0
            for blk in range(NB):
                j = blk % 2
                t0 = blk * TB
                lo, hi = t0 - 2, t0 + TB + 2
                xi = xin[j]
                lo_c, hi_c = 0, TB + 4
                if lo < 0:
                    g.memset("pool", xi[:, :, 0:2], 0.0, [b_xin[j]])
                    lo_c, lo = 2, 0
                if hi > NT:
                    g.memset("pool", xi[:, :, TB + 2:TB + 4], 0.0, [b_xin[j]])
                    hi_c, hi = TB + 2, NT
                g.dma("sp", xi[:, :, lo_c:hi_c], xbcT[:, :, lo:hi].rearrange("c p t -> p c t"), b_xin[j],
                      writes=[b_xin[j]])
                if t0 == NT // 2:
                    g.ts("dve", xi[:, :, 0:2], xi[:, :, 0:2], fl[:, 0:1], None, ALU.mult, None, [b_xin[j], b_c], [b_xin[j]])
                if t0 + TB == NT // 2:
                    g.ts("dve", xi[:, :, TB + 2:TB + 4], xi[:, :, TB + 2:TB + 4], fl[:, 0:1], None, ALU.mult, None,
                         [b_xin[j], b_c], [b_xin[j]])
                dq = dtt[j]
                g.dma("sp", dq[:, 0, :, :], dt_tm[t0:t0 + TB, :].rearrange("(i p) c -> p i c", p=128), b_dtt[j],
                      writes=[b_dtt[j]])
                g.tt("dve", dq[:, 0, :, :], dq[:, 0, :, :], dbias[:, None, :].to_broadcast([128, NTI, 32]), ALU.add,
                     [b_dtt[j], b_c], [b_dtt[j]])
                g.act(dq[:, 1, :, :], dq[:, 0, :, :], AF.Exp, [b_dtt[j]], [b_dtt[j]])
                g.ts("dve", dq[:, 1, :, :], dq[:, 1, :, :], 1.0, None, ALU.add, None, [b_dtt[j]], [b_dtt[j]])
                g.act(dq[:, 1, :, :], dq[:, 1, :, :], AF.Ln, [b_dtt[j]], [b_dtt[j]])
                g.tt("dve", dq[:, 2, :, :], dq[:, 1, :, :], aB[:, None, :].to_broadcast([128, NTI, 32]), ALU.mult,
                     [b_dtt[j], b_c], [b_dtt[j]])
                g.dma("pool", dtsp[t0:t0 + TB, :].rearrange("(i p) c -> p i c", p=128), dq[:, 1, :, :], b_dtt[j],
                      reads=[b_dtt[j]])
                g.act(dq[:, 3, :, :], dq[:, 1, :, :], AF.Ln, [b_dtt[j]], [b_dtt[j]])
                g.dma("pool", dta_d[t0:t0 + TB, :].rearrange("(i p) c -> p i c", p=128), dq[:, 2, :, :], b_dtt[j],
                      reads=[b_dtt[j]])
                g.dma("pool", dlog_d[t0:t0 + TB, :].rearrange("(i p) c -> p i c", p=128), dq[:, 3, :, :], b_dtt[j],
                      reads=[b_dtt[j]])
                for c in range(12):
                    a = na % 3
                    na += 1
                    g.act(acc[a][:], xi[:, c, 0:TB], AF.Identity, [b_xin[j], b_c], [b_acc[a]],
                          bias=cb[:, c:c + 1], scale=cw[:, c, 0:1])
                    for k in range(1, 5):
                        g.stt("dve", acc[a][:], xi[:, c, k:k + TB], cw[:, c, k:k + 1], acc[a][:],
                              ALU.mult, ALU.add, [b_xin[j], b_c, b_acc[a]], [b_acc[a]])
                    if c < 10:
                        g.act(sil[:, c, :], acc[a][:], AF.Silu, [b_acc[a]], [b_sil])
                    if c >= 8:
                        q = nsb % 2
                        nsb += 1
                        g.act(silb[q][:], acc[a][:], AF.Silu, [b_acc[a]], [b_silb[q]])
                        g.dma("pool", BCT[c - 8, :, t0:t0 + TB], silb[q][:], b_silb[q], reads=[b_silb[q]])
                for ti in range(NTI):
                    i = nt_ % 2
                    nt_ += 1
                    tk = slice(t0 + ti * 128, t0 + (ti + 1) * 128)
                    for c in range(8):
                        g.tr(pt[i][:, c * 128:(c + 1) * 128], sil[:, c, ti * 128:(ti + 1) * 128], identf[:],
                             [b_sil, b_c], [b_pt[i]])
                    for c in range(2):
                        g.tr(pbt[i][:, c * 128:(c + 1) * 128], sil[:, 8 + c, ti * 128:(ti + 1) * 128], identf[:],
                             [b_sil, b_c], [b_pbt[i]])
                    g.cp("act", xst[i][:], pt[i][:], [b_pt[i]], [b_xst[i]])
                    g.cp("dve", bst[i][:], pbt[i][:, 0:256], [b_pbt[i]], [b_bst[i]])
                    g.dma("pool", xs_tm[tk, :], xst[i][:], b_xst[i], reads=[b_xst[i]])
                    g.dma("pool", B_tm[tk, :], bst[i][:], b_bst[i], reads=[b_bst[i]])
            g.flush(ENGS)

    def phase_ssd_pass(l, d):
        with ExitStack() as es:
            tri = sb(es, "tri", [128, 128], F32)
            ones = sb(es, "ones", [128, 128], F32)
            identf = sb(es, "identf", [128, 128], F32)
            identb = sb(es, "identb", [128, 128], BF16)
            maskB = sb(es, "maskB", [128, 16, 128], F32)
            fl = sb(es, "fl", [128, 2], F32)
            b_c = Buf("consts")
            xs = [sb(es, "xs%d" % i, [128, 1024], F32) for i in range(3)]
            xsb = [sb(es, "xsb%d" % i, [128, 1024], BF16) for i in range(2)]
            b_xsb = [Buf("xsb0"), Buf("xsb1")]
            Bt = [sb(es, "Bt%d" % i, [128, 256], BF16) for i in range(3)]
            bct = [sb(es, "bct%d" % i, [128, 4, 128], BF16) for i in range(3)]
            dts = [sb(es, "dts%d" % i, [128, 3, 16], F32) for i in range(3)]
            b_in = [Buf("in0"), Buf("in1"), Buf("in2")]
            R_ = [sb(es, "R%d" % i, [128, 16, 128], F32) for i in range(2)]
            b_R_ = [Buf("R0"), Buf("R1")]
            cs_sb_ = [sb(es, "cs_sb%d" % i, [128, 16], F32) for i in range(2)]
            ecs_ = [sb(es, "ecs%d" % i, [128, 16], F32) for i in range(2)]
            b_cs_ = [Buf("cs_sb0"), Buf("cs_sb1")]
            b_ecs_ = [Buf("ecs0"), Buf("ecs1")]
            seg_ = [sb(es, "seg%d" % i, [128, 16, 128], F32) for i in range(2)]
            b_seg_ = [Buf("seg0"), Buf("seg1")]
            cbT_ = [sb(es, "cbT%d" % i, [128, 2, 128], F32) for i in range(2)]
            b_cbT_ = [Buf("cbT0"), Buf("cbT1")]
            MT_ = [sb(es, "MT%d" % i, [128, 16, 128], BF16) for i in range(2)]
            b_MT_ = [Buf("MT0"), Buf("MT1")]
            xr = sb(es, "xr", [128, 1024], BF16)
            b_xr = Buf("xr")
            xd_ = [sb(es, "xd%d" % i, [128, 1024], BF16) for i in range(2)]
            b_xd_ = [Buf("xd0"), Buf("xd1")]
            S = sb(es, "S", [128, 1024], F32)
            Sbf = sb(es, "Sbf", [128, 1024], BF16)
            b_S, b_Sbf = Buf("S"), Buf("Sbf")
            decB_ = [sb(es, "decB%d" % i, [128, 16], F32) for i in range(2)]
            b_decB_ = [Buf("decB0"), Buf("decB1")]
            t1_ = [sb(es, "t1_%d" % i, [128, 1024], F32) for i in range(2)]
            b_t1_ = [Buf("t1_0"), Buf("t1_1")]
            yd = [sb(es, "yd%d" % i, [128, 1024], F32) for i in range(2)]
            b_yd = [Buf("yd0"), Buf("yd1")]
            pA = ps(es, "pA", [128, 1024], F32)
            pM = ps(es, "pM", [128, 512], F32)
            pY = ps(es, "pY", [128, 1024], F32)
            pO = ps(es, "pO", [128, 1024], F32)
            b_pA, b_pM, b_pY, b_pO = Buf("pA"), Buf("pM"), Buf("pY"), Buf("pO")
            if d == 1:
                dB = sb(es, "dB", [128, 16], F32)
                nwB = sb(es, "nwB", [128, 1024], F32)
                yfl = [sb(es, "yfl%d" % i, [128, 1024], F32) for i in range(3)]
                zt = [sb(es, "zt%d" % i, [128, 1024], F32) for i in range(3)]
                b_in2 = [Buf("in2_0"), Buf("in2_1"), Buf("in2_2")]
                sq = sb(es, "sq", [128, 1024], F32)
                b_sq = Buf("sq")
                t2 = sb(es, "t2", [128, 1024], F32)
                b_t2 = Buf("t2")
                gs = sb(es, "gs", [128, 4], F32)
                b_gs = Buf("gs")
                ynb = sb(es, "ynb", [128, 1024], BF16)
                b_ynb = Buf("ynb")
                yTs = [sb(es, "yTs%d" % i, [128, 8, 128], BF16) for i in range(2)]
                b_yTs = [Buf("yTs0"), Buf("yTs1")]
                pT = ps(es, "pT", [128, 1024], BF16)
                b_pT = Buf("pT")
                g.dma("sp", dB[:], ssm_d[l:l + 1, :].partition_broadcast(128), b_c, writes=[b_c])
                g.dma("sp", nwB[:], ssm_norm_w[l:l + 1, :].partition_broadcast(128), b_c, writes=[b_c])
                g.dma("pool", identb[:], ident_in, b_c, writes=[b_c])
            g.dma("sp", tri[:], tri_in[d], b_c, writes=[b_c])
            g.dma("sp", ones[:], ones_in, b_c, writes=[b_c])
            g.dma("sp", identf[:], ident_in, b_c, writes=[b_c])
            g.dma("sp", fl[:], flag, b_c, writes=[b_c])
            for h in range(16):
                g.dma("sp", maskB[:, h, :], maskadd_in[d], b_c, writes=[b_c])
            g.memset("dve", S[:], 0.0, [b_S])
            g.memset("pool", Sbf[:], 0.0, [b_Sbf])
            last = 127 if d == 0 else 0
            order = list(range(NCH)) if d == 0 else list(range(NCH - 1, -1, -1))
            pend = []

            def drain(k):
                for _ in range(k):
                    if pend:
                        pend.pop(0)()

            for n, c in enumerate(order):
                i = n % 3
                j2 = n % 2
                R, b_R, cs_sb, b_cs, ecs, b_ecs = R_[j2], b_R_[j2], cs_sb_[j2], b_cs_[j2], ecs_[j2], b_ecs_[j2]
                seg, b_seg, cbT, b_cbT, MT, b_MT = seg_[j2], b_seg_[j2], cbT_[j2], b_cbT_[j2], MT_[j2], b_MT_[j2]
                xd, b_xd, decB, b_decB, t1, b_t1 = xd_[j2], b_xd_[j2], decB_[j2], b_decB_[j2], t1_[j2], b_t1_[j2]
                tk = slice(c * 128, (c + 1) * 128)
                g.dma("sp", xs[i][:], xs_tm[tk, :], b_in[i], writes=[b_in[i]])
                g.dma("sp", Bt[i][:], B_tm[tk, :], b_in[i], writes=[b_in[i]])
                g.dma("sp", bct[i][:], BCT[:, :, tk].rearrange("c p t -> p c t"), b_in[i], writes=[b_in[i]])
                g.dma("sp", dts[i][:, 0, :], dtsp[tk, d * 16:(d + 1) * 16], b_in[i], writes=[b_in[i]])
                g.dma("sp", dts[i][:, 1, :], dta_d[tk, d * 16:(d + 1) * 16], b_in[i], writes=[b_in[i]])
                g.dma("sp", dts[i][:, 2, :], dlog_d[tk, d * 16:(d + 1) * 16], b_in[i], writes=[b_in[i]])
                if d == 1:
                    g.dma("sp", yfl[i][:], yf_d[tk, :], b_in2[i], writes=[b_in2[i]])
                    g.dma("sp", zt[i][:], z_tm[tk, :], b_in2[i], writes=[b_in2[i]])
                dta = dts[i][:, 1, :]
                dsp = dts[i][:, 0, :]
                if (d == 0 and c == MIDC) or (d == 1 and c == MIDC - 1):
                    g.ts("dve", S[:], S[:], fl[:, 0:1], None, ALU.mult, None, [b_S, b_c], [b_S])
                    g.ts("pool", Sbf[:], Sbf[:], fl[:, 0:1], None, ALU.mult, None, [b_Sbf, b_c], [b_Sbf])
                g.mm(pM[:, 0:16], tri[:], dta, True, True, [b_c, b_in[i]], [b_pM])
                for gi in range(2):
                    g.mm(pM[:, 128 + gi * 128:256 + gi * 128], bct[i][:, gi, :], bct[i][:, 2 + gi, :], True, True,
                         [b_in[i]], [b_pM])
                g.tt("dve", cs_sb[:], pM[:, 0:16], dts[i][:, 2, :], ALU.subtract, [b_pM, b_in[i]], [b_cs])
                g.cp("act", cbT[:], pM[:, 128:384].rearrange("p (g t) -> p g t", g=2), [b_pM], [b_cbT])
                drain(2)
                g.act(ecs[:], pM[:, 0:16], AF.Exp, [b_pM], [b_ecs])
                g.cp("act", xsb[j2][:], xs[i][:], [b_in[i]], [b_xsb[j2]])
                g.tt("dve", R[:], tri[:, None, :].to_broadcast([128, 16, 128]), dta[:, :, None].to_broadcast([128, 16, 128]),
                     ALU.mult, [b_c, b_in[i]], [b_R])
                drain(2)
                for gi in range(2):
                    g.mm(pO[:, gi * 512:(gi + 1) * 512], bct[i][:, 2 + gi, :], Sbf[:, gi * 512:(gi + 1) * 512], True, True,
                         [b_in[i], b_Sbf], [b_pO])
                for half in range(2):
                    hs = slice(half * 8, half * 8 + 8)
                    for q in range(2):
                        h0 = half * 8 + q * 4
                        g.mm(pA[:, q * 512:(q + 1) * 512], identf[:], maskB[:, h0:h0 + 4, :].rearrange("p h t -> p (h t)"),
                             True, False, [b_c], [b_pA])
                        g.mm(pA[:, q * 512:(q + 1) * 512], ones[:], R[:, h0:h0 + 4, :].rearrange("p h t -> p (h t)"),
                             False, True, [b_c, b_R], [b_pA])
                    g.tt("dve", seg[:, hs, :], pA[:].rearrange("p (h t) -> p h t", h=8),
                         cs_sb[:, hs][:, :, None].to_broadcast([128, 8, 128]), ALU.subtract, [b_pA, b_cs], [b_seg])
                    g.act(decB[:, hs], pA[:].rearrange("p (h t) -> p h t", h=8)[:, :, last], AF.Exp, [b_pA], [b_decB])
                    if half == 1:
                        pass
                    drain(2)
                    g.act(seg[:, hs, :], seg[:, hs, :], AF.Exp, [b_seg], [b_seg])
                    g.tt("pool" if half == 0 else "dve", MT[:, hs, :], seg[:, hs, :],
                         cbT[:, half:half + 1, :].to_broadcast([128, 8, 128]), ALU.mult, [b_seg, b_cbT], [b_MT])
                drain(2)
                for h in range(16):
                    g.mm(pY[:, h * 64:(h + 1) * 64], MT[:, h, :], xsb[j2][:, h * 64:(h + 1) * 64], True, True,
                         [b_MT, b_xsb[j2]], [b_pY])
                drain(2)
                g.tt("dve", t1[:].rearrange("p (h q) -> p h q", h=16), pO[:].rearrange("p (h q) -> p h q", h=16),
                     ecs[:, :, None].to_broadcast([128, 16, 64]), ALU.mult, [b_pO, b_ecs], [b_t1])
                g.tt("dve", yd[j2][:], pY[:], t1[:], ALU.add, [b_pY, b_t1], [b_yd[j2]])
                g.tt("pool", xd[:].rearrange("p (h q) -> p h q", h=16), xsb[j2][:].rearrange("p (h q) -> p h q", h=16),
                     seg[:, :, last:last + 1].to_broadcast([128, 16, 64]), ALU.mult, [b_xsb[j2], b_seg], [b_xd])
                drain(2)
                for gi in range(2):
                    g.mm(pO[:, gi * 512:(gi + 1) * 512], Bt[i][:, gi * 128:(gi + 1) * 128], xd[:, gi * 512:(gi + 1) * 512],
                         True, True, [b_in[i], b_xd], [b_pO])
                g.tt("pool", S[:].rearrange("p (h q) -> p h q", h=16), S[:].rearrange("p (h q) -> p h q", h=16),
                     decB[:, :, None].to_broadcast([128, 16, 64]), ALU.mult, [b_S, b_decB], [b_S])
                g.tt("dve", S[:], S[:], pO[:], ALU.add, [b_S, b_pO], [b_S])
                g.cp("act", Sbf[:], S[:], [b_S], [b_Sbf])
                if d == 0:
                    g.dma("pool", yf_d[tk, :], yd[j2][:], b_yd[j2], reads=[b_yd[j2]])
                else:
                    drain(len(pend))
                    y = yd[j2]
                    b_y = b_yd[j2]
                    yfl_i, zt_i, xs_i, b2_i, bi_i, yTs_j, b_yTs_j = yfl[i], zt[i], xs[i], b_in2[i], b_in[i], yTs[j2], b_yTs[j2]

                    def mk(y=y, b_y=b_y, yfl_i=yfl_i, zt_i=zt_i, xs_i=xs_i, b2_i=b2_i, bi_i=bi_i, yTs_j=yTs_j,
                           b_yTs_j=b_yTs_j, tk=tk):
                        ops = []
                        ops.append(lambda: g.tt("pool", y[:], y[:], yfl_i[:], ALU.add, [b_y, b2_i], [b_y]))
                        ops.append(lambda: g.tt("pool", t2[:].rearrange("p (h q) -> p h q", h=16),
                                                xs_i[:].rearrange("p (h q) -> p h q", h=16),
                                                dB[:, :, None].to_broadcast([128, 16, 64]), ALU.mult, [bi_i, b_c], [b_t2]))
                        ops.append(lambda: g.tt("dve", y[:], y[:], t2[:], ALU.add, [b_y, b_t2], [b_y]))
                        ops.append(lambda: g.tt("dve", y[:], y[:], zt_i[:], ALU.mult, [b_y, b2_i], [b_y]))
                        ops.append(lambda: g.act(sq[:], y[:], AF.Square, [b_y], [b_sq]))
                        ops.append(lambda: g.red("dve", gs[:, 0:2], sq[:].rearrange("p (g q) -> p g q", g=2), [b_sq], [b_gs]))
                        ops.append(lambda: g.ts("dve", gs[:, 2:4], gs[:, 0:2], 1.0 / 512, EPS, ALU.mult, ALU.add, [b_gs], [b_gs]))
                        ops.append(lambda: g.act(gs[:, 2:4], gs[:, 2:4], AF.Sqrt, [b_gs], [b_gs]))
                        ops.append(lambda: g._record("dve", lambda e: e.reciprocal(out=gs[:, 2:4], in_=gs[:, 2:4]),
                                                     [b_gs], [b_gs]))
                        ops.append(lambda: g.tt("dve", y[:].rearrange("p (g q) -> p g q", g=2),
                                                y[:].rearrange("p (g q) -> p g q", g=2),
                                                gs[:, 2:4][:, :, None].to_broadcast([128, 2, 512]), ALU.mult, [b_y, b_gs], [b_y]))
                        ops.append(lambda: g.tt("pool", ynb[:], y[:], nwB[:], ALU.mult, [b_y, b_c], [b_ynb]))

                        def trs():
                            for k in range(8):
                                g.tr(pT[:, k * 128:(k + 1) * 128], ynb[:, k * 128:(k + 1) * 128], identb[:], [b_ynb, b_c], [b_pT])
                        ops.append(trs)
                        ops.append(lambda: g.cp("act", yTs_j[:], pT[:].rearrange("p (k t) -> p k t", k=8), [b_pT], [b_yTs_j]))
                        ops.append(lambda: g.dma("pool", yT[0][:, :, tk].rearrange("k p t -> p k t"), yTs_j[:], b_yTs_j,
                                                 reads=[b_yTs_j]))
                        return ops
                    pend.extend(mk())
            drain(len(pend))
            g.flush(ENGS)

    gla_gate_up = din("gla_gate_up", [DEPTH, 2, 16, 512])
    gla_gate_b = din("gla_gate_b", [DEPTH, 2, 512])
    gla_norm_w = din("gla_norm_w", [DEPTH, 256])
    of_d = dscr("of", [NT, 1024])

    def phase_gla_pass(l, d):
        with ExitStack() as es:
            triS = sb(es, "triS", [128, 128], F32)
            uS = sb(es, "uS", [128, 128], F32)
            m01 = sb(es, "m01", [128, 128], F32)
            ones = sb(es, "ones", [128, 128], F32)
            up = sb(es, "up", [16, 512], F32)
            gbr = sb(es, "gbr", [1, 512], F32)
            fl = sb(es, "fl", [128, 2], F32)
            b_c = Buf("consts")
            qT = [sb(es, "qT%d" % i, [128, 4, 128], F32) for i in range(2)]
            kT = [sb(es, "kT%d" % i, [128, 4, 128], F32) for i in range(2)]
            ktm = [sb(es, "ktm%d" % i, [128, 512], F32) for i in range(2)]
            v = [sb(es, "v%d" % i, [128, 1024], BF16) for i in range(2)]
            dn = [sb(es, "dn%d" % i, [16, 128], F32) for i in range(2)]
            b_in = [Buf("in0"), Buf("in1")]
            lsp_ = [sb(es, "lsp%d" % i, [128, 512], F32) for i in range(2)]
            b_lsp_ = [Buf("lsp0"), Buf("lsp1")]
            eb_ = [sb(es, "eb%d" % i, [128, 4, 128], F32) for i in range(2)]
            enb_ = [sb(es, "enb%d" % i, [128, 4, 128], F32) for i in range(2)]
            er_ = [sb(es, "er%d" % i, [128, 512], F32) for i in range(2)]
            b_eb_, b_enb_, b_er_ = [Buf("eb0"), Buf("eb1")], [Buf("enb0"), Buf("enb1")], [Buf("er0"), Buf("er1")]
            qe_ = [sb(es, "qe%d" % i, [128, 4, 128], BF16) for i in range(2)]
            ke_ = [sb(es, "ke%d" % i, [128, 4, 128], BF16) for i in range(2)]
            kdec_ = [sb(es, "kdec%d" % i, [128, 512], BF16) for i in range(2)]
            att_ = [sb(es, "att%d" % i, [128, 4, 128], BF16) for i in range(2)]
            b_qe_, b_ke_ = [Buf("qe0"), Buf("qe1")], [Buf("ke0"), Buf("ke1")]
            b_kdec_, b_att_ = [Buf("kdec0"), Buf("kdec1")], [Buf("att0"), Buf("att1")]
            S = sb(es, "S", [128, 4, 256], F32)
            Sbf = sb(es, "Sbf", [128, 4, 256], BF16)
            b_S, b_Sbf = Buf("S"), Buf("Sbf")
            od = [sb(es, "od%d" % i, [128, 1024], F32) for i in range(2)]
            b_od = [Buf("od0"), Buf("od1")]
            pLR = ps(es, "pLR", [128, 512], F32)
            pB = ps(es, "pB", [128, 512], F32)
            pAt = ps(es, "pAt", [128, 512], F32)
            pOo = ps(es, "pOo", [128, 1024], F32)
            pSp = ps(es, "pSp", [128, 1024], F32)
            b_pLR, b_pB, b_pAt, b_pOo, b_pSp = Buf("pLR"), Buf("pB"), Buf("pAt"), Buf("pOo"), Buf("pSp")
            if d == 1:
                identb = sb(es, "identb", [128, 128], BF16)
                nwB = sb(es, "nwB", [128, 1024], F32)
                ofl = [sb(es, "ofl%d" % i, [128, 1024], F32) for i in range(2)]
                gg = [sb(es, "gg%d" % i, [128, 1024], F32) for i in range(2)]
                b_in2 = [Buf("in2_0"), Buf("in2_1")]
                sq = sb(es, "sq", [128, 1024], F32)
                b_sq = Buf("sq")
                gs = sb(es, "gs", [128, 8], F32)
                b_gs = Buf("gs")
                onb = sb(es, "onb", [128, 1024], BF16)
                b_onb = Buf("onb")
                yTs = [sb(es, "yTs%d" % i, [128, 8, 128], BF16) for i in range(2)]
                b_yTs = [Buf("yTs0"), Buf("yTs1")]
                pT = ps(es, "pT", [128, 1024], BF16)
                b_pT = Buf("pT")
                g.dma("pool", identb[:], ident_in, b_c, writes=[b_c])
                for h in range(4):
                    g.dma("sp", nwB[:, h * 256:(h + 1) * 256], gla_norm_w[l:l + 1, :].partition_broadcast(128), b_c,
                          writes=[b_c])
            g.dma("sp", triS[:], tri_in[d], b_c, writes=[b_c])
            g.dma("sp", uS[:], ustrict_in[d], b_c, writes=[b_c])
            g.dma("sp", m01[:], mask01_in[d], b_c, writes=[b_c])
            g.dma("sp", ones[:], ones_in, b_c, writes=[b_c])
            g.dma("sp", up[:], gla_gate_up[l, d], b_c, writes=[b_c])
            g.dma("sp", gbr[:], gla_gate_b[l, d:d + 1, :], b_c, writes=[b_c])
            g.dma("sp", fl[:], flag, b_c, writes=[b_c])
            g.ts("dve", triS[:], triS[:], -1.0 / 16.0, None, ALU.mult, None, [b_c], [b_c])
            g.ts("dve", uS[:], uS[:], -1.0 / 16.0, None, ALU.mult, None, [b_c], [b_c])
            g.memset("dve", S[:], 0.0, [b_S])
            g.memset("pool", Sbf[:], 0.0, [b_Sbf])
            last = 127 if d == 0 else 0
            order = list(range(NCH)) if d == 0 else list(range(NCH - 1, -1, -1))
            for n, c in enumerate(order):
                i = n % 2
                lsp, b_lsp, eb, b_eb, enb, b_enb, er, b_er = lsp_[i], b_lsp_[i], eb_[i], b_eb_[i], enb_[i], b_enb_[i], er_[i], b_er_[i]
                qe, b_qe, ke, b_ke, kdec, b_kdec, att, b_att = qe_[i], b_qe_[i], ke_[i], b_ke_[i], kdec_[i], b_kdec_[i], att_[i], b_att_[i]
                tk = slice(c * 128, (c + 1) * 128)
                g.dma("sp", qT[i][:], gqT[:, :, tk].rearrange("h p t -> p h t"), b_in[i], writes=[b_in[i]])
                g.dma("sp", kT[i][:], gkT[:, :, tk].rearrange("h p t -> p h t"), b_in[i], writes=[b_in[i]])
                g.dma("sp", ktm[i][:], kg_tm[tk, :], b_in[i], writes=[b_in[i]])
                g.dma("sp", v[i][:], vg_tm[tk, :], b_in[i], writes=[b_in[i]])
                g.dma("sp", dn[i][:], dnT[d, :, tk], b_in[i], writes=[b_in[i]])
                if d == 1:
                    g.dma("sp", ofl[i][:], of_d[tk, :], b_in2[i], writes=[b_in2[i]])
                    g.dma("sp", gg[i][:], gg_tm[tk, :], b_in2[i], writes=[b_in2[i]])
                if (d == 0 and c == MIDC) or (d == 1 and c == MIDC - 1):
                    g.ts("dve", S[:], S[:], fl[:, 0:1], None, ALU.mult, None, [b_S, b_c], [b_S])
                    g.ts("pool", Sbf[:], Sbf[:], fl[:, 0:1], None, ALU.mult, None, [b_Sbf, b_c], [b_Sbf])
                g.mm(pLR[:], dn[i][:], up[:], True, False, [b_in[i], b_c], [b_pLR])
                g.mm(pLR[:], ones[0:1, :], gbr[:], False, True, [b_c], [b_pLR])
                g.act(lsp[:], pLR[:], AF.Exp, [b_pLR], [b_lsp], scale=-1.0)
                g.ts("dve", lsp[:], lsp[:], 1.0, None, ALU.add, None, [b_lsp], [b_lsp])
                g.act(lsp[:], lsp[:], AF.Ln, [b_lsp], [b_lsp])
                for h in range(4):
                    g.mm(pB[:, h * 128:(h + 1) * 128], lsp[:, h * 128:(h + 1) * 128], triS[:], True, True,
                         [b_lsp, b_c], [b_pB])
                g.mm(pLR[:], uS[:], lsp[:], True, True, [b_c, b_lsp], [b_pLR])
                pB3 = pB[:].rearrange("p (h t) -> p h t", h=4)
                g.act(eb[:], pB3, AF.Exp, [b_pB], [b_eb])
                g.act(enb[:], pB3, AF.Exp, [b_pB], [b_enb], scale=-1.0)
                g.act(er[:], pLR[:], AF.Exp, [b_pLR], [b_er])
                g.tt("dve", qe[:], qT[i][:], eb[:], ALU.mult, [b_in[i], b_eb], [b_qe])
                g.tt("pool", ke[:], kT[i][:], enb[:], ALU.mult, [b_in[i], b_enb], [b_ke])
                g.tt("pool", kdec[:], ktm[i][:], er[:], ALU.mult, [b_in[i], b_er], [b_kdec])
                for h in range(4):
                    g.mm(pAt[:, h * 128:(h + 1) * 128], ke[:, h, :], qe[:, h, :], True, True, [b_ke, b_qe], [b_pAt])
                g.tt("dve", att[:], pAt[:].rearrange("p (h t) -> p h t", h=4), m01[:, None, :].to_broadcast([128, 4, 128]),
                     ALU.mult, [b_pAt, b_c], [b_att])
                for h in range(4):
                    g.mm(pOo[:, h * 256:(h + 1) * 256], att[:, h, :], v[i][:, h * 256:(h + 1) * 256], True, False,
                         [b_att, b_in[i]], [b_pOo])
                    g.mm(pOo[:, h * 256:(h + 1) * 256], qe[:, h, :], Sbf[:, h, :], False, True, [b_qe, b_Sbf], [b_pOo])
                for h in range(4):
                    g.mm(pSp[:, h * 256:(h + 1) * 256], kdec[:, h * 128:(h + 1) * 128], v[i][:, h * 256:(h + 1) * 256],
                         True, True, [b_kdec, b_in[i]], [b_pSp])
                for h in range(4):
                    g.stt("dve", S[:, h, :], S[:, h, :], eb[:, h, last:last + 1], pSp[:, h * 256:(h + 1) * 256],
                          ALU.mult, ALU.add, [b_S, b_eb, b_pSp], [b_S])
                g.cp("act", Sbf[:], S[:], [b_S], [b_Sbf])
                if d == 0:
                    g.cp("act", od[i][:], pOo[:], [b_pOo], [b_od[i]])
                    g.dma("pool", of_d[tk, :], od[i][:], b_od[i], reads=[b_od[i]])
                else:
                    o = od[i]
                    g.tt("dve", o[:], pOo[:], ofl[i][:], ALU.add, [b_pOo, b_in2[i]], [b_od[i]])
                    g.act(sq[:], o[:], AF.Square, [b_od[i]], [b_sq])
                    g.red("dve", gs[:, 0:4], sq[:].rearrange("p (h q) -> p h q", h=4), [b_sq], [b_gs])
                    g.ts("dve", gs[:, 4:8], gs[:, 0:4], 1.0 / 256, EPS, ALU.mult, ALU.add, [b_gs], [b_gs])
                    g.act(gs[:, 4:8], gs[:, 4:8], AF.Sqrt, [b_gs], [b_gs])
                    g._record("dve", lambda e, gs=gs: e.reciprocal(out=gs[:, 4:8], in_=gs[:, 4:8]), [b_gs], [b_gs])
                    g.tt("dve", o[:].rearrange("p (h q) -> p h q", h=4), o[:].rearrange("p (h q) -> p h q", h=4),
                         gs[:, 4:8][:, :, None].to_broadcast([128, 4, 256]), ALU.mult, [b_od[i], b_gs], [b_od[i]])
                    g.tt("pool", o[:], o[:], nwB[:], ALU.mult, [b_od[i], b_c], [b_od[i]])
                    g.tt("pool", onb[:], o[:], gg[i][:], ALU.mult, [b_od[i], b_in2[i]], [b_onb])
                    for k in range(8):
                        g.tr(pT[:, k * 128:(k + 1) * 128], onb[:, k * 128:(k + 1) * 128], identb[:], [b_onb, b_c], [b_pT])
                    g.cp("act", yTs[i][:], pT[:].rearrange("p (k t) -> p k t", k=8), [b_pT], [b_yTs[i]])
                    g.dma("pool", yT[1][:, :, tk].rearrange("k p t -> p k t"), yTs[i][:], b_yTs[i], reads=[b_yTs[i]])
            g.flush(ENGS)

    na_rpb = din("na_rpb", [DEPTH, 16, 15, 31])
    jflip_in = din("jflip", [64, 64])
    mint_in = din("m_int", [128, 16, 64])
    medge_in = din("m_edge", [128, 16, 64])
    rpbpad = dscr("rpbpad", [16, 17, 192])

    def phase_na(l):
        with ExitStack() as es:
            T2 = [sb(es, "T2i", [128, 16, 16, 64], BF16), sb(es, "T2e", [128, 16, 16, 64], BF16)]
            b_T2 = Buf("T2")
            Mx = [sb(es, "Mi", [128, 16, 64], F32), sb(es, "Me", [128, 16, 64], F32)]
            jf = sb(es, "jf", [64, 64], F32)
            identb = sb(es, "identb", [128, 128], BF16)
            fl = sb(es, "fl", [128, 2], F32)
            b_c = Buf("consts")
            padt = sb(es, "padt", [16, 17, 192], F32)
            rp = sb(es, "rp", [16, 15, 31], F32)
            b_pad = Buf("pad")
            Tpp = [sb(es, "Tpp%d" % i, [64, 4, 17, 64], F32) for i in range(2)]
            b_Tpp = [Buf("Tpp0"), Buf("Tpp1")]
            eW = [sb(es, "eW%d" % i, [128, 640], F32) for i in range(3)]
            b_eW = [Buf("eW%d" % i) for i in range(3)]
            Kc = [sb(es, "Kc%d" % i, [128, 8, 128], BF16) for i in range(8)]
            Vc = [sb(es, "Vc%d" % i, [128, 16, 65], BF16) for i in range(8)]
            b_K = [Buf("K%d" % i) for i in range(8)]
            b_V = [Buf("V%d" % i) for i in range(8)]
            Qc = [sb(es, "Qc%d" % i, [128, 8, 128], BF16) for i in range(2)]
            b_Q = [Buf("Q0"), Buf("Q1")]
            Pt = [sb(es, "Pt%d" % i, [128, 5, 2, 64], BF16) for i in range(3)]
            b_P = [Buf("P%d" % i) for i in range(3)]
            rden = sb(es, "rden", [128, 16], F32)
            b_rden = Buf("rden")
            oA = sb(es, "oA", [128, 1024], F32)
            oB = sb(es, "oB", [128, 1024], F32)
            b_oA, b_oB = Buf("oA"), Buf("oB")
            onb = sb(es, "onb", [128, 1024], BF16)
            b_onb = Buf("onb")
            yTs = [sb(es, "yTs%d" % i, [128, 8, 128], BF16) for i in range(2)]
            b_yTs = [Buf("yTs0"), Buf("yTs1")]
            pS_t = [ps(es, "pS%d" % i, [128, 1024], F32) for i in range(2)]
            pS = [t[:, 0:640] for t in pS_t]
            b_pS = [Buf("pS%d" % i) for i in range(2)]
            pOv = [ps(es, "pOv%d" % i, [128, 512], F32) for i in range(3)]
            b_pOv = Buf("pOv")
            pT = ps(es, "pT", [128, 1024], BF16)
            b_pT = Buf("pT")
            g.dma("sp", Mx[0][:], mint_in, b_c, writes=[b_c])
            g.dma("sp", Mx[1][:], medge_in, b_c, writes=[b_c])
            g.dma("sp", jf[:], jflip_in, b_c, writes=[b_c])
            g.dma("sp", fl[:], flag, b_c, writes=[b_c])
            g.dma("pool", identb[:], ident_in, b_c, writes=[b_c])
            for i in range(8):
                g.memset("pool", Vc[i][:], 1.0, [b_V[i]])
            g.memset("dve", padt[:], 0.0, [b_pad])
            g.dma("sp", rp[:], na_rpb[l], b_pad, writes=[b_pad])
            g.cp("dve", padt[:, 1:16, 64:95], rp[:], [b_pad], [b_pad])
            g.dma("sp", rpbpad, padt[:], b_pad, reads=[b_pad], writes=[b_pad])
            nb = 0
            for hg in range(4):
                tp = Tpp[hg % 2]
                b_tp = b_Tpp[hg % 2]
                for hh in range(4):
                    h = hg * 4 + hh
                    src = bass.AP(tensor=rpbpad.tensor, offset=h * 17 * 192 + 16, ap=[[1, 64], [192, 17], [1, 64]])
                    g.dma("sp", tp[:, hh, :, :], src, b_tp, reads=[b_pad], writes=[b_tp])
                for hh in range(4):
                    h = hg * 4 + hh
                    for sbt in range(2):
                        q = nb % 2
                        nb += 1
                        for si in range(8):
                            s_ = sbt * 8 + si
                            g.mm(pS[q][:, si * 64:(si + 1) * 64], tp[:, hh, s_:s_ + 2, :].rearrange("p r k -> p (r k)"),
                                 jf[:], True, True, [b_tp, b_c], [b_pS[q]])
                        g.act(eW[q][:, 0:512], pS[q][:, 0:512], AF.Exp, [b_pS[q]], [b_eW[q]])
                        g.tt("dve", T2[0][:, h, sbt * 8:sbt * 8 + 8, :], eW[q][:, 0:512].rearrange("p (s q) -> p s q", s=8),
                             Mx[0][:, sbt * 8:sbt * 8 + 8, :], ALU.mult, [b_eW[q], b_c], [b_T2])
                        g.tt("pool", T2[1][:, h, sbt * 8:sbt * 8 + 8, :], eW[q][:, 0:512].rearrange("p (s q) -> p s q", s=8),
                             Mx[1][:, sbt * 8:sbt * 8 + 8, :], ALU.mult, [b_eW[q], b_c], [b_T2])
            if NA_SPLIT:
                g.flush(ENGS)
            loaded = [-1]
            cnt = {"s": 0, "p": 0, "y": 0}

            def ensure(upto):
                while loaded[0] < min(upto, NCH - 1):
                    kt = loaded[0] + 1
                    sl_ = kt % 8
                    tk_ = slice(kt * 128, (kt + 1) * 128)
                    g.dma("sp", Kc[sl_][:], nkT[:, :, tk_].rearrange("c p t -> p c t"), b_K[sl_], writes=[b_K[sl_]])
                    g.dma("sp", Vc[sl_][:, :, 0:64], vn_tm[tk_, :].rearrange("p (h d) -> p h d", h=16), b_V[sl_],
                          writes=[b_V[sl_]])
                    loaded[0] = kt

            def na_pair(qp, kts, var, qi, o_dst, b_o):
                nt = len(kts)
                d0 = kts[0] - qp

                def st_s(h):
                    ch, p0 = h // 2, (h % 2) * 64
                    q = h % 2
                    for ti, kt in enumerate(kts):
                        g.mm(pS[q][:, ti * 128:(ti + 1) * 128], Kc[kt % 8][p0:p0 + 64, ch, :], Qc[qi][p0:p0 + 64, ch, :],
                             True, True, [b_K[kt % 8], b_Q[qi]], [b_pS[q]])

                def st_e(h):
                    q = h % 3
                    g.act(eW[q][:, 0:nt * 128], pS[h % 2][:, 0:nt * 128], AF.Exp, [b_pS[h % 2]], [b_eW[q]])
                    e4 = eW[q][:, 0:nt * 128].rearrange("p (t r c) -> p t r c", t=nt, r=2)
                    for qr2 in range(2):
                        s0 = 2 * d0 + 8 - qr2
                        base = T2[var][:, h, s0, :]
                        tv = bass.AP(tensor=base.tensor, offset=base.offset, ap=[list(base.ap[0]), [128, nt], [1, 64]])
                        g.tt("dve" if qr2 == 0 else "pool", Pt[q][:, 0:nt, qr2, :], e4[:, :, qr2, :], tv, ALU.mult,
                             [b_eW[q], b_T2], [b_P[q]])

                def st_v(h):
                    q = h % 3
                    bank, off = h // 7, (h % 7) * 65
                    for ti, kt in enumerate(kts):
                        g.mm(pOv[bank][:, off:off + 65], Pt[q][:, ti, :, :].rearrange("p r c -> p (r c)"),
                             Vc[kt % 8][:, h, :], ti == 0, ti == nt - 1, [b_P[q], b_V[kt % 8]], [b_pOv])

                LAG = NA_LAG
                for step in range(16 + 2 * LAG):
                    if step < 16:
                        st_s(step)
                    if LAG <= step < 16 + LAG:
                        st_e(step - LAG)
                    if step >= 2 * LAG:
                        st_v(step - 2 * LAG)
                for bank in range(3):
                    nh = 7 if bank < 2 else 2
                    pv = pOv[bank][:, 0:nh * 65].rearrange("p (h e) -> p h e", h=nh)
                    g._record("dve", lambda e, bank=bank, nh=nh, pv=pv: e.reciprocal(out=rden[:, bank * 7:bank * 7 + nh],
                                                                                   in_=pv[:, :, 64]), [b_pOv], [b_rden])
                    g.tt("dve", o_dst[:, bank * 448:bank * 448 + nh * 64].rearrange("p (h d) -> p h d", h=nh), pv[:, :, 0:64],
                         rden[:, bank * 7:bank * 7 + nh][:, :, None].to_broadcast([128, nh, 64]), ALU.mult,
                         [b_pOv, b_rden], [b_o])

            for qp in range(NCH):
                qi = qp % 2
                tk = slice(qp * 128, (qp + 1) * 128)
                ensure(qp + 3)
                g.dma("sp", Qc[qi][:], nqT[:, :, tk].rearrange("c p t -> p c t"), b_Q[qi], writes=[b_Q[qi]])
                interior = [qp - 2, qp - 1, qp, qp + 1, qp + 2]
                if qp < 2:
                    na_pair(qp, [0, 1, 2, 3], 1, qi, onb, b_onb)
                elif qp >= NCH - 2:
                    na_pair(qp, [NCH - 4, NCH - 3, NCH - 2, NCH - 1], 1, qi, onb, b_onb)
                elif MIDC - 2 <= qp < MIDC + 2:
                    na_pair(qp, interior, 0, qi, oA, b_oA)
                    ekts = [MIDC - 4, MIDC - 3, MIDC - 2, MIDC - 1] if qp < MIDC else [MIDC, MIDC + 1, MIDC + 2, MIDC + 3]
                    na_pair(qp, ekts, 1, qi, oB, b_oB)
                    g.ts("dve", oA[:], oA[:], fl[:, 0:1], None, ALU.mult, None, [b_oA, b_c], [b_oA])
                    g.stt("dve", onb[:], oB[:], fl[:, 1:2], oA[:], ALU.mult, ALU.add, [b_oB, b_oA, b_c], [b_onb])
                else:
                    na_pair(qp, interior, 0, qi, onb, b_onb)
                yi = cnt["y"] % 2
                cnt["y"] += 1
                for k in range(8):
                    g.tr(pT[:, k * 128:(k + 1) * 128], onb[:, k * 128:(k + 1) * 128], identb[:], [b_onb, b_c], [b_pT])
                g.cp("act", yTs[yi][:], pT[:].rearrange("p (k t) -> p k t", k=8), [b_pT], [b_yTs[yi]])
                g.dma("pool", yT[2][:, :, tk].rearrange("k p t -> p k t"), yTs[yi][:], b_yTs[yi], reads=[b_yTs[yi]])
            g.flush(ENGS, sched=False)

    stages = []
    stages.append(("p0", phase_p0))
    for l in range(DEPTH):
        x_src = x_in if l == 0 else xc
        stages.append(("ffn1_%d" % l, lambda l=l, x_src=x_src: phase_ffn(
            "f1l%d" % l, ffn1_w_in[l], ffn1_w_out[l], x_src, hT_f1, xa, ln_mix_w[l:l + 1, :], hT_mix)))
        stages.append(("ptm_%d" % l, lambda l=l: phase_inproj_tm(l)))
        stages.append(("pfm_%d" % l, lambda l=l: phase_inproj_fm(l)))
        stages.append(("ssdprep_%d" % l, lambda l=l: phase_ssd_prep(l)))
        stages.append(("ssdf_%d" % l, lambda l=l: phase_ssd_pass(l, 0)))
        stages.append(("ssdb_%d" % l, lambda l=l: phase_ssd_pass(l, 1)))
        stages.append(("glaf_%d" % l, lambda l=l: phase_gla_pass(l, 0)))
        stages.append(("glab_%d" % l, lambda l=l: phase_gla_pass(l, 1)))
        stages.append(("na_%d" % l, lambda l=l: phase_na(l)))
        stages.append(("merge_%d" % l, lambda l=l: phase_merge(l, xa, xb, ln_ffn2_w[l:l + 1, :], hT_f2)))
        if l + 1 < DEPTH:
            stages.append(("ffn2_%d" % l, lambda l=l: phase_ffn(
                "f2l%d" % l, ffn2_w_in[l], ffn2_w_out[l], xb, hT_f2, xc, ln_ffn1_w[l + 1:l + 2, :], hT_f1)))
        else:
            stages.append(("ffn2_%d" % l, lambda l=l: phase_ffn(
                "f2l%d" % l, ffn2_w_in[l], ffn2_w_out[l], xb, hT_f2, None, ln_final_w[0:1, :], None, final_out=y_out)))
    only = getattr(cfg, "only", None)
    for name, fn in stages:
        if only is None or name in only:
            fn()
        if cfg.stop_after == name:
            break
    g.close()
    return nc, g


def _consts():
    k = np.arange(128)
    tri_f = (k[:, None] <= k[None, :]).astype(np.float32)
    tri_b = (k[:, None] >= k[None, :]).astype(np.float32)
    allow_f = (k[:, None] <= k[None, :])
    allow_b = (k[:, None] >= k[None, :])
    c = {
        "ident": np.eye(128, dtype=np.float32),
        "ones": np.ones((128, 128), np.float32),
        "tri": np.stack([tri_f, tri_b]),
        "maskadd": np.stack([np.where(allow_f, 0.0, -30000.0), np.where(allow_b, 0.0, -30000.0)]).astype(np.float32),
        "mask01": np.stack([allow_f, allow_b]).astype(np.float32),
        "ustrict": np.stack([(k[:, None] > k[None, :]), (k[:, None] < k[None, :])]).astype(np.float32),
    }
    kc = np.arange(64)[:, None]
    qc = np.arange(64)[None, :]
    cs = np.clip(qc - 8, 0, 48)
    colmask = ((kc >= cs) & (kc < cs + 16)).astype(np.float32)
    m_int = np.zeros((128, 16, 64), np.float32)
    m_edge = np.zeros((128, 16, 64), np.float32)
    for kr2 in range(2):
        for s_ in range(16):
            ro = s_ + kr2 - 1
            if 0 <= ro <= 14:
                m_edge[kr2 * 64:(kr2 + 1) * 64, s_, :] = colmask
            if 3 <= ro <= 10:
                m_int[kr2 * 64:(kr2 + 1) * 64, s_, :] = colmask
    jflip = np.zeros((64, 64), np.float32)
    jflip[63 - np.arange(64), np.arange(64)] = 1.0
    c.update({"jflip": jflip, "m_int": m_int, "m_edge": m_edge})
    return c


_PROGRAM_CACHE = {}


def run_streams(streams, flags, w, cfg_extra=None):
    nt = streams[0].shape[0]
    cfg = Cfg(nt)
    nc, g = build_program(cfg)
    base = dict(_consts())
    f32 = np.float32
    for k in ("ln_ffn1_w", "ffn1_w_in", "ffn1_w_out", "ln_mix_w", "w_in", "ssm_conv_w", "ssm_conv_b", "ssm_d",
              "ssm_norm_w", "gla_gate_up", "gla_gate_b", "gla_norm_w", "na_rpb", "w_branch_a", "w_branch_b",
              "w_branch_c", "w_out", "ln_ffn2_w", "ffn2_w_in", "ffn2_w_out"):
        base[k] = np.ascontiguousarray(np.asarray(w[k], dtype=f32))
    base["ssm_dt_bias"] = np.ascontiguousarray(np.asarray(w["ssm_dt_bias"], f32).reshape(DEPTH, 32))
    base["ssm_a_log"] = np.ascontiguousarray(np.asarray(w["ssm_a_log"], f32).reshape(DEPTH, 32))
    base["ln_final_w"] = np.ascontiguousarray(np.asarray(w["ln_final_w"], f32).reshape(1, D))
    in_maps = []
    for x, f in zip(streams, flags):
        m = dict(base)
        m["x"] = np.ascontiguousarray(np.asarray(x, f32))
        m["flag"] = np.tile(np.array([[f, 1.0 - f]], f32), (128, 1))
        in_maps.append(m)
    if SPREAD_DIES and len(in_maps) == 4:
        idle = dict(base)
        idle["x"] = np.zeros((nt, D), f32)
        idle["flag"] = np.tile(np.array([[0.0, 1.0]], f32), (128, 1))
        placed = [in_maps[0], in_maps[1], idle, idle, in_maps[2], in_maps[3], idle, idle]
        res = run_bass_kernel_spmd(nc, placed, core_ids=list(range(8)))
        outs = [res.results[i] for i in (0, 1, 4, 5)]
    else:
        res = run_bass_kernel_spmd(nc, in_maps, core_ids=list(range(len(in_maps))))
        outs = res.results
    return [np.asarray(r["y"], dtype=f32) for r in outs]


def kernel(**inputs):
    xp = np.asarray(inputs["x_prompt"], np.float32)
    xs = np.asarray(inputs["x_sample"], np.float32)
    B, S, _ = xp.shape
    B2, S2, _ = xs.shape
    assert S2 == 2 * S and B % 2 == 0
    streams, flags = [], []
    for i in range(B // 2):
        streams.append(xp[2 * i:2 * i + 2].reshape(2 * S, D))
        flags.append(0.0)
    for i in range(B2):
        streams.append(xs[i])
        flags.append(1.0)
    outs = run_streams(streams, flags, inputs)
    yp = np.stack([o.reshape(2, S, D) for o in outs[:B // 2]]).reshape(B, S, D)
    ys = np.stack(outs[B // 2:]).reshape(B2, S2, D)
    return (yp.astype(np.float32), ys.astype(np.float32))
```

```python
import numpy as np
from contextlib import ExitStack
import concourse.bass as bass
import concourse.mybir as mybir
from concourse.bass_utils import run_bass_kernel_spmd

F32 = mybir.dt.float32
BF16 = mybir.dt.bfloat16
AF = mybir.ActivationFunctionType
ALU = mybir.AluOpType
AX = mybir.AxisListType

D = 1024
DFF = 2816
DEPTH = 2
EPS = 1e-6
ENGS = ("pe", "act", "dve", "pool", "sp")
NA_LAG = 1
NA_PAIRED = True
STRICT = True


class Buf:
    __slots__ = ("name", "w", "r", "sem", "excl")

    def __init__(self, name):
        self.name = name
        self.w = []
        self.r = []
        self.sem = None
        self.excl = name.startswith("p") and name != "pad"


class Op:
    __slots__ = ("eng", "seq", "fn", "rw", "war", "signal", "sigval", "dma", "waits", "cost",
                 "deps", "nun", "users", "ready", "done", "pos")

    def __init__(self, eng, seq, fn, rw, war, dma, cost):
        self.eng = eng
        self.seq = seq
        self.fn = fn
        self.rw = rw
        self.war = war
        self.signal = False
        self.sigval = 0
        self.dma = dma
        self.waits = None
        self.cost = cost


def _fsize(ap):
    n = 1
    for d_ in tuple(ap.shape)[1:]:
        n *= int(d_)
    return n


DEBUG_LINES = False
SPREAD_DIES = True
NA_SPLIT = False
SCHED = True
SCHED_K = 6
SYNC_NS = 180.0
DMA_LAT_NS = 2600.0


class Graph:
    def __init__(self, nc, n_dma_sems=56):
        self.nc = nc
        self.es = ExitStack()
        self.eng_sem = {e: self.es.enter_context(nc.semaphore("s_" + e)) for e in ENGS}
        self.eng_cnt = {e: 0 for e in ENGS}
        self.dma_sem = [self.es.enter_context(nc.semaphore("d%d" % i)) for i in range(n_dma_sems)]
        self.dma_cnt = [0] * n_dma_sems
        self.next_slot = 0
        self.ops = {e: [] for e in ENGS}
        self.waited = {e: {} for e in ENGS}
        self.n_instr = 0
        self.seq = 0

    def close(self):
        self.es.close()

    def slot(self, buf):
        if buf.sem is None:
            assert self.next_slot < len(self.dma_sem), "too many dma buffers in one phase"
            buf.sem = self.next_slot
            self.next_slot += 1
        return buf.sem

    def _record(self, eng, fn, reads, writes, dma=None, cost=400.0):
        rw, war = set(), set()
        ex = [b for b in reads if b.excl and b not in writes]
        if ex:
            reads = [b for b in reads if not b.excl or b in writes]
            writes = list(writes) + ex
        for b in reads:
            rw.update(b.w)
        for b in writes:
            for o in b.w:
                if dma is not None and o.dma is not None and o.dma[0] == dma[0]:
                    rw.update(o.rw)
                    war.update(o.war)
                    continue
                rw.add(o)
            war.update(b.r)
        self.seq += 1
        op = Op(eng, self.seq, fn, rw, war, dma, cost)
        if DEBUG_LINES:
            import sys as _s
            f_ = _s._getframe(2)
            op.waits = (f_.f_lineno, f_.f_back.f_lineno if f_.f_back else 0)
        self.ops[eng].append(op)
        for b in reads:
            b.r.append(op)
        for b in writes:
            b.w = [op]
            b.r = []
        return op

    def op(self, eng, fn, reads=(), writes=()):
        return self._record(eng, fn, reads, writes)

    def dma(self, eng, out, in_, sbuf, reads=(), writes=(), slow=False):
        s = self.slot(sbuf)
        self.dma_cnt[s] += 16
        cnt = self.dma_cnt[s]
        sem = self.dma_sem[s]
        kw = {"allow_slow_non_contiguous": True} if slow else {}

        def fn(e):
            return e.dma_start(out=out, in_=in_, **kw).then_inc(sem, 16)
        return self._record(eng, fn, reads, writes, dma=(s, cnt), cost=DMA_LAT_NS)

    def mm(self, out, lhsT, rhs, start, stop, reads, writes):
        n = _fsize(rhs)
        cost = 64.0 + n * (0.42 if rhs.dtype == BF16 else 0.85)
        return self._record("pe", lambda e: e.matmul(out, lhsT, rhs, start=start, stop=stop), reads, writes, cost=cost)

    def tr(self, out, in_, ident, reads, writes):
        return self._record("pe", lambda e: e.transpose(out=out, in_=in_, identity=ident), reads, writes, cost=300.0)

    def _ec(self, eng, out):
        n = _fsize(out)
        if eng == "pool":
            return 300.0 + 2.0 * n
        return 220.0 + 1.0 * n

    def act(self, out, in_, func, reads, writes, bias=None, scale=None):
        kw = {}
        if bias is not None:
            kw["bias"] = bias
        if scale is not None:
            kw["scale"] = scale
        return self._record("act", lambda e: e.activation(out=out, in_=in_, func=func, **kw), reads, writes,
                            cost=self._ec("act", out))

    def cp(self, eng, out, in_, reads, writes):
        if eng == "act":
            return self._record("act", lambda e: e.copy(out=out, in_=in_), reads, writes, cost=self._ec(eng, out))
        return self._record(eng, lambda e: e.tensor_copy(out=out, in_=in_), reads, writes, cost=self._ec(eng, out))

    def tt(self, eng, out, in0, in1, op, reads, writes):
        return self._record(eng, lambda e: e.tensor_tensor(out=out, in0=in0, in1=in1, op=op), reads, writes,
                            cost=self._ec(eng, out))

    def ts(self, eng, out, in0, s1, s2, op0, op1, reads, writes):
        if op1 is None:
            return self._record(eng, lambda e: e.tensor_scalar(out=out, in0=in0, scalar1=s1, scalar2=None, op0=op0),
                                reads, writes, cost=self._ec(eng, out))
        return self._record(eng, lambda e: e.tensor_scalar(out=out, in0=in0, scalar1=s1, scalar2=s2, op0=op0, op1=op1),
                            reads, writes, cost=self._ec(eng, out))

    def stt(self, eng, out, in0, scalar, in1, op0, op1, reads, writes):
        return self._record(eng, lambda e: e.scalar_tensor_tensor(out=out, in0=in0, scalar=scalar, in1=in1,
                                                                  op0=op0, op1=op1), reads, writes,
                            cost=self._ec(eng, out))

    def red(self, eng, out, in_, reads, writes):
        return self._record(eng, lambda e: e.reduce_sum(out=out, in_=in_, axis=AX.X), reads, writes,
                            cost=self._ec(eng, in_))

    def memset(self, eng, ap, val, writes):
        return self._record(eng, lambda e: e.memset(ap, val), (), writes, cost=self._ec(eng, ap))

    def _schedule(self):
        import bisect
        ops = self.ops
        allops = [o for e in ENGS for o in ops[e]]
        for o in allops:
            o.rw = set(d_ for d_ in o.rw if d_.fn is not None)
            o.war = set(d_ for d_ in o.war if d_.fn is not None)
            o.deps = list(o.rw | o.war)
            o.nun = len(o.deps)
            o.users = []
            o.ready = 0.0
            o.done = False
        for o in allops:
            for d_ in o.deps:
                d_.users.append(o)
        avail = {e: [] for e in ENGS}
        for o in allops:
            if o.nun == 0:
                avail[o.eng].append((o.seq, o))
        for e in ENGS:
            avail[e].sort(key=lambda t: t[0])
        free = {e: 0.0 for e in ENGS}
        new = {e: [] for e in ENGS}
        remaining = len(allops)
        while remaining:
            best = None
            for e in ENGS:
                av = avail[e]
                fe = free[e]
                for k in range(min(SCHED_K, len(av))):
                    o = av[k][1]
                    st = o.ready if o.ready > fe else fe
                    if best is None or st < best[0] or (st == best[0] and o.seq < best[2].seq):
                        best = (st, k, o)
                    if o.ready <= fe:
                        break
            st, k, o = best
            e = o.eng
            del avail[e][k]
            fin = st + o.cost
            free[e] = st + 64.0 if o.dma is not None else fin
            o.done = True
            new[e].append(o)
            remaining -= 1
            for u in o.users:
                r = fin + SYNC_NS
                if r > u.ready:
                    u.ready = r
                u.nun -= 1
                if u.nun == 0:
                    bisect.insort(avail[u.eng], (u.seq, u))
        self.ops = new
        if DEBUG_LINES:
            for e in ENGS:
                print("ENGINE", e)
                for o in new[e][:DEBUG_LINES]:
                    print("   seq", o.seq, "line", o.waits, "dma" if o.dma else "", "deps", sorted(d_.seq for d_ in o.deps)[:8])

    def flush(self, engines, sched=True):
        if SCHED and sched:
            self._schedule()
        ops = self.ops
        for e in ENGS:
            for i, op in enumerate(ops[e]):
                op.pos = i
        for e in ENGS:
            for op in ops[e]:
                need = set()
                for d_ in op.rw:
                    if d_.dma is not None or d_.eng != e:
                        need.add(d_)
                    elif op.dma is not None or (STRICT and e in ("act", "dve", "pool")):
                        need.add(d_)
                for d_ in op.war:
                    if d_.dma is not None or d_.eng != e:
                        need.add(d_)
                    elif op.dma is not None:
                        need.add(d_)
                for d_ in need:
                    if d_.dma is None and d_.eng == e:
                        assert d_.pos < op.pos
                op.waits = need
                for d_ in need:
                    if d_.dma is None:
                        d_.signal = True
        for e in ENGS:
            for op in reversed(ops[e]):
                if op.dma is None:
                    op.signal = True
                    break
        for e in ENGS:
            c = self.eng_cnt[e]
            for op in ops[e]:
                if op.signal and op.dma is None:
                    c += 1
                    op.sigval = c
            self.eng_cnt[e] = c
        nc = self.nc
        final_eng = dict(self.eng_cnt)
        final_dma = list(self.dma_cnt)
        g = self

        def emit(e, handle):
            waited = g.waited[e]
            for op in ops[e]:
                wl = {}
                for d_ in op.waits:
                    if d_.dma is not None:
                        key, val = ("d", d_.dma[0]), d_.dma[1]
                    else:
                        key, val = ("c", d_.eng), d_.sigval
                    if wl.get(key, 0) < val:
                        wl[key] = val
                for key, val in wl.items():
                    if waited.get(key, 0) >= val:
                        continue
                    waited[key] = val
                    sem = g.dma_sem[key[1]] if key[0] == "d" else g.eng_sem[key[1]]
                    handle.wait_ge(sem, val)
                    g.n_instr += 1
                ins = op.fn(handle)
                g.n_instr += 1
                if op.signal and op.dma is None:
                    ins.then_inc(g.eng_sem[e], 1)
            for x in ENGS:
                if x != e and waited.get(("c", x), 0) < final_eng[x]:
                    waited[("c", x)] = final_eng[x]
                    handle.wait_ge(g.eng_sem[x], final_eng[x])
            for s_, v in enumerate(final_dma):
                if v > 0 and waited.get(("d", s_), 0) < v:
                    waited[("d", s_)] = v
                    handle.wait_ge(g.dma_sem[s_], v)

        with nc.Block() as block:
            @block.tensor
            def _(h):
                emit("pe", h)

            @block.scalar
            def _(h):
                emit("act", h)

            @block.vector
            def _(h):
                emit("dve", h)

            @block.gpsimd
            def _(h):
                emit("pool", h)

            @block.sync
            def _(h):
                emit("sp", h)
        for e in ENGS:
            for op in ops[e]:
                op.fn = None
        self.ops = {e: [] for e in ENGS}
        self.next_slot = 0


C_Z, C_XBC, C_DTF, C_DTB = 0, 1024, 2560, 2576
C_GQ, C_GK, C_GV, C_GG = 2592, 3104, 3616, 4640
C_DNF, C_DNB = 5664, 5680
C_NQ, C_NK, C_NV = 5696, 6720, 7744
C_GATE = 8768
IN_WIDTH = 11840


class Cfg:
    def __init__(self, nt, stop_after=None, debug_outs=(), debug_ins=()):
        self.nt = nt
        self.seg = nt // 2
        self.stop_after = stop_after
        self.debug_outs = debug_outs
        self.debug_ins = debug_ins
        self.only = None


def build_program(cfg):
    nc = bass.Bass("TRN2", target_bir_lowering=False)
    NT = cfg.nt
    TB = 512
    NB = NT // TB
    NTI = TB // 128

    cfg.in_shapes = {}

    def din(name, shape, dt=F32):
        cfg.in_shapes[name] = (list(shape), dt)
        return nc.dram_tensor(name, list(shape), dt, kind="ExternalInput").ap()

    def dout(name, shape, dt=F32):
        return nc.dram_tensor(name, list(shape), dt, kind="ExternalOutput").ap()

    def dscr(name, shape, dt=F32):
        kind = "Internal"
        if name in cfg.debug_outs:
            kind = "ExternalOutput"
        if name in cfg.debug_ins:
            kind = "ExternalInput"
            cfg.in_shapes[name] = (list(shape), dt)
        return nc.dram_tensor(name, list(shape), dt, kind=kind).ap()

    x_in = din("x", [NT, D])
    flag = din("flag", [128, 2])
    ident_in = din("ident", [128, 128])
    ln_ffn1_w = din("ln_ffn1_w", [DEPTH, D])
    ffn1_w_in = din("ffn1_w_in", [DEPTH, D, 2 * DFF])
    ffn1_w_out = din("ffn1_w_out", [DEPTH, DFF, D])
    ln_mix_w = din("ln_mix_w", [DEPTH, D])
    w_in = din("w_in", [DEPTH, D, IN_WIDTH])
    w_branch = [din("w_branch_" + c, [DEPTH, D, D]) for c in "abc"]
    w_out = din("w_out", [DEPTH, D, D])
    ln_ffn2_w = din("ln_ffn2_w", [DEPTH, D])
    ffn2_w_in = din("ffn2_w_in", [DEPTH, D, 2 * DFF])
    ffn2_w_out = din("ffn2_w_out", [DEPTH, DFF, D])
    ln_final_w = din("ln_final_w", [1, D])
    y_out = dout("y", [NT, D])

    xa = dscr("xa", [NT, D])
    xb = dscr("xb", [NT, D])
    xc = dscr("xc", [NT, D])
    hT_f1 = dscr("hT_f1", [8, 128, NT], BF16)
    hT_mix = dscr("hT_mix", [8, 128, NT], BF16)
    hT_f2 = dscr("hT_f2", [8, 128, NT], BF16)
    z_tm = dscr("z_tm", [NT, 1024])
    kg_tm = dscr("kg_tm", [NT, 512])
    vg_tm = dscr("vg_tm", [NT, 1024], BF16)
    gg_tm = dscr("gg_tm", [NT, 1024])
    vn_tm = dscr("vn_tm", [NT, 1024], BF16)
    dt_tm = dscr("dt_tm", [NT, 32])
    xbcT = dscr("xbcT", [12, 128, NT])
    gqT = dscr("gqT", [4, 128, NT])
    gkT = dscr("gkT", [4, 128, NT])
    nqT = dscr("nqT", [8, 128, NT], BF16)
    nkT = dscr("nkT", [8, 128, NT], BF16)
    gatesT = dscr("gatesT", [24, 128, NT], BF16)
    dnT = dscr("dnT", [2, 16, NT])
    yT = [dscr("yT_" + c, [8, 128, NT], BF16) for c in "abc"]

    g = Graph(nc)

    uid = [0]

    def sb(es, name, shape, dt):
        uid[0] += 1
        return es.enter_context(nc.sbuf_tensor("s%d_%s" % (uid[0], name), list(shape), dt))

    def ps(es, name, shape, dt):
        uid[0] += 1
        return es.enter_context(nc.psum_tensor("p%d_%s" % (uid[0], name), list(shape), dt))

    class NormCtx:
        def __init__(self, es, tag, transpose=True):
            self.transpose = transpose
            self.wB = sb(es, "wB_" + tag, [128, D], F32)
            self.sq = sb(es, "sq_" + tag, [128, D], F32)
            self.ss = [sb(es, "ss%d_" % i + tag, [128, 2], F32) for i in range(2)]
            self.b_wB, self.b_sq = Buf("wB"), Buf("sq")
            self.b_ss = [Buf("ss0"), Buf("ss1")]
            if transpose:
                self.ident = sb(es, "ident_" + tag, [128, 128], BF16)
                self.hn = [sb(es, "hn%d_" % i + tag, [128, D], BF16) for i in range(2)]
                self.hTs = sb(es, "hTs_" + tag, [128, 8, TB], BF16)
                self.pT = [ps(es, "pT%d_" % i + tag, [128, D], BF16) for i in range(2)]
                self.b_ident = Buf("ident")
                self.b_hn = [Buf("hn0"), Buf("hn1")]
                self.b_hTs = Buf("hTs")
                self.b_pT = [Buf("pT0"), Buf("pT1")]
            else:
                self.yt = [sb(es, "yt%d_" % i + tag, [128, D], F32) for i in range(2)]
                self.b_yt = [Buf("yt0"), Buf("yt1")]
            self.n = 0

        def setup(self, w_row_ap):
            if self.transpose:
                g.dma("pool", self.ident[:], ident_in, self.b_ident, writes=[self.b_ident])
            g.dma("sp", self.wB[:], w_row_ap.partition_broadcast(128), self.b_wB, writes=[self.b_wB])

        def stats(self, xt_ap, b_x):
            i = self.n % 2
            self.n += 1
            ss = self.ss[i]
            g.act(self.sq[:], xt_ap, AF.Square, [b_x], [self.b_sq])
            g.red("dve", ss[:, 0:1], self.sq[:], [self.b_sq], [self.b_ss[i]])
            g.ts("dve", ss[:, 1:2], ss[:, 0:1], 1.0 / D, EPS, ALU.mult, ALU.add, [self.b_ss[i]], [self.b_ss[i]])
            g.act(ss[:, 1:2], ss[:, 1:2], AF.Sqrt, [self.b_ss[i]], [self.b_ss[i]])
            g._record("dve", lambda e: e.reciprocal(out=ss[:, 1:2], in_=ss[:, 1:2]), [self.b_ss[i]], [self.b_ss[i]])
            return i

        def run(self, xt_ap, b_x, hT_dram, blk, ti):
            self.flush_pending()
            i = self.stats(xt_ap, b_x)
            hn = self.hn[i]
            g.stt("dve", hn[:], xt_ap, self.ss[i][:, 1:2], self.wB[:], ALU.mult, ALU.mult,
                  [b_x, self.b_ss[i], self.b_wB], [self.b_hn[i]])
            self.pending = (i, hT_dram, blk, ti)

        def flush_pending(self):
            if getattr(self, "pending", None) is None:
                return
            i, hT_dram, blk, ti = self.pending
            self.pending = None
            hn, pT, hTs = self.hn[i], self.pT[i], self.hTs
            for k in range(8):
                g.tr(pT[:, k * 128:(k + 1) * 128], hn[:, k * 128:(k + 1) * 128], self.ident[:],
                     [self.b_hn[i], self.b_ident], [self.b_pT[i]])
            g.cp("act", hTs[:, :, ti * 128:(ti + 1) * 128], pT[:].rearrange("p (k t) -> p k t", k=8),
                 [self.b_pT[i]], [self.b_hTs])
            if ti == NTI - 1:
                g.dma("pool", hT_dram[:, :, blk * TB:(blk + 1) * TB].rearrange("k p t -> p k t"), hTs[:],
                      self.b_hTs, reads=[self.b_hTs])

        def run_final(self, xt_ap, b_x, y_dram, t):
            i = self.stats(xt_ap, b_x)
            g.stt("dve", self.yt[i][:], xt_ap, self.ss[i][:, 1:2], self.wB[:], ALU.mult, ALU.mult,
                  [b_x, self.b_ss[i], self.b_wB], [self.b_yt[i]])
            g.dma("pool", y_dram[t * 128:(t + 1) * 128, :], self.yt[i][:], self.b_yt[i], reads=[self.b_yt[i]])

    def phase_p0():
        with ExitStack() as es:
            nrm = NormCtx(es, "p0")
            xt = [sb(es, "p0x%d" % i, [128, D], F32) for i in range(2)]
            b_xt = [Buf("x0"), Buf("x1")]
            nrm.setup(ln_ffn1_w[0:1, :])
            for blk in range(NB):
                for ti in range(NTI):
                    t = blk * NTI + ti
                    i = t % 2
                    g.dma("sp", xt[i][:], x_in[t * 128:(t + 1) * 128, :], b_xt[i], writes=[b_xt[i]])
                    nrm.run(xt[i][:], b_xt[i], hT_f1, blk, ti)
            nrm.flush_pending()
            g.flush(ENGS)

    def phase_ffn(tag, w_in_ap, w_out_ap, x_src, hT_src, x_dst, next_w_row, hT_dst, final_out=None):
        with ExitStack() as es:
            w1 = sb(es, "w1_" + tag, [128, 8, 2 * DFF], BF16)
            w2 = sb(es, "w2_" + tag, [128, 22, D], BF16)
            b_w1, b_w2 = Buf("w1"), Buf("w2")
            hTb = sb(es, "hTb_" + tag, [128, 8, TB], BF16)
            b_hTb = Buf("hTb")
            gT = sb(es, "gT_" + tag, [128, 22, TB], BF16)
            b_gT = Buf("gT")
            sa = [sb(es, "sa%d_" % i + tag, [128, TB], F32) for i in range(2)]
            b_sa = [Buf("sa0"), Buf("sa1")]
            xt = sb(es, "xt_" + tag, [128, D], F32)
            b_xt = Buf("xt")
            xn = [sb(es, "xn%d_" % i + tag, [128, D], F32) for i in range(2)]
            b_xn = [Buf("xn0"), Buf("xn1")]
            pa = [ps(es, "pa%d_" % i + tag, [128, TB], F32) for i in range(2)]
            pb = [ps(es, "pb%d_" % i + tag, [128, TB], F32) for i in range(2)]
            b_pa = [Buf("pa0"), Buf("pa1")]
            b_pb = [Buf("pb0"), Buf("pb1")]
            po = [ps(es, "po%d_" % i + tag, [128, 512], F32) for i in range(2)]
            b_po = [Buf("po0"), Buf("po1")]
            nrm = NormCtx(es, tag, transpose=(final_out is None))
            for k in range(8):
                g.dma("pool", w1[:, k, :], w_in_ap[k * 128:(k + 1) * 128, :], b_w1, writes=[b_w1])
            for k in range(22):
                g.dma("pool", w2[:, k, :], w_out_ap[k * 128:(k + 1) * 128, :], b_w2, writes=[b_w2])
            nrm.setup(next_w_row)

            def load_h(blk):
                g.dma("sp", hTb[:], hT_src[:, :, blk * TB:(blk + 1) * TB].rearrange("k p t -> p k t"),
                      b_hTb, writes=[b_hTb])
            load_h(0)
            ntl = 0
            for blk in range(NB):
                for c in range(22):
                    q = c % 2
                    for k in range(8):
                        g.mm(pa[q][:], w1[:, k, c * 128:(c + 1) * 128], hTb[:, k, :], k == 0, k == 7,
                             [b_w1, b_hTb], [b_pa[q]])
                    for k in range(8):
                        g.mm(pb[q][:], w1[:, k, DFF + c * 128:DFF + (c + 1) * 128], hTb[:, k, :], k == 0, k == 7,
                             [b_w1, b_hTb], [b_pb[q]])
                    g.act(sa[q][:], pa[q][:], AF.Silu, [b_pa[q]], [b_sa[q]])
                    g.tt("dve", gT[:, c, :], sa[q][:], pb[q][:], ALU.mult, [b_sa[q], b_pb[q]], [b_gT])
                if blk + 1 < NB:
                    load_h(blk + 1)
                for ti in range(NTI):
                    t = blk * NTI + ti
                    i = ntl % 2
                    ntl += 1
                    g.dma("sp", xt[:], x_src[t * 128:(t + 1) * 128, :], b_xt, writes=[b_xt])
                    for ch in range(2):
                        for k in range(22):
                            g.mm(po[ch][:], gT[:, k, ti * 128:(ti + 1) * 128], w2[:, k, ch * 512:(ch + 1) * 512],
                                 k == 0, k == 21, [b_gT, b_w2], [b_po[ch]])
                        g.stt("dve", xn[i][:, ch * 512:(ch + 1) * 512], po[ch][:], 0.5,
                              xt[:, ch * 512:(ch + 1) * 512], ALU.mult, ALU.add, [b_po[ch], b_xt], [b_xn[i]])
                    if final_out is None:
                        nrm.flush_pending()
                    if final_out is None:
                        g.dma("pool", x_dst[t * 128:(t + 1) * 128, :], xn[i][:], b_xn[i], reads=[b_xn[i]])
                        nrm.run(xn[i][:], b_xn[i], hT_dst, blk, ti)
                    else:
                        nrm.run_final(xn[i][:], b_xn[i], final_out, t)
            if final_out is None:
                nrm.flush_pending()
            g.flush(ENGS)

    def phase_inproj_tm(l):
        with ExitStack() as es:
            W = w_in[l]
            parts = [(C_Z, 1024, z_tm, F32), (C_GK, 512, kg_tm, F32), (C_GV, 1024, vg_tm, BF16),
                     (C_GG, 1024, gg_tm, F32), (C_NV, 1024, vn_tm, BF16), (C_DTF, 32, dt_tm, F32)]
            silu_parts = (0, 3)
            NC_TM = sum(p[1] for p in parts)
            wt = sb(es, "wtm", [128, 8, NC_TM], BF16)
            b_wt = Buf("wtm")
            hTb = [sb(es, "tm_hTb%d" % i, [128, 8, TB], BF16) for i in range(2)]
            b_hTb = [Buf("hTb0"), Buf("hTb1")]
            st = [[sb(es, "tm_st%d_%d" % (i, pi), [128, p[1]], p[3]) for pi, p in enumerate(parts)] for i in range(2)]
            b_st = [[Buf("st%d_%d" % (i, pi)) for pi in range(len(parts))] for i in range(2)]
            pp = [ps(es, "tm_p%d" % i, [128, 512], F32) for i in range(4)]
            b_pp = [Buf("pp%d" % i) for i in range(4)]
            off = 0
            offs = []
            for (c0, wd, _, _) in parts:
                offs.append(off)
                for k in range(8):
                    g.dma("pool", wt[:, k, off:off + wd], W[k * 128:(k + 1) * 128, c0:c0 + wd], b_wt, writes=[b_wt])
                off += wd
            nps = 0
            nev = 0
            for blk in range(NB):
                j = blk % 2
                g.dma("sp", hTb[j][:], hT_mix[:, :, blk * TB:(blk + 1) * TB].rearrange("k p t -> p k t"),
                      b_hTb[j], writes=[b_hTb[j]])
                for ti in range(NTI):
                    t = blk * NTI + ti
                    i = t % 2
                    for pi, (c0, wd, dst, dt) in enumerate(parts):
                        for cb in range(0, wd, 512):
                            n = min(512, wd - cb)
                            q = nps % 4
                            nps += 1
                            for k in range(8):
                                g.mm(pp[q][:, 0:n], hTb[j][:, k, ti * 128:(ti + 1) * 128],
                                     wt[:, k, offs[pi] + cb:offs[pi] + cb + n], k == 0, k == 7,
                                     [b_hTb[j], b_wt], [b_pp[q]])
                            if pi in silu_parts:
                                g.act(st[i][pi][:, cb:cb + n], pp[q][:, 0:n], AF.Silu, [b_pp[q]], [b_st[i][pi]])
                            else:
                                g.cp("dve", st[i][pi][:, cb:cb + n], pp[q][:, 0:n], [b_pp[q]], [b_st[i][pi]])
                        g.dma("pool", dst[t * 128:(t + 1) * 128, :], st[i][pi][:], b_st[i][pi], reads=[b_st[i][pi]])
            g.flush(ENGS)

    def phase_inproj_fm(l):
        with ExitStack() as es:
            W = w_in[l]
            parts = [(C_XBC, 12, xbcT, F32, None, 1.0), (C_GQ, 4, gqT, F32, None, 128.0 ** -0.5),
                     (C_GK, 4, gkT, F32, None, 1.0), (C_NQ, 8, nqT, BF16, None, 0.125),
                     (C_NK, 8, nkT, BF16, None, 1.0), (C_GATE, 24, gatesT, BF16, AF.Sigmoid, 1.0)]
            NCH = sum(p[1] for p in parts)
            NC_FM = NCH * 128 + 32
            wt = sb(es, "wfm", [128, 8, NC_FM], BF16)
            b_wt = Buf("wfm")
            hTb = [sb(es, "fm_hTb%d" % i, [128, 8, TB], BF16) for i in range(2)]
            b_hTb = [Buf("hTb0"), Buf("hTb1")]
            stf = [sb(es, "fm_stf%d" % i, [128, TB], F32) for i in range(4)]
            stb = [sb(es, "fm_stb%d" % i, [128, TB], BF16) for i in range(4)]
            b_stf = [Buf("stf%d" % i) for i in range(4)]
            b_stb = [Buf("stb%d" % i) for i in range(4)]
            pp = [ps(es, "fm_p%d" % i, [128, 512], F32) for i in range(4)]
            b_pp = [Buf("pp%d" % i) for i in range(4)]
            off = 0
            offs = []
            for (c0, nch, _, _, _, _) in parts:
                offs.append(off)
                for k in range(8):
                    g.dma("pool", wt[:, k, off:off + nch * 128], W[k * 128:(k + 1) * 128, c0:c0 + nch * 128],
                          b_wt, writes=[b_wt])
                off += nch * 128
            off_dn = off
            for k in range(8):
                g.dma("pool", wt[:, k, off:off + 32], W[k * 128:(k + 1) * 128, C_DNF:C_DNF + 32], b_wt, writes=[b_wt])
            nps = 0
            nf = 0
            nbf = 0
            nev = 0
            for blk in range(NB):
                j = blk % 2
                tok = slice(blk * TB, (blk + 1) * TB)
                g.dma("sp", hTb[j][:], hT_mix[:, :, tok].rearrange("k p t -> p k t"), b_hTb[j], writes=[b_hTb[j]])
                for pi, (c0, nch, dst, dt, func, scale) in enumerate(parts):
                    for c in range(nch):
                        q = nps % 4
                        nps += 1
                        wc = offs[pi] + c * 128
                        for k in range(8):
                            g.mm(pp[q][:], wt[:, k, wc:wc + 128], hTb[j][:, k, :], k == 0, k == 7,
                                 [b_wt, b_hTb[j]], [b_pp[q]])
                        if dt == F32:
                            s_, b_s = stf[nf % 4], b_stf[nf % 4]
                            nf += 1
                        else:
                            s_, b_s = stb[nbf % 4], b_stb[nbf % 4]
                            nbf += 1
                        if func is not None:
                            g.act(s_[:], pp[q][:], func, [b_pp[q]], [b_s])
                        elif scale != 1.0:
                            if nev % 2 == 0:
                                g.act(s_[:], pp[q][:], AF.Copy, [b_pp[q]], [b_s], scale=scale)
                            else:
                                g.ts("dve", s_[:], pp[q][:], scale, None, ALU.mult, None, [b_pp[q]], [b_s])
                            nev += 1
                        else:
                            g.cp("act" if nev % 2 == 0 else "dve", s_[:], pp[q][:], [b_pp[q]], [b_s])
                            nev += 1
                        g.dma("pool", dst[c, :, tok], s_[:], b_s, reads=[b_s])
                for dd in range(2):
                    q = nps % 4
                    nps += 1
                    wc = off_dn + dd * 16
                    for k in range(8):
                        g.mm(pp[q][0:16, :], wt[:, k, wc:wc + 16], hTb[j][:, k, :], k == 0, k == 7,
                             [b_wt, b_hTb[j]], [b_pp[q]])
                    s_, b_s = stf[nf % 4], b_stf[nf % 4]
                    nf += 1
                    g.cp("dve", s_[0:16, :], pp[q][0:16, :], [b_pp[q]], [b_s])
                    g.dma("pool", dnT[dd, :, tok], s_[0:16, :], b_s, reads=[b_s])
            g.flush(ENGS)

    def phase_merge(l, x_src, x_dst, next_w_row, hT_dst):
        with ExitStack() as es:
            wb_ = [sb(es, "wbr%d" % i, [128, 8, D], BF16) for i in range(3)]
            wo = sb(es, "wo", [128, 8, D], BF16)
            b_w = Buf("w")
            yb = [sb(es, "yb%d" % i, [128, 8, TB], BF16) for i in range(3)]
            b_yb = [Buf("yb%d" % i) for i in range(3)]
            gt = sb(es, "gt", [128, 24, TB], BF16)
            b_gt = Buf("gt")
            tmp = [sb(es, "mtmp%d" % i, [128, TB], F32) for i in range(3)]
            b_tmp = [Buf("tmp%d" % i) for i in range(3)]
            mT = sb(es, "mT", [128, 8, TB], BF16)
            b_mT = Buf("mT")
            xt = sb(es, "m_xt", [128, D], F32)
            b_xt = Buf("xt")
            xn = [sb(es, "m_xn%d" % i, [128, D], F32) for i in range(2)]
            b_xn = [Buf("xn0"), Buf("xn1")]
            pp = [ps(es, "m_p%d" % i, [128, 512], F32) for i in range(6)]
            b_pp = [Buf("pp%d" % i) for i in range(6)]
            nrm = NormCtx(es, "mrg")
            for i in range(3):
                for k in range(8):
                    g.dma("pool", wb_[i][:, k, :], w_branch[i][l, k * 128:(k + 1) * 128, :], b_w, writes=[b_w])
            for k in range(8):
                g.dma("pool", wo[:, k, :], w_out[l, k * 128:(k + 1) * 128, :], b_w, writes=[b_w])
            nrm.setup(next_w_row)
            ntl = 0
            for blk in range(NB):
                tok = slice(blk * TB, (blk + 1) * TB)
                for i in range(3):
                    g.dma("sp", yb[i][:], yT[i][:, :, tok].rearrange("k p t -> p k t"), b_yb[i], writes=[b_yb[i]])
                g.dma("sp", gt[:], gatesT[:, :, tok].rearrange("k p t -> p k t"), b_gt, writes=[b_gt])
                for oc in range(8):
                    par = (oc % 2) * 3
                    for i in range(3):
                        for k in range(8):
                            g.mm(pp[par + i][:], wb_[i][:, k, oc * 128:(oc + 1) * 128], yb[i][:, k, :], k == 0, k == 7,
                                 [b_w, b_yb[i]], [b_pp[par + i]])
                    for i in range(3):
                        g.tt("dve", tmp[i][:], pp[par + i][:], gt[:, i * 8 + oc, :], ALU.mult,
                             [b_pp[par + i], b_gt], [b_tmp[i]])
                    g.tt("pool", tmp[0][:], tmp[0][:], tmp[1][:], ALU.add, [b_tmp[0], b_tmp[1]], [b_tmp[0]])
                    g.tt("pool", mT[:, oc, :], tmp[0][:], tmp[2][:], ALU.add, [b_tmp[0], b_tmp[2]], [b_mT])
                for ti in range(NTI):
                    t = blk * NTI + ti
                    i = ntl % 2
                    ntl += 1
                    g.dma("sp", xt[:], x_src[t * 128:(t + 1) * 128, :], b_xt, writes=[b_xt])
                    for ch in range(2):
                        q = ch * 3
                        for k in range(8):
                            g.mm(pp[q][:], mT[:, k, ti * 128:(ti + 1) * 128], wo[:, k, ch * 512:(ch + 1) * 512],
                                 k == 0, k == 7, [b_mT, b_w], [b_pp[q]])
                        g.tt("dve", xn[i][:, ch * 512:(ch + 1) * 512], pp[q][:], xt[:, ch * 512:(ch + 1) * 512], ALU.add,
                             [b_pp[q], b_xt], [b_xn[i]])
                    nrm.flush_pending()
                    g.dma("pool", x_dst[t * 128:(t + 1) * 128, :], xn[i][:], b_xn[i], reads=[b_xn[i]])
                    nrm.run(xn[i][:], b_xn[i], hT_dst, blk, ti)
            nrm.flush_pending()
            g.flush(ENGS)

    ssm_conv_w = din("ssm_conv_w", [DEPTH, 5, 1536])
    ssm_conv_b = din("ssm_conv_b", [DEPTH, 1536])
    ssm_dt_bias = din("ssm_dt_bias", [DEPTH, 32])
    ssm_a_log = din("ssm_a_log", [DEPTH, 32])
    ssm_d = din("ssm_d", [DEPTH, 16])
    ssm_norm_w = din("ssm_norm_w", [DEPTH, 1024])
    tri_in = din("tri", [2, 128, 128])
    maskadd_in = din("maskadd", [2, 128, 128])
    mask01_in = din("mask01", [2, 128, 128])
    ustrict_in = din("ustrict", [2, 128, 128])
    ones_in = din("ones", [128, 128])
    xs_tm = dscr("xs_tm", [NT, 1024])
    B_tm = dscr("B_tm", [NT, 256], BF16)
    BCT = dscr("BCT", [4, 128, NT], BF16)
    dtsp = dscr("dtsp", [NT, 32])
    dta_d = dscr("dta", [NT, 32])
    dlog_d = dscr("dlog", [NT, 32])
    yf_d = dscr("yf", [NT, 1024])
    NCH = NT // 128
    MIDC = NCH // 2

    def phase_ssd_prep(l):
        with ExitStack() as es:
            cw = sb(es, "cw", [128, 12, 5], F32)
            cb = sb(es, "cb", [128, 12], F32)
            fl = sb(es, "fl", [128, 2], F32)
            identf = sb(es, "identf", [128, 128], F32)
            dbias = sb(es, "dbias", [128, 32], F32)
            aB = sb(es, "aB", [128, 32], F32)
            b_c = Buf("consts")
            xin = [sb(es, "xin%d" % i, [128, 12, TB + 4], F32) for i in range(2)]
            b_xin = [Buf("xin0"), Buf("xin1")]
            acc = [sb(es, "acc%d" % i, [128, TB], F32) for i in range(3)]
            b_acc = [Buf("acc%d" % i) for i in range(3)]
            sil = sb(es, "sil", [128, 10, TB], F32)
            b_sil = Buf("sil")
            silb = [sb(es, "silb%d" % i, [128, TB], BF16) for i in range(2)]
            b_silb = [Buf("silb0"), Buf("silb1")]
            xst = [sb(es, "xst%d" % i, [128, 1024], F32) for i in range(2)]
            b_xst = [Buf("xst0"), Buf("xst1")]
            bst = [sb(es, "bst%d" % i, [128, 256], BF16) for i in range(2)]
            b_bst = [Buf("bst0"), Buf("bst1")]
            dtt = [sb(es, "dtt%d" % i, [128, 4, NTI, 32], F32) for i in range(2)]
            b_dtt = [Buf("dtt0"), Buf("dtt1")]
            pt = [ps(es, "pp_t%d" % i, [128, 1024], F32) for i in range(2)]
            b_pt = [Buf("pt0"), Buf("pt1")]
            pbt = [ps(es, "pp_b%d" % i, [128, 512], F32) for i in range(2)]
            b_pbt = [Buf("pbt0"), Buf("pbt1")]
            for k in range(5):
                g.dma("sp", cw[:, :, k], ssm_conv_w[l, k].rearrange("(c p) -> p c", p=128), b_c, writes=[b_c], slow=True)
            g.dma("sp", cb[:], ssm_conv_b[l].rearrange("(c p) -> p c", p=128), b_c, writes=[b_c], slow=True)
            g.dma("sp", fl[:], flag, b_c, writes=[b_c])
            g.dma("sp", identf[:], ident_in, b_c, writes=[b_c])
            g.dma("sp", dbias[:], ssm_dt_bias[l:l + 1, :].partition_broadcast(128), b_c, writes=[b_c])
            g.dma("sp", aB[:], ssm_a_log[l:l + 1, :].partition_broadcast(128), b_c, writes=[b_c])
            g.act(aB[:], aB[:], AF.Exp, [b_c], [b_c])
            g.ts("dve", aB[:], aB[:], -1.0, None, ALU.mult, None, [b_c], [b_c])
            na = 0
            nsb = 0
            nt_ = 0
            for blk in range(NB):
                j = blk % 2
                t0 = blk * TB
                lo, hi = t0 - 2, t0 + TB + 2
                xi = xin[j]
                lo_c, hi_c = 0, TB + 4
                if lo < 0:
                    g.memset("pool", xi[:, :, 0:2], 0.0, [b_xin[j]])
                    lo_c, lo = 2, 0
                if hi > NT:
                    g.memset("pool", xi[:, :, TB + 2:TB + 4], 0.0, [b_xin[j]])
                    hi_c, hi = TB + 2, NT
                g.dma("sp", xi[:, :, lo_c:hi_c], xbcT[:, :, lo:hi].rearrange("c p t -> p c t"), b_xin[j],
                      writes=[b_xin[j]])
                if t0 == NT // 2:
                    g.ts("dve", xi[:, :, 0:2], xi[:, :, 0:2], fl[:, 0:1], None, ALU.mult, None, [b_xin[j], b_c], [b_xin[j]])
                if t0 + TB == NT // 2:
                    g.ts("dve", xi[:, :, TB + 2:TB + 4], xi[:, :, TB + 2:TB + 4], fl[:, 0:1], None, ALU.mult, None,
                         [b_xin[j], b_c], [b_xin[j]])
                dq = dtt[j]
                g.dma("sp", dq[:, 0, :, :], dt_tm[t0:t0 + TB, :].rearrange("(i p) c -> p i c", p=128), b_dtt[j],
                      writes=[b_dtt[j]])
                g.tt("dve", dq[:, 0, :, :], dq[:, 0, :, :], dbias[:, None, :].to_broadcast([128, NTI, 32]), ALU.add,
                     [b_dtt[j], b_c], [b_dtt[j]])
                g.act(dq[:, 1, :, :], dq[:, 0, :, :], AF.Exp, [b_dtt[j]], [b_dtt[j]])
                g.ts("dve", dq[:, 1, :, :], dq[:, 1, :, :], 1.0, None, ALU.add, None, [b_dtt[j]], [b_dtt[j]])
                g.act(dq[:, 1, :, :], dq[:, 1, :, :], AF.Ln, [b_dtt[j]], [b_dtt[j]])
                g.tt("dve", dq[:, 2, :, :], dq[:, 1, :, :], aB[:, None, :].to_broadcast([128, NTI, 32]), ALU.mult,
                     [b_dtt[j], b_c], [b_dtt[j]])
                g.dma("pool", dtsp[t0:t0 + TB, :].rearrange("(i p) c -> p i c", p=128), dq[:, 1, :, :], b_dtt[j],
                      reads=[b_dtt[j]])
                g.act(dq[:, 3, :, :], dq[:, 1, :, :], AF.Ln, [b_dtt[j]], [b_dtt[j]])
                g.dma("pool", dta_d[t0:t0 + TB, :].rearrange("(i p) c -> p i c", p=128), dq[:, 2, :, :], b_dtt[j],
                      reads=[b_dtt[j]])
                g.dma("pool", dlog_d[t0:t0 + TB, :].rearrange("(i p) c -> p i c", p=128), dq[:, 3, :, :], b_dtt[j],
                      reads=[b_dtt[j]])
                for c in range(12):
                    a = na % 3
                    na += 1
                    g.act(acc[a][:], xi[:, c, 0:TB], AF.Identity, [b_xin[j], b_c], [b_acc[a]],
                          bias=cb[:, c:c + 1], scale=cw[:, c, 0:1])
                    for k in range(1, 5):
                        g.stt("dve", acc[a][:], xi[:, c, k:k + TB], cw[:, c, k:k + 1], acc[a][:],
                              ALU.mult, ALU.add, [b_xin[j], b_c, b_acc[a]], [b_acc[a]])
                    if c < 10:
                        g.act(sil[:, c, :], acc[a][:], AF.Silu, [b_acc[a]], [b_sil])
                    if c >= 8:
                        q = nsb % 2
                        nsb += 1
                        g.act(silb[q][:], acc[a][:], AF.Silu, [b_acc[a]], [b_silb[q]])
                        g.dma("pool", BCT[c - 8, :, t0:t0 + TB], silb[q][:], b_silb[q], reads=[b_silb[q]])
                for ti in range(NTI):
                    i = nt_ % 2
                    nt_ += 1
                    tk = slice(t0 + ti * 128, t0 + (ti + 1) * 128)
                    for c in range(8):
                        g.tr(pt[i][:, c * 128:(c + 1) * 128], sil[:, c, ti * 128:(ti + 1) * 128], identf[:],
                             [b_sil, b_c], [b_pt[i]])
                    for c in range(2):
                        g.tr(pbt[i][:, c * 128:(c + 1) * 128], sil[:, 8 + c, ti * 128:(ti + 1) * 128], identf[:],
                             [b_sil, b_c], [b_pbt[i]])
                    g.cp("act", xst[i][:], pt[i][:], [b_pt[i]], [b_xst[i]])
                    g.cp("dve", bst[i][:], pbt[i][:, 0:256], [b_pbt[i]], [b_bst[i]])
                    g.dma("pool", xs_tm[tk, :], xst[i][:], b_xst[i], reads=[b_xst[i]])
                    g.dma("pool", B_tm[tk, :], bst[i][:], b_bst[i], reads=[b_bst[i]])
            g.flush(ENGS)

    def phase_ssd_pass(l, d):
        with ExitStack() as es:
            tri = sb(es, "tri", [128, 128], F32)
            ones = sb(es, "ones", [128, 128], F32)
            identf = sb(es, "identf", [128, 128], F32)
            identb = sb(es, "identb", [128, 128], BF16)
            maskB = sb(es, "maskB", [128, 16, 128], F32)
            fl = sb(es, "fl", [128, 2], F32)
            b_c = Buf("consts")
            xs = [sb(es, "xs%d" % i, [128, 1024], F32) for i in range(3)]
            xsb = [sb(es, "xsb%d" % i, [128, 1024], BF16) for i in range(2)]
            b_xsb = [Buf("xsb0"), Buf("xsb1")]
            Bt = [sb(es, "Bt%d" % i, [128, 256], BF16) for i in range(3)]
            bct = [sb(es, "bct%d" % i, [128, 4, 128], BF16) for i in range(3)]
            dts = [sb(es, "dts%d" % i, [128, 3, 16], F32) for i in range(3)]
            b_in = [Buf("in0"), Buf("in1"), Buf("in2")]
            R_ = [sb(es, "R%d" % i, [128, 16, 128], F32) for i in range(2)]
            b_R_ = [Buf("R0"), Buf("R1")]
            cs_sb_ = [sb(es, "cs_sb%d" % i, [128, 16], F32) for i in range(2)]
            ecs_ = [sb(es, "ecs%d" % i, [128, 16], F32) for i in range(2)]
            b_cs_ = [Buf("cs_sb0"), Buf("cs_sb1")]
            b_ecs_ = [Buf("ecs0"), Buf("ecs1")]
            seg_ = [sb(es, "seg%d" % i, [128, 16, 128], F32) for i in range(2)]
            b_seg_ = [Buf("seg0"), Buf("seg1")]
            cbT_ = [sb(es, "cbT%d" % i, [128, 2, 128], F32) for i in range(2)]
            b_cbT_ = [Buf("cbT0"), Buf("cbT1")]
            MT_ = [sb(es, "MT%d" % i, [128, 16, 128], BF16) for i in range(2)]
            b_MT_ = [Buf("MT0"), Buf("MT1")]
            xr = sb(es, "xr", [128, 1024], BF16)
            b_xr = Buf("xr")
            xd_ = [sb(es, "xd%d" % i, [128, 1024], BF16) for i in range(2)]
            b_xd_ = [Buf("xd0"), Buf("xd1")]
            S = sb(es, "S", [128, 1024], F32)
            Sbf = sb(es, "Sbf", [128, 1024], BF16)
            b_S, b_Sbf = Buf("S"), Buf("Sbf")
            decB_ = [sb(es, "decB%d" % i, [128, 16], F32) for i in range(2)]
            b_decB_ = [Buf("decB0"), Buf("decB1")]
            t1_ = [sb(es, "t1_%d" % i, [128, 1024], F32) for i in range(2)]
            b_t1_ = [Buf("t1_0"), Buf("t1_1")]
            yd = [sb(es, "yd%d" % i, [128, 1024], F32) for i in range(2)]
            b_yd = [Buf("yd0"), Buf("yd1")]
            pA = ps(es, "pA", [128, 1024], F32)
            pM = ps(es, "pM", [128, 512], F32)
            pY = ps(es, "pY", [128, 1024], F32)
            pO = ps(es, "pO", [128, 1024], F32)
            b_pA, b_pM, b_pY, b_pO = Buf("pA"), Buf("pM"), Buf("pY"), Buf("pO")
            if d == 1:
                dB = sb(es, "dB", [128, 16], F32)
                nwB = sb(es, "nwB", [128, 1024], F32)
                yfl = [sb(es, "yfl%d" % i, [128, 1024], F32) for i in range(3)]
                zt = [sb(es, "zt%d" % i, [128, 1024], F32) for i in range(3)]
                b_in2 = [Buf("in2_0"), Buf("in2_1"), Buf("in2_2")]
                sq = sb(es, "sq", [128, 1024], F32)
                b_sq = Buf("sq")
                t2 = sb(es, "t2", [128, 1024], F32)
                b_t2 = Buf("t2")
                gs = sb(es, "gs", [128, 4], F32)
                b_gs = Buf("gs")
                ynb = sb(es, "ynb", [128, 1024], BF16)
                b_ynb = Buf("ynb")
                yTs = [sb(es, "yTs%d" % i, [128, 8, 128], BF16) for i in range(2)]
                b_yTs = [Buf("yTs0"), Buf("yTs1")]
                pT = ps(es, "pT", [128, 1024], BF16)
                b_pT = Buf("pT")
                g.dma("sp", dB[:], ssm_d[l:l + 1, :].partition_broadcast(128), b_c, writes=[b_c])
                g.dma("sp", nwB[:], ssm_norm_w[l:l + 1, :].partition_broadcast(128), b_c, writes=[b_c])
                g.dma("pool", identb[:], ident_in, b_c, writes=[b_c])
            g.dma("sp", tri[:], tri_in[d], b_c, writes=[b_c])
            g.dma("sp", ones[:], ones_in, b_c, writes=[b_c])
            g.dma("sp", identf[:], ident_in, b_c, writes=[b_c])
            g.dma("sp", fl[:], flag, b_c, writes=[b_c])
            for h in range(16):
                g.dma("sp", maskB[:, h, :], maskadd_in[d], b_c, writes=[b_c])
            g.memset("dve", S[:], 0.0, [b_S])
            g.memset("pool", Sbf[:], 0.0, [b_Sbf])
            last = 127 if d == 0 else 0
            order = list(range(NCH)) if d == 0 else list(range(NCH - 1, -1, -1))
            pend = []

            def drain(k):
                for _ in range(k):
                    if pend:
                        pend.pop(0)()

            for n, c in enumerate(order):
                i = n % 3
                j2 = n % 2
                R, b_R, cs_sb, b_cs, ecs, b_ecs = R_[j2], b_R_[j2], cs_sb_[j2], b_cs_[j2], ecs_[j2], b_ecs_[j2]
                seg, b_seg, cbT, b_cbT, MT, b_MT = seg_[j2], b_seg_[j2], cbT_[j2], b_cbT_[j2], MT_[j2], b_MT_[j2]
                xd, b_xd, decB, b_decB, t1, b_t1 = xd_[j2], b_xd_[j2], decB_[j2], b_decB_[j2], t1_[j2], b_t1_[j2]
                tk = slice(c * 128, (c + 1) * 128)
                g.dma("sp", xs[i][:], xs_tm[tk, :], b_in[i], writes=[b_in[i]])
                g.dma("sp", Bt[i][:], B_tm[tk, :], b_in[i], writes=[b_in[i]])
                g.dma("sp", bct[i][:], BCT[:, :, tk].rearrange("c p t -> p c t"), b_in[i], writes=[b_in[i]])
                g.dma("sp", dts[i][:, 0, :], dtsp[tk, d * 16:(d + 1) * 16], b_in[i], writes=[b_in[i]])
                g.dma("sp", dts[i][:, 1, :], dta_d[tk, d * 16:(d + 1) * 16], b_in[i], writes=[b_in[i]])
                g.dma("sp", dts[i][:, 2, :], dlog_d[tk, d * 16:(d + 1) * 16], b_in[i], writes=[b_in[i]])
                if d == 1:
                    g.dma("sp", yfl[i][:], yf_d[tk, :], b_in2[i], writes=[b_in2[i]])
                    g.dma("sp", zt[i][:], z_tm[tk, :], b_in2[i], writes=[b_in2[i]])
                dta = dts[i][:, 1, :]
                dsp = dts[i][:, 0, :]
                if (d == 0 and c == MIDC) or (d == 1 and c == MIDC - 1):
                    g.ts("dve", S[:], S[:], fl[:, 0:1], None, ALU.mult, None, [b_S, b_c], [b_S])
                    g.ts("pool", Sbf[:], Sbf[:], fl[:, 0:1], None, ALU.mult, None, [b_Sbf, b_c], [b_Sbf])
                g.mm(pM[:, 0:16], tri[:], dta, True, True, [b_c, b_in[i]], [b_pM])
                for gi in range(2):
                    g.mm(pM[:, 128 + gi * 128:256 + gi * 128], bct[i][:, gi, :], bct[i][:, 2 + gi, :], True, True,
                         [b_in[i]], [b_pM])
                g.tt("dve", cs_sb[:], pM[:, 0:16], dts[i][:, 2, :], ALU.subtract, [b_pM, b_in[i]], [b_cs])
                g.cp("act", cbT[:], pM[:, 128:384].rearrange("p (g t) -> p g t", g=2), [b_pM], [b_cbT])
                drain(2)
                g.act(ecs[:], pM[:, 0:16], AF.Exp, [b_pM], [b_ecs])
                g.cp("act", xsb[j2][:], xs[i][:], [b_in[i]], [b_xsb[j2]])
                g.tt("dve", R[:], tri[:, None, :].to_broadcast([128, 16, 128]), dta[:, :, None].to_broadcast([128, 16, 128]),
                     ALU.mult, [b_c, b_in[i]], [b_R])
                drain(2)
                for gi in range(2):
                    g.mm(pO[:, gi * 512:(gi + 1) * 512], bct[i][:, 2 + gi, :], Sbf[:, gi * 512:(gi + 1) * 512], True, True,
                         [b_in[i], b_Sbf], [b_pO])
                for half in range(2):
                    hs = slice(half * 8, half * 8 + 8)
                    for q in range(2):
                        h0 = half * 8 + q * 4
                        g.mm(pA[:, q * 512:(q + 1) * 512], identf[:], maskB[:, h0:h0 + 4, :].rearrange("p h t -> p (h t)"),
                             True, False, [b_c], [b_pA])
                        g.mm(pA[:, q * 512:(q + 1) * 512], ones[:], R[:, h0:h0 + 4, :].rearrange("p h t -> p (h t)"),
                             False, True, [b_c, b_R], [b_pA])
                    g.tt("dve", seg[:, hs, :], pA[:].rearrange("p (h t) -> p h t", h=8),
                         cs_sb[:, hs][:, :, None].to_broadcast([128, 8, 128]), ALU.subtract, [b_pA, b_cs], [b_seg])
                    g.act(decB[:, hs], pA[:].rearrange("p (h t) -> p h t", h=8)[:, :, last], AF.Exp, [b_pA], [b_decB])
                    if half == 1:
                        pass
                    drain(2)
                    g.act(seg[:, hs, :], seg[:, hs, :], AF.Exp, [b_seg], [b_seg])
                    g.tt("pool" if half == 0 else "dve", MT[:, hs, :], seg[:, hs, :],
                         cbT[:, half:half + 1, :].to_broadcast([128, 8, 128]), ALU.mult, [b_seg, b_cbT], [b_MT])
                drain(2)
                for h in range(16):
                    g.mm(pY[:, h * 64:(h + 1) * 64], MT[:, h, :], xsb[j2][:, h * 64:(h + 1) * 64], True, True,
                         [b_MT, b_xsb[j2]], [b_pY])
                drain(2)
                g.tt("dve", t1[:].rearrange("p (h q) -> p h q", h=16), pO[:].rearrange("p (h q) -> p h q", h=16),
                     ecs[:, :, None].to_broadcast([128, 16, 64]), ALU.mult, [b_pO, b_ecs], [b_t1])
                g.tt("dve", yd[j2][:], pY[:], t1[:], ALU.add, [b_pY, b_t1], [b_yd[j2]])
                g.tt("pool", xd[:].rearrange("p (h q) -> p h q", h=16), xsb[j2][:].rearrange("p (h q) -> p h q", h=16),
                     seg[:, :, last:last + 1].to_broadcast([128, 16, 64]), ALU.mult, [b_xsb[j2], b_seg], [b_xd])
                drain(2)
                for gi in range(2):
                    g.mm(pO[:, gi * 512:(gi + 1) * 512], Bt[i][:, gi * 128:(gi + 1) * 128], xd[:, gi * 512:(gi + 1) * 512],
                         True, True, [b_in[i], b_xd], [b_pO])
                g.tt("pool", S[:].rearrange("p (h q) -> p h q", h=16), S[:].rearrange("p (h q) -> p h q", h=16),
                     decB[:, :, None].to_broadcast([128, 16, 64]), ALU.mult, [b_S, b_decB], [b_S])
                g.tt("dve", S[:], S[:], pO[:], ALU.add, [b_S, b_pO], [b_S])
                g.cp("act", Sbf[:], S[:], [b_S], [b_Sbf])
                if d == 0:
                    g.dma("pool", yf_d[tk, :], yd[j2][:], b_yd[j2], reads=[b_yd[j2]])
                else:
                    drain(len(pend))
                    y = yd[j2]
                    b_y = b_yd[j2]
                    yfl_i, zt_i, xs_i, b2_i, bi_i, yTs_j, b_yTs_j = yfl[i], zt[i], xs[i], b_in2[i], b_in[i], yTs[j2], b_yTs[j2]

                    def mk(y=y, b_y=b_y, yfl_i=yfl_i, zt_i=zt_i, xs_i=xs_i, b2_i=b2_i, bi_i=bi_i, yTs_j=yTs_j,
                           b_yTs_j=b_yTs_j, tk=tk):
                        ops = []
                        ops.append(lambda: g.tt("pool", y[:], y[:], yfl_i[:], ALU.add, [b_y, b2_i], [b_y]))
                        ops.append(lambda: g.tt("pool", t2[:].rearrange("p (h q) -> p h q", h=16),
                                                xs_i[:].rearrange("p (h q) -> p h q", h=16),
                                                dB[:, :, None].to_broadcast([128, 16, 64]), ALU.mult, [bi_i, b_c], [b_t2]))
                        ops.append(lambda: g.tt("dve", y[:], y[:], t2[:], ALU.add, [b_y, b_t2], [b_y]))
                        ops.append(lambda: g.tt("dve", y[:], y[:], zt_i[:], ALU.mult, [b_y, b2_i], [b_y]))
                        ops.append(lambda: g.act(sq[:], y[:], AF.Square, [b_y], [b_sq]))
                        ops.append(lambda: g.red("dve", gs[:, 0:2], sq[:].rearrange("p (g q) -> p g q", g=2), [b_sq], [b_gs]))
                        ops.append(lambda: g.ts("dve", gs[:, 2:4], gs[:, 0:2], 1.0 / 512, EPS, ALU.mult, ALU.add, [b_gs], [b_gs]))
                        ops.append(lambda: g.act(gs[:, 2:4], gs[:, 2:4], AF.Sqrt, [b_gs], [b_gs]))
                        ops.append(lambda: g._record("dve", lambda e: e.reciprocal(out=gs[:, 2:4], in_=gs[:, 2:4]),
                                                     [b_gs], [b_gs]))
                        ops.append(lambda: g.tt("dve", y[:].rearrange("p (g q) -> p g q", g=2),
                                                y[:].rearrange("p (g q) -> p g q", g=2),
                                                gs[:, 2:4][:, :, None].to_broadcast([128, 2, 512]), ALU.mult, [b_y, b_gs], [b_y]))
                        ops.append(lambda: g.tt("pool", ynb[:], y[:], nwB[:], ALU.mult, [b_y, b_c], [b_ynb]))

                        def trs():
                            for k in range(8):
                                g.tr(pT[:, k * 128:(k + 1) * 128], ynb[:, k * 128:(k + 1) * 128], identb[:], [b_ynb, b_c], [b_pT])
                        ops.append(trs)
                        ops.append(lambda: g.cp("act", yTs_j[:], pT[:].rearrange("p (k t) -> p k t", k=8), [b_pT], [b_yTs_j]))
                        ops.append(lambda: g.dma("pool", yT[0][:, :, tk].rearrange("k p t -> p k t"), yTs_j[:], b_yTs_j,
                                                 reads=[b_yTs_j]))
                        return ops
                    pend.extend(mk())
            drain(len(pend))
            g.flush(ENGS)

    gla_gate_up = din("gla_gate_up", [DEPTH, 2, 16, 512])
    gla_gate_b = din("gla_gate_b", [DEPTH, 2, 512])
    gla_norm_w = din("gla_norm_w", [DEPTH, 256])
    of_d = dscr("of", [NT, 1024])

    def phase_gla_pass(l, d):
        with ExitStack() as es:
            triS = sb(es, "triS", [128, 128], F32)
            uS = sb(es, "uS", [128, 128], F32)
            m01 = sb(es, "m01", [128, 128], F32)
            ones = sb(es, "ones", [128, 128], F32)
            up = sb(es, "up", [16, 512], F32)
            gbr = sb(es, "gbr", [1, 512], F32)
            fl = sb(es, "fl", [128, 2], F32)
            b_c = Buf("consts")
            qT = [sb(es, "qT%d" % i, [128, 4, 128], F32) for i in range(2)]
            kT = [sb(es, "kT%d" % i, [128, 4, 128], F32) for i in range(2)]
            ktm = [sb(es, "ktm%d" % i, [128, 512], F32) for i in range(2)]
            v = [sb(es, "v%d" % i, [128, 1024], BF16) for i in range(2)]
            dn = [sb(es, "dn%d" % i, [16, 128], F32) for i in range(2)]
            b_in = [Buf("in0"), Buf("in1")]
            lsp_ = [sb(es, "lsp%d" % i, [128, 512], F32) for i in range(2)]
            b_lsp_ = [Buf("lsp0"), Buf("lsp1")]
            eb_ = [sb(es, "eb%d" % i, [128, 4, 128], F32) for i in range(2)]
            enb_ = [sb(es, "enb%d" % i, [128, 4, 128], F32) for i in range(2)]
            er_ = [sb(es, "er%d" % i, [128, 512], F32) for i in range(2)]
            b_eb_, b_enb_, b_er_ = [Buf("eb0"), Buf("eb1")], [Buf("enb0"), Buf("enb1")], [Buf("er0"), Buf("er1")]
            qe_ = [sb(es, "qe%d" % i, [128, 4, 128], BF16) for i in range(2)]
            ke_ = [sb(es, "ke%d" % i, [128, 4, 128], BF16) for i in range(2)]
            kdec_ = [sb(es, "kdec%d" % i, [128, 512], BF16) for i in range(2)]
            att_ = [sb(es, "att%d" % i, [128, 4, 128], BF16) for i in range(2)]
            b_qe_, b_ke_ = [Buf("qe0"), Buf("qe1")], [Buf("ke0"), Buf("ke1")]
            b_kdec_, b_att_ = [Buf("kdec0"), Buf("kdec1")], [Buf("att0"), Buf("att1")]
            S = sb(es, "S", [128, 4, 256], F32)
            Sbf = sb(es, "Sbf", [128, 4, 256], BF16)
            b_S, b_Sbf = Buf("S"), Buf("Sbf")
            od = [sb(es, "od%d" % i, [128, 1024], F32) for i in range(2)]
            b_od = [Buf("od0"), Buf("od1")]
            pLR = ps(es, "pLR", [128, 512], F32)
            pB = ps(es, "pB", [128, 512], F32)
            pAt = ps(es, "pAt", [128, 512], F32)
            pOo = ps(es, "pOo", [128, 1024], F32)
            pSp = ps(es, "pSp", [128, 1024], F32)
            b_pLR, b_pB, b_pAt, b_pOo, b_pSp = Buf("pLR"), Buf("pB"), Buf("pAt"), Buf("pOo"), Buf("pSp")
            if d == 1:
                identb = sb(es, "identb", [128, 128], BF16)
                nwB = sb(es, "nwB", [128, 1024], F32)
                ofl = [sb(es, "ofl%d" % i, [128, 1024], F32) for i in range(2)]
                gg = [sb(es, "gg%d" % i, [128, 1024], F32) for i in range(2)]
                b_in2 = [Buf("in2_0"), Buf("in2_1")]
                sq = sb(es, "sq", [128, 1024], F32)
                b_sq = Buf("sq")
                gs = sb(es, "gs", [128, 8], F32)
                b_gs = Buf("gs")
                onb = sb(es, "onb", [128, 1024], BF16)
                b_onb = Buf("onb")
                yTs = [sb(es, "yTs%d" % i, [128, 8, 128], BF16) for i in range(2)]
                b_yTs = [Buf("yTs0"), Buf("yTs1")]
                pT = ps(es, "pT", [128, 1024], BF16)
                b_pT = Buf("pT")
                g.dma("pool", identb[:], ident_in, b_c, writes=[b_c])
                for h in range(4):
                    g.dma("sp", nwB[:, h * 256:(h + 1) * 256], gla_norm_w[l:l + 1, :].partition_broadcast(128), b_c,
                          writes=[b_c])
            g.dma("sp", triS[:], tri_in[d], b_c, writes=[b_c])
            g.dma("sp", uS[:], ustrict_in[d], b_c, writes=[b_c])
            g.dma("sp", m01[:], mask01_in[d], b_c, writes=[b_c])
            g.dma("sp", ones[:], ones_in, b_c, writes=[b_c])
            g.dma("sp", up[:], gla_gate_up[l, d], b_c, writes=[b_c])
            g.dma("sp", gbr[:], gla_gate_b[l, d:d + 1, :], b_c, writes=[b_c])
            g.dma("sp", fl[:], flag, b_c, writes=[b_c])
            g.ts("dve", triS[:], triS[:], -1.0 / 16.0, None, ALU.mult, None, [b_c], [b_c])
            g.ts("dve", uS[:], uS[:], -1.0 / 16.0, None, ALU.mult, None, [b_c], [b_c])
            g.memset("dve", S[:], 0.0, [b_S])
            g.memset("pool", Sbf[:], 0.0, [b_Sbf])
            last = 127 if d == 0 else 0
            order = list(range(NCH)) if d == 0 else list(range(NCH - 1, -1, -1))
            for n, c in enumerate(order):
                i = n % 2
                lsp, b_lsp, eb, b_eb, enb, b_enb, er, b_er = lsp_[i], b_lsp_[i], eb_[i], b_eb_[i], enb_[i], b_enb_[i], er_[i], b_er_[i]
                qe, b_qe, ke, b_ke, kdec, b_kdec, att, b_att = qe_[i], b_qe_[i], ke_[i], b_ke_[i], kdec_[i], b_kdec_[i], att_[i], b_att_[i]
                tk = slice(c * 128, (c + 1) * 128)
                g.dma("sp", qT[i][:], gqT[:, :, tk].rearrange("h p t -> p h t"), b_in[i], writes=[b_in[i]])
                g.dma("sp", kT[i][:], gkT[:, :, tk].rearrange("h p t -> p h t"), b_in[i], writes=[b_in[i]])
                g.dma("sp", ktm[i][:], kg_tm[tk, :], b_in[i], writes=[b_in[i]])
                g.dma("sp", v[i][:], vg_tm[tk, :], b_in[i], writes=[b_in[i]])
                g.dma("sp", dn[i][:], dnT[d, :, tk], b_in[i], writes=[b_in[i]])
                if d == 1:
                    g.dma("sp", ofl[i][:], of_d[tk, :], b_in2[i], writes=[b_in2[i]])
                    g.dma("sp", gg[i][:], gg_tm[tk, :], b_in2[i], writes=[b_in2[i]])
                if (d == 0 and c == MIDC) or (d == 1 and c == MIDC - 1):
                    g.ts("dve", S[:], S[:], fl[:, 0:1], None, ALU.mult, None, [b_S, b_c], [b_S])
                    g.ts("pool", Sbf[:], Sbf[:], fl[:, 0:1], None, ALU.mult, None, [b_Sbf, b_c], [b_Sbf])
                g.mm(pLR[:], dn[i][:], up[:], True, False, [b_in[i], b_c], [b_pLR])
                g.mm(pLR[:], ones[0:1, :], gbr[:], False, True, [b_c], [b_pLR])
                g.act(lsp[:], pLR[:], AF.Exp, [b_pLR], [b_lsp], scale=-1.0)
                g.ts("dve", lsp[:], lsp[:], 1.0, None, ALU.add, None, [b_lsp], [b_lsp])
                g.act(lsp[:], lsp[:], AF.Ln, [b_lsp], [b_lsp])
                for h in range(4):
                    g.mm(pB[:, h * 128:(h + 1) * 128], lsp[:, h * 128:(h + 1) * 128], triS[:], True, True,
                         [b_lsp, b_c], [b_pB])
                g.mm(pLR[:], uS[:], lsp[:], True, True, [b_c, b_lsp], [b_pLR])
                pB3 = pB[:].rearrange("p (h t) -> p h t", h=4)
                g.act(eb[:], pB3, AF.Exp, [b_pB], [b_eb])
                g.act(enb[:], pB3, AF.Exp, [b_pB], [b_enb], scale=-1.0)
                g.act(er[:], pLR[:], AF.Exp, [b_pLR], [b_er])
                g.tt("dve", qe[:], qT[i][:], eb[:], ALU.mult, [b_in[i], b_eb], [b_qe])
                g.tt("pool", ke[:], kT[i][:], enb[:], ALU.mult, [b_in[i], b_enb], [b_ke])
                g.tt("pool", kdec[:], ktm[i][:], er[:], ALU.mult, [b_in[i], b_er], [b_kdec])
                for h in range(4):
                    g.mm(pAt[:, h * 128:(h + 1) * 128], ke[:, h, :], qe[:, h, :], True, True, [b_ke, b_qe], [b_pAt])
                g.tt("dve", att[:], pAt[:].rearrange("p (h t) -> p h t", h=4), m01[:, None, :].to_broadcast([128, 4, 128]),
                     ALU.mult, [b_pAt, b_c], [b_att])
                for h in range(4):
                    g.mm(pOo[:, h * 256:(h + 1) * 256], att[:, h, :], v[i][:, h * 256:(h + 1) * 256], True, False,
                         [b_att, b_in[i]], [b_pOo])
                    g.mm(pOo[:, h * 256:(h + 1) * 256], qe[:, h, :], Sbf[:, h, :], False, True, [b_qe, b_Sbf], [b_pOo])
                for h in range(4):
                    g.mm(pSp[:, h * 256:(h + 1) * 256], kdec[:, h * 128:(h + 1) * 128], v[i][:, h * 256:(h + 1) * 256],
                         True, True, [b_kdec, b_in[i]], [b_pSp])
                for h in range(4):
                    g.stt("dve", S[:, h, :], S[:, h, :], eb[:, h, last:last + 1], pSp[:, h * 256:(h + 1) * 256],
                          ALU.mult, ALU.add, [b_S, b_eb, b_pSp], [b_S])
                g.cp("act", Sbf[:], S[:], [b_S], [b_Sbf])
                if d == 0:
                    g.cp("act", od[i][:], pOo[:], [b_pOo], [b_od[i]])
                    g.dma("pool", of_d[tk, :], od[i][:], b_od[i], reads=[b_od[i]])
                else:
                    o = od[i]
                    g.tt("dve", o[:], pOo[:], ofl[i][:], ALU.add, [b_pOo, b_in2[i]], [b_od[i]])
                    g.act(sq[:], o[:], AF.Square, [b_od[i]], [b_sq])
                    g.red("dve", gs[:, 0:4], sq[:].rearrange("p (h q) -> p h q", h=4), [b_sq], [b_gs])
                    g.ts("dve", gs[:, 4:8], gs[:, 0:4], 1.0 / 256, EPS, ALU.mult, ALU.add, [b_gs], [b_gs])
                    g.act(gs[:, 4:8], gs[:, 4:8], AF.Sqrt, [b_gs], [b_gs])
                    g._record("dve", lambda e, gs=gs: e.reciprocal(out=gs[:, 4:8], in_=gs[:, 4:8]), [b_gs], [b_gs])
                    g.tt("dve", o[:].rearrange("p (h q) -> p h q", h=4), o[:].rearrange("p (h q) -> p h q", h=4),
                         gs[:, 4:8][:, :, None].to_broadcast([128, 4, 256]), ALU.mult, [b_od[i], b_gs], [b_od[i]])
                    g.tt("pool", o[:], o[:], nwB[:], ALU.mult, [b_od[i], b_c], [b_od[i]])
                    g.tt("pool", onb[:], o[:], gg[i][:], ALU.mult, [b_od[i], b_in2[i]], [b_onb])
                    for k in range(8):
                        g.tr(pT[:, k * 128:(k + 1) * 128], onb[:, k * 128:(k + 1) * 128], identb[:], [b_onb, b_c], [b_pT])
                    g.cp("act", yTs[i][:], pT[:].rearrange("p (k t) -> p k t", k=8), [b_pT], [b_yTs[i]])
                    g.dma("pool", yT[1][:, :, tk].rearrange("k p t -> p k t"), yTs[i][:], b_yTs[i], reads=[b_yTs[i]])
            g.flush(ENGS)

    na_rpb = din("na_rpb", [DEPTH, 16, 15, 31])
    jflip_in = din("jflip", [64, 64])
    mint_in = din("m_int", [128, 16, 64])
    medge_in = din("m_edge", [128, 16, 64])
    rpbpad = dscr("rpbpad", [16, 17, 192])

    def phase_na(l):
        with ExitStack() as es:
            T2 = [sb(es, "T2i", [128, 16, 16, 64], BF16), sb(es, "T2e", [128, 16, 16, 64], BF16)]
            b_T2 = Buf("T2")
            Mx = [sb(es, "Mi", [128, 16, 64], F32), sb(es, "Me", [128, 16, 64], F32)]
            jf = sb(es, "jf", [64, 64], F32)
            identb = sb(es, "identb", [128, 128], BF16)
            fl = sb(es, "fl", [128, 2], F32)
            b_c = Buf("consts")
            padt = sb(es, "padt", [16, 17, 192], F32)
            rp = sb(es, "rp", [16, 15, 31], F32)
            b_pad = Buf("pad")
            Tpp = [sb(es, "Tpp%d" % i, [64, 4, 17, 64], F32) for i in range(2)]
            b_Tpp = [Buf("Tpp0"), Buf("Tpp1")]
            eW = [sb(es, "eW%d" % i, [128, 640], F32) for i in range(3)]
            b_eW = [Buf("eW%d" % i) for i in range(3)]
            Kc = [sb(es, "Kc%d" % i, [128, 8, 128], BF16) for i in range(8)]
            Vc = [sb(es, "Vc%d" % i, [128, 16, 65], BF16) for i in range(8)]
            b_K = [Buf("K%d" % i) for i in range(8)]
            b_V = [Buf("V%d" % i) for i in range(8)]
            Qc = [sb(es, "Qc%d" % i, [128, 8, 128], BF16) for i in range(2)]
            b_Q = [Buf("Q0"), Buf("Q1")]
            Pt = [sb(es, "Pt%d" % i, [128, 5, 2, 64], BF16) for i in range(3)]
            b_P = [Buf("P%d" % i) for i in range(3)]
            rden = sb(es, "rden", [128, 16], F32)
            b_rden = Buf("rden")
            oA = sb(es, "oA", [128, 1024], F32)
            oB = sb(es, "oB", [128, 1024], F32)
            b_oA, b_oB = Buf("oA"), Buf("oB")
            onb = sb(es, "onb", [128, 1024], BF16)
            b_onb = Buf("onb")
            yTs = [sb(es, "yTs%d" % i, [128, 8, 128], BF16) for i in range(2)]
            b_yTs = [Buf("yTs0"), Buf("yTs1")]
            pS_t = [ps(es, "pS%d" % i, [128, 1024], F32) for i in range(2)]
            pS = [t[:, 0:640] for t in pS_t]
            b_pS = [Buf("pS%d" % i) for i in range(2)]
            pOv = [ps(es, "pOv%d" % i, [128, 512], F32) for i in range(3)]
            b_pOv = Buf("pOv")
            pT = ps(es, "pT", [128, 1024], BF16)
            b_pT = Buf("pT")
            g.dma("sp", Mx[0][:], mint_in, b_c, writes=[b_c])
            g.dma("sp", Mx[1][:], medge_in, b_c, writes=[b_c])
            g.dma("sp", jf[:], jflip_in, b_c, writes=[b_c])
            g.dma("sp", fl[:], flag, b_c, writes=[b_c])
            g.dma("pool", identb[:], ident_in, b_c, writes=[b_c])
            for i in range(8):
                g.memset("pool", Vc[i][:], 1.0, [b_V[i]])
            g.memset("dve", padt[:], 0.0, [b_pad])
            g.dma("sp", rp[:], na_rpb[l], b_pad, writes=[b_pad])
            g.cp("dve", padt[:, 1:16, 64:95], rp[:], [b_pad], [b_pad])
            g.dma("sp", rpbpad, padt[:], b_pad, reads=[b_pad], writes=[b_pad])
            nb = 0
            for hg in range(4):
                tp = Tpp[hg % 2]
                b_tp = b_Tpp[hg % 2]
                for hh in range(4):
                    h = hg * 4 + hh
                    src = bass.AP(tensor=rpbpad.tensor, offset=h * 17 * 192 + 16, ap=[[1, 64], [192, 17], [1, 64]])
                    g.dma("sp", tp[:, hh, :, :], src, b_tp, reads=[b_pad], writes=[b_tp])
                for hh in range(4):
                    h = hg * 4 + hh
                    for sbt in range(2):
                        q = nb % 2
                        nb += 1
                        for si in range(8):
                            s_ = sbt * 8 + si
                            g.mm(pS[q][:, si * 64:(si + 1) * 64], tp[:, hh, s_:s_ + 2, :].rearrange("p r k -> p (r k)"),
                                 jf[:], True, True, [b_tp, b_c], [b_pS[q]])
                        g.act(eW[q][:, 0:512], pS[q][:, 0:512], AF.Exp, [b_pS[q]], [b_eW[q]])
                        g.tt("dve", T2[0][:, h, sbt * 8:sbt * 8 + 8, :], eW[q][:, 0:512].rearrange("p (s q) -> p s q", s=8),
                             Mx[0][:, sbt * 8:sbt * 8 + 8, :], ALU.mult, [b_eW[q], b_c], [b_T2])
                        g.tt("pool", T2[1][:, h, sbt * 8:sbt * 8 + 8, :], eW[q][:, 0:512].rearrange("p (s q) -> p s q", s=8),
                             Mx[1][:, sbt * 8:sbt * 8 + 8, :], ALU.mult, [b_eW[q], b_c], [b_T2])
            if NA_SPLIT:
                g.flush(ENGS)
            loaded = [-1]
            cnt = {"s": 0, "p": 0, "y": 0}

            def ensure(upto):
                while loaded[0] < min(upto, NCH - 1):
                    kt = loaded[0] + 1
                    sl_ = kt % 8
                    tk_ = slice(kt * 128, (kt + 1) * 128)
                    g.dma("sp", Kc[sl_][:], nkT[:, :, tk_].rearrange("c p t -> p c t"), b_K[sl_], writes=[b_K[sl_]])
                    g.dma("sp", Vc[sl_][:, :, 0:64], vn_tm[tk_, :].rearrange("p (h d) -> p h d", h=16), b_V[sl_],
                          writes=[b_V[sl_]])
                    loaded[0] = kt

            def na_pair(qp, kts, var, qi, o_dst, b_o):
                nt = len(kts)
                d0 = kts[0] - qp

                def st_s(h):
                    ch, p0 = h // 2, (h % 2) * 64
                    q = h % 2
                    for ti, kt in enumerate(kts):
                        g.mm(pS[q][:, ti * 128:(ti + 1) * 128], Kc[kt % 8][p0:p0 + 64, ch, :], Qc[qi][p0:p0 + 64, ch, :],
                             True, True, [b_K[kt % 8], b_Q[qi]], [b_pS[q]])

                def st_e(h):
                    q = h % 3
                    g.act(eW[q][:, 0:nt * 128], pS[h % 2][:, 0:nt * 128], AF.Exp, [b_pS[h % 2]], [b_eW[q]])
                    e4 = eW[q][:, 0:nt * 128].rearrange("p (t r c) -> p t r c", t=nt, r=2)
                    for qr2 in range(2):
                        s0 = 2 * d0 + 8 - qr2
                        base = T2[var][:, h, s0, :]
                        tv = bass.AP(tensor=base.tensor, offset=base.offset, ap=[list(base.ap[0]), [128, nt], [1, 64]])
                        g.tt("dve" if qr2 == 0 else "pool", Pt[q][:, 0:nt, qr2, :], e4[:, :, qr2, :], tv, ALU.mult,
                             [b_eW[q], b_T2], [b_P[q]])

                def st_v(h):
                    q = h % 3
                    bank, off = h // 7, (h % 7) * 65
                    for ti, kt in enumerate(kts):
                        g.mm(pOv[bank][:, off:off + 65], Pt[q][:, ti, :, :].rearrange("p r c -> p (r c)"),
                             Vc[kt % 8][:, h, :], ti == 0, ti == nt - 1, [b_P[q], b_V[kt % 8]], [b_pOv])

                if NA_PAIRED:
                    def st_s2(c):
                        for ti, kt in enumerate(kts):
                            for hh in range(2):
                                p0 = hh * 64
                                g.mm(pS[hh][:, ti * 128:(ti + 1) * 128], Kc[kt % 8][p0:p0 + 64, c, :],
                                     Qc[qi][p0:p0 + 64, c, :], True, True, [b_K[kt % 8], b_Q[qi]], [b_pS[hh]])
                    for c in range(9):
                        if c < 8:
                            st_s2(c)
                        if c >= 1:
                            st_v(2 * c - 2)
                            st_v(2 * c - 1)
                        if c < 8:
                            st_e(2 * c)
                            st_e(2 * c + 1)
                else:
                    LAG = NA_LAG
                    for step in range(16 + 2 * LAG):
                        if step < 16:
                            st_s(step)
                        if LAG <= step < 16 + LAG:
                            st_e(step - LAG)
                        if step >= 2 * LAG:
                            st_v(step - 2 * LAG)
                for bank in range(3):
                    nh = 7 if bank < 2 else 2
                    pv = pOv[bank][:, 0:nh * 65].rearrange("p (h e) -> p h e", h=nh)
                    g._record("dve", lambda e, bank=bank, nh=nh, pv=pv: e.reciprocal(out=rden[:, bank * 7:bank * 7 + nh],
                                                                                   in_=pv[:, :, 64]), [b_pOv], [b_rden])
                    g.tt("dve", o_dst[:, bank * 448:bank * 448 + nh * 64].rearrange("p (h d) -> p h d", h=nh), pv[:, :, 0:64],
                         rden[:, bank * 7:bank * 7 + nh][:, :, None].to_broadcast([128, nh, 64]), ALU.mult,
                         [b_pOv, b_rden], [b_o])

            for qp in range(NCH):
                qi = qp % 2
                tk = slice(qp * 128, (qp + 1) * 128)
                ensure(qp + 3)
                g.dma("sp", Qc[qi][:], nqT[:, :, tk].rearrange("c p t -> p c t"), b_Q[qi], writes=[b_Q[qi]])
                interior = [qp - 2, qp - 1, qp, qp + 1, qp + 2]
                if qp < 2:
                    na_pair(qp, [0, 1, 2, 3], 1, qi, onb, b_onb)
                elif qp >= NCH - 2:
                    na_pair(qp, [NCH - 4, NCH - 3, NCH - 2, NCH - 1], 1, qi, onb, b_onb)
                elif MIDC - 2 <= qp < MIDC + 2:
                    na_pair(qp, interior, 0, qi, oA, b_oA)
                    ekts = [MIDC - 4, MIDC - 3, MIDC - 2, MIDC - 1] if qp < MIDC else [MIDC, MIDC + 1, MIDC + 2, MIDC + 3]
                    na_pair(qp, ekts, 1, qi, oB, b_oB)
                    g.ts("dve", oA[:], oA[:], fl[:, 0:1], None, ALU.mult, None, [b_oA, b_c], [b_oA])
                    g.stt("dve", onb[:], oB[:], fl[:, 1:2], oA[:], ALU.mult, ALU.add, [b_oB, b_oA, b_c], [b_onb])
                else:
                    na_pair(qp, interior, 0, qi, onb, b_onb)
                yi = cnt["y"] % 2
                cnt["y"] += 1
                for k in range(8):
                    g.tr(pT[:, k * 128:(k + 1) * 128], onb[:, k * 128:(k + 1) * 128], identb[:], [b_onb, b_c], [b_pT])
                g.cp("act", yTs[yi][:], pT[:].rearrange("p (k t) -> p k t", k=8), [b_pT], [b_yTs[yi]])
                g.dma("pool", yT[2][:, :, tk].rearrange("k p t -> p k t"), yTs[yi][:], b_yTs[yi], reads=[b_yTs[yi]])
            g.flush(ENGS, sched=False)

    stages = []
    stages.append(("p0", phase_p0))
    for l in range(DEPTH):
        x_src = x_in if l == 0 else xc
        stages.append(("ffn1_%d" % l, lambda l=l, x_src=x_src: phase_ffn(
            "f1l%d" % l, ffn1_w_in[l], ffn1_w_out[l], x_src, hT_f1, xa, ln_mix_w[l:l + 1, :], hT_mix)))
        stages.append(("ptm_%d" % l, lambda l=l: phase_inproj_tm(l)))
        stages.append(("pfm_%d" % l, lambda l=l: phase_inproj_fm(l)))
        stages.append(("ssdprep_%d" % l, lambda l=l: phase_ssd_prep(l)))
        stages.append(("ssdf_%d" % l, lambda l=l: phase_ssd_pass(l, 0)))
        stages.append(("ssdb_%d" % l, lambda l=l: phase_ssd_pass(l, 1)))
        stages.append(("glaf_%d" % l, lambda l=l: phase_gla_pass(l, 0)))
        stages.append(("glab_%d" % l, lambda l=l: phase_gla_pass(l, 1)))
        stages.append(("na_%d" % l, lambda l=l: phase_na(l)))
        stages.append(("merge_%d" % l, lambda l=l: phase_merge(l, xa, xb, ln_ffn2_w[l:l + 1, :], hT_f2)))
        if l + 1 < DEPTH:
            stages.append(("ffn2_%d" % l, lambda l=l: phase_ffn(
                "f2l%d" % l, ffn2_w_in[l], ffn2_w_out[l], xb, hT_f2, xc, ln_ffn1_w[l + 1:l + 2, :], hT_f1)))
        else:
            stages.append(("ffn2_%d" % l, lambda l=l: phase_ffn(
                "f2l%d" % l, ffn2_w_in[l], ffn2_w_out[l], xb, hT_f2, None, ln_final_w[0:1, :], None, final_out=y_out)))
    only = getattr(cfg, "only", None)
    for name, fn in stages:
        if only is None or name in only:
            fn()
        if cfg.stop_after == name:
            break
    g.close()
    return nc, g


def _consts():
    k = np.arange(128)
    tri_f = (k[:, None] <= k[None, :]).astype(np.float32)
    tri_b = (k[:, None] >= k[None, :]).astype(np.float32)
    allow_f = (k[:, None] <= k[None, :])
    allow_b = (k[:, None] >= k[None, :])
    c = {
        "ident": np.eye(128, dtype=np.float32),
        "ones": np.ones((128, 128), np.float32),
        "tri": np.stack([tri_f, tri_b]),
        "maskadd": np.stack([np.where(allow_f, 0.0, -30000.0), np.where(allow_b, 0.0, -30000.0)]).astype(np.float32),
        "mask01": np.stack([allow_f, allow_b]).astype(np.float32),
        "ustrict": np.stack([(k[:, None] > k[None, :]), (k[:, None] < k[None, :])]).astype(np.float32),
    }
    kc = np.arange(64)[:, None]
    qc = np.arange(64)[None, :]
    cs = np.clip(qc - 8, 0, 48)
    colmask = ((kc >= cs) & (kc < cs + 16)).astype(np.float32)
    m_int = np.zeros((128, 16, 64), np.float32)
    m_edge = np.zeros((128, 16, 64), np.float32)
    for kr2 in range(2):
        for s_ in range(16):
            ro = s_ + kr2 - 1
            if 0 <= ro <= 14:
                m_edge[kr2 * 64:(kr2 + 1) * 64, s_, :] = colmask
            if 3 <= ro <= 10:
                m_int[kr2 * 64:(kr2 + 1) * 64, s_, :] = colmask
    jflip = np.zeros((64, 64), np.float32)
    jflip[63 - np.arange(64), np.arange(64)] = 1.0
    c.update({"jflip": jflip, "m_int": m_int, "m_edge": m_edge})
    return c


_PROGRAM_CACHE = {}


def run_streams(streams, flags, w, cfg_extra=None):
    nt = streams[0].shape[0]
    cfg = Cfg(nt)
    nc, g = build_program(cfg)
    base = dict(_consts())
    f32 = np.float32
    for k in ("ln_ffn1_w", "ffn1_w_in", "ffn1_w_out", "ln_mix_w", "w_in", "ssm_conv_w", "ssm_conv_b", "ssm_d",
              "ssm_norm_w", "gla_gate_up", "gla_gate_b", "gla_norm_w", "na_rpb", "w_branch_a", "w_branch_b",
              "w_branch_c", "w_out", "ln_ffn2_w", "ffn2_w_in", "ffn2_w_out"):
        base[k] = np.ascontiguousarray(np.asarray(w[k], dtype=f32))
    base["ssm_dt_bias"] = np.ascontiguousarray(np.asarray(w["ssm_dt_bias"], f32).reshape(DEPTH, 32))
    base["ssm_a_log"] = np.ascontiguousarray(np.asarray(w["ssm_a_log"], f32).reshape(DEPTH, 32))
    base["ln_final_w"] = np.ascontiguousarray(np.asarray(w["ln_final_w"], f32).reshape(1, D))
    in_maps = []
    for x, f in zip(streams, flags):
        m = dict(base)
        m["x"] = np.ascontiguousarray(np.asarray(x, f32))
        m["flag"] = np.tile(np.array([[f, 1.0 - f]], f32), (128, 1))
        in_maps.append(m)
    if SPREAD_DIES and len(in_maps) == 4:
        idle = dict(base)
        idle["x"] = np.zeros((nt, D), f32)
        idle["flag"] = np.tile(np.array([[0.0, 1.0]], f32), (128, 1))
        placed = [in_maps[0], in_maps[1], idle, idle, in_maps[2], in_maps[3], idle, idle]
        res = run_bass_kernel_spmd(nc, placed, core_ids=list(range(8)))
        outs = [res.results[i] for i in (0, 1, 4, 5)]
    else:
        res = run_bass_kernel_spmd(nc, in_maps, core_ids=list(range(len(in_maps))))
        outs = res.results
    return [np.asarray(r["y"], dtype=f32) for r in outs]


def kernel(**inputs):
    xp = np.asarray(inputs["x_prompt"], np.float32)
    xs = np.asarray(inputs["x_sample"], np.float32)
    B, S, _ = xp.shape
    B2, S2, _ = xs.shape
    assert S2 == 2 * S and B % 2 == 0
    streams, flags = [], []
    for i in range(B // 2):
        streams.append(xp[2 * i:2 * i + 2].reshape(2 * S, D))
        flags.append(0.0)
    for i in range(B2):
        streams.append(xs[i])
        flags.append(1.0)
    outs = run_streams(streams, flags, inputs)
    yp = np.stack([o.reshape(2, S, D) for o in outs[:B // 2]]).reshape(B, S, D)
    ys = np.stack(outs[B // 2:]).reshape(B2, S2, D)
    return (yp.astype(np.float32), ys.astype(np.float32))
```
